# Optimizing a Trainium2 kernel written in Bass

```python
import jax, jax.numpy as jnp
from jax import lax
import numpy as np

D_MODEL = 1024
BATCH = 32
SEQ = 256
DEPTH = 4
DEC_BATCH = 4
DEC_SEQ = 1024
PAST_LEN = 512

GRID_W = 64
N_MIXERS = 4
N_HGRN = (DEPTH + 3) // 4
N_SCONV = (DEPTH + 2) // 4
N_RGLRU = (DEPTH + 1) // 4
N_MLA = DEPTH // 4
DEEPNORM_ALPHA = (2 * DEPTH) ** 0.25
DEEPNORM_BETA = (8 * DEPTH) ** -0.25

HG_HEADS = 8
HG_KDIM = 128
HG_VDIM = D_MODEL // HG_HEADS
HG_F = HG_HEADS * HG_KDIM
HG_V = HG_HEADS * HG_VDIM
HG_CHUNK = 32
SC_WIDTH = D_MODEL
SC_KERNEL = 3
RG_WIDTH = D_MODEL
RG_HEADS = 4
RG_BLOCK = RG_WIDTH // RG_HEADS
RG_KERNEL = 4
RG_C = 8.0
MLA_HEADS = 16
MLA_Q_RANK = 384
MLA_KV_RANK = 256
MLA_NOPE = 64
MLA_ROPE = 32
MLA_VDIM = 64
ROPE_BASE = 10000.0
Q_BLOCK = 128

kernel_name = 'hybrid_flow_backbone_step'

f32 = jnp.float32


def layer_norm(x, g, b, eps=1e-5):
    xf = x.astype(f32)
    mu = jnp.mean(xf, -1, keepdims=True)
    var = jnp.mean(jnp.square(xf - mu), -1, keepdims=True)
    return ((xf - mu) * lax.rsqrt(var + eps) * g.astype(f32) + b.astype(f32)).astype(x.dtype)


def rms_norm(x, g, eps=1e-6):
    xf = x.astype(f32)
    y = xf * lax.rsqrt(jnp.mean(xf * xf, -1, keepdims=True) + eps)
    return (y * g.astype(f32)).astype(x.dtype)


def modulation(cvec, w, b):
    return jnp.split(jax.nn.silu(cvec) @ w + b, 3, axis=-1)


def depthwise_conv(x, w, b, pad):
    y = lax.conv_general_dilated(x, w[:, None, :].astype(x.dtype), (1,), [pad],
                                 dimension_numbers=('NWC', 'WIO', 'NWC'),
                                 feature_group_count=x.shape[-1])
    return y + b


def axial_rope(x):
    n = x.shape[1]
    rows = n // GRID_W
    row = jnp.repeat(jnp.arange(rows), GRID_W).astype(f32)
    col = jnp.tile(jnp.arange(GRID_W), rows).astype(f32)
    half = MLA_ROPE // 2
    inv_freq = ROPE_BASE ** (-jnp.arange(0, half, 2, dtype=f32) / half)
    bshape = (n,) + (1,) * (x.ndim - 3) + (half // 2,)
    parts = []
    for pos, xa in ((row, x[..., :half]), (col, x[..., half:])):
        ang = (pos[:, None] * inv_freq[None]).reshape(bshape)
        cos, sin = jnp.cos(ang).astype(x.dtype), jnp.sin(ang).astype(x.dtype)
        x1, x2 = xa[..., :half // 2], xa[..., half // 2:]
        parts += [x1 * cos - x2 * sin, x1 * sin + x2 * cos]
    return jnp.concatenate(parts, -1)


def blocked_attention(q, k, v, scale):
    b_, nq, hh, dq = q.shape
    qb = q.reshape(b_, nq // Q_BLOCK, Q_BLOCK, hh, dq).swapaxes(0, 1)

    def one_block(qi):
        s = jnp.einsum('bqhd,bkhd->bhqk', qi, k).astype(f32) * scale
        p = jax.nn.softmax(s, axis=-1).astype(v.dtype)
        return jnp.einsum('bhqk,bkhd->bqhd', p, v)

    o = lax.map(one_block, qb)
    return o.swapaxes(0, 1).reshape(b_, nq, hh, v.shape[-1])


def gla_chunk(q, k, v, logf, s0):
    b_, n, hh, dk = q.shape
    dv = v.shape[-1]
    nc = n // HG_CHUNK
    q, k, v, logf = [t.reshape(b_, nc, HG_CHUNK, hh, t.shape[-1]) for t in (q, k, v, logf)]
    cum = jnp.cumsum(logf, axis=2)
    ref = cum[:, :, HG_CHUNK // 2 - 1:HG_CHUNK // 2]
    q_rel = q * jnp.exp(cum - ref)
    k_rel = k * jnp.exp(ref - cum)
    causal = jnp.tril(jnp.ones((HG_CHUNK, HG_CHUNK), dtype=bool))
    att = jnp.where(causal, jnp.einsum('bnthk,bnshk->bnhts', q_rel, k_rel), 0)
    o = jnp.einsum('bnhts,bnshv->bnthv', att.astype(v.dtype), v)
    total = cum[:, :, -1]
    upd = jnp.einsum('bnshk,bnshv->bnhkv', k * jnp.exp(total[:, :, None] - cum), v)

    def step(s, inp):
        dec, u = inp
        return dec[..., None] * s + u, s

    s_fin, s_prev = lax.scan(step, s0, (jnp.exp(total).swapaxes(0, 1), upd.swapaxes(0, 1)))
    o = o + jnp.einsum('bnthk,nbhkv->bnthv', q * jnp.exp(cum), s_prev)
    return o.reshape(b_, n, hh, dv), s_fin


def hgrn2_mixer(h, w_in, lb, norm_g, w_out, s0):
    b_, n, _ = h.shape
    q, f_fw, f_bw, i, g = jnp.split(h @ w_in, [HG_F, 2 * HG_F, 3 * HG_F, 3 * HG_F + HG_V], axis=-1)
    q = (jax.nn.silu(q) * HG_KDIM ** -0.5).reshape(b_, n, HG_HEADS, HG_KDIM)
    v = i.reshape(b_, n, HG_HEADS, HG_VDIM)
    outs, finals = [], []
    for d, fl in enumerate((f_fw, f_bw)):
        f = (lb[d] + (1 - lb[d]) * jax.nn.sigmoid(fl)).reshape(b_, n, HG_HEADS, HG_KDIM)
        args = (q, 1 - f, v, jnp.log(f))
        if d == 1:
            args = tuple(jnp.flip(t, 1) for t in args)
        o, s = gla_chunk(*args, s0[:, d])
        outs.append(jnp.flip(o, 1) if d == 1 else o)
        finals.append(s)
    o = rms_norm(outs[0] + outs[1], norm_g.reshape(HG_HEADS, HG_VDIM))
    y = (o.reshape(b_, n, HG_V) * jax.nn.silu(g)) @ w_out
    return y, jnp.stack(finals, axis=1)


def short_conv_mixer(h, w_in, conv_w, conv_b, w_out):
    bg, cg, v, g = jnp.split(h @ w_in, 4, axis=-1)
    z = depthwise_conv(cg * v, conv_w, conv_b, (SC_KERNEL // 2, SC_KERNEL // 2))
    return (jax.nn.silu(g) * bg * z) @ w_out


def rglru_mixer(h, w_in, conv_w, conv_b, w_gate, b_gate, lam, w_out, h0):
    b_, n, _ = h.shape
    u, g = jnp.split(h @ w_in, 2, axis=-1)
    u = depthwise_conv(u, conv_w, conv_b, (RG_KERNEL // 2, RG_KERNEL - 1 - RG_KERNEL // 2))
    ub = u.reshape(b_, n, RG_HEADS, RG_BLOCK)
    gates = jax.nn.sigmoid(jnp.einsum('bnhi,dhio->bndho', ub, w_gate) + b_gate)
    r = gates[..., :RG_BLOCK].reshape(b_, n, 2, RG_WIDTH)
    i = gates[..., RG_BLOCK:].reshape(b_, n, 2, RG_WIDTH)
    a = jnp.exp(-RG_C * jax.nn.softplus(-lam) * r)
    xin = jnp.sqrt(1 - a * a) * (i * u[:, :, None])

    def comb(lhs, rhs):
        return lhs[0] * rhs[0], rhs[0] * lhs[1] + rhs[1]

    hs = []
    for d in range(2):
        a_cum, b_cum = lax.associative_scan(comb, (a[:, :, d], xin[:, :, d]), axis=1, reverse=(d == 1))
        hs.append(a_cum * h0[:, d, None] + b_cum)
    y = ((hs[0] + hs[1]) * jax.nn.silu(g)) @ w_out
    return y, jnp.stack([hs[0][:, -1], hs[1][:, 0]], axis=1)


def mla_kv(ckv, kpe, w_kvb):
    b_, n, _ = ckv.shape
    kv = (ckv @ w_kvb).reshape(b_, n, MLA_HEADS, MLA_NOPE + MLA_VDIM)
    k = jnp.concatenate([kv[..., :MLA_NOPE],
                         jnp.broadcast_to(kpe[:, :, None], (b_, n, MLA_HEADS, MLA_ROPE))], -1)
    return k, kv[..., MLA_NOPE:]


def mla_mixer(h, w_in, q_norm, kv_norm, w_qb, w_kvb, w_out, ctx_ckv=None, ctx_kpe=None):
    b_, n, _ = h.shape
    cq, ckv, kpe, g = jnp.split(h @ w_in, [MLA_Q_RANK, MLA_Q_RANK + MLA_KV_RANK,
                                           MLA_Q_RANK + MLA_KV_RANK + MLA_ROPE], axis=-1)
    q = (rms_norm(cq, q_norm) @ w_qb).reshape(b_, n, MLA_HEADS, MLA_NOPE + MLA_ROPE)
    ckv = rms_norm(ckv, kv_norm)
    if ctx_ckv is None:
        k, v = mla_kv(ckv, kpe, w_kvb)
    else:
        q = jnp.concatenate([q[..., :MLA_NOPE], axial_rope(q[..., MLA_NOPE:])], -1)
        k_lat, v_lat = mla_kv(ckv, axial_rope(kpe), w_kvb)
        k_ctx, v_ctx = mla_kv(ctx_ckv, ctx_kpe, w_kvb)
        k = jnp.concatenate([k_ctx, k_lat], axis=1)
        v = jnp.concatenate([v_ctx, v_lat], axis=1)
    o = blocked_attention(q, k, v, (MLA_NOPE + MLA_ROPE) ** -0.5)
    y = (o.reshape(b_, n, MLA_HEADS * MLA_VDIM) * jax.nn.silu(g)) @ w_out
    return y, ckv, kpe


def setup_inputs(seed: int = 0) -> dict:
    key = jax.random.key(seed)
    ks = iter(jax.random.split(key, 48))
    D = D_MODEL

    def nrm(shape, s):
        return jax.random.normal(next(ks), shape, f32) * s

    u = jax.random.uniform(next(ks), (N_RGLRU, 2, RG_WIDTH), f32, 0.9, 0.999)
    base = u ** (1.0 / RG_C)
    rg_lambda = jnp.log(base) - jnp.log1p(-base)
    beta = DEEPNORM_BETA
    return {
        'x_prompt': nrm((BATCH, SEQ, D), 1.0),
        'x_sample': nrm((DEC_BATCH, DEC_SEQ, D), 1.0),
        'state_hgrn': nrm((DEC_BATCH, N_HGRN, 2, HG_HEADS, HG_KDIM, HG_VDIM), 0.5),
        'state_rglru': nrm((DEC_BATCH, N_RGLRU, 2, RG_WIDTH), 0.5),
        'cache_mla_ckv': nrm((DEC_BATCH, N_MLA, PAST_LEN, MLA_KV_RANK), 1.0),
        'cache_mla_kpe': nrm((DEC_BATCH, N_MLA, PAST_LEN, MLA_ROPE), 1.0),
        'c': nrm((DEC_BATCH, D), 1.0),
        'c_ctx': nrm((D,), 1.0),
        'ada_w': nrm((DEPTH, D, 3 * D), D ** -0.5),
        'ada_b': nrm((DEPTH, 3 * D), 0.02),
        'ln_g': 1.0 + nrm((DEPTH, D), 0.02),
        'ln_b': nrm((DEPTH, D), 0.02),
        'hg_w_in': nrm((N_HGRN, D, 3 * HG_F + 2 * HG_V), D ** -0.5),
        'hg_lb_logits': nrm((2, DEPTH + 1, HG_F), 0.1),
        'hg_norm_g': 1.0 + nrm((N_HGRN, HG_V), 0.02),
        'hg_w_out': nrm((N_HGRN, HG_V, D), beta * HG_V ** -0.5),
        'sc_w_in': nrm((N_SCONV, D, 4 * SC_WIDTH), D ** -0.5),
        'sc_conv_w': nrm((N_SCONV, SC_KERNEL, SC_WIDTH), SC_KERNEL ** -0.5),
        'sc_conv_b': nrm((N_SCONV, SC_WIDTH), 0.02),
        'sc_w_out': nrm((N_SCONV, SC_WIDTH, D), beta * SC_WIDTH ** -0.5),
        'rg_w_in': nrm((N_RGLRU, D, 2 * RG_WIDTH), D ** -0.5),
        'rg_conv_w': nrm((N_RGLRU, RG_KERNEL, RG_WIDTH), RG_KERNEL ** -0.5),
        'rg_conv_b': nrm((N_RGLRU, RG_WIDTH), 0.02),
        'rg_w_gate': nrm((N_RGLRU, 2, RG_HEADS, RG_BLOCK, 2 * RG_BLOCK), RG_BLOCK ** -0.5),
        'rg_b_gate': nrm((N_RGLRU, 2, RG_HEADS, 2 * RG_BLOCK), 0.02),
        'rg_lambda': rg_lambda,
        'rg_w_out': nrm((N_RGLRU, RG_WIDTH, D), beta * RG_WIDTH ** -0.5),
        'mla_w_in': nrm((N_MLA, D, MLA_Q_RANK + MLA_KV_RANK + MLA_ROPE + MLA_HEADS * MLA_VDIM), D ** -0.5),
        'mla_q_norm': 1.0 + nrm((N_MLA, MLA_Q_RANK), 0.02),
        'mla_kv_norm': 1.0 + nrm((N_MLA, MLA_KV_RANK), 0.02),
        'mla_w_qb': nrm((N_MLA, MLA_Q_RANK, MLA_HEADS * (MLA_NOPE + MLA_ROPE)), MLA_Q_RANK ** -0.5),
        'mla_w_kvb': nrm((N_MLA, MLA_KV_RANK, MLA_HEADS * (MLA_NOPE + MLA_VDIM)), MLA_KV_RANK ** -0.5),
        'mla_w_out': nrm((N_MLA, MLA_HEADS * MLA_VDIM, D), beta * (MLA_HEADS * MLA_VDIM) ** -0.5),
    }


def reference(x_prompt, x_sample, state_hgrn, state_rglru, cache_mla_ckv, cache_mla_kpe, c, c_ctx,
              ada_w, ada_b, ln_g, ln_b, hg_w_in, hg_lb_logits, hg_norm_g, hg_w_out,
              sc_w_in, sc_conv_w, sc_conv_b, sc_w_out,
              rg_w_in, rg_conv_w, rg_conv_b, rg_w_gate, rg_b_gate, rg_lambda, rg_w_out,
              mla_w_in, mla_q_norm, mla_kv_norm, mla_w_qb, mla_w_kvb, mla_w_out):
    hg_lb = jnp.cumsum(jax.nn.softmax(hg_lb_logits.astype(f32), axis=1), axis=1)

    x = x_prompt
    bsz = x.shape[0]
    hg_new, rg_new, ckv_new, kpe_new = [], [], [], []
    for l in range(DEPTH):
        m, j = l % N_MIXERS, l // N_MIXERS
        shift, scale, gate = modulation(c_ctx, ada_w[l], ada_b[l])
        h = x * (1 + scale) + shift
        if m == 0:
            s0 = jnp.zeros((bsz, 2, HG_HEADS, HG_KDIM, HG_VDIM), x.dtype)
            out, s = hgrn2_mixer(h, hg_w_in[j], hg_lb[:, l].astype(x.dtype), hg_norm_g[j], hg_w_out[j], s0)
            hg_new.append(s)
        elif m == 1:
            out = short_conv_mixer(h, sc_w_in[j], sc_conv_w[j], sc_conv_b[j], sc_w_out[j])
        elif m == 2:
            h0 = jnp.zeros((bsz, 2, RG_WIDTH), x.dtype)
            out, s = rglru_mixer(h, rg_w_in[j], rg_conv_w[j], rg_conv_b[j], rg_w_gate[j], rg_b_gate[j],
                                 rg_lambda[j], rg_w_out[j], h0)
            rg_new.append(s)
        else:
            out, ckv, kpe = mla_mixer(h, mla_w_in[j], mla_q_norm[j], mla_kv_norm[j], mla_w_qb[j],
                                      mla_w_kvb[j], mla_w_out[j])
            ckv_new.append(ckv)
            kpe_new.append(kpe)
        x = layer_norm(DEEPNORM_ALPHA * x + gate * out, ln_g[l], ln_b[l])
    y_prompt = x

    x = x_sample
    for l in range(DEPTH):
        m, j = l % N_MIXERS, l // N_MIXERS
        shift, scale, gate = [t[:, None] for t in modulation(c, ada_w[l], ada_b[l])]
        h = x * (1 + scale) + shift
        if m == 0:
            out, _ = hgrn2_mixer(h, hg_w_in[j], hg_lb[:, l].astype(x.dtype), hg_norm_g[j], hg_w_out[j],
                                 state_hgrn[:, j])
        elif m == 1:
            out = short_conv_mixer(h, sc_w_in[j], sc_conv_w[j], sc_conv_b[j], sc_w_out[j])
        elif m == 2:
            out, _ = rglru_mixer(h, rg_w_in[j], rg_conv_w[j], rg_conv_b[j], rg_w_gate[j], rg_b_gate[j],
                                 rg_lambda[j], rg_w_out[j], state_rglru[:, j])
        else:
            out, _, _ = mla_mixer(h, mla_w_in[j], mla_q_norm[j], mla_kv_norm[j], mla_w_qb[j],
                                  mla_w_kvb[j], mla_w_out[j], cache_mla_ckv[:, j], cache_mla_kpe[:, j])
        x = layer_norm(DEEPNORM_ALPHA * x + gate * out, ln_g[l], ln_b[l])
    y_sample = x

    new_state_hgrn = jnp.stack(hg_new, axis=1)
    new_state_rglru = jnp.stack(rg_new, axis=1)
    new_cache_mla_ckv = jnp.stack(ckv_new, axis=1)
    new_cache_mla_kpe = jnp.stack(kpe_new, axis=1)
    return (y_prompt, y_sample, new_state_hgrn, new_state_rglru, new_cache_mla_ckv, new_cache_mla_kpe)
```

```python
import numpy as np
import concourse.bass as bass
import concourse.mybir as mybir
from concourse.bass_utils import run_bass_kernel_spmd
from contextlib import ExitStack
F32 = mybir.dt.float32; BF16 = mybir.dt.bfloat16
AF = mybir.ActivationFunctionType; ALU = mybir.AluOpType
AX = mybir.AxisListType

D = 1024; T = 2048; NT = 16; ALPHA = 8.0 ** 0.25
LN_EPS = 1e-5 / (ALPHA * ALPHA)

class Prog:
    ENGS = ("pe", "act", "dve", "pool", "sp")
    SEM_M = 1000
    DSEM_MAX = 1600
    def __init__(self, nc, es):
        self.nc = nc; self.es = es
        self.q = {e: [] for e in self.ENGS}
        self.cnt = {e: 0 for e in self.ENGS}
        self.esems = {}
        self.seen = {e: {} for e in self.ENGS}
        self.lastw = {}; self.readers = {}
        self.dsems = {}
        self.dall = {}
        self.semh = {}
    def sb(self, name, shape, dt, es=None):
        self._names = getattr(self, "_names", {})
        n = self._names.get(name, 0); self._names[name] = n + 1
        if n: name = f"{name}__{n}"
        return (es or self.es).enter_context(self.nc.sbuf_tensor(name, list(shape), dt))
    def ps(self, name, shape, dt):
        return self.es.enter_context(self.nc.psum_tensor(name, list(shape), dt))
    def esem(self, eng, epoch):
        k = (eng, epoch)
        if k not in self.esems:
            self.esems[k] = self.es.enter_context(self.nc.semaphore(f"s_{eng}_{epoch}"))
        return self.esems[k]
    def dsem(self, name):
        d = self.dsems.get(name)
        if d is None or d[1] + 16 > self.DSEM_MAX:
            ep = 0 if d is None else d[2] + 1
            h = self.es.enter_context(self.nc.semaphore(f"d_{name}_{ep}"))
            d = [h, 0, ep]
            self.dsems[name] = d
            self.semh[f"d_{name}#{ep}"] = h
            self.dall.setdefault(name, []).append(d)
        return d
    def _handle(self, sk, val):
        if sk in self.ENGS:
            ep = (val - 1) // self.SEM_M
            return (self.esem(sk, ep), val - ep * self.SEM_M)
        return (self.semh[sk], val)
    def _need(self, eng, waits, dep):
        if dep is None: return
        sk, val = dep
        if sk == "pe" and eng == "pe": return
        if self.seen[eng].get(sk, 0) >= val: return
        waits[sk] = max(waits.get(sk, 0), val)
    def _deps(self, eng, reads, writes):
        waits = {}
        for k in reads: self._need(eng, waits, self.lastw.get(k))
        for k in writes:
            self._need(eng, waits, self.lastw.get(k))
            for sk, v in self.readers.get(k, {}).items(): self._need(eng, waits, (sk, v))
        for sk, v in waits.items(): self.seen[eng][sk] = v
        return [self._handle(sk, v) for sk, v in waits.items()]
    def _mark(self, dep, reads, writes):
        for k in writes:
            self.lastw[k] = dep; self.readers[k] = {}
        for k in reads:
            self.readers.setdefault(k, {})[dep[0]] = dep[1]
    def op(self, eng, fns, reads=(), writes=()):
        writes = list(writes) + [k for k in reads if k.startswith("ps") and k not in writes]
        waits = self._deps(eng, reads, writes)
        self.cnt[eng] += 1
        idx = self.cnt[eng]
        h, _ = self._handle(eng, idx)
        self.q[eng].append((fns, waits, (h, 1)))
        self._mark((eng, idx), reads, writes)
    @staticmethod
    def _mk(method, kw):
        def fn(e):
            return getattr(e, method)(**kw)
        return fn
    def I(self, eng, method, reads=(), writes=(), **kw):
        self.op(eng, [self._mk(method, kw)], reads, writes)
    def G(self, eng, items, reads=(), writes=()):
        self.op(eng, [self._mk(m, kw) for (m, kw) in items], reads, writes)
    def D(self, queue, semname, out, in_, reads=(), writes=(), **kw):
        waits = self._deps(queue, reads, writes)
        d = self.dsem(semname)
        d[1] += 16
        self.q[queue].append(([self._mk("dma_start", dict(out=out, in_=in_, **kw))], waits, (d[0], 16)))
        self._mark((f"d_{semname}#{d[2]}", d[1]), reads, writes)
    def barrier(self, engs=("pe", "act", "dve", "pool", "sp")):
        targets = [(e, self.cnt[e]) for e in ("pe", "act", "dve", "pool") if self.cnt[e] > 0]
        for n, lst in self.dall.items():
            for d in lst:
                if d[1] > 0: targets.append((f"d_{n}#{d[2]}", d[1]))
        for e in engs:
            waits = {}
            for dep in targets:
                if dep[0] == e and e == "pe": continue
                if self.seen[e].get(dep[0], 0) >= dep[1]: continue
                waits[dep[0]] = dep[1]; self.seen[e][dep[0]] = dep[1]
            if waits:
                self.q[e].append((None, [self._handle(sk, v) for sk, v in waits.items()], None))
    def final_wait(self, queue, semnames):
        for n in semnames:
            for d in self.dall[n]:
                self.q[queue].append((None, [(d[0], d[1])], None))
    def emit(self, block):
        def run(e, lst):
            for fns, waits, inc in lst:
                for (h, v) in waits: e.wait_ge(h, v)
                if fns is None: continue
                for i, fn in enumerate(fns):
                    ins = fn(e)
                    if i == len(fns) - 1 and inc is not None: ins.then_inc(inc[0], inc[1])
        @block.tensor
        def _(e): run(e, self.q["pe"])
        @block.scalar
        def _(e): run(e, self.q["act"])
        @block.vector
        def _(e): run(e, self.q["dve"])
        @block.gpsimd
        def _(e): run(e, self.q["pool"])
        @block.sync
        def _(e): run(e, self.q["sp"])


class Ctx:
    pass

def mmK(P, out, pairs, reads, writes):
    n = len(pairs)
    P.G("pe", [("matmul", dict(out=out, lhsT=a, rhs=b, start=(i == 0), stop=(i == n - 1))) for i, (a, b) in enumerate(pairs)],
        reads=reads, writes=writes)

class WRing:
    def __init__(self, P, nslot=3, elems=4096):
        self.P = P; self.n = nslot; self.i = 0
        self.bufs = [P.sb(f"wr{i}", [128, elems], BF16) for i in range(nslot)]
    def load(self, dram_ap, shape_str, **dims):
        s = self.i % self.n; self.i += 1
        shp = dram_ap.shape
        n = 1
        for v in shp[1:]: n *= v
        view = self.bufs[s][:, 0:n]
        if len(shp) == 3:
            view = view.rearrange("p (a b) -> p a b", a=shp[1])
        elif len(shp) == 4:
            view = view.rearrange("p (a b c) -> p a b c", a=shp[1], b=shp[2])
        key = f"wr{s}"
        self.P.D("pool", key, view, dram_ap, writes=[key])
        return view, key

    def load_multi(self, aps):
        s = self.i % self.n; self.i += 1
        J = len(aps); K_, N_ = aps[0].shape[1], aps[0].shape[2]
        view = self.bufs[s][:, 0:K_ * J * N_].rearrange("p (a b c) -> p a b c", a=K_, b=J)
        key = f"wr{s}"
        for j, ap in enumerate(aps):
            self.P.D("pool", key, view[:, :, j, :], ap, writes=[key])
        return view, key

    def load_parts(self, aps):
        s = self.i % self.n; self.i += 1
        key = f"wr{s}"; off = 0; views = []
        for ap in aps:
            a, b = ap.shape[1], ap.shape[2]
            v = self.bufs[s][:, off:off + a * b].rearrange("p (a b) -> p a b", a=a)
            off += a * b
            self.P.D("pool", key, v, ap, writes=[key])
            views.append(v)
        assert off <= 4096
        return views, key
def declare_io(nc, C):
    def din(name, shape):
        return nc.dram_tensor(name, list(shape), F32, kind="ExternalInput").ap()
    def dout(name, shape):
        return nc.dram_tensor(name, list(shape), F32, kind="ExternalOutput").ap()
    C.xin = din("xin", [T, D]); C.condT = din("condT", [128, 8, 2])
    C.ada_w = din("ada_w", [4, D, 3 * D]); C.ada_bT = din("ada_bT", [128, 4, 24])
    C.ln_gT = din("ln_gT", [128, 4, 8]); C.ln_bT = din("ln_bT", [128, 4, 8])
    C.ident = din("ident", [128, 128]); C.ln_g_row = din("ln_g_row", [4, D]); C.ln_b_row = din("ln_b_row", [4, D])
    C.sc_w_in = din("sc_w_in", [D, 4 * D]); C.sc_cw = din("sc_cw", [128, 3, 8]); C.sc_cb = din("sc_cb", [128, 8])
    C.sc_w_out = din("sc_w_out", [D, D])
    C.rg_w_in = din("rg_w_in", [D, 2 * D]); C.rg_cw = din("rg_cw", [128, 4, 8]); C.rg_cb = din("rg_cb", [128, 8])
    C.rg_w_gate = din("rg_w_gate", [2, 4, 256, 512]); C.rg_bg = din("rg_bg", [128, 2, 4, 4])
    C.rg_lam = din("rg_lam", [128, 2, 8]); C.rg_w_out = din("rg_w_out", [D, D])
    C.rg_h0 = din("rg_h0", [128, 2, 8])
    C.hg_w_in = din("hg_w_in", [D, 5 * D]); C.hg_lbl = din("hg_lbl", [128, 16, 5]); C.hg_ng = din("hg_ng", [128, 8])
    C.hg_w_out = din("hg_w_out", [D, D]); C.hg_s0 = din("hg_s0", [2, 8, 128, 128])
    C.hg_masks = din("hg_masks", [128, 4, 128])
    C.mla_w_in = din("mla_w_in", [D, 1696]); C.mla_qn = din("mla_qn", [128, 3]); C.mla_kvn = din("mla_kvn", [128, 2])
    C.mla_w_qb = din("mla_w_qb", [384, 1536]); C.mla_w_qbsw = din("mla_w_qbsw", [384, 512])
    C.mla_w_kpesw = din("mla_w_kpesw", [D, 32])
    C.mla_w_kvb = din("mla_w_kvb", [256, 2048]); C.mla_w_out = din("mla_w_out", [D, D])
    C.mla_ckv_ctx = din("mla_ckv_ctx", [512, 256]); C.mla_kpe_ctx = din("mla_kpe_ctx", [512, 32])
    C.rope_cs = din("rope_cs", [128, 2, 1024])
    C.y = dout("y", [T, D])
    C.o_hg = dout("o_hg", [4, 2, 8, 128, 128]); C.o_rg = dout("o_rg", [8, D])
    C.o_ckv = dout("o_ckv", [1024, 256]); C.o_kpe = dout("o_kpe", [1024, 32])

def setup_persistent(P, C):
    C.xT = P.sb("xT", [128, 8, T], F32)
    C.hT = P.sb("hT", [128, 8, T], BF16)
    C.mT = P.sb("mT", [128, 8, T], BF16)
    C.ring = WRing(P, nslot=3, elems=4096)
    C.identf = P.sb("identf", [128, 128], F32)
    C.identb = P.sb("identb", [128, 128], BF16)
    C.onesb = P.sb("onesb", [128, 128], BF16)
    C.condf = P.sb("condf", [128, 8, 2], F32)
    C.scond = P.sb("scond", [128, 8, 2], BF16)
    C.adab = P.sb("adab", [128, 4, 24], F32)
    C.lng = P.sb("lng", [128, 4, 8], F32); C.lnb = P.sb("lnb", [128, 4, 8], F32)
    C.mod = P.sb("mod", [128, 24, 2], F32)
    C.colsb = [P.sb(f"cols{i}", [128, 3, 8, 2], F32) for i in range(2)]
    C.ps = [P.ps(f"ps{i}", [128, 512], F32) for i in range(8)]
    P.D("sp", "identf", C.identf[:], C.ident[:, :], writes=["identf"])
    P.D("sp", "condf", C.condf[:], C.condT[:, :, :], writes=["condf"])
    P.D("sp", "adab", C.adab[:], C.ada_bT[:, :, :], writes=["adab"])
    P.D("sp", "lng", C.lng[:], C.ln_gT[:, :, :], writes=["lng"])
    P.D("sp", "lnb", C.lnb[:], C.ln_bT[:, :, :], writes=["lnb"])
    P.I("dve", "tensor_copy", reads=["identf"], writes=["identb"], out=C.identb[:], in_=C.identf[:])
    P.I("dve", "memset", writes=["onesb"], ap=C.onesb[:], constant=1.0)
    P.I("act", "activation", reads=["condf"], writes=["scond"], out=C.scond[:], in_=C.condf[:], func=AF.Silu)

def xk(g, fc):
    return f"xT{g}_{fc}"

def input_transposes(P, C):
    with ExitStack() as es:
        st = [P.sb(f"xst{i}", [128, D], F32, es) for i in range(2)]
        for t in range(NT):
            b = t % 2
            P.D("sp", f"xst{b}", st[b][:], C.xin[t * 128:(t + 1) * 128, :], writes=[f"xst{b}"])
            for half in range(2):
                pb = C.ps[(t * 2 + half) % 4]; pk = f"ps{(t * 2 + half) % 4}"
                P.G("pe", [("transpose", dict(out=pb[:, i * 128:(i + 1) * 128], in_=st[b][:, (half * 4 + i) * 128:(half * 4 + i + 1) * 128],
                                              identity=C.identf[:])) for i in range(4)], reads=[f"xst{b}", "identf"], writes=[pk])
                eng = "act" if half == 0 else "dve"
                outap = C.xT[:, half * 4:half * 4 + 4, t * 128:(t + 1) * 128]
                inap = pb[:].rearrange("p (c t) -> p c t", c=4)
                if eng == "act":
                    P.I("act", "activation", reads=[pk], writes=[xk(t // 4, half * 4 + i) for i in range(4)], out=outap, in_=inap, func=AF.Copy)
                else:
                    P.I("dve", "tensor_copy", reads=[pk], writes=[xk(t // 4, half * 4 + i) for i in range(4)], out=outap, in_=inap)
        P.barrier()

def output_transposes(P, C):
    with ExitStack() as es:
        st = [P.sb(f"yst{i}", [128, D], F32, es) for i in range(2)]
        for t in range(NT):
            b = t % 2
            for half in range(2):
                pb = C.ps[(t * 2 + half) % 4]; pk = f"ps{(t * 2 + half) % 4}"
                P.G("pe", [("transpose", dict(out=pb[:, i * 128:(i + 1) * 128], in_=C.xT[:, half * 4 + i, t * 128:(t + 1) * 128],
                                              identity=C.identf[:])) for i in range(4)], reads=[xk(t // 4, half * 4 + i) for i in range(4)] + ["identf"], writes=[pk])
                if half == 0:
                    P.I("act", "activation", reads=[pk], writes=[f"yst{b}"], out=st[b][:, 0:512], in_=pb[:], func=AF.Copy)
                else:
                    P.I("dve", "tensor_copy", reads=[pk], writes=[f"yst{b}"], out=st[b][:, 512:1024], in_=pb[:])
            P.D("sp", f"yout{b}", C.y[t * 128:(t + 1) * 128, :], st[b][:], reads=[f"yst{b}"])
        P.barrier()

def ada_phase(P, C, l):
    cols = C.colsb[l % 2]; ck = f"cols{l % 2}"
    wv_all = C.ada_w[l].rearrange("(k p) n -> p k n", p=128)
    psA = C.ps[4]
    for piece in range(6):
        wv, wk = C.ring.load(wv_all[:, :, piece * 512:(piece + 1) * 512], "")
        for f4 in range(4):
            fc = piece * 4 + f4
            mmK(P, psA[:, fc * 2:fc * 2 + 2], [(wv[:, kc, f4 * 128:(f4 + 1) * 128], C.scond[:, kc, :]) for kc in range(8)],
                reads=[wk, "scond"], writes=["ps4"])
    P.I("dve", "tensor_tensor", reads=["ps4", "adab"], writes=["mod"], out=C.mod[:],
        in0=psA[:, 0:48].rearrange("p (f j) -> p f j", j=2), in1=C.adab[:, l, :].unsqueeze(2).to_broadcast([128, 24, 2]), op=ALU.add)
    P.I("dve", "tensor_copy", reads=["mod"], writes=[ck], out=cols[:, 0], in_=C.mod[:, 0:8, :])
    P.I("dve", "tensor_scalar_add", reads=["mod"], writes=[ck], out=cols[:, 1], in0=C.mod[:, 8:16, :], scalar1=1.0)
    P.I("dve", "tensor_scalar_mul", reads=["mod"], writes=[ck], out=cols[:, 2], in0=C.mod[:, 16:24, :], scalar1=1.0 / ALPHA)

def modulate_phase(P, C, l):
    cols = C.colsb[l % 2]; ck = f"cols{l % 2}"
    for j in range(2):
        for c in range(8):
            sl = slice(j * 1024, (j + 1) * 1024)
            rk = [xk(2 * j, c), xk(2 * j + 1, c), ck]
            if (c + j) % 2 == 0:
                P.I("dve", "tensor_scalar", reads=rk, writes=[f"hT{j}"], out=C.hT[:, c, sl], in0=C.xT[:, c, sl],
                    scalar1=cols[:, 1, c, j:j + 1], scalar2=cols[:, 0, c, j:j + 1], op0=ALU.mult, op1=ALU.add)
            else:
                P.I("act", "activation", reads=rk, writes=[f"hT{j}"], out=C.hT[:, c, sl], in_=C.xT[:, c, sl], func=AF.Identity,
                    scale=cols[:, 1, c, j:j + 1], bias=cols[:, 0, c, j:j + 1])

def load_wout(P, C, w_dram):
    C.wout_dram = w_dram

def wout_prefetch(P, C):
    wv = C.wout_dram.rearrange("(k p) n -> p k n", p=128)
    return [C.ring.load(wv[:, :, 0:512], ""), C.ring.load(wv[:, :, 512:1024], "")]

def wout_ln_phase(P, C, l, pre, next_ada=None, final=False):
    cols = C.colsb[l % 2]; ck = f"cols{l % 2}"
    with ExitStack() as es:
        zn = [P.sb(f"ln_zn{i}", [128, D], F32, es) for i in range(4)]
        st = [P.sb(f"ln_st{i}", [128, 12], F32, es) for i in range(2)]
        mv = [P.sb(f"ln_mv{i}", [128, 2], F32, es) for i in range(2)]
        rs = [P.sb(f"ln_rs{i}", [128, 2], F32, es) for i in range(2)]
        epsc = P.sb("ln_eps", [128, 1], F32, es)
        P.I("dve", "memset", writes=["ln_eps"], ap=epsc[:], constant=LN_EPS)
        if final:
            gbc = P.sb("ln_gbc", [128, D], F32, es); bbc = P.sb("ln_bbc", [128, D], F32, es)
            rows = P.sb("ln_rows", [1, 2, D], F32, es); onerow = P.sb("ln_onerow", [1, 128], F32, es)
            P.D("sp", "ln_rows", rows[0:1, 0, :], C.ln_g_row[l:l + 1, :], writes=["ln_rows"])
            P.D("sp", "ln_rows", rows[0:1, 1, :], C.ln_b_row[l:l + 1, :], writes=["ln_rows"])
            P.I("dve", "memset", writes=["ln_onerow"], ap=onerow[:], constant=1.0)
            for i, dst in enumerate((gbc, bbc)):
                for half in range(2):
                    P.I("pe", "matmul", reads=["ln_rows", "ln_onerow"], writes=["ps4"], out=C.ps[4][:], lhsT=onerow[0:1, :],
                        rhs=rows[0:1, i, half * 512:(half + 1) * 512], start=True, stop=True)
                    P.I("dve", "tensor_copy", reads=["ps4"], writes=["ln_gbc" if i == 0 else "ln_bbc"], out=dst[:, half * 512:(half + 1) * 512], in_=C.ps[4][:])
        yi = [0]; ti = [0]
        def zpass(g):
            j = g // 2; gs = slice(g * 512, (g + 1) * 512)
            for fc in range(8):
                wv, wk = pre[fc // 4]; f4 = fc % 4
                py = C.ps[6 + yi[0] % 2]; pyk = f"ps{6 + yi[0] % 2}"; yi[0] += 1
                mmK(P, py[:], [(wv[:, kc, f4 * 128:(f4 + 1) * 128], C.mT[:, kc, gs]) for kc in range(8)], reads=[wk, f"mT{g}"], writes=[pyk])
                P.I("dve", "scalar_tensor_tensor", reads=[pyk, xk(g, fc), ck], writes=[xk(g, fc)], out=C.xT[:, fc, gs], in0=py[:],
                    scalar=cols[:, 2, fc, j:j + 1], in1=C.xT[:, fc, gs], op0=ALU.mult, op1=ALU.add)
        def norm_tiles(g):
            for tl in range(4):
                tcols = slice(g * 512 + tl * 128, g * 512 + (tl + 1) * 128)
                i2 = ti[0] % 2; ti[0] += 1
                pb = [C.ps[2 * i2], C.ps[2 * i2 + 1]]; pbk = [f"ps{2 * i2}", f"ps{2 * i2 + 1}"]
                for half in range(2):
                    P.G("pe", [("transpose", dict(out=pb[half][:, i * 128:(i + 1) * 128], in_=C.xT[:, half * 4 + i, tcols], identity=C.identf[:]))
                               for i in range(4)], reads=[xk(g, half * 4 + i) for i in range(4)] + ["identf"], writes=[pbk[half]])
                    P.I("dve", "bn_stats", reads=[pbk[half]], writes=[f"ln_st{i2}"], out=st[i2][:, half * 6:(half + 1) * 6], in_=pb[half][:])
                P.I("dve", "bn_aggr", reads=[f"ln_st{i2}"], writes=[f"ln_mv{i2}"], out=mv[i2][:], in_=st[i2][:])
                P.I("act", "activation", reads=[f"ln_mv{i2}", "ln_eps"], writes=[f"ln_rs{i2}"], out=rs[i2][:, 0:1], in_=mv[i2][:, 1:2], func=AF.Sqrt,
                    bias=epsc[:, 0:1], scale=1.0)
                P.I("dve", "reciprocal", reads=[f"ln_rs{i2}"], writes=[f"ln_rs{i2}"], out=rs[i2][:, 0:1], in_=rs[i2][:, 0:1])
                P.I("dve", "scalar_tensor_tensor", reads=[f"ln_mv{i2}", f"ln_rs{i2}"], writes=[f"ln_rs{i2}"], out=rs[i2][:, 1:2], in0=mv[i2][:, 0:1],
                    scalar=-1.0, in1=rs[i2][:, 0:1], op0=ALU.mult, op1=ALU.mult)
                for half in range(2):
                    P.I("act", "activation", reads=[pbk[half], f"ln_rs{i2}"], writes=[f"ln_zn{tl}"], out=zn[tl][:, half * 512:(half + 1) * 512],
                        in_=pb[half][:], func=AF.Identity, scale=rs[i2][:, 0:1], bias=rs[i2][:, 1:2])
                if final:
                    P.I("dve", "tensor_tensor", reads=[f"ln_zn{tl}", "ln_gbc"], writes=[f"ln_zn{tl}"], out=zn[tl][:], in0=zn[tl][:], in1=gbc[:], op=ALU.mult)
                    P.I("dve", "tensor_tensor", reads=[f"ln_zn{tl}", "ln_bbc"], writes=[f"ln_zn{tl}"], out=zn[tl][:], in0=zn[tl][:], in1=bbc[:], op=ALU.add)
                    gt = g * 4 + tl
                    P.D("sp", f"yout{tl}", C.y[gt * 128:(gt + 1) * 128, :], zn[tl][:], reads=[f"ln_zn{tl}"])
        def back(g):
            gs = slice(g * 512, (g + 1) * 512)
            for fc in range(8):
                bi = 4 + fc % 2
                P.G("pe", [("transpose", dict(out=C.ps[bi][:, tl * 128:(tl + 1) * 128], in_=zn[tl][:, fc * 128:(fc + 1) * 128], identity=C.identf[:]))
                           for tl in range(4)], reads=[f"ln_zn{tl}" for tl in range(4)] + ["identf"], writes=[f"ps{bi}"])
                if fc % 2 == 0:
                    P.I("act", "activation", reads=[f"ps{bi}", "lng", "lnb"], writes=[xk(g, fc)], out=C.xT[:, fc, gs], in_=C.ps[bi][:], func=AF.Identity,
                        scale=C.lng[:, l, fc:fc + 1], bias=C.lnb[:, l, fc:fc + 1])
                else:
                    P.I("dve", "tensor_scalar", reads=[f"ps{bi}", "lng", "lnb"], writes=[xk(g, fc)], out=C.xT[:, fc, gs], in0=C.ps[bi][:],
                        scalar1=C.lng[:, l, fc:fc + 1], scalar2=C.lnb[:, l, fc:fc + 1], op0=ALU.mult, op1=ALU.add)
        zpass(0)
        for g in range(4):
            if g + 1 < 4:
                zpass(g + 1)
            norm_tiles(g)
            if g == 3 and next_ada is not None:
                ada_phase(P, C, next_ada)
            if not final:
                back(g)
        P.barrier()
QS = 128.0 ** -0.5

def _roundrobin(gens):
    gens = list(gens)
    while gens:
        nxt = []
        for g in gens:
            try:
                next(g); nxt.append(g)
            except StopIteration:
                pass
        gens = nxt

def hgrn_layer(P, C):
    with ExitStack() as es:
        lbl = P.sb("hg_lbl_s", [128, 16, 5], F32, es); lbm = P.sb("hg_lbm", [128, 16], F32, es)
        lb = P.sb("hg_lb", [128, 16], F32, es); oml = P.sb("hg_oml", [128, 16], F32, es)
        ng = P.sb("hg_ng_s", [128, 8], F32, es); eps6 = P.sb("hg_eps", [128, 1], F32, es)
        one = P.sb("hg_one", [128, 1], F32, es)
        mk = P.sb("hg_mk", [128, 2, 128], F32, es); rm = P.sb("hg_rm", [128, 2, 512], BF16, es)
        mstage = P.sb("hg_mst", [128, 2, 128], F32, es)
        P.D("sp", "hg_lbl", lbl[:], C.hg_lbl[:, :, :], writes=["hg_lbl"])
        P.D("sp", "hg_ng", ng[:], C.hg_ng[:, :], writes=["hg_ng"])
        P.D("sp", "hg_mk", mk[:], C.hg_masks[:, 0:2, :], writes=["hg_mk"])
        P.D("sp", "hg_mst", mstage[:], C.hg_masks[:, 2:4, :], writes=["hg_mst"])
        P.I("dve", "memset", writes=["hg_eps"], ap=eps6[:], constant=1e-6)
        P.I("dve", "memset", writes=["hg_one"], ap=one[:], constant=1.0)
        for d in range(2):
            for r in range(4):
                P.I("dve", "tensor_copy", reads=["hg_mst"], writes=["hg_rm"], out=rm[:, d, r * 128:(r + 1) * 128], in_=mstage[:, d, :])
        P.I("dve", "reduce_max", reads=["hg_lbl"], writes=["hg_lbm"], out=lbm[:], in_=lbl[:], axis=AX.X)
        P.I("dve", "tensor_tensor", reads=["hg_lbl", "hg_lbm"], writes=["hg_lbl"], out=lbl[:], in0=lbl[:],
            in1=lbm[:].unsqueeze(2).to_broadcast([128, 16, 5]), op=ALU.subtract)
        P.I("act", "activation", reads=["hg_lbl"], writes=["hg_lbl"], out=lbl[:], in_=lbl[:], func=AF.Exp)
        P.I("dve", "reduce_sum", reads=["hg_lbl"], writes=["hg_lbm"], out=lbm[:], in_=lbl[:], axis=AX.X)
        P.I("dve", "reciprocal", reads=["hg_lbm"], writes=["hg_lbm"], out=lbm[:], in_=lbm[:])
        P.I("dve", "tensor_tensor", reads=["hg_lbl", "hg_lbm"], writes=["hg_lb"], out=lb[:], in0=lbl[:, :, 0], in1=lbm[:], op=ALU.mult)
        P.I("dve", "tensor_scalar", reads=["hg_lb"], writes=["hg_oml"], out=oml[:], in0=lb[:], scalar1=-1.0, scalar2=1.0, op0=ALU.mult, op1=ALU.add)

        vtok = P.sb("hg_vtok", [128, 16, 128], BF16, es)
        qS = P.sb("hg_q", [128, 1024], F32, es)
        ob = P.sb("hg_o", [128, 1024], F32, es)
        sgB = P.sb("hg_sgb", [128, 1024], BF16, es)
        Sf = P.sb("hg_Sf", [128, 8, 128], F32, es); Sb = P.sb("hg_Sb", [128, 8, 128], BF16, es)
        attS = [P.sb(f"hg_att{i}", [128, 128], BF16, es) for i in range(2)]
        U = []
        for u in range(2):
            B_ = Ctx()
            B_.u = u
            B_.cum = P.sb(f"hg_cum{u}", [128, 512], F32, es); B_.kS = P.sb(f"hg_k{u}", [128, 512], F32, es)
            B_.A = P.sb(f"hg_A{u}", [128, 512], F32, es); B_.B = P.sb(f"hg_B{u}", [128, 512], F32, es)
            B_.qrel = P.sb(f"hg_qrel{u}", [128, 512], BF16, es); B_.krel = P.sb(f"hg_krel{u}", [128, 512], BF16, es)
            B_.qcum = P.sb(f"hg_qcum{u}", [128, 512], BF16, es); B_.kdT = P.sb(f"hg_kdT{u}", [128, 512], BF16, es)
            B_.kdtok = P.sb(f"hg_kdtok{u}", [128, 4, 128], BF16, es); B_.etot = P.sb(f"hg_etot{u}", [128, 16], F32, es)
            U.append(B_)
        wv_all = C.hg_w_in.rearrange("(k p) n -> p k n", p=128)
        cnt = {"w": 0, "att": 0, "x": 0, "y": 0, "u": 0}
        def wps():
            i = cnt["w"] % 2; cnt["w"] += 1
            return C.ps[i], f"ps{i}"
        def quarter(bank, name):
            i = cnt[name] % 4; cnt[name] += 1
            return C.ps[bank][:, i * 128:(i + 1) * 128], f"ps{bank}"
        def uslot():
            i = cnt["u"] % 2; cnt["u"] += 1
            return C.ps[6 + i][:, 0:128], f"ps{6 + i}"

        def prep(B_, wv, wk, hd, blk, d, sg_):
            u = B_.u
            K = lambda n: f"hg_{n}{u}"
            cs = slice(blk * 1024 + sg_ * 512, blk * 1024 + (sg_ + 1) * 512); ls = slice(sg_ * 512, (sg_ + 1) * 512)
            ridx = 15 if d == 0 else 16; tidx = 31 if d == 0 else 0
            cum, kS, Ab, Bb = B_.cum, B_.kS, B_.A, B_.B
            pt, pk = wps()
            mmK(P, pt[:], [(wv[:, kc, 1 + d, :], C.hT[:, kc, cs]) for kc in range(8)], reads=[wk, f"hT{blk}"], writes=[pk]); yield
            lbc = lb[:, d * 8 + hd:d * 8 + hd + 1]; omc = oml[:, d * 8 + hd:d * 8 + hd + 1]
            P.I("act", "activation", reads=[pk], writes=[K("cum")], out=cum[:], in_=pt[:], func=AF.Exp, scale=-1.0); yield
            P.I("act", "activation", reads=[K("cum"), "hg_lb", "hg_one"], writes=[K("A")], out=Ab[:], in_=cum[:], func=AF.Ln, scale=lbc, bias=one[:, 0:1]); yield
            P.I("act", "activation", reads=[K("cum"), "hg_one"], writes=[K("B")], out=Bb[:], in_=cum[:], func=AF.Ln, scale=1.0, bias=one[:, 0:1]); yield
            P.I("dve", "tensor_tensor", reads=[K("A"), K("B")], writes=[K("cum")], out=cum[:], in0=Ab[:], in1=Bb[:], op=ALU.subtract); yield
            P.I("dve", "tensor_tensor", reads=[pk, K("B")], writes=[K("B")], out=Bb[:], in0=Bb[:], in1=pt[:], op=ALU.add); yield
            P.I("act", "activation", reads=[K("B")], writes=[K("k")], out=kS[:], in_=Bb[:], func=AF.Exp, scale=-1.0); yield
            rv = slice(None) if d == 0 else slice(None, None, -1)
            P.I("dve", "tensor_tensor_scan", reads=[K("cum"), "hg_rm"], writes=[K("cum")], out=cum[:, rv], data0=rm[:, d, rv],
                data1=cum[:, rv], initial=0.0, op0=ALU.mult, op1=ALU.add); yield
            c3 = cum[:].rearrange("p (c t) -> p c t", t=32)
            A3 = Ab[:].rearrange("p (c t) -> p c t", t=32)
            P.I("dve", "tensor_tensor", reads=[K("cum")], writes=[K("A")], out=A3, in0=c3,
                in1=c3[:, :, ridx:ridx + 1].to_broadcast([128, 16, 32]), op=ALU.subtract); yield
            P.I("act", "activation", reads=[K("cum")], writes=[K("B")], out=Bb[:], in_=cum[:], func=AF.Exp); yield
            P.I("dve", "scalar_tensor_tensor", reads=["hg_q", K("B")], writes=[K("qcum")], out=B_.qcum[:], in0=qS[:, ls], scalar=QS,
                in1=Bb[:], op0=ALU.mult, op1=ALU.mult); yield
            P.I("act", "activation", reads=[K("cum")], writes=[K("etot")], out=B_.etot[:], in_=c3[:, :, tidx], func=AF.Exp); yield
            P.I("act", "activation", reads=[K("A")], writes=[K("B")], out=Bb[:], in_=Ab[:], func=AF.Exp); yield
            P.I("dve", "scalar_tensor_tensor", reads=["hg_q", K("B")], writes=[K("qrel")], out=B_.qrel[:], in0=qS[:, ls], scalar=QS,
                in1=Bb[:], op0=ALU.mult, op1=ALU.mult); yield
            P.I("act", "activation", reads=[K("A")], writes=[K("B")], out=Bb[:], in_=Ab[:], func=AF.Exp, scale=-1.0); yield
            P.I("dve", "scalar_tensor_tensor", reads=[K("k"), K("B"), "hg_oml"], writes=[K("krel")], out=B_.krel[:], in0=kS[:], scalar=omc, in1=Bb[:],
                op0=ALU.mult, op1=ALU.mult); yield
            P.I("dve", "tensor_tensor", reads=[K("cum")], writes=[K("A")], out=A3, in0=c3,
                in1=c3[:, :, tidx:tidx + 1].to_broadcast([128, 16, 32]), op=ALU.subtract); yield
            P.I("act", "activation", reads=[K("A")], writes=[K("B")], out=Bb[:], in_=Ab[:], func=AF.Exp, scale=-1.0); yield
            P.I("dve", "scalar_tensor_tensor", reads=[K("k"), K("B"), "hg_oml"], writes=[K("kdT")], out=B_.kdT[:], in0=kS[:], scalar=omc, in1=Bb[:],
                op0=ALU.mult, op1=ALU.mult); yield
            psT = C.ps[2][:].bitcast(BF16)
            P.G("pe", [("transpose", dict(out=psT[:, i * 128:(i + 1) * 128], in_=B_.kdT[:, i * 128:(i + 1) * 128], identity=C.identb[:]))
                       for i in range(4)], reads=[K("kdT"), "identb"], writes=["ps2"])
            P.I("act", "activation", reads=["ps2"], writes=[K("kdtok")], out=B_.kdtok[:], in_=psT[:, 0:512].rearrange("p (a b) -> p a b", a=4),
                func=AF.Copy); yield

        for hd in range(8):
            if hd == 4:
                load_wout(P, C, C.hg_w_out)
            wv, wk = C.ring.load_multi([wv_all[:, :, j * 1024 + hd * 128:j * 1024 + (hd + 1) * 128] for j in range(4)])
            gv, gk = C.ring.load(wv_all[:, :, 4096 + hd * 128:4096 + (hd + 1) * 128], "")
            for t4 in range(4):
                vb = 2 if t4 % 2 == 0 else 4
                P.G("pe", [("matmul", dict(out=C.ps[vb][:, i * 128:(i + 1) * 128], lhsT=C.hT[:, kc, (t4 * 4 + i) * 128:(t4 * 4 + i + 1) * 128],
                                           rhs=wv[:, kc, 3, :], start=(kc == 0), stop=(kc == 7))) for i in range(4) for kc in range(8)],
                    reads=[wk, f"hT{t4 // 2}"], writes=[f"ps{vb}"])
                if t4 % 2 == 0:
                    P.I("act", "activation", reads=[f"ps{vb}"], writes=["hg_vtok"], out=vtok[:, t4 * 4:t4 * 4 + 4, :],
                        in_=C.ps[vb][:].rearrange("p (a b) -> p a b", a=4), func=AF.Copy)
                else:
                    P.I("dve", "tensor_copy", reads=[f"ps{vb}"], writes=["hg_vtok"], out=vtok[:, t4 * 4:t4 * 4 + 4, :],
                        in_=C.ps[vb][:].rearrange("p (a b) -> p a b", a=4))
            for blk in range(2):
                for g2 in range(2):
                    cs = slice(blk * 1024 + g2 * 512, blk * 1024 + (g2 + 1) * 512); ls = slice(g2 * 512, (g2 + 1) * 512)
                    pt, pk = wps()
                    mmK(P, pt[:], [(wv[:, kc, 0, :], C.hT[:, kc, cs]) for kc in range(8)], reads=[wk, f"hT{blk}"], writes=[pk])
                    P.I("act", "activation", reads=[pk], writes=["hg_q"], out=qS[:, ls], in_=pt[:], func=AF.Silu)
                for g2 in range(2):
                    cs = slice(blk * 1024 + g2 * 512, blk * 1024 + (g2 + 1) * 512); ls = slice(g2 * 512, (g2 + 1) * 512)
                    pt, pk = wps()
                    mmK(P, pt[:], [(gv[:, kc, :], C.hT[:, kc, cs]) for kc in range(8)], reads=[gk, f"hT{blk}"], writes=[pk])
                    P.I("act", "activation", reads=[pk], writes=["hg_sgb"], out=sgB[:, ls], in_=pt[:], func=AF.Silu)
                P.I("pool", "memset", writes=[f"hg_o{t_}" for t_ in range(8)], ap=ob[:], constant=0.0)
                if blk == 0:
                    P.I("pool", "memset", writes=[f"hg_Sf{c_}" for c_ in range(8)], ap=Sf[:], constant=0.0)
                    P.I("pool", "memset", writes=[f"hg_Sb{c_}" for c_ in range(8)], ap=Sb[:], constant=0.0)
                else:
                    for d in range(2):
                        P.D("sp", f"hg_s0{d}", Sf[:, d, :], C.hg_s0[d, hd], writes=[f"hg_Sf{d}"])
                        P.I("act", "activation", reads=[f"hg_Sf{d}"], writes=[f"hg_Sb{d}"], out=Sb[:, d, :], in_=Sf[:, d, :], func=AF.Copy)
                for step in range(2):
                    units = [(0, step, U[0]), (1, 1 - step, U[1])]
                    _roundrobin([prep(B_, wv, wk, hd, blk, d, sg_) for (d, sg_, B_) in units])
                    chains = []
                    for (d, sg_, B_) in units:
                        if blk == 0:
                            cl = [(d * 4 + 2 * sg_, [0, 1]), (d * 4 + 2 * sg_ + 1, [2, 3])]
                        else:
                            cl = [(d, [0, 1, 2, 3])]
                        for ch, tl in cl:
                            chains.append((d, sg_, B_, ch, tl if d == 0 else tl[::-1]))
                    npos = len(chains[0][4])
                    for pos in range(npos):
                        info = []
                        for (d, sg_, B_, ch, tl) in chains:
                            u = B_.u
                            tloc = tl[pos]; gt = blk * 8 + sg_ * 4 + tloc; ts_ = slice(tloc * 128, (tloc + 1) * 128)
                            pa, pak = quarter(3, "att")
                            P.I("pe", "matmul", reads=[f"hg_krel{u}", f"hg_qrel{u}"], writes=[pak], out=pa, lhsT=B_.krel[:, ts_], rhs=B_.qrel[:, ts_], start=True, stop=True)
                            ai = cnt["att"] % 2
                            P.I("dve", "tensor_tensor", reads=[pak, "hg_mk"], writes=[f"hg_att{ai}"], out=attS[ai][:], in0=pa, in1=mk[:, d, :], op=ALU.mult)
                            px, pxk = quarter(4, "x")
                            P.I("pe", "matmul", reads=["hg_vtok", f"hg_att{ai}"], writes=[pxk], out=px, lhsT=vtok[:, gt, :], rhs=attS[ai][:], start=True, stop=True)
                            py, pyk = quarter(5, "y")
                            info.append((d, sg_, B_, ch, tloc, gt, px, pxk, py, pyk))
                        for jj in range(4):
                            for (d, sg_, B_, ch, tloc, gt, px, pxk, py, pyk) in info:
                                u = B_.u
                                j = jj if d == 0 else 3 - jj
                                qs_ = slice(tloc * 128 + j * 32, tloc * 128 + (j + 1) * 32)
                                P.I("pe", "matmul", reads=[f"hg_Sb{ch}", f"hg_qcum{u}"], writes=[pyk], out=py[:, j * 32:(j + 1) * 32], lhsT=Sb[:, ch, :],
                                    rhs=B_.qcum[:, qs_], start=True, stop=True)
                                pu, puk = uslot()
                                P.I("pe", "matmul", reads=[f"hg_kdtok{u}", "hg_vtok"], writes=[puk], out=pu, lhsT=B_.kdtok[j * 32:(j + 1) * 32, tloc, :],
                                    rhs=vtok[j * 32:(j + 1) * 32, gt, :], start=True, stop=True, tile_position=(j * 32, 0))
                                P.I("dve", "scalar_tensor_tensor", reads=[f"hg_Sf{ch}", puk, f"hg_etot{u}"], writes=[f"hg_Sf{ch}"], out=Sf[:, ch, :],
                                    in0=Sf[:, ch, :], scalar=B_.etot[:, tloc * 4 + j:tloc * 4 + j + 1], in1=pu, op0=ALU.mult, op1=ALU.add)
                                P.I("act", "activation", reads=[f"hg_Sf{ch}"], writes=[f"hg_Sb{ch}"], out=Sb[:, ch, :], in_=Sf[:, ch, :], func=AF.Copy)
                        for (d, sg_, B_, ch, tloc, gt, px, pxk, py, pyk) in info:
                            t8 = sg_ * 4 + tloc
                            os_ = slice(t8 * 128, (t8 + 1) * 128); ok_ = f"hg_o{t8}"
                            P.I("dve", "tensor_tensor", reads=[pxk, ok_], writes=[ok_], out=ob[:, os_], in0=ob[:, os_], in1=px, op=ALU.add)
                            P.I("dve", "tensor_tensor", reads=[pyk, ok_], writes=[ok_], out=ob[:, os_], in0=ob[:, os_], in1=py, op=ALU.add)
                    if blk == 0:
                        for (d, sg_, B_, ch, tl) in chains:
                            P.D("sp", f"o_hg{ch}", C.o_hg[ch % 4, d, hd], Sf[:, ch, :], reads=[f"hg_Sf{ch}"])
                osq = U[0].qrel; rsb = U[0].B; sgb = U[0].A
                for g2 in range(2):
                    cs = slice(blk * 1024 + g2 * 512, blk * 1024 + (g2 + 1) * 512); ls = slice(g2 * 512, (g2 + 1) * 512)
                    okeys = [f"hg_o{g2 * 4 + t_}" for t_ in range(4)]
                    P.I("act", "activation", reads=okeys, writes=["hg_qrel0"], out=osq[:], in_=ob[:, ls], func=AF.Square)
                    pt, pk = wps()
                    P.I("pe", "matmul", reads=["hg_qrel0", "onesb"], writes=[pk], out=pt[:], lhsT=C.onesb[:], rhs=osq[:], start=True, stop=True)
                    P.I("act", "activation", reads=[pk, "hg_eps"], writes=["hg_B0"], out=rsb[:], in_=pt[:], func=AF.Ln, bias=eps6[:, 0:1], scale=1.0 / 128.0)
                    P.I("act", "activation", reads=["hg_B0"], writes=["hg_B0"], out=rsb[:], in_=rsb[:], func=AF.Exp, scale=-0.5)
                    P.I("dve", "tensor_tensor", reads=["hg_B0"] + okeys, writes=["hg_B0"], out=rsb[:], in0=rsb[:], in1=ob[:, ls], op=ALU.mult)
                    P.I("dve", "scalar_tensor_tensor", reads=["hg_B0", "hg_sgb", "hg_ng"], writes=[f"mT{blk * 2 + g2}"], out=C.mT[:, hd, cs], in0=rsb[:],
                        scalar=ng[:, hd:hd + 1], in1=sgB[:, ls], op0=ALU.mult, op1=ALU.mult)
def sconv_layer(P, C):
    with ExitStack() as es:
        cw = P.sb("sc_cw_s", [128, 3, 8], F32, es); cb = P.sb("sc_cb_s", [128, 8], F32, es)
        P.D("sp", "sc_cw", cw[:], C.sc_cw[:, :, :], writes=["sc_cw"])
        P.D("sp", "sc_cb", cb[:], C.sc_cb[:, :], writes=["sc_cb"])
        pb_ = [P.sb(f"sc_p{i}", [128, 1024], F32, es) for i in range(2)]
        zb_ = [P.sb(f"sc_z{i}", [128, 1024], F32, es) for i in range(2)]
        cgS = [P.sb(f"sc_cg{i}", [128, 512], F32, es) for i in range(2)]
        sgS = [P.sb(f"sc_sg{i}", [128, 512], F32, es) for i in range(2)]
        tS = [P.sb(f"sc_t{i}", [128, 512], F32, es) for i in range(2)]
        wv_all = C.sc_w_in.rearrange("(k p) n -> p k n", p=128)
        pi = 0
        for c in range(8):
            if c == 4:
                load_wout(P, C, C.sc_w_out)
            wv, wk = C.ring.load_multi([wv_all[:, :, j * 1024 + c * 128:j * 1024 + (c + 1) * 128] for j in range(4)])
            for blk in range(2):
                p = pb_[blk]; z = zb_[blk]; pk = f"sc_p{blk}"; zk = f"sc_z{blk}"
                for g2 in range(2):
                    cs = slice(blk * 1024 + g2 * 512, blk * 1024 + (g2 + 1) * 512); ls = slice(g2 * 512, (g2 + 1) * 512)
                    pa = pi % 4; pbk = (pi + 1) % 4; pi += 2
                    mmK(P, C.ps[pa][:], [(wv[:, kc, 1, :], C.hT[:, kc, cs]) for kc in range(8)], reads=[wk, f"hT{blk}"], writes=[f"ps{pa}"])
                    mmK(P, C.ps[pbk][:], [(wv[:, kc, 2, :], C.hT[:, kc, cs]) for kc in range(8)], reads=[wk, f"hT{blk}"], writes=[f"ps{pbk}"])
                    i2 = g2
                    P.I("act", "activation", reads=[f"ps{pa}"], writes=[f"sc_cg{i2}"], out=cgS[i2][:], in_=C.ps[pa][:], func=AF.Copy)
                    P.I("dve", "tensor_tensor", reads=[f"sc_cg{i2}", f"ps{pbk}"], writes=[pk], out=p[:, ls], in0=cgS[i2][:], in1=C.ps[pbk][:], op=ALU.mult)
                P.I("dve", "tensor_scalar", reads=[pk, "sc_cw", "sc_cb"], writes=[zk], out=z[:], in0=p[:], scalar1=cw[:, 1, c:c + 1],
                    scalar2=cb[:, c:c + 1], op0=ALU.mult, op1=ALU.add)
                if blk == 0:
                    z3 = z[:].rearrange("p (s t) -> p s t", s=4); p3 = p[:].rearrange("p (s t) -> p s t", s=4)
                    zlo, plo, zhi, phi = z3[:, :, 1:], p3[:, :, :-1], z3[:, :, :-1], p3[:, :, 1:]
                else:
                    zlo, plo, zhi, phi = z[:, 1:], p[:, :-1], z[:, :-1], p[:, 1:]
                P.I("dve", "scalar_tensor_tensor", reads=[pk, zk, "sc_cw"], writes=[zk], out=zlo, in0=plo, scalar=cw[:, 0, c:c + 1], in1=zlo,
                    op0=ALU.mult, op1=ALU.add)
                P.I("dve", "scalar_tensor_tensor", reads=[pk, zk, "sc_cw"], writes=[zk], out=zhi, in0=phi, scalar=cw[:, 2, c:c + 1], in1=zhi,
                    op0=ALU.mult, op1=ALU.add)
                for g2 in range(2):
                    cs = slice(blk * 1024 + g2 * 512, blk * 1024 + (g2 + 1) * 512); ls = slice(g2 * 512, (g2 + 1) * 512)
                    g = blk * 2 + g2
                    pa = pi % 4; pbk = (pi + 1) % 4; pi += 2
                    mmK(P, C.ps[pa][:], [(wv[:, kc, 0, :], C.hT[:, kc, cs]) for kc in range(8)], reads=[wk, f"hT{blk}"], writes=[f"ps{pa}"])
                    mmK(P, C.ps[pbk][:], [(wv[:, kc, 3, :], C.hT[:, kc, cs]) for kc in range(8)], reads=[wk, f"hT{blk}"], writes=[f"ps{pbk}"])
                    i2 = g2
                    P.I("act", "activation", reads=[f"ps{pbk}"], writes=[f"sc_sg{i2}"], out=sgS[i2][:], in_=C.ps[pbk][:], func=AF.Silu)
                    P.I("dve", "tensor_tensor", reads=[f"sc_sg{i2}", f"ps{pa}"], writes=[f"sc_t{i2}"], out=tS[i2][:], in0=sgS[i2][:], in1=C.ps[pa][:], op=ALU.mult)
                    P.I("dve", "tensor_tensor", reads=[f"sc_t{i2}", zk], writes=[f"mT{g}"], out=C.mT[:, c, cs], in0=tS[i2][:], in1=z[:, ls], op=ALU.mult)
def rglru_layer(P, C):
    with ExitStack() as es:
        cw = P.sb("rg_cw_s", [128, 4, 8], F32, es); cb = P.sb("rg_cb_s", [128, 8], F32, es)
        bg = P.sb("rg_bg_s", [128, 2, 4, 4], F32, es); lam = P.sb("rg_lam_s", [128, 2, 8], F32, es)
        clam = P.sb("rg_clam", [128, 2, 8], F32, es); h0 = P.sb("rg_h0_s", [128, 2, 8], F32, es)
        one = P.sb("rg_one", [128, 1], F32, es)
        rgst = P.sb("rg_state", [128, 8, 8], F32, es)
        P.D("sp", "rg_cw", cw[:], C.rg_cw[:, :, :], writes=["rg_cw"])
        P.D("sp", "rg_cb", cb[:], C.rg_cb[:, :], writes=["rg_cb"])
        P.D("sp", "rg_bg", bg[:], C.rg_bg[:, :, :, :], writes=["rg_bg"])
        P.D("sp", "rg_lam", lam[:], C.rg_lam[:, :, :], writes=["rg_lam"])
        P.D("sp", "rg_h0", h0[:], C.rg_h0[:, :, :], writes=["rg_h0"])
        P.I("dve", "memset", writes=["rg_one"], ap=one[:], constant=1.0)
        P.I("act", "activation", reads=["rg_lam"], writes=["rg_clam"], out=clam[:], in_=lam[:], func=AF.Exp, scale=-1.0)
        P.I("act", "activation", reads=["rg_clam", "rg_one"], writes=["rg_clam"], out=clam[:], in_=clam[:], func=AF.Ln, bias=one[:, 0:1], scale=1.0)
        P.I("dve", "tensor_scalar_mul", reads=["rg_clam"], writes=["rg_clam"], out=clam[:], in0=clam[:], scalar1=-4.0)
        half = P.sb("rg_half", [128, 1], F32, es)
        P.I("dve", "memset", writes=["rg_half"], ap=half[:], constant=0.5)
        P.I("dve", "tensor_scalar_mul", reads=["rg_bg"], writes=["rg_bg"], out=bg[:], in0=bg[:], scalar1=0.5)
        P.I("dve", "tensor_scalar_mul", reads=["rg_h0"], writes=["rg_h0"], out=h0[:], in0=h0[:], scalar1=2.0)
        uraw = P.sb("rg_uraw", [128, 1024], F32, es)
        uc = P.sb("rg_uc", [128, 2, 1024], F32, es); ucb = P.sb("rg_ucb", [128, 2, 1024], BF16, es)
        abufs = [P.sb(f"rg_a{i}", [128, 1024], F32, es) for i in range(2)]
        xbs = [[P.sb(f"rg_x{o}{i}", [128, 1024], F32, es) for i in range(2)] for o in range(2)]
        sqt = [P.sb(f"rg_sq{i}", [128, 512], F32, es) for i in range(4)]
        sgS = [P.sb(f"rg_sg{i}", [128, 512], F32, es) for i in range(2)]
        wv_all = C.rg_w_in.rearrange("(k p) n -> p k n", p=128)
        pi = [0]
        def nps():
            i = pi[0] % 6; pi[0] += 1
            return C.ps[i], f"ps{i}"
        def seg(ap2d, blk, lo, hi):
            if blk == 0:
                v = ap2d.rearrange("p (s t) -> p s t", s=4)
                return v[:, :, lo:256 + hi]
            return ap2d[:, lo:1024 + hi]
        for hh in range(4):
            if hh == 2:
                load_wout(P, C, C.rg_w_out)
            wv, wk = C.ring.load_multi([wv_all[:, :, j * 1024 + c * 128:j * 1024 + (c + 1) * 128] for j in range(2) for c in (2 * hh, 2 * hh + 1)])
            gv, gk = C.ring.load_multi([C.rg_w_gate[d, hh].rearrange("(k p) n -> p k n", p=128) for d in range(2)])
            for blk in range(2):
                for cc in range(2):
                    c = 2 * hh + cc
                    for g2 in range(2):
                        cs = slice(blk * 1024 + g2 * 512, blk * 1024 + (g2 + 1) * 512); ls = slice(g2 * 512, (g2 + 1) * 512)
                        pt, pk = nps()
                        mmK(P, pt[:], [(wv[:, kc, cc, :], C.hT[:, kc, cs]) for kc in range(8)], reads=[wk, f"hT{blk}"], writes=[pk])
                        P.I("act", "activation", reads=[pk], writes=["rg_uraw"], out=uraw[:, ls], in_=pt[:], func=AF.Copy)
                    ucc = uc[:, cc, :]
                    P.I("dve", "tensor_scalar", reads=["rg_uraw", "rg_cw", "rg_cb"], writes=["rg_uc"], out=ucc, in0=uraw[:],
                        scalar1=cw[:, 2, c:c + 1], scalar2=cb[:, c:c + 1], op0=ALU.mult, op1=ALU.add)
                    for (k, lo_o, hi_o, lo_i, hi_i) in ((0, 2, 0, 0, -2), (1, 1, 0, 0, -1), (3, 0, -1, 1, 0)):
                        o_ = seg(ucc, blk, lo_o, hi_o); i_ = seg(uraw[:], blk, lo_i, hi_i)
                        P.I("dve", "scalar_tensor_tensor", reads=["rg_uraw", "rg_uc", "rg_cw"], writes=["rg_uc"], out=o_, in0=i_,
                            scalar=cw[:, k, c:c + 1], in1=o_, op0=ALU.mult, op1=ALU.add)
                    P.I("pool", "tensor_copy", reads=["rg_uc"], writes=["rg_ucb"], out=ucb[:, cc, :], in_=ucc)
                for oc in range(2):
                    c = 2 * hh + oc
                    xb = xbs[oc]
                    for d in range(2):
                        xin = xb[d]; xk = f"rg_x{oc}{d}"; abuf = abufs[d]; ak = f"rg_a{d}"
                        for g2 in range(2):
                            ls = slice(g2 * 512, (g2 + 1) * 512)
                            pr, prk = nps(); pq, pqk = nps()
                            mmK(P, pr[:], [(gv[:, kc, d, oc * 128:(oc + 1) * 128], ucb[:, kc, ls]) for kc in range(2)], reads=[gk, "rg_ucb"], writes=[prk])
                            mmK(P, pq[:], [(gv[:, kc, d, 256 + oc * 128:256 + (oc + 1) * 128], ucb[:, kc, ls]) for kc in range(2)], reads=[gk, "rg_ucb"], writes=[pqk])
                            P.I("act", "activation", reads=[prk, "rg_bg"], writes=[ak], out=abuf[:, ls], in_=pr[:], func=AF.Tanh,
                                bias=bg[:, d, hh, oc:oc + 1], scale=0.5)
                            P.I("act", "activation", reads=[ak, "rg_clam"], writes=[ak], out=abuf[:, ls], in_=abuf[:, ls], func=AF.Exp,
                                scale=clam[:, d, c:c + 1], bias=clam[:, d, c:c + 1])
                            P.I("act", "activation", reads=[pqk, "rg_bg"], writes=[xk], out=xin[:, ls], in_=pq[:], func=AF.Tanh,
                                bias=bg[:, d, hh, 2 + oc:3 + oc], scale=0.5)
                            sq = sqt[d * 2 + g2]; sk = f"rg_sq{d * 2 + g2}"
                            P.I("act", "activation", reads=[ak], writes=[sk], out=sq[:], in_=abuf[:, ls], func=AF.Square)
                        for g2 in range(2):
                            ls = slice(g2 * 512, (g2 + 1) * 512)
                            sq = sqt[d * 2 + g2]; sk = f"rg_sq{d * 2 + g2}"
                            P.I("act", "activation", reads=[sk, "rg_one"], writes=[sk], out=sq[:], in_=sq[:], func=AF.Sqrt, bias=one[:, 0:1], scale=-1.0)
                            P.I("dve", "scalar_tensor_tensor", reads=[xk, sk], writes=[xk], out=xin[:, ls], in0=xin[:, ls], scalar=1.0, in1=sq[:],
                                op0=ALU.add, op1=ALU.mult)
                            P.I("dve", "tensor_tensor", reads=[xk, "rg_uc"], writes=[xk], out=xin[:, ls], in0=xin[:, ls], in1=uc[:, oc, ls], op=ALU.mult)
                        seqs = [(s * 256, 256) for s in range(4)] if blk == 0 else [(0, 1024)]
                        for (o0, L) in seqs:
                            sl = slice(o0, o0 + L) if d == 0 else slice(o0 + L - 1, (o0 - 1) if o0 > 0 else None, -1)
                            init = 0.0 if blk == 0 else h0[:, d, c:c + 1]
                            P.I("dve", "tensor_tensor_scan", reads=[ak, xk, "rg_h0"], writes=[xk], out=xin[:, sl], data0=abuf[:, sl],
                                data1=xin[:, sl], initial=init, op0=ALU.mult, op1=ALU.add)
                        if blk == 0:
                            x3 = xin[:].rearrange("p (s t) -> p s t", s=4)
                            src = x3[:, :, 255:256] if d == 0 else x3[:, :, 0:1]
                            dst = rgst[:, c, :].rearrange("p (s d) -> p s d", d=2)[:, :, d:d + 1]
                            P.I("dve", "tensor_scalar_mul", reads=[xk], writes=["rg_state"], out=dst, in0=src, scalar1=0.5)
                    P.I("dve", "tensor_tensor", reads=[f"rg_x{oc}0", f"rg_x{oc}1"], writes=[f"rg_x{oc}0"], out=xb[0][:], in0=xb[0][:], in1=xb[1][:], op=ALU.add)
                    for g2 in range(2):
                        cs = slice(blk * 1024 + g2 * 512, blk * 1024 + (g2 + 1) * 512); ls = slice(g2 * 512, (g2 + 1) * 512)
                        pt, pk = nps()
                        mmK(P, pt[:], [(wv[:, kc, 2 + oc, :], C.hT[:, kc, cs]) for kc in range(8)], reads=[wk, f"hT{blk}"], writes=[pk])
                        P.I("act", "activation", reads=[pk], writes=[f"rg_sg{g2}"], out=sgS[g2][:], in_=pt[:], func=AF.Tanh, scale=0.5)
                        P.I("dve", "scalar_tensor_tensor", reads=[f"rg_sg{g2}", pk], writes=[f"rg_sg{g2}"], out=sgS[g2][:], in0=sgS[g2][:], scalar=1.0, in1=pt[:],
                            op0=ALU.add, op1=ALU.mult)
                        P.I("dve", "scalar_tensor_tensor", reads=[f"rg_sg{g2}", f"rg_x{oc}0"], writes=[f"mT{blk * 2 + g2}"], out=C.mT[:, c, cs], in0=sgS[g2][:],
                            scalar=0.25, in1=xb[0][:, ls], op0=ALU.mult, op1=ALU.mult)
        srow = uraw[0:8, :]
        for half in range(2):
            pt, pk = nps()
            P.G("pe", [("transpose", dict(out=pt[0:8, i * 128:(i + 1) * 128], in_=rgst[:, half * 4 + i, :], identity=C.identf[:])) for i in range(4)],
                reads=["rg_state", "identf"], writes=[pk])
            P.I("dve", "tensor_copy", reads=[pk], writes=["rg_uraw"], out=srow[:, half * 512:(half + 1) * 512], in_=pt[0:8, :])
        P.D("sp", "o_rg", C.o_rg[:, :], srow, reads=["rg_uraw"])
SM_SCALE = 96.0 ** -0.5

def mla_layer(P, C):
    with ExitStack() as es:
        qn = P.sb("ml_qn", [128, 3], F32, es); kvn = P.sb("ml_kvn", [128, 2], F32, es)
        eps6 = P.sb("ml_eps", [128, 1], F32, es)
        rope = P.sb("ml_rope", [128, 2, 1024], F32, es)
        cqnT = P.sb("ml_cqnT", [128, 3, 1024], BF16, es)
        ckvnT = P.sb("ml_ckvnT", [128, 2, 1536], BF16, es)
        KK = P.sb("ml_KK", [128, 1536], BF16, es)
        KK2 = P.sb("ml_KK2", [128, 1536], BF16, es)
        P.D("sp", "ml_qn", qn[:], C.mla_qn[:, :], writes=["ml_qn"])
        P.D("sp", "ml_kvn", kvn[:], C.mla_kvn[:, :], writes=["ml_kvn"])
        P.D("sp", "ml_rope", rope[64:96], C.rope_cs[64:96, :, :], writes=["ml_rope"])
        P.I("dve", "memset", writes=["ml_eps"], ap=eps6[:], constant=1e-6)
        wv_all = C.mla_w_in.rearrange("(k p) n -> p k n", p=128)
        wqb_all = C.mla_w_qb.rearrange("(k p) n -> p k n", p=128)
        wqs_all = C.mla_w_qbsw.rearrange("(k p) n -> p k n", p=128)
        wkvb_all = C.mla_w_kvb.rearrange("(k p) n -> p k n", p=128)
        wks_all = C.mla_w_kpesw.rearrange("(k p) n -> p k n", p=128)
        load_wout(P, C, C.mla_w_out)
        for blk in range(2):
            nkeys = 1024 if blk == 0 else 1536
            with ExitStack() as esA:
                sq = [P.sb(f"ml_sq{i}", [128, 512], BF16, esA) for i in range(2)]
                rstd = P.sb("ml_rstd", [128, 512], F32, esA)
                ckvf = P.sb("ml_ckvf", [128, 2, 512], F32, esA)
                t1 = P.sb("ml_t1", [128, 256], F32, esA); t2 = P.sb("ml_t2", [128, 256], F32, esA)
                stg = [P.sb(f"ml_stg{i}", [128, 256], F32, esA) for i in range(2)]
                stk = P.sb("ml_stk", [128, 32], F32, esA); stkb = P.sb("ml_stkb", [128, 32], BF16, esA)
                (wcq,), wcqk = C.ring.load_parts([wv_all[:, :, 0:384]])
                (wkv, wks), wkvk = C.ring.load_parts([wv_all[:, :, 384:672], wks_all[:, :, :]])
                for gq in range(2):
                    cs = slice(blk * 1024 + gq * 512, blk * 1024 + (gq + 1) * 512); ls = slice(gq * 512, (gq + 1) * 512)
                    hk = f"hT{blk}"
                    for c in range(3):
                        mmK(P, C.ps[c][:], [(wcq[:, kc, c * 128:(c + 1) * 128], C.hT[:, kc, cs]) for kc in range(8)], reads=[wcqk, hk], writes=[f"ps{c}"])
                        P.I("act", "activation", reads=[f"ps{c}"], writes=[f"ml_sq{c % 2}"], out=sq[c % 2][:], in_=C.ps[c][:], func=AF.Square)
                        P.I("pe", "matmul", reads=[f"ml_sq{c % 2}", "onesb"], writes=["ps3"], out=C.ps[3][:], lhsT=C.onesb[:], rhs=sq[c % 2][:],
                            start=(c == 0), stop=(c == 2))
                    P.I("act", "activation", reads=["ps3", "ml_eps"], writes=["ml_rstd"], out=rstd[:], in_=C.ps[3][:], func=AF.Sqrt, bias=eps6[:, 0:1], scale=1.0 / 384.0)
                    P.I("dve", "reciprocal", reads=["ml_rstd"], writes=["ml_rstd"], out=rstd[:], in_=rstd[:])
                    for c in range(3):
                        P.I("dve", "scalar_tensor_tensor", reads=[f"ps{c}", "ml_rstd", "ml_qn"], writes=["ml_cqnT"], out=cqnT[:, c, ls], in0=C.ps[c][:],
                            scalar=qn[:, c:c + 1], in1=rstd[:], op0=ALU.mult, op1=ALU.mult)
                    for c in range(2):
                        mmK(P, C.ps[4 + c][:], [(wkv[:, kc, c * 128:(c + 1) * 128], C.hT[:, kc, cs]) for kc in range(8)], reads=[wkvk, hk], writes=[f"ps{4 + c}"])
                        P.I("act", "activation", reads=[f"ps{4 + c}"], writes=[f"ml_sq{c % 2}"], out=sq[c % 2][:], in_=C.ps[4 + c][:], func=AF.Square)
                        P.I("pe", "matmul", reads=[f"ml_sq{c % 2}", "onesb"], writes=["ps6"], out=C.ps[6][:], lhsT=C.onesb[:], rhs=sq[c % 2][:],
                            start=(c == 0), stop=(c == 1))
                    P.I("act", "activation", reads=["ps6", "ml_eps"], writes=["ml_rstd"], out=rstd[:], in_=C.ps[6][:], func=AF.Sqrt, bias=eps6[:, 0:1], scale=1.0 / 256.0)
                    P.I("dve", "reciprocal", reads=["ml_rstd"], writes=["ml_rstd"], out=rstd[:], in_=rstd[:])
                    for c in range(2):
                        P.I("dve", "scalar_tensor_tensor", reads=[f"ps{4 + c}", "ml_rstd", "ml_kvn"], writes=["ml_ckvf"], out=ckvf[:, c, :], in0=C.ps[4 + c][:],
                            scalar=kvn[:, c:c + 1], in1=rstd[:], op0=ALU.mult, op1=ALU.mult)
                    P.I("act", "activation", reads=["ml_ckvf"], writes=["ml_ckvnT"], out=ckvnT[:, :, ls], in_=ckvf[:], func=AF.Copy)
                    mmK(P, C.ps[7][64:96, :], [(wkv[:, kc, 256:288], C.hT[:, kc, cs]) for kc in range(8)], reads=[wkvk, hk], writes=["ps7"])
                    if blk == 0:
                        P.I("act", "activation", reads=["ps7"], writes=["ml_KKpe"], out=KK[64:96, ls], in_=C.ps[7][64:96, :], func=AF.Copy)
                    else:
                        mmK(P, C.ps[3][64:96, :], [(wks[:, kc, :], C.hT[:, kc, cs]) for kc in range(8)], reads=[wkvk, hk], writes=["ps3"])
                        for hq in range(2):
                            l2 = slice(hq * 256, (hq + 1) * 256); pos = slice(gq * 512 + hq * 256, gq * 512 + (hq + 1) * 256)
                            P.I("dve", "tensor_tensor", reads=["ps7", "ml_rope"], writes=["ml_t1"], out=t1[64:96, :], in0=C.ps[7][64:96, l2], in1=rope[64:96, 0, pos], op=ALU.mult)
                            P.I("dve", "tensor_tensor", reads=["ps3", "ml_rope"], writes=["ml_t2"], out=t2[64:96, :], in0=C.ps[3][64:96, l2], in1=rope[64:96, 1, pos], op=ALU.mult)
                            P.I("dve", "tensor_tensor", reads=["ml_t1", "ml_t2"], writes=["ml_KKpe"], out=KK[64:96, pos], in0=t1[64:96, :], in1=t2[64:96, :], op=ALU.add)
                    if blk == 0:
                        for tl in range(4):
                            gt = gq * 4 + tl; b = tl % 2
                            P.G("pe", [("transpose", dict(out=C.ps[2][:, c * 128:(c + 1) * 128], in_=ckvf[:, c, tl * 128:(tl + 1) * 128], identity=C.identf[:]))
                                       for c in range(2)], reads=["ml_ckvf", "identf"], writes=["ps2"])
                            P.I("dve", "tensor_copy", reads=["ps2"], writes=[f"ml_stg{b}"], out=stg[b][:], in_=C.ps[2][:, 0:256])
                            P.D("sp", f"o_ckv{b}", C.o_ckv[gt * 128:(gt + 1) * 128, :], stg[b][:], reads=[f"ml_stg{b}"])
                            mmK(P, C.ps[2][:, 256:288], [(C.hT[:, kc, gt * 128:(gt + 1) * 128], wkv[:, kc, 256:288]) for kc in range(8)], reads=[wkvk, hk], writes=["ps2"])
                            P.I("dve", "tensor_copy", reads=["ps2"], writes=["ml_stk"], out=stk[:], in_=C.ps[2][:, 256:288])
                            P.D("sp", "o_kpe", C.o_kpe[gt * 128:(gt + 1) * 128, :], stk[:], reads=["ml_stk"])
                if blk == 1:
                    for tl in range(4):
                        b = tl % 2
                        P.D("sp", f"ml_stg{b}", stg[b][:], C.mla_ckv_ctx[tl * 128:(tl + 1) * 128, :], writes=[f"ml_stg{b}"])
                        P.G("pe", [("transpose", dict(out=C.ps[2][:, c * 128:(c + 1) * 128], in_=stg[b][:, c * 128:(c + 1) * 128], identity=C.identf[:]))
                                   for c in range(2)], reads=[f"ml_stg{b}", "identf"], writes=["ps2"])
                        P.I("act", "activation", reads=["ps2"], writes=["ml_ckvnT"], out=ckvnT[:, :, 1024 + tl * 128:1024 + (tl + 1) * 128],
                            in_=C.ps[2][:, 0:256].rearrange("p (c t) -> p c t", c=2), func=AF.Copy)
                        P.D("sp", "ml_stk", stk[:], C.mla_kpe_ctx[tl * 128:(tl + 1) * 128, :], writes=["ml_stk"])
                        P.I("dve", "tensor_copy", reads=["ml_stk"], writes=["ml_stkb"], out=stkb[:], in_=stk[:])
                        psb = C.ps[2][:].bitcast(BF16)
                        P.I("pe", "transpose", reads=["ml_stkb", "identb"], writes=["ps2"], out=psb[64:96, 512:640], in_=stkb[:], identity=C.identb[:])
                        P.I("dve", "tensor_copy", reads=["ps2"], writes=["ml_KKpe"], out=KK[64:96, 1024 + tl * 128:1024 + (tl + 1) * 128], in_=psb[64:96, 512:640])
                P.I("dve", "tensor_copy", reads=["ml_KKpe"], writes=["ml_KKpe2"], out=KK2[64:96, 0:nkeys], in_=KK[64:96, 0:nkeys])
                P.barrier()
            with ExitStack() as esB:
                KKs = [KK, KK2]
                Vh = [P.sb(f"ml_Vh{i}", [128, 12, 65], BF16, esB) for i in range(2)]
                QQ = [P.sb(f"ml_QQ{i}", [128, 1024], BF16, esB) for i in range(2)]
                PT = [P.sb(f"ml_PT{i}", [128, 512], BF16, esB) for i in range(2)]
                sgT = [P.sb(f"ml_sgT{i}", [128, 1024], BF16, esB) for i in range(2)]
                on2 = P.sb("ml_on2", [128, 8, 128], BF16, esB)
                rcp = P.sb("ml_rcp", [128, 4], F32, esB)
                u1 = P.sb("ml_u1", [128, 256], F32, esB); u2 = P.sb("ml_u2", [128, 256], F32, esB)
                for i in range(2):
                    P.I("dve", "memset", writes=[f"ml_Vh{i}"], ap=Vh[i][:, :, 64:65], constant=1.0)
                nkt = nkeys // 128
                units = [(slice(s_ * 256, (s_ + 1) * 256), [2 * s_, 2 * s_ + 1]) for s_ in range(4)] if blk == 0 else \
                        [(slice(g_ * 512, (g_ + 1) * 512), list(range(12))) for g_ in range(2)]
                sc_i = [0]
                hk = f"hT{blk}"
                W = {}

                def proj(h):
                    hp, hh = h // 2, h % 2; bsel = h % 2
                    if hh == 0:
                        W[hp] = C.ring.load_parts([wqb_all[:, :, hp * 192:(hp + 1) * 192], wqs_all[:, :, hp * 64:(hp + 1) * 64],
                                                   wkvb_all[:, :, hp * 256:(hp + 1) * 256], wv_all[:, :, 672 + hp * 128:672 + (hp + 1) * 128]])
                    (wqb, wqs, wkb, wg), wk = W[hp]
                    if hh == 0:
                        for g2 in range(2):
                            cs = slice(blk * 1024 + g2 * 512, blk * 1024 + (g2 + 1) * 512); ls = slice(g2 * 512, (g2 + 1) * 512)
                            mmK(P, C.ps[6][:], [(wg[:, kc, :], C.hT[:, kc, cs]) for kc in range(8)], reads=[wk, hk], writes=["ps6"])
                            P.I("act", "activation", reads=["ps6"], writes=[f"ml_sgT{hp % 2}"], out=sgT[hp % 2][:, ls], in_=C.ps[6][:], func=AF.Silu)
                            yield
                    KKb = KKs[bsel]
                    for kg in range(nkeys // 512):
                        ks = slice(kg * 512, (kg + 1) * 512)
                        mmK(P, C.ps[6][0:64, :], [(wkb[:, kc, hh * 128:hh * 128 + 64], ckvnT[:, kc, ks]) for kc in range(2)], reads=[wk, "ml_ckvnT"], writes=["ps6"])
                        P.I("dve", "tensor_copy", reads=["ps6"], writes=[f"ml_KKn{bsel}"], out=KKb[0:64, ks], in_=C.ps[6][0:64, :])
                        yield
                    for k8 in range((nkt + 7) // 8):
                        n8 = min(8, nkt - k8 * 8)
                        P.G("pe", [("matmul", dict(out=C.ps[7][:, j * 64:(j + 1) * 64], lhsT=ckvnT[:, kc, (k8 * 8 + j) * 128:(k8 * 8 + j + 1) * 128],
                                                   rhs=wkb[:, kc, hh * 128 + 64:hh * 128 + 128], start=(kc == 0), stop=(kc == 1)))
                                   for j in range(n8) for kc in range(2)], reads=[wk, "ml_ckvnT"], writes=["ps7"])
                        P.I("dve", "tensor_copy", reads=["ps7"], writes=[f"ml_Vh{bsel}"], out=Vh[bsel][:, k8 * 8:k8 * 8 + n8, 0:64],
                            in_=C.ps[7][:, 0:n8 * 64].rearrange("p (a b) -> p a b", a=n8))
                        yield
                    Qb = QQ[bsel]
                    for g2 in range(2):
                        ls = slice(g2 * 512, (g2 + 1) * 512)
                        mmK(P, C.ps[6][0:64, :], [(wqb[:, kc, hh * 96:hh * 96 + 64], cqnT[:, kc, ls]) for kc in range(3)], reads=[wk, "ml_cqnT"], writes=["ps6"])
                        P.I("dve", "tensor_copy", reads=["ps6"], writes=[f"ml_QQn{bsel}"], out=Qb[0:64, ls], in_=C.ps[6][0:64, :])
                        yield
                        mmK(P, C.ps[7][64:96, :], [(wqb[:, kc, hh * 96 + 64:hh * 96 + 96], cqnT[:, kc, ls]) for kc in range(3)], reads=[wk, "ml_cqnT"], writes=["ps7"])
                        if blk == 0:
                            P.I("dve", "tensor_copy", reads=["ps7"], writes=[f"ml_QQr{bsel}"], out=Qb[64:96, ls], in_=C.ps[7][64:96, :])
                        else:
                            mmK(P, C.ps[6][64:96, :], [(wqs[:, kc, hh * 32:(hh + 1) * 32], cqnT[:, kc, ls]) for kc in range(3)], reads=[wk, "ml_cqnT"], writes=["ps6"])
                            for hq in range(2):
                                l2 = slice(hq * 256, (hq + 1) * 256); pos = slice(g2 * 512 + hq * 256, g2 * 512 + (hq + 1) * 256)
                                P.I("dve", "tensor_tensor", reads=["ps7", "ml_rope"], writes=["ml_u1"], out=u1[64:96, :], in0=C.ps[7][64:96, l2], in1=rope[64:96, 0, pos], op=ALU.mult)
                                P.I("dve", "tensor_tensor", reads=["ps6", "ml_rope"], writes=["ml_u2"], out=u2[64:96, :], in0=C.ps[6][64:96, l2], in1=rope[64:96, 1, pos], op=ALU.mult)
                                P.I("dve", "tensor_tensor", reads=["ml_u1", "ml_u2"], writes=[f"ml_QQr{bsel}"], out=Qb[64:96, pos], in0=u1[64:96, :], in1=u2[64:96, :], op=ALU.add)
                        yield

                def attn(h):
                    hh = h % 2; bsel = h % 2
                    KKb = KKs[bsel]; Qb = QQ[bsel]; Vb = Vh[bsel]
                    kkeys = [f"ml_KKn{bsel}", "ml_KKpe" if bsel == 0 else "ml_KKpe2", f"ml_QQn{bsel}", f"ml_QQr{bsel}"]
                    for (qsl, kts) in units:
                        nq = qsl.stop - qsl.start; nqt = nq // 128; qt0 = qsl.start // 128
                        def qk(kt):
                            sb_ = sc_i[0] % 2; sc_i[0] += 1
                            P.I("pe", "matmul", reads=kkeys, writes=[f"ps{sb_}"], out=C.ps[sb_][:, 0:nq],
                                lhsT=KKb[0:96, kt * 128:(kt + 1) * 128], rhs=Qb[0:96, qsl], start=True, stop=True)
                            P.I("act", "activation", reads=[f"ps{sb_}"], writes=[f"ml_PT{sb_}"], out=PT[sb_][:, 0:nq], in_=C.ps[sb_][:, 0:nq], func=AF.Exp, scale=SM_SCALE)
                            return sb_
                        pend = qk(kts[0])
                        for ki, kt in enumerate(kts):
                            pi_ = pend
                            if ki + 1 < len(kts):
                                pend = qk(kts[ki + 1])
                            for qt in range(nqt):
                                P.I("pe", "matmul", reads=[f"ml_PT{pi_}", f"ml_Vh{bsel}"], writes=[f"ps{2 + qt}"], out=C.ps[2 + qt][:, 0:65],
                                    lhsT=PT[pi_][:, qt * 128:(qt + 1) * 128], rhs=Vb[:, kt, :], start=(ki == 0), stop=(ki == len(kts) - 1))
                            yield
                        for qt in range(nqt):
                            P.I("dve", "reciprocal", reads=[f"ps{2 + qt}"], writes=["ml_rcp"], out=rcp[:, qt:qt + 1], in_=C.ps[2 + qt][:, 64:65])
                            P.I("dve", "tensor_scalar", reads=[f"ps{2 + qt}", "ml_rcp"], writes=["ml_on2"], out=on2[:, qt0 + qt, hh * 64:(hh + 1) * 64],
                                in0=C.ps[2 + qt][:, 0:64], scalar1=rcp[:, qt:qt + 1], scalar2=None, op0=ALU.mult)
                        yield

                def finish_pair(hp):
                    psb = C.ps[7][:].bitcast(BF16)
                    for g2 in range(2):
                        cs = slice(blk * 1024 + g2 * 512, blk * 1024 + (g2 + 1) * 512); ls = slice(g2 * 512, (g2 + 1) * 512)
                        P.G("pe", [("transpose", dict(out=psb[:, i * 128:(i + 1) * 128], in_=on2[:, g2 * 4 + i, :], identity=C.identb[:])) for i in range(4)],
                            reads=["ml_on2", "identb"], writes=["ps7"])
                        P.I("dve", "tensor_tensor", reads=["ps7", f"ml_sgT{hp % 2}"], writes=[f"mT{blk * 2 + g2}"], out=C.mT[:, hp, cs], in0=psb[:, 0:512],
                            in1=sgT[hp % 2][:, ls], op=ALU.mult)

                _roundrobin([proj(0)])
                for h in range(16):
                    gens = [attn(h)]
                    if h + 1 < 16:
                        gens.append(proj(h + 1))
                    _roundrobin(gens)
                    if h % 2 == 1:
                        finish_pair(h // 2)
                P.barrier()
LAYERS = (0, 1, 2, 3)
_CACHE = {}

def build_nc(layers):
    nc = bass.Bass("TRN2", target_bir_lowering=False)
    C = Ctx()
    declare_io(nc, C)
    with ExitStack() as es:
        P = Prog(nc, es)
        setup_persistent(P, C)
        ada_phase(P, C, layers[0])
        input_transposes(P, C)
        for li, l in enumerate(layers):
            modulate_phase(P, C, l)
            [hgrn_layer, sconv_layer, rglru_layer, mla_layer][l % 4](P, C)
            pre = wout_prefetch(P, C)
            P.barrier()
            last = (li + 1 == len(layers))
            wout_ln_phase(P, C, l, pre, next_ada=(None if last else layers[li + 1]), final=last)
        outs = [n for n in P.dall if n.startswith(("yout", "o_hg", "o_rg", "o_ckv", "o_kpe"))]
        P.final_wait("sp", outs)
        with nc.Block() as block:
            P.emit(block)
        C.counts = dict(P.cnt); print('instr counts', C.counts, 'nsem', len(P.esems) + sum(len(v) for v in P.dall.values()))
    return nc, C

def colT(v, nch):
    return np.ascontiguousarray(np.asarray(v, np.float32).reshape(nch, 128).T)

def make_in_maps(I):
    f = lambda a: np.ascontiguousarray(np.asarray(a, np.float32))
    half = 8; r = np.arange(1024) // 64; cpos = np.arange(1024) % 64
    inv = (10000.0 ** (-np.arange(0, 16, 2, dtype=np.float32) / 16)).astype(np.float32)
    cs = np.zeros((128, 2, 1024), np.float32)
    for part, pos in ((0, r), (1, cpos)):
        ang = pos[None, :].astype(np.float32) * inv[:, None]
        co, si = np.cos(ang), np.sin(ang)
        b = 64 + part * 16
        cs[b:b + 8, 0] = co; cs[b + 8:b + 16, 0] = co
        cs[b:b + 8, 1] = -si; cs[b + 8:b + 16, 1] = si
    s_ = np.arange(128)[:, None]; t_ = np.arange(128)[None, :]
    same = (s_ // 32) == (t_ // 32)
    masks = np.zeros((128, 4, 128), np.float32)
    masks[:, 0] = same & (s_ <= t_); masks[:, 1] = same & (s_ >= t_)
    masks[:, 2] = np.tile((np.arange(128) % 32 != 0).astype(np.float32), (128, 1))
    masks[:, 3] = np.tile((np.arange(128) % 32 != 31).astype(np.float32), (128, 1))
    swap = np.arange(32).reshape(2, 2, 8)[:, ::-1, :].reshape(-1)
    wqb = f(I['mla_w_qb'][0])
    qb_r = wqb.reshape(384, 16, 96)[:, :, 64:]
    shared = {
        'ada_w': f(I['ada_w']), 'ada_bT': np.ascontiguousarray(f(I['ada_b']).reshape(4, 24, 128).transpose(2, 0, 1)),
        'ln_gT': np.ascontiguousarray(f(I['ln_g']).reshape(4, 8, 128).transpose(2, 0, 1)),
        'ln_bT': np.ascontiguousarray(f(I['ln_b']).reshape(4, 8, 128).transpose(2, 0, 1)),
        'ident': np.eye(128, dtype=np.float32), 'ln_g_row': f(I['ln_g']), 'ln_b_row': f(I['ln_b']),
        'sc_w_in': f(I['sc_w_in'][0]), 'sc_cw': np.ascontiguousarray(f(I['sc_conv_w'][0]).reshape(3, 8, 128).transpose(2, 0, 1)),
        'sc_cb': colT(I['sc_conv_b'][0], 8), 'sc_w_out': f(I['sc_w_out'][0]),
        'rg_w_in': f(I['rg_w_in'][0]), 'rg_cw': np.ascontiguousarray(f(I['rg_conv_w'][0]).reshape(4, 8, 128).transpose(2, 0, 1)),
        'rg_cb': colT(I['rg_conv_b'][0], 8), 'rg_w_gate': f(I['rg_w_gate'][0]),
        'rg_bg': np.ascontiguousarray(f(I['rg_b_gate'][0]).reshape(2, 4, 4, 128).transpose(3, 0, 1, 2)),
        'rg_lam': np.ascontiguousarray(f(I['rg_lambda'][0]).reshape(2, 8, 128).transpose(2, 0, 1)),
        'rg_w_out': f(I['rg_w_out'][0]),
        'hg_w_in': f(I['hg_w_in'][0]),
        'hg_lbl': np.ascontiguousarray(f(I['hg_lb_logits']).reshape(2, 5, 8, 128).transpose(3, 0, 2, 1).reshape(128, 16, 5)),
        'hg_ng': colT(I['hg_norm_g'][0], 8), 'hg_w_out': f(I['hg_w_out'][0]), 'hg_masks': masks,
        'mla_w_in': f(I['mla_w_in'][0]), 'mla_qn': colT(I['mla_q_norm'][0], 3), 'mla_kvn': colT(I['mla_kv_norm'][0], 2),
        'mla_w_qb': wqb, 'mla_w_qbsw': np.ascontiguousarray(qb_r[:, :, swap].reshape(384, 512)),
        'mla_w_kpesw': np.ascontiguousarray(f(I['mla_w_in'][0])[:, 640:672][:, swap]),
        'mla_w_kvb': f(I['mla_w_kvb'][0]), 'mla_w_out': f(I['mla_w_out'][0]), 'rope_cs': cs,
    }
    maps = []
    xp = f(I['x_prompt']); xs = f(I['x_sample'])
    for cid in range(8):
        k = cid // 2
        m = dict(shared)
        m['xin'] = np.ascontiguousarray(np.concatenate([xp[4 * cid:4 * cid + 4].reshape(1024, D), xs[k]], 0))
        cond = np.stack([f(I['c_ctx']), f(I['c'])[k]], 1)
        m['condT'] = np.ascontiguousarray(cond.reshape(8, 128, 2).transpose(1, 0, 2))
        m['rg_h0'] = np.ascontiguousarray(f(I['state_rglru'])[k, 0].reshape(2, 8, 128).transpose(2, 0, 1))
        m['hg_s0'] = f(I['state_hgrn'])[k, 0]
        m['mla_ckv_ctx'] = f(I['cache_mla_ckv'])[k, 0]; m['mla_kpe_ctx'] = f(I['cache_mla_kpe'])[k, 0]
        maps.append(m)
    return maps

def run_layers(I, layers, trace=False):
    key = tuple(layers)
    if key not in _CACHE:
        _CACHE[key] = build_nc(layers)
    nc, C = _CACHE[key]
    maps = make_in_maps(I)
    res = run_bass_kernel_spmd(nc, maps, core_ids=list(range(8)), trace=trace)
    R = res.results
    y_prompt = np.concatenate([R[c]['y'][:1024].reshape(4, 256, D) for c in range(8)], 0)
    y_sample = np.stack([R[2 * k]['y'][1024:] for k in range(4)], 0)
    o_hg = np.concatenate([R[c]['o_hg'] for c in range(8)], 0)[:, None]
    o_rg = np.concatenate([R[c]['o_rg'].reshape(4, 2, D) for c in range(8)], 0)[:, None]
    o_ckv = np.concatenate([R[c]['o_ckv'].reshape(4, 256, 256) for c in range(8)], 0)[:, None]
    o_kpe = np.concatenate([R[c]['o_kpe'].reshape(4, 256, 32) for c in range(8)], 0)[:, None]
    outs = tuple(np.ascontiguousarray(a, dtype=np.float32) for a in (y_prompt, y_sample, o_hg, o_rg, o_ckv, o_kpe))
    return outs, res

def kernel(**inputs):
    outs, _ = run_layers(inputs, LAYERS)
    return outs
```

```python
import numpy as np
import concourse.bass as bass
import concourse.mybir as mybir
from concourse.bass_utils import run_bass_kernel_spmd
from contextlib import ExitStack
import numpy as np
import concourse.bass as bass
import concourse.mybir as mybir
from concourse.bass_utils import run_bass_kernel_spmd
from contextlib import ExitStack
F32 = mybir.dt.float32; BF16 = mybir.dt.bfloat16
AF = mybir.ActivationFunctionType; ALU = mybir.AluOpType
AX = mybir.AxisListType

D = 1024; T = 2048; NT = 16; ALPHA = 8.0 ** 0.25
LN_EPS = 1e-5 / (ALPHA * ALPHA)

class Prog:
    ENGS = ("pe", "act", "dve", "pool", "sp")
    SEM_M = 1000
    DSEM_MAX = 1600
    def __init__(self, nc, es):
        self.nc = nc; self.es = es
        self.q = {e: [] for e in self.ENGS}
        self.cnt = {e: 0 for e in self.ENGS}
        self.esems = {}
        self.seen = {e: {} for e in self.ENGS}
        self.lastw = {}; self.readers = {}
        self.dsems = {}
        self.dall = {}
        self.semh = {}
    def sb(self, name, shape, dt, es=None):
        self._names = getattr(self, "_names", {})
        n = self._names.get(name, 0); self._names[name] = n + 1
        if n: name = f"{name}__{n}"
        return (es or self.es).enter_context(self.nc.sbuf_tensor(name, list(shape), dt))
    def ps(self, name, shape, dt):
        return self.es.enter_context(self.nc.psum_tensor(name, list(shape), dt))
    def esem(self, eng, epoch):
        k = (eng, epoch)
        if k not in self.esems:
            self.esems[k] = self.es.enter_context(self.nc.semaphore(f"s_{eng}_{epoch}"))
        return self.esems[k]
    def dsem(self, name):
        d = self.dsems.get(name)
        if d is None or d[1] + 16 > self.DSEM_MAX:
            ep = 0 if d is None else d[2] + 1
            h = self.es.enter_context(self.nc.semaphore(f"d_{name}_{ep}"))
            d = [h, 0, ep]
            self.dsems[name] = d
            self.semh[f"d_{name}#{ep}"] = h
            self.dall.setdefault(name, []).append(d)
        return d
    def _handle(self, sk, val):
        if sk in self.ENGS:
            ep = (val - 1) // self.SEM_M
            return (self.esem(sk, ep), val - ep * self.SEM_M)
        return (self.semh[sk], val)
    def _need(self, eng, waits, dep):
        if dep is None: return
        sk, val = dep
        if sk == "pe" and eng == "pe": return
        if self.seen[eng].get(sk, 0) >= val: return
        waits[sk] = max(waits.get(sk, 0), val)
    def _deps(self, eng, reads, writes):
        waits = {}
        for k in reads: self._need(eng, waits, self.lastw.get(k))
        for k in writes:
            self._need(eng, waits, self.lastw.get(k))
            for sk, v in self.readers.get(k, {}).items(): self._need(eng, waits, (sk, v))
        for sk, v in waits.items(): self.seen[eng][sk] = v
        return [self._handle(sk, v) for sk, v in waits.items()]
    def _mark(self, dep, reads, writes):
        for k in writes:
            self.lastw[k] = dep; self.readers[k] = {}
        for k in reads:
            self.readers.setdefault(k, {})[dep[0]] = dep[1]
    def op(self, eng, fns, reads=(), writes=()):
        writes = list(writes) + [k for k in reads if k.startswith("ps") and k not in writes]
        waits = self._deps(eng, reads, writes)
        self.cnt[eng] += 1
        idx = self.cnt[eng]
        h, _ = self._handle(eng, idx)
        self.q[eng].append((fns, waits, (h, 1)))
        self._mark((eng, idx), reads, writes)
    @staticmethod
    def _mk(method, kw):
        def fn(e):
            return getattr(e, method)(**kw)
        return fn
    def I(self, eng, method, reads=(), writes=(), **kw):
        self.op(eng, [self._mk(method, kw)], reads, writes)
    def G(self, eng, items, reads=(), writes=()):
        self.op(eng, [self._mk(m, kw) for (m, kw) in items], reads, writes)
    def D(self, queue, semname, out, in_, reads=(), writes=(), **kw):
        waits = self._deps(queue, reads, writes)
        d = self.dsem(semname)
        d[1] += 16
        self.q[queue].append(([self._mk("dma_start", dict(out=out, in_=in_, **kw))], waits, (d[0], 16)))
        self._mark((f"d_{semname}#{d[2]}", d[1]), reads, writes)
    def barrier(self, engs=("pe", "act", "dve", "pool", "sp")):
        targets = [(e, self.cnt[e]) for e in ("pe", "act", "dve", "pool") if self.cnt[e] > 0]
        for n, lst in self.dall.items():
            for d in lst:
                if d[1] > 0: targets.append((f"d_{n}#{d[2]}", d[1]))
        for e in engs:
            waits = {}
            for dep in targets:
                if dep[0] == e and e == "pe": continue
                if self.seen[e].get(dep[0], 0) >= dep[1]: continue
                waits[dep[0]] = dep[1]; self.seen[e][dep[0]] = dep[1]
            if waits:
                self.q[e].append((None, [self._handle(sk, v) for sk, v in waits.items()], None))
    def final_wait(self, queue, semnames):
        for n in semnames:
            for d in self.dall[n]:
                self.q[queue].append((None, [(d[0], d[1])], None))
    def emit(self, block):
        def run(e, lst):
            for fns, waits, inc in lst:
                for (h, v) in waits: e.wait_ge(h, v)
                if fns is None: continue
                for i, fn in enumerate(fns):
                    ins = fn(e)
                    if i == len(fns) - 1 and inc is not None: ins.then_inc(inc[0], inc[1])
        @block.tensor
        def _(e): run(e, self.q["pe"])
        @block.scalar
        def _(e): run(e, self.q["act"])
        @block.vector
        def _(e): run(e, self.q["dve"])
        @block.gpsimd
        def _(e): run(e, self.q["pool"])
        @block.sync
        def _(e): run(e, self.q["sp"])


class Ctx:
    pass

def mmK(P, out, pairs, reads, writes):
    n = len(pairs)
    P.G("pe", [("matmul", dict(out=out, lhsT=a, rhs=b, start=(i == 0), stop=(i == n - 1))) for i, (a, b) in enumerate(pairs)],
        reads=reads, writes=writes)

class WRing:
    def __init__(self, P, nslot=3, elems=4096):
        self.P = P; self.n = nslot; self.i = 0
        self.bufs = [P.sb(f"wr{i}", [128, elems], BF16) for i in range(nslot)]
    def load(self, dram_ap, shape_str, **dims):
        s = self.i % self.n; self.i += 1
        shp = dram_ap.shape
        n = 1
        for v in shp[1:]: n *= v
        view = self.bufs[s][:, 0:n]
        if len(shp) == 3:
            view = view.rearrange("p (a b) -> p a b", a=shp[1])
        elif len(shp) == 4:
            view = view.rearrange("p (a b c) -> p a b c", a=shp[1], b=shp[2])
        key = f"wr{s}"
        self.P.D("pool", key, view, dram_ap, writes=[key])
        return view, key

    def load_multi(self, aps):
        s = self.i % self.n; self.i += 1
        J = len(aps); K_, N_ = aps[0].shape[1], aps[0].shape[2]
        view = self.bufs[s][:, 0:K_ * J * N_].rearrange("p (a b c) -> p a b c", a=K_, b=J)
        key = f"wr{s}"
        for j, ap in enumerate(aps):
            self.P.D("pool", key, view[:, :, j, :], ap, writes=[key])
        return view, key

    def load_parts(self, aps):
        s = self.i % self.n; self.i += 1
        key = f"wr{s}"; off = 0; views = []
        for ap in aps:
            a, b = ap.shape[1], ap.shape[2]
            v = self.bufs[s][:, off:off + a * b].rearrange("p (a b) -> p a b", a=a)
            off += a * b
            self.P.D("pool", key, v, ap, writes=[key])
            views.append(v)
        assert off <= 4096
        return views, key
def declare_io(nc, C):
    def din(name, shape):
        return nc.dram_tensor(name, list(shape), F32, kind="ExternalInput").ap()
    def dout(name, shape):
        return nc.dram_tensor(name, list(shape), F32, kind="ExternalOutput").ap()
    C.xin = din("xin", [T, D]); C.condT = din("condT", [128, 8, 2])
    C.ada_w = din("ada_w", [4, D, 3 * D]); C.ada_bT = din("ada_bT", [128, 4, 24])
    C.ln_gT = din("ln_gT", [128, 4, 8]); C.ln_bT = din("ln_bT", [128, 4, 8])
    C.ident = din("ident", [128, 128])
    C.sc_w_in = din("sc_w_in", [D, 4 * D]); C.sc_cw = din("sc_cw", [128, 3, 8]); C.sc_cb = din("sc_cb", [128, 8])
    C.sc_w_out = din("sc_w_out", [D, D])
    C.rg_w_in = din("rg_w_in", [D, 2 * D]); C.rg_cw = din("rg_cw", [128, 4, 8]); C.rg_cb = din("rg_cb", [128, 8])
    C.rg_w_gate = din("rg_w_gate", [2, 4, 256, 512]); C.rg_bg = din("rg_bg", [128, 2, 4, 4])
    C.rg_lam = din("rg_lam", [128, 2, 8]); C.rg_w_out = din("rg_w_out", [D, D])
    C.rg_h0 = din("rg_h0", [128, 2, 8])
    C.hg_w_in = din("hg_w_in", [D, 5 * D]); C.hg_lbl = din("hg_lbl", [128, 16, 5]); C.hg_ng = din("hg_ng", [128, 8])
    C.hg_w_out = din("hg_w_out", [D, D]); C.hg_s0 = din("hg_s0", [2, 8, 128, 128])
    C.hg_masks = din("hg_masks", [128, 4, 128])
    C.mla_w_in = din("mla_w_in", [D, 1696]); C.mla_qn = din("mla_qn", [128, 3]); C.mla_kvn = din("mla_kvn", [128, 2])
    C.mla_w_qb = din("mla_w_qb", [384, 1536]); C.mla_w_qbsw = din("mla_w_qbsw", [384, 512])
    C.mla_w_kpesw = din("mla_w_kpesw", [D, 32])
    C.mla_w_kvb = din("mla_w_kvb", [256, 2048]); C.mla_w_out = din("mla_w_out", [D, D])
    C.mla_ckv_ctx = din("mla_ckv_ctx", [512, 256]); C.mla_kpe_ctx = din("mla_kpe_ctx", [512, 32])
    C.rope_cs = din("rope_cs", [128, 2, 1024])
    C.y = dout("y", [T, D])
    C.o_hg = dout("o_hg", [4, 2, 8, 128, 128]); C.o_rg = dout("o_rg", [8, D])
    C.o_ckv = dout("o_ckv", [1024, 256]); C.o_kpe = dout("o_kpe", [1024, 32])

def setup_persistent(P, C):
    C.xT = P.sb("xT", [128, 8, T], F32)
    C.hT = P.sb("hT", [128, 8, T], BF16)
    C.mT = P.sb("mT", [128, 8, T], BF16)
    C.ring = WRing(P, nslot=3, elems=4096)
    C.identf = P.sb("identf", [128, 128], F32)
    C.identb = P.sb("identb", [128, 128], BF16)
    C.onesb = P.sb("onesb", [128, 128], BF16)
    C.condf = P.sb("condf", [128, 8, 2], F32)
    C.scond = P.sb("scond", [128, 8, 2], BF16)
    C.adab = P.sb("adab", [128, 4, 24], F32)
    C.lng = P.sb("lng", [128, 4, 8], F32); C.lnb = P.sb("lnb", [128, 4, 8], F32)
    C.mod = P.sb("mod", [128, 24, 2], F32)
    C.colsb = [P.sb(f"cols{i}", [128, 3, 8, 2], F32) for i in range(2)]
    C.ps = [P.ps(f"ps{i}", [128, 512], F32) for i in range(8)]
    P.D("sp", "identf", C.identf[:], C.ident[:, :], writes=["identf"])
    P.D("sp", "condf", C.condf[:], C.condT[:, :, :], writes=["condf"])
    P.D("sp", "adab", C.adab[:], C.ada_bT[:, :, :], writes=["adab"])
    P.D("sp", "lng", C.lng[:], C.ln_gT[:, :, :], writes=["lng"])
    P.D("sp", "lnb", C.lnb[:], C.ln_bT[:, :, :], writes=["lnb"])
    P.I("dve", "tensor_copy", reads=["identf"], writes=["identb"], out=C.identb[:], in_=C.identf[:])
    P.I("dve", "memset", writes=["onesb"], ap=C.onesb[:], constant=1.0)
    P.I("act", "activation", reads=["condf"], writes=["scond"], out=C.scond[:], in_=C.condf[:], func=AF.Silu)

def xk(g, fc):
    return f"xT{g}_{fc}"

def input_transposes(P, C):
    with ExitStack() as es:
        st = [P.sb(f"xst{i}", [128, D], F32, es) for i in range(2)]
        for t in range(NT):
            b = t % 2
            P.D("sp", f"xst{b}", st[b][:], C.xin[t * 128:(t + 1) * 128, :], writes=[f"xst{b}"])
            for half in range(2):
                pb = C.ps[(t * 2 + half) % 4]; pk = f"ps{(t * 2 + half) % 4}"
                P.G("pe", [("transpose", dict(out=pb[:, i * 128:(i + 1) * 128], in_=st[b][:, (half * 4 + i) * 128:(half * 4 + i + 1) * 128],
                                              identity=C.identf[:])) for i in range(4)], reads=[f"xst{b}", "identf"], writes=[pk])
                eng = "act" if half == 0 else "dve"
                outap = C.xT[:, half * 4:half * 4 + 4, t * 128:(t + 1) * 128]
                inap = pb[:].rearrange("p (c t) -> p c t", c=4)
                if eng == "act":
                    P.I("act", "activation", reads=[pk], writes=[xk(t // 4, half * 4 + i) for i in range(4)], out=outap, in_=inap, func=AF.Copy)
                else:
                    P.I("dve", "tensor_copy", reads=[pk], writes=[xk(t // 4, half * 4 + i) for i in range(4)], out=outap, in_=inap)
        P.barrier()

def output_transposes(P, C):
    with ExitStack() as es:
        st = [P.sb(f"yst{i}", [128, D], F32, es) for i in range(2)]
        for t in range(NT):
            b = t % 2
            for half in range(2):
                pb = C.ps[(t * 2 + half) % 4]; pk = f"ps{(t * 2 + half) % 4}"
                P.G("pe", [("transpose", dict(out=pb[:, i * 128:(i + 1) * 128], in_=C.xT[:, half * 4 + i, t * 128:(t + 1) * 128],
                                              identity=C.identf[:])) for i in range(4)], reads=[xk(t // 4, half * 4 + i) for i in range(4)] + ["identf"], writes=[pk])
                if half == 0:
                    P.I("act", "activation", reads=[pk], writes=[f"yst{b}"], out=st[b][:, 0:512], in_=pb[:], func=AF.Copy)
                else:
                    P.I("dve", "tensor_copy", reads=[pk], writes=[f"yst{b}"], out=st[b][:, 512:1024], in_=pb[:])
            P.D("sp", f"yout{b}", C.y[t * 128:(t + 1) * 128, :], st[b][:], reads=[f"yst{b}"])
        P.barrier()

def ada_phase(P, C, l):
    cols = C.colsb[l % 2]; ck = f"cols{l % 2}"
    wv_all = C.ada_w[l].rearrange("(k p) n -> p k n", p=128)
    psA = C.ps[4]
    for piece in range(6):
        wv, wk = C.ring.load(wv_all[:, :, piece * 512:(piece + 1) * 512], "")
        for f4 in range(4):
            fc = piece * 4 + f4
            mmK(P, psA[:, fc * 2:fc * 2 + 2], [(wv[:, kc, f4 * 128:(f4 + 1) * 128], C.scond[:, kc, :]) for kc in range(8)],
                reads=[wk, "scond"], writes=["ps4"])
    P.I("dve", "tensor_tensor", reads=["ps4", "adab"], writes=["mod"], out=C.mod[:],
        in0=psA[:, 0:48].rearrange("p (f j) -> p f j", j=2), in1=C.adab[:, l, :].unsqueeze(2).to_broadcast([128, 24, 2]), op=ALU.add)
    P.I("dve", "tensor_copy", reads=["mod"], writes=[ck], out=cols[:, 0], in_=C.mod[:, 0:8, :])
    P.I("dve", "tensor_scalar_add", reads=["mod"], writes=[ck], out=cols[:, 1], in0=C.mod[:, 8:16, :], scalar1=1.0)
    P.I("dve", "tensor_scalar_mul", reads=["mod"], writes=[ck], out=cols[:, 2], in0=C.mod[:, 16:24, :], scalar1=1.0 / ALPHA)

def modulate_phase(P, C, l):
    cols = C.colsb[l % 2]; ck = f"cols{l % 2}"
    for j in range(2):
        for c in range(8):
            sl = slice(j * 1024, (j + 1) * 1024)
            rk = [xk(2 * j, c), xk(2 * j + 1, c), ck]
            if (c + j) % 2 == 0:
                P.I("dve", "tensor_scalar", reads=rk, writes=[f"hT{j}"], out=C.hT[:, c, sl], in0=C.xT[:, c, sl],
                    scalar1=cols[:, 1, c, j:j + 1], scalar2=cols[:, 0, c, j:j + 1], op0=ALU.mult, op1=ALU.add)
            else:
                P.I("act", "activation", reads=rk, writes=[f"hT{j}"], out=C.hT[:, c, sl], in_=C.xT[:, c, sl], func=AF.Identity,
                    scale=cols[:, 1, c, j:j + 1], bias=cols[:, 0, c, j:j + 1])

def load_wout(P, C, w_dram):
    C.wout_dram = w_dram

def wout_prefetch(P, C):
    wv = C.wout_dram.rearrange("(k p) n -> p k n", p=128)
    return [C.ring.load(wv[:, :, 0:512], ""), C.ring.load(wv[:, :, 512:1024], "")]

def wout_ln_phase(P, C, l, pre, next_ada=None):
    cols = C.colsb[l % 2]; ck = f"cols{l % 2}"
    with ExitStack() as es:
        zn = [P.sb(f"ln_zn{i}", [128, D], F32, es) for i in range(4)]
        st = [P.sb(f"ln_st{i}", [128, 12], F32, es) for i in range(2)]
        mv = [P.sb(f"ln_mv{i}", [128, 2], F32, es) for i in range(2)]
        rs = [P.sb(f"ln_rs{i}", [128, 2], F32, es) for i in range(2)]
        epsc = P.sb("ln_eps", [128, 1], F32, es)
        P.I("dve", "memset", writes=["ln_eps"], ap=epsc[:], constant=LN_EPS)
        yi = [0]; ti = [0]
        def zpass(g):
            j = g // 2; gs = slice(g * 512, (g + 1) * 512)
            for fc in range(8):
                wv, wk = pre[fc // 4]; f4 = fc % 4
                py = C.ps[6 + yi[0] % 2]; pyk = f"ps{6 + yi[0] % 2}"; yi[0] += 1
                mmK(P, py[:], [(wv[:, kc, f4 * 128:(f4 + 1) * 128], C.mT[:, kc, gs]) for kc in range(8)], reads=[wk, f"mT{g}"], writes=[pyk])
                P.I("dve", "scalar_tensor_tensor", reads=[pyk, xk(g, fc), ck], writes=[xk(g, fc)], out=C.xT[:, fc, gs], in0=py[:],
                    scalar=cols[:, 2, fc, j:j + 1], in1=C.xT[:, fc, gs], op0=ALU.mult, op1=ALU.add)
        def norm_tiles(g):
            for tl in range(4):
                tcols = slice(g * 512 + tl * 128, g * 512 + (tl + 1) * 128)
                i2 = ti[0] % 2; ti[0] += 1
                pb = [C.ps[2 * i2], C.ps[2 * i2 + 1]]; pbk = [f"ps{2 * i2}", f"ps{2 * i2 + 1}"]
                for half in range(2):
                    P.G("pe", [("transpose", dict(out=pb[half][:, i * 128:(i + 1) * 128], in_=C.xT[:, half * 4 + i, tcols], identity=C.identf[:]))
                               for i in range(4)], reads=[xk(g, half * 4 + i) for i in range(4)] + ["identf"], writes=[pbk[half]])
                    P.I("dve", "bn_stats", reads=[pbk[half]], writes=[f"ln_st{i2}"], out=st[i2][:, half * 6:(half + 1) * 6], in_=pb[half][:])
                P.I("dve", "bn_aggr", reads=[f"ln_st{i2}"], writes=[f"ln_mv{i2}"], out=mv[i2][:], in_=st[i2][:])
                P.I("act", "activation", reads=[f"ln_mv{i2}", "ln_eps"], writes=[f"ln_rs{i2}"], out=rs[i2][:, 0:1], in_=mv[i2][:, 1:2], func=AF.Sqrt,
                    bias=epsc[:, 0:1], scale=1.0)
                P.I("dve", "reciprocal", reads=[f"ln_rs{i2}"], writes=[f"ln_rs{i2}"], out=rs[i2][:, 0:1], in_=rs[i2][:, 0:1])
                P.I("dve", "scalar_tensor_tensor", reads=[f"ln_mv{i2}", f"ln_rs{i2}"], writes=[f"ln_rs{i2}"], out=rs[i2][:, 1:2], in0=mv[i2][:, 0:1],
                    scalar=-1.0, in1=rs[i2][:, 0:1], op0=ALU.mult, op1=ALU.mult)
                for half in range(2):
                    P.I("act", "activation", reads=[pbk[half], f"ln_rs{i2}"], writes=[f"ln_zn{tl}"], out=zn[tl][:, half * 512:(half + 1) * 512],
                        in_=pb[half][:], func=AF.Identity, scale=rs[i2][:, 0:1], bias=rs[i2][:, 1:2])
        def back(g):
            gs = slice(g * 512, (g + 1) * 512)
            for fc in range(8):
                bi = 4 + fc % 2
                P.G("pe", [("transpose", dict(out=C.ps[bi][:, tl * 128:(tl + 1) * 128], in_=zn[tl][:, fc * 128:(fc + 1) * 128], identity=C.identf[:]))
                           for tl in range(4)], reads=[f"ln_zn{tl}" for tl in range(4)] + ["identf"], writes=[f"ps{bi}"])
                if fc % 2 == 0:
                    P.I("act", "activation", reads=[f"ps{bi}", "lng", "lnb"], writes=[xk(g, fc)], out=C.xT[:, fc, gs], in_=C.ps[bi][:], func=AF.Identity,
                        scale=C.lng[:, l, fc:fc + 1], bias=C.lnb[:, l, fc:fc + 1])
                else:
                    P.I("dve", "tensor_scalar", reads=[f"ps{bi}", "lng", "lnb"], writes=[xk(g, fc)], out=C.xT[:, fc, gs], in0=C.ps[bi][:],
                        scalar1=C.lng[:, l, fc:fc + 1], scalar2=C.lnb[:, l, fc:fc + 1], op0=ALU.mult, op1=ALU.add)
        zpass(0)
        for g in range(4):
            if g + 1 < 4:
                zpass(g + 1)
            norm_tiles(g)
            if g == 3 and next_ada is not None:
                ada_phase(P, C, next_ada)
            back(g)
        P.barrier()
QS = 128.0 ** -0.5
CH = 64
NCH = 512 // CH
JT = 128 // CH

def _roundrobin(gens):
    gens = list(gens)
    while gens:
        nxt = []
        for g in gens:
            try:
                next(g); nxt.append(g)
            except StopIteration:
                pass
        gens = nxt

def hgrn_layer(P, C):
    with ExitStack() as es:
        lbl = P.sb("hg_lbl_s", [128, 16, 5], F32, es); lbm = P.sb("hg_lbm", [128, 16], F32, es)
        lb = P.sb("hg_lb", [128, 16], F32, es); oml = P.sb("hg_oml", [128, 16], F32, es)
        ng = P.sb("hg_ng_s", [128, 8], F32, es); eps6 = P.sb("hg_eps", [128, 1], F32, es)
        one = P.sb("hg_one", [128, 1], F32, es)
        mk = P.sb("hg_mk", [128, 2, 128], F32, es); rm = P.sb("hg_rm", [128, 2, 512], BF16, es)
        mstage = P.sb("hg_mst", [128, 2, 128], F32, es)
        P.D("sp", "hg_lbl", lbl[:], C.hg_lbl[:, :, :], writes=["hg_lbl"])
        P.D("sp", "hg_ng", ng[:], C.hg_ng[:, :], writes=["hg_ng"])
        P.D("sp", "hg_mk", mk[:], C.hg_masks[:, 0:2, :], writes=["hg_mk"])
        P.D("sp", "hg_mst", mstage[:], C.hg_masks[:, 2:4, :], writes=["hg_mst"])
        P.I("dve", "memset", writes=["hg_eps"], ap=eps6[:], constant=1e-6)
        P.I("dve", "memset", writes=["hg_one"], ap=one[:], constant=1.0)
        for d in range(2):
            for r in range(4):
                P.I("dve", "tensor_copy", reads=["hg_mst"], writes=["hg_rm"], out=rm[:, d, r * 128:(r + 1) * 128], in_=mstage[:, d, :])
        P.I("dve", "reduce_max", reads=["hg_lbl"], writes=["hg_lbm"], out=lbm[:], in_=lbl[:], axis=AX.X)
        P.I("dve", "tensor_tensor", reads=["hg_lbl", "hg_lbm"], writes=["hg_lbl"], out=lbl[:], in0=lbl[:],
            in1=lbm[:].unsqueeze(2).to_broadcast([128, 16, 5]), op=ALU.subtract)
        P.I("act", "activation", reads=["hg_lbl"], writes=["hg_lbl"], out=lbl[:], in_=lbl[:], func=AF.Exp)
        P.I("dve", "reduce_sum", reads=["hg_lbl"], writes=["hg_lbm"], out=lbm[:], in_=lbl[:], axis=AX.X)
        P.I("dve", "reciprocal", reads=["hg_lbm"], writes=["hg_lbm"], out=lbm[:], in_=lbm[:])
        P.I("dve", "tensor_tensor", reads=["hg_lbl", "hg_lbm"], writes=["hg_lb"], out=lb[:], in0=lbl[:, :, 0], in1=lbm[:], op=ALU.mult)
        P.I("dve", "tensor_scalar", reads=["hg_lb"], writes=["hg_oml"], out=oml[:], in0=lb[:], scalar1=-1.0, scalar2=1.0, op0=ALU.mult, op1=ALU.add)

        vtok = P.sb("hg_vtok", [128, 16, 128], BF16, es)
        qS = P.sb("hg_q", [128, 1024], F32, es)
        ob = P.sb("hg_o", [128, 1024], F32, es)
        sgB = P.sb("hg_sgb", [128, 1024], BF16, es)
        Sf = P.sb("hg_Sf", [128, 8, 128], F32, es); Sb = P.sb("hg_Sb", [128, 8, 128], BF16, es)
        attS = [P.sb(f"hg_att{i}", [128, 128], BF16, es) for i in range(2)]
        U = []
        for u in range(2):
            B_ = Ctx()
            B_.u = u
            B_.cum = P.sb(f"hg_cum{u}", [128, 512], F32, es); B_.kS = P.sb(f"hg_k{u}", [128, 512], F32, es)
            B_.A = P.sb(f"hg_A{u}", [128, 512], F32, es); B_.B = P.sb(f"hg_B{u}", [128, 512], F32, es)
            B_.qrel = P.sb(f"hg_qrel{u}", [128, 512], BF16, es); B_.krel = P.sb(f"hg_krel{u}", [128, 512], BF16, es)
            B_.qcum = P.sb(f"hg_qcum{u}", [128, 512], BF16, es); B_.kdT = P.sb(f"hg_kdT{u}", [128, 512], BF16, es)
            B_.kdtok = P.sb(f"hg_kdtok{u}", [128, 4, 128], BF16, es); B_.etot = P.sb(f"hg_etot{u}", [128, 16], F32, es)
            U.append(B_)
        wv_all = C.hg_w_in.rearrange("(k p) n -> p k n", p=128)
        P.I("dve", "memset", writes=["hg_qrel0"], ap=U[0].qrel[:], constant=0.0)
        P.I("pe", "matmul", reads=["hg_qrel0", "identb"], writes=["ps3"], out=C.ps[3][:], lhsT=C.identb[:], rhs=U[0].qrel[:], start=True, stop=True)
        cnt = {"w": 0, "att": 0, "x": 0, "y": 0, "u": 0}
        def wps():
            i = cnt["w"] % 2; cnt["w"] += 1
            return C.ps[i], f"ps{i}"
        def quarter(bank, name):
            i = cnt[name] % 4; cnt[name] += 1
            return C.ps[bank][:, i * 128:(i + 1) * 128], f"ps{bank}"
        def uslot():
            i = cnt["u"] % 2; cnt["u"] += 1
            return C.ps[6 + i][:, 0:128], f"ps{6 + i}"

        def prep(B_, wv, wk, hd, blk, d, sg_):
            u = B_.u
            K = lambda n: f"hg_{n}{u}"
            cs = slice(blk * 1024 + sg_ * 512, blk * 1024 + (sg_ + 1) * 512); ls = slice(sg_ * 512, (sg_ + 1) * 512)
            ridx = (CH // 2 - 1) if d == 0 else (CH // 2); tidx = (CH - 1) if d == 0 else 0
            cum, kS, Ab, Bb = B_.cum, B_.kS, B_.A, B_.B
            pt, pk = wps()
            mmK(P, pt[:], [(wv[:, kc, 1 + d, :], C.hT[:, kc, cs]) for kc in range(8)], reads=[wk, f"hT{blk}"], writes=[pk]); yield
            lbc = lb[:, d * 8 + hd:d * 8 + hd + 1]; omc = oml[:, d * 8 + hd:d * 8 + hd + 1]
            P.I("act", "activation", reads=[pk], writes=[K("cum")], out=cum[:], in_=pt[:], func=AF.Exp, scale=-1.0); yield
            P.I("act", "activation", reads=[K("cum"), "hg_lb", "hg_one"], writes=[K("A")], out=Ab[:], in_=cum[:], func=AF.Ln, scale=lbc, bias=one[:, 0:1]); yield
            P.I("act", "activation", reads=[K("cum"), "hg_one"], writes=[K("B")], out=Bb[:], in_=cum[:], func=AF.Ln, scale=1.0, bias=one[:, 0:1]); yield
            P.I("dve", "tensor_tensor", reads=[K("A"), K("B")], writes=[K("cum")], out=cum[:], in0=Ab[:], in1=Bb[:], op=ALU.subtract); yield
            P.I("dve", "tensor_tensor", reads=[pk, K("B")], writes=[K("B")], out=Bb[:], in0=Bb[:], in1=pt[:], op=ALU.add); yield
            P.I("act", "activation", reads=[K("B")], writes=[K("k")], out=kS[:], in_=Bb[:], func=AF.Exp, scale=-1.0); yield
            rv = slice(None) if d == 0 else slice(None, None, -1)
            P.I("dve", "tensor_tensor_scan", reads=[K("cum"), "hg_rm"], writes=[K("cum")], out=cum[:, rv], data0=rm[:, d, rv],
                data1=cum[:, rv], initial=0.0, op0=ALU.mult, op1=ALU.add); yield
            c3 = cum[:].rearrange("p (c t) -> p c t", t=CH)
            A3 = Ab[:].rearrange("p (c t) -> p c t", t=CH)
            P.I("dve", "tensor_tensor", reads=[K("cum")], writes=[K("A")], out=A3, in0=c3,
                in1=c3[:, :, ridx:ridx + 1].to_broadcast([128, NCH, CH]), op=ALU.subtract); yield
            P.I("act", "activation", reads=[K("cum")], writes=[K("B")], out=Bb[:], in_=cum[:], func=AF.Exp); yield
            P.I("dve", "scalar_tensor_tensor", reads=["hg_q", K("B")], writes=[K("qcum")], out=B_.qcum[:], in0=qS[:, ls], scalar=QS,
                in1=Bb[:], op0=ALU.mult, op1=ALU.mult); yield
            P.I("act", "activation", reads=[K("cum")], writes=[K("etot")], out=B_.etot[:, 0:NCH], in_=c3[:, :, tidx], func=AF.Exp); yield
            P.I("act", "activation", reads=[K("A")], writes=[K("B")], out=Bb[:], in_=Ab[:], func=AF.Exp); yield
            P.I("dve", "scalar_tensor_tensor", reads=["hg_q", K("B")], writes=[K("qrel")], out=B_.qrel[:], in0=qS[:, ls], scalar=QS,
                in1=Bb[:], op0=ALU.mult, op1=ALU.mult); yield
            P.I("act", "activation", reads=[K("A")], writes=[K("B")], out=Bb[:], in_=Ab[:], func=AF.Exp, scale=-1.0); yield
            P.I("dve", "scalar_tensor_tensor", reads=[K("k"), K("B"), "hg_oml"], writes=[K("krel")], out=B_.krel[:], in0=kS[:], scalar=omc, in1=Bb[:],
                op0=ALU.mult, op1=ALU.mult); yield
            P.I("dve", "tensor_tensor", reads=[K("cum")], writes=[K("A")], out=A3, in0=c3,
                in1=c3[:, :, tidx:tidx + 1].to_broadcast([128, NCH, CH]), op=ALU.subtract); yield
            P.I("act", "activation", reads=[K("A")], writes=[K("B")], out=Bb[:], in_=Ab[:], func=AF.Exp, scale=-1.0); yield
            P.I("dve", "scalar_tensor_tensor", reads=[K("k"), K("B"), "hg_oml"], writes=[K("kdT")], out=B_.kdT[:], in0=kS[:], scalar=omc, in1=Bb[:],
                op0=ALU.mult, op1=ALU.mult); yield
            psT = C.ps[2][:].bitcast(BF16)
            P.G("pe", [("transpose", dict(out=psT[:, i * 128:(i + 1) * 128], in_=B_.kdT[:, i * 128:(i + 1) * 128], identity=C.identb[:]))
                       for i in range(4)], reads=[K("kdT"), "identb"], writes=["ps2"])
            P.I("act", "activation", reads=["ps2"], writes=[K("kdtok")], out=B_.kdtok[:], in_=psT[:, 0:512].rearrange("p (a b) -> p a b", a=4),
                func=AF.Copy); yield

        for hd in range(8):
            if hd == 4:
                load_wout(P, C, C.hg_w_out)
            wv, wk = C.ring.load_multi([wv_all[:, :, j * 1024 + hd * 128:j * 1024 + (hd + 1) * 128] for j in range(4)])
            gv, gk = C.ring.load(wv_all[:, :, 4096 + hd * 128:4096 + (hd + 1) * 128], "")
            for t4 in range(4):
                vb = 2 if t4 % 2 == 0 else 4
                P.G("pe", [("matmul", dict(out=C.ps[vb][:, i * 128:(i + 1) * 128], lhsT=C.hT[:, kc, (t4 * 4 + i) * 128:(t4 * 4 + i + 1) * 128],
                                           rhs=wv[:, kc, 3, :], start=(kc == 0), stop=(kc == 7))) for i in range(4) for kc in range(8)],
                    reads=[wk, f"hT{t4 // 2}"], writes=[f"ps{vb}"])
                if t4 % 2 == 0:
                    P.I("act", "activation", reads=[f"ps{vb}"], writes=["hg_vtok"], out=vtok[:, t4 * 4:t4 * 4 + 4, :],
                        in_=C.ps[vb][:].rearrange("p (a b) -> p a b", a=4), func=AF.Copy)
                else:
                    P.I("dve", "tensor_copy", reads=[f"ps{vb}"], writes=["hg_vtok"], out=vtok[:, t4 * 4:t4 * 4 + 4, :],
                        in_=C.ps[vb][:].rearrange("p (a b) -> p a b", a=4))
            for blk in range(2):
                for g2 in range(2):
                    cs = slice(blk * 1024 + g2 * 512, blk * 1024 + (g2 + 1) * 512); ls = slice(g2 * 512, (g2 + 1) * 512)
                    pt, pk = wps()
                    mmK(P, pt[:], [(wv[:, kc, 0, :], C.hT[:, kc, cs]) for kc in range(8)], reads=[wk, f"hT{blk}"], writes=[pk])
                    P.I("act", "activation", reads=[pk], writes=["hg_q"], out=qS[:, ls], in_=pt[:], func=AF.Silu)
                for g2 in range(2):
                    cs = slice(blk * 1024 + g2 * 512, blk * 1024 + (g2 + 1) * 512); ls = slice(g2 * 512, (g2 + 1) * 512)
                    pt, pk = wps()
                    mmK(P, pt[:], [(gv[:, kc, :], C.hT[:, kc, cs]) for kc in range(8)], reads=[gk, f"hT{blk}"], writes=[pk])
                    P.I("act", "activation", reads=[pk], writes=["hg_sgb"], out=sgB[:, ls], in_=pt[:], func=AF.Silu)
                P.I("pool", "memset", writes=[f"hg_o{t_}" for t_ in range(8)], ap=ob[:], constant=0.0)
                if blk == 0:
                    P.I("pool", "memset", writes=[f"hg_Sf{c_}" for c_ in range(8)], ap=Sf[:], constant=0.0)
                    P.I("pool", "memset", writes=[f"hg_Sb{c_}" for c_ in range(8)], ap=Sb[:], constant=0.0)
                else:
                    for d in range(2):
                        P.D("sp", f"hg_s0{d}", Sf[:, d, :], C.hg_s0[d, hd], writes=[f"hg_Sf{d}"])
                        P.I("act", "activation", reads=[f"hg_Sf{d}"], writes=[f"hg_Sb{d}"], out=Sb[:, d, :], in_=Sf[:, d, :], func=AF.Copy)
                for step in range(2):
                    units = [(0, step, U[0]), (1, 1 - step, U[1])]
                    _roundrobin([prep(B_, wv, wk, hd, blk, d, sg_) for (d, sg_, B_) in units])
                    chains = []
                    for (d, sg_, B_) in units:
                        if blk == 0:
                            cl = [(d * 4 + 2 * sg_, [0, 1]), (d * 4 + 2 * sg_ + 1, [2, 3])]
                        else:
                            cl = [(d, [0, 1, 2, 3])]
                        for ch, tl in cl:
                            chains.append((d, sg_, B_, ch, tl if d == 0 else tl[::-1]))
                    npos = len(chains[0][4])
                    for pos in range(npos):
                        info = []
                        for (d, sg_, B_, ch, tl) in chains:
                            u = B_.u
                            tloc = tl[pos]; gt = blk * 8 + sg_ * 4 + tloc; ts_ = slice(tloc * 128, (tloc + 1) * 128)
                            pa, pak = quarter(3, "att")
                            t0 = tloc * 128
                            blocks = []
                            for cb in range(JT):
                                b0 = cb * CH; hC = CH // 2
                                if d == 0:
                                    blocks += [(b0, hC, b0, CH), (b0 + hC, hC, b0 + hC, hC)]
                                else:
                                    blocks += [(b0 + hC, hC, b0, CH), (b0, hC, b0, hC)]
                            P.G("pe", [("matmul", dict(out=pa[s0:s0 + sn, q0:q0 + qn], lhsT=B_.krel[:, t0 + s0:t0 + s0 + sn], rhs=B_.qrel[:, t0 + q0:t0 + q0 + qn],
                                                       start=True, stop=True, tile_position=(0, s0))) for (s0, sn, q0, qn) in blocks],
                                reads=[f"hg_krel{u}", f"hg_qrel{u}"], writes=[pak])
                            ai = cnt["att"] % 2
                            P.I("dve", "tensor_tensor", reads=[pak, "hg_mk"], writes=[f"hg_att{ai}"], out=attS[ai][:], in0=pa, in1=mk[:, d, :], op=ALU.mult)
                            px, pxk = quarter(4, "x")
                            P.I("pe", "matmul", reads=["hg_vtok", f"hg_att{ai}"], writes=[pxk], out=px, lhsT=vtok[:, gt, :], rhs=attS[ai][:], start=True, stop=True)
                            py, pyk = quarter(5, "y")
                            info.append((d, sg_, B_, ch, tloc, gt, px, pxk, py, pyk))
                        for jj in range(JT):
                            for (d, sg_, B_, ch, tloc, gt, px, pxk, py, pyk) in info:
                                u = B_.u
                                j = jj if d == 0 else JT - 1 - jj
                                qs_ = slice(tloc * 128 + j * CH, tloc * 128 + (j + 1) * CH)
                                P.I("pe", "matmul", reads=[f"hg_Sb{ch}", f"hg_qcum{u}"], writes=[pyk], out=py[:, j * CH:(j + 1) * CH], lhsT=Sb[:, ch, :],
                                    rhs=B_.qcum[:, qs_], start=True, stop=True)
                                pu, puk = uslot()
                                P.I("pe", "matmul", reads=[f"hg_kdtok{u}", "hg_vtok"], writes=[puk], out=pu, lhsT=B_.kdtok[j * CH:(j + 1) * CH, tloc, :],
                                    rhs=vtok[j * CH:(j + 1) * CH, gt, :], start=True, stop=True, tile_position=(j * CH, 0))
                                P.I("dve", "scalar_tensor_tensor", reads=[f"hg_Sf{ch}", puk, f"hg_etot{u}"], writes=[f"hg_Sf{ch}"], out=Sf[:, ch, :],
                                    in0=Sf[:, ch, :], scalar=B_.etot[:, tloc * JT + j:tloc * JT + j + 1], in1=pu, op0=ALU.mult, op1=ALU.add)
                                P.I("act", "activation", reads=[f"hg_Sf{ch}"], writes=[f"hg_Sb{ch}"], out=Sb[:, ch, :], in_=Sf[:, ch, :], func=AF.Copy)
                        for (d, sg_, B_, ch, tloc, gt, px, pxk, py, pyk) in info:
                            t8 = sg_ * 4 + tloc
                            os_ = slice(t8 * 128, (t8 + 1) * 128); ok_ = f"hg_o{t8}"
                            P.I("dve", "tensor_tensor", reads=[pxk, ok_], writes=[ok_], out=ob[:, os_], in0=ob[:, os_], in1=px, op=ALU.add)
                            P.I("dve", "tensor_tensor", reads=[pyk, ok_], writes=[ok_], out=ob[:, os_], in0=ob[:, os_], in1=py, op=ALU.add)
                    if blk == 0:
                        for (d, sg_, B_, ch, tl) in chains:
                            P.D("sp", f"o_hg{ch}", C.o_hg[ch % 4, d, hd], Sf[:, ch, :], reads=[f"hg_Sf{ch}"])
                osq = U[0].qrel; rsb = U[0].B; sgb = U[0].A
                for g2 in range(2):
                    cs = slice(blk * 1024 + g2 * 512, blk * 1024 + (g2 + 1) * 512); ls = slice(g2 * 512, (g2 + 1) * 512)
                    okeys = [f"hg_o{g2 * 4 + t_}" for t_ in range(4)]
                    P.I("act", "activation", reads=okeys, writes=["hg_qrel0"], out=osq[:], in_=ob[:, ls], func=AF.Square)
                    pt, pk = wps()
                    P.I("pe", "matmul", reads=["hg_qrel0", "onesb"], writes=[pk], out=pt[:], lhsT=C.onesb[:], rhs=osq[:], start=True, stop=True)
                    P.I("act", "activation", reads=[pk, "hg_eps"], writes=["hg_B0"], out=rsb[:], in_=pt[:], func=AF.Ln, bias=eps6[:, 0:1], scale=1.0 / 128.0)
                    P.I("act", "activation", reads=["hg_B0"], writes=["hg_B0"], out=rsb[:], in_=rsb[:], func=AF.Exp, scale=-0.5)
                    P.I("dve", "tensor_tensor", reads=["hg_B0"] + okeys, writes=["hg_B0"], out=rsb[:], in0=rsb[:], in1=ob[:, ls], op=ALU.mult)
                    P.I("dve", "scalar_tensor_tensor", reads=["hg_B0", "hg_sgb", "hg_ng"], writes=[f"mT{blk * 2 + g2}"], out=C.mT[:, hd, cs], in0=rsb[:],
                        scalar=ng[:, hd:hd + 1], in1=sgB[:, ls], op0=ALU.mult, op1=ALU.mult)
def sconv_layer(P, C):
    with ExitStack() as es:
        cw = P.sb("sc_cw_s", [128, 3, 8], F32, es); cb = P.sb("sc_cb_s", [128, 8], F32, es)
        P.D("sp", "sc_cw", cw[:], C.sc_cw[:, :, :], writes=["sc_cw"])
        P.D("sp", "sc_cb", cb[:], C.sc_cb[:, :], writes=["sc_cb"])
        pb_ = [P.sb(f"sc_p{i}", [128, 1024], F32, es) for i in range(2)]
        zb_ = [P.sb(f"sc_z{i}", [128, 1024], F32, es) for i in range(2)]
        cgS = [P.sb(f"sc_cg{i}", [128, 512], F32, es) for i in range(2)]
        sgS = [P.sb(f"sc_sg{i}", [128, 512], F32, es) for i in range(2)]
        tS = [P.sb(f"sc_t{i}", [128, 512], F32, es) for i in range(2)]
        wv_all = C.sc_w_in.rearrange("(k p) n -> p k n", p=128)
        pi = 0
        for c in range(8):
            if c == 4:
                load_wout(P, C, C.sc_w_out)
            wv, wk = C.ring.load_multi([wv_all[:, :, j * 1024 + c * 128:j * 1024 + (c + 1) * 128] for j in range(4)])
            for blk in range(2):
                p = pb_[blk]; z = zb_[blk]; pk = f"sc_p{blk}"; zk = f"sc_z{blk}"
                for g2 in range(2):
                    cs = slice(blk * 1024 + g2 * 512, blk * 1024 + (g2 + 1) * 512); ls = slice(g2 * 512, (g2 + 1) * 512)
                    pa = pi % 4; pbk = (pi + 1) % 4; pi += 2
                    mmK(P, C.ps[pa][:], [(wv[:, kc, 1, :], C.hT[:, kc, cs]) for kc in range(8)], reads=[wk, f"hT{blk}"], writes=[f"ps{pa}"])
                    mmK(P, C.ps[pbk][:], [(wv[:, kc, 2, :], C.hT[:, kc, cs]) for kc in range(8)], reads=[wk, f"hT{blk}"], writes=[f"ps{pbk}"])
                    i2 = g2
                    P.I("act", "activation", reads=[f"ps{pa}"], writes=[f"sc_cg{i2}"], out=cgS[i2][:], in_=C.ps[pa][:], func=AF.Copy)
                    P.I("dve", "tensor_tensor", reads=[f"sc_cg{i2}", f"ps{pbk}"], writes=[pk], out=p[:, ls], in0=cgS[i2][:], in1=C.ps[pbk][:], op=ALU.mult)
                P.I("dve", "tensor_scalar", reads=[pk, "sc_cw", "sc_cb"], writes=[zk], out=z[:], in0=p[:], scalar1=cw[:, 1, c:c + 1],
                    scalar2=cb[:, c:c + 1], op0=ALU.mult, op1=ALU.add)
                if blk == 0:
                    z3 = z[:].rearrange("p (s t) -> p s t", s=4); p3 = p[:].rearrange("p (s t) -> p s t", s=4)
                    zlo, plo, zhi, phi = z3[:, :, 1:], p3[:, :, :-1], z3[:, :, :-1], p3[:, :, 1:]
                else:
                    zlo, plo, zhi, phi = z[:, 1:], p[:, :-1], z[:, :-1], p[:, 1:]
                P.I("dve", "scalar_tensor_tensor", reads=[pk, zk, "sc_cw"], writes=[zk], out=zlo, in0=plo, scalar=cw[:, 0, c:c + 1], in1=zlo,
                    op0=ALU.mult, op1=ALU.add)
                P.I("dve", "scalar_tensor_tensor", reads=[pk, zk, "sc_cw"], writes=[zk], out=zhi, in0=phi, scalar=cw[:, 2, c:c + 1], in1=zhi,
                    op0=ALU.mult, op1=ALU.add)
                for g2 in range(2):
                    cs = slice(blk * 1024 + g2 * 512, blk * 1024 + (g2 + 1) * 512); ls = slice(g2 * 512, (g2 + 1) * 512)
                    g = blk * 2 + g2
                    pa = pi % 4; pbk = (pi + 1) % 4; pi += 2
                    mmK(P, C.ps[pa][:], [(wv[:, kc, 0, :], C.hT[:, kc, cs]) for kc in range(8)], reads=[wk, f"hT{blk}"], writes=[f"ps{pa}"])
                    mmK(P, C.ps[pbk][:], [(wv[:, kc, 3, :], C.hT[:, kc, cs]) for kc in range(8)], reads=[wk, f"hT{blk}"], writes=[f"ps{pbk}"])
                    i2 = g2
                    P.I("act", "activation", reads=[f"ps{pbk}"], writes=[f"sc_sg{i2}"], out=sgS[i2][:], in_=C.ps[pbk][:], func=AF.Silu)
                    P.I("dve", "tensor_tensor", reads=[f"sc_sg{i2}", f"ps{pa}"], writes=[f"sc_t{i2}"], out=tS[i2][:], in0=sgS[i2][:], in1=C.ps[pa][:], op=ALU.mult)
                    P.I("dve", "tensor_tensor", reads=[f"sc_t{i2}", zk], writes=[f"mT{g}"], out=C.mT[:, c, cs], in0=tS[i2][:], in1=z[:, ls], op=ALU.mult)
def rglru_layer(P, C):
    with ExitStack() as es:
        cw = P.sb("rg_cw_s", [128, 4, 8], F32, es); cb = P.sb("rg_cb_s", [128, 8], F32, es)
        bg = P.sb("rg_bg_s", [128, 2, 4, 4], F32, es); lam = P.sb("rg_lam_s", [128, 2, 8], F32, es)
        clam = P.sb("rg_clam", [128, 2, 8], F32, es); h0 = P.sb("rg_h0_s", [128, 2, 8], F32, es)
        one = P.sb("rg_one", [128, 1], F32, es)
        rgst = P.sb("rg_state", [128, 8, 8], F32, es)
        P.D("sp", "rg_cw", cw[:], C.rg_cw[:, :, :], writes=["rg_cw"])
        P.D("sp", "rg_cb", cb[:], C.rg_cb[:, :], writes=["rg_cb"])
        P.D("sp", "rg_bg", bg[:], C.rg_bg[:, :, :, :], writes=["rg_bg"])
        P.D("sp", "rg_lam", lam[:], C.rg_lam[:, :, :], writes=["rg_lam"])
        P.D("sp", "rg_h0", h0[:], C.rg_h0[:, :, :], writes=["rg_h0"])
        P.I("dve", "memset", writes=["rg_one"], ap=one[:], constant=1.0)
        P.I("act", "activation", reads=["rg_lam"], writes=["rg_clam"], out=clam[:], in_=lam[:], func=AF.Exp, scale=-1.0)
        P.I("act", "activation", reads=["rg_clam", "rg_one"], writes=["rg_clam"], out=clam[:], in_=clam[:], func=AF.Ln, bias=one[:, 0:1], scale=1.0)
        P.I("dve", "tensor_scalar_mul", reads=["rg_clam"], writes=["rg_clam"], out=clam[:], in0=clam[:], scalar1=-4.0)
        half = P.sb("rg_half", [128, 1], F32, es)
        P.I("dve", "memset", writes=["rg_half"], ap=half[:], constant=0.5)
        P.I("dve", "tensor_scalar_mul", reads=["rg_bg"], writes=["rg_bg"], out=bg[:], in0=bg[:], scalar1=0.5)
        P.I("dve", "tensor_scalar_mul", reads=["rg_h0"], writes=["rg_h0"], out=h0[:], in0=h0[:], scalar1=2.0)
        uraw = P.sb("rg_uraw", [128, 1024], F32, es)
        uc = P.sb("rg_uc", [128, 2, 1024], F32, es); ucb = P.sb("rg_ucb", [128, 2, 1024], BF16, es)
        abufs = [P.sb(f"rg_a{i}", [128, 1024], F32, es) for i in range(2)]
        xbs = [[P.sb(f"rg_x{o}{i}", [128, 1024], F32, es) for i in range(2)] for o in range(2)]
        sqt = [P.sb(f"rg_sq{i}", [128, 512], F32, es) for i in range(4)]
        sgS = [P.sb(f"rg_sg{i}", [128, 512], F32, es) for i in range(2)]
        wv_all = C.rg_w_in.rearrange("(k p) n -> p k n", p=128)
        pi = [0]
        def nps():
            i = pi[0] % 6; pi[0] += 1
            return C.ps[i], f"ps{i}"
        def seg(ap2d, blk, lo, hi):
            if blk == 0:
                v = ap2d.rearrange("p (s t) -> p s t", s=4)
                return v[:, :, lo:256 + hi]
            return ap2d[:, lo:1024 + hi]
        for hh in range(4):
            if hh == 2:
                load_wout(P, C, C.rg_w_out)
            wv, wk = C.ring.load_multi([wv_all[:, :, j * 1024 + c * 128:j * 1024 + (c + 1) * 128] for j in range(2) for c in (2 * hh, 2 * hh + 1)])
            gv, gk = C.ring.load_multi([C.rg_w_gate[d, hh].rearrange("(k p) n -> p k n", p=128) for d in range(2)])
            for blk in range(2):
                for cc in range(2):
                    c = 2 * hh + cc
                    for g2 in range(2):
                        cs = slice(blk * 1024 + g2 * 512, blk * 1024 + (g2 + 1) * 512); ls = slice(g2 * 512, (g2 + 1) * 512)
                        pt, pk = nps()
                        mmK(P, pt[:], [(wv[:, kc, cc, :], C.hT[:, kc, cs]) for kc in range(8)], reads=[wk, f"hT{blk}"], writes=[pk])
                        P.I("act", "activation", reads=[pk], writes=["rg_uraw"], out=uraw[:, ls], in_=pt[:], func=AF.Copy)
                    ucc = uc[:, cc, :]
                    P.I("dve", "tensor_scalar", reads=["rg_uraw", "rg_cw", "rg_cb"], writes=["rg_uc"], out=ucc, in0=uraw[:],
                        scalar1=cw[:, 2, c:c + 1], scalar2=cb[:, c:c + 1], op0=ALU.mult, op1=ALU.add)
                    for (k, lo_o, hi_o, lo_i, hi_i) in ((0, 2, 0, 0, -2), (1, 1, 0, 0, -1), (3, 0, -1, 1, 0)):
                        o_ = seg(ucc, blk, lo_o, hi_o); i_ = seg(uraw[:], blk, lo_i, hi_i)
                        P.I("dve", "scalar_tensor_tensor", reads=["rg_uraw", "rg_uc", "rg_cw"], writes=["rg_uc"], out=o_, in0=i_,
                            scalar=cw[:, k, c:c + 1], in1=o_, op0=ALU.mult, op1=ALU.add)
                    P.I("pool", "tensor_copy", reads=["rg_uc"], writes=["rg_ucb"], out=ucb[:, cc, :], in_=ucc)
                for oc in range(2):
                    c = 2 * hh + oc
                    xb = xbs[oc]
                    for d in range(2):
                        xin = xb[d]; xk = f"rg_x{oc}{d}"; abuf = abufs[d]; ak = f"rg_a{d}"
                        for g2 in range(2):
                            ls = slice(g2 * 512, (g2 + 1) * 512)
                            pr, prk = nps(); pq, pqk = nps()
                            mmK(P, pr[:], [(gv[:, kc, d, oc * 128:(oc + 1) * 128], ucb[:, kc, ls]) for kc in range(2)], reads=[gk, "rg_ucb"], writes=[prk])
                            mmK(P, pq[:], [(gv[:, kc, d, 256 + oc * 128:256 + (oc + 1) * 128], ucb[:, kc, ls]) for kc in range(2)], reads=[gk, "rg_ucb"], writes=[pqk])
                            P.I("act", "activation", reads=[prk, "rg_bg"], writes=[ak], out=abuf[:, ls], in_=pr[:], func=AF.Tanh,
                                bias=bg[:, d, hh, oc:oc + 1], scale=0.5)
                            P.I("act", "activation", reads=[ak, "rg_clam"], writes=[ak], out=abuf[:, ls], in_=abuf[:, ls], func=AF.Exp,
                                scale=clam[:, d, c:c + 1], bias=clam[:, d, c:c + 1])
                            P.I("act", "activation", reads=[pqk, "rg_bg"], writes=[xk], out=xin[:, ls], in_=pq[:], func=AF.Tanh,
                                bias=bg[:, d, hh, 2 + oc:3 + oc], scale=0.5)
                            sq = sqt[d * 2 + g2]; sk = f"rg_sq{d * 2 + g2}"
                            P.I("act", "activation", reads=[ak], writes=[sk], out=sq[:], in_=abuf[:, ls], func=AF.Square)
                        for g2 in range(2):
                            ls = slice(g2 * 512, (g2 + 1) * 512)
                            sq = sqt[d * 2 + g2]; sk = f"rg_sq{d * 2 + g2}"
                            P.I("act", "activation", reads=[sk, "rg_one"], writes=[sk], out=sq[:], in_=sq[:], func=AF.Sqrt, bias=one[:, 0:1], scale=-1.0)
                            P.I("dve", "scalar_tensor_tensor", reads=[xk, sk], writes=[xk], out=xin[:, ls], in0=xin[:, ls], scalar=1.0, in1=sq[:],
                                op0=ALU.add, op1=ALU.mult)
                            P.I("dve", "tensor_tensor", reads=[xk, "rg_uc"], writes=[xk], out=xin[:, ls], in0=xin[:, ls], in1=uc[:, oc, ls], op=ALU.mult)
                        seqs = [(s * 256, 256) for s in range(4)] if blk == 0 else [(0, 1024)]
                        for (o0, L) in seqs:
                            sl = slice(o0, o0 + L) if d == 0 else slice(o0 + L - 1, (o0 - 1) if o0 > 0 else None, -1)
                            init = 0.0 if blk == 0 else h0[:, d, c:c + 1]
                            P.I("dve", "tensor_tensor_scan", reads=[ak, xk, "rg_h0"], writes=[xk], out=xin[:, sl], data0=abuf[:, sl],
                                data1=xin[:, sl], initial=init, op0=ALU.mult, op1=ALU.add)
                        if blk == 0:
                            x3 = xin[:].rearrange("p (s t) -> p s t", s=4)
                            src = x3[:, :, 255:256] if d == 0 else x3[:, :, 0:1]
                            dst = rgst[:, c, :].rearrange("p (s d) -> p s d", d=2)[:, :, d:d + 1]
                            P.I("dve", "tensor_scalar_mul", reads=[xk], writes=["rg_state"], out=dst, in0=src, scalar1=0.5)
                    P.I("dve", "tensor_tensor", reads=[f"rg_x{oc}0", f"rg_x{oc}1"], writes=[f"rg_x{oc}0"], out=xb[0][:], in0=xb[0][:], in1=xb[1][:], op=ALU.add)
                    for g2 in range(2):
                        cs = slice(blk * 1024 + g2 * 512, blk * 1024 + (g2 + 1) * 512); ls = slice(g2 * 512, (g2 + 1) * 512)
                        pt, pk = nps()
                        mmK(P, pt[:], [(wv[:, kc, 2 + oc, :], C.hT[:, kc, cs]) for kc in range(8)], reads=[wk, f"hT{blk}"], writes=[pk])
                        P.I("act", "activation", reads=[pk], writes=[f"rg_sg{g2}"], out=sgS[g2][:], in_=pt[:], func=AF.Tanh, scale=0.5)
                        P.I("dve", "scalar_tensor_tensor", reads=[f"rg_sg{g2}", pk], writes=[f"rg_sg{g2}"], out=sgS[g2][:], in0=sgS[g2][:], scalar=1.0, in1=pt[:],
                            op0=ALU.add, op1=ALU.mult)
                        P.I("dve", "scalar_tensor_tensor", reads=[f"rg_sg{g2}", f"rg_x{oc}0"], writes=[f"mT{blk * 2 + g2}"], out=C.mT[:, c, cs], in0=sgS[g2][:],
                            scalar=0.25, in1=xb[0][:, ls], op0=ALU.mult, op1=ALU.mult)
        srow = uraw[0:8, :]
        for half in range(2):
            pt, pk = nps()
            P.G("pe", [("transpose", dict(out=pt[0:8, i * 128:(i + 1) * 128], in_=rgst[:, half * 4 + i, :], identity=C.identf[:])) for i in range(4)],
                reads=["rg_state", "identf"], writes=[pk])
            P.I("dve", "tensor_copy", reads=[pk], writes=["rg_uraw"], out=srow[:, half * 512:(half + 1) * 512], in_=pt[0:8, :])
        P.D("sp", "o_rg", C.o_rg[:, :], srow, reads=["rg_uraw"])
SM_SCALE = 96.0 ** -0.5

def mla_layer(P, C):
    with ExitStack() as es:
        qn = P.sb("ml_qn", [128, 3], F32, es); kvn = P.sb("ml_kvn", [128, 2], F32, es)
        eps6 = P.sb("ml_eps", [128, 1], F32, es)
        rope = P.sb("ml_rope", [128, 2, 1024], F32, es)
        cqnT = P.sb("ml_cqnT", [128, 3, 1024], BF16, es)
        ckvnT = P.sb("ml_ckvnT", [128, 2, 1536], BF16, es)
        KK = P.sb("ml_KK", [128, 1536], BF16, es)
        KK2 = P.sb("ml_KK2", [128, 1536], BF16, es)
        P.D("sp", "ml_qn", qn[:], C.mla_qn[:, :], writes=["ml_qn"])
        P.D("sp", "ml_kvn", kvn[:], C.mla_kvn[:, :], writes=["ml_kvn"])
        P.D("sp", "ml_rope", rope[64:96], C.rope_cs[64:96, :, :], writes=["ml_rope"])
        P.I("dve", "memset", writes=["ml_eps"], ap=eps6[:], constant=1e-6)
        wv_all = C.mla_w_in.rearrange("(k p) n -> p k n", p=128)
        wqb_all = C.mla_w_qb.rearrange("(k p) n -> p k n", p=128)
        wqs_all = C.mla_w_qbsw.rearrange("(k p) n -> p k n", p=128)
        wkvb_all = C.mla_w_kvb.rearrange("(k p) n -> p k n", p=128)
        wks_all = C.mla_w_kpesw.rearrange("(k p) n -> p k n", p=128)
        load_wout(P, C, C.mla_w_out)
        for blk in range(2):
            nkeys = 1024 if blk == 0 else 1536
            with ExitStack() as esA:
                sq = [P.sb(f"ml_sq{i}", [128, 512], BF16, esA) for i in range(2)]
                rstd = P.sb("ml_rstd", [128, 512], F32, esA)
                ckvf = P.sb("ml_ckvf", [128, 2, 512], F32, esA)
                t1 = P.sb("ml_t1", [128, 256], F32, esA); t2 = P.sb("ml_t2", [128, 256], F32, esA)
                stg = [P.sb(f"ml_stg{i}", [128, 256], F32, esA) for i in range(2)]
                stk = P.sb("ml_stk", [128, 32], F32, esA); stkb = P.sb("ml_stkb", [128, 32], BF16, esA)
                (wcq,), wcqk = C.ring.load_parts([wv_all[:, :, 0:384]])
                (wkv, wks), wkvk = C.ring.load_parts([wv_all[:, :, 384:672], wks_all[:, :, :]])
                for gq in range(2):
                    cs = slice(blk * 1024 + gq * 512, blk * 1024 + (gq + 1) * 512); ls = slice(gq * 512, (gq + 1) * 512)
                    hk = f"hT{blk}"
                    for c in range(3):
                        mmK(P, C.ps[c][:], [(wcq[:, kc, c * 128:(c + 1) * 128], C.hT[:, kc, cs]) for kc in range(8)], reads=[wcqk, hk], writes=[f"ps{c}"])
                        P.I("act", "activation", reads=[f"ps{c}"], writes=[f"ml_sq{c % 2}"], out=sq[c % 2][:], in_=C.ps[c][:], func=AF.Square)
                        P.I("pe", "matmul", reads=[f"ml_sq{c % 2}", "onesb"], writes=["ps3"], out=C.ps[3][:], lhsT=C.onesb[:], rhs=sq[c % 2][:],
                            start=(c == 0), stop=(c == 2))
                    P.I("act", "activation", reads=["ps3", "ml_eps"], writes=["ml_rstd"], out=rstd[:], in_=C.ps[3][:], func=AF.Sqrt, bias=eps6[:, 0:1], scale=1.0 / 384.0)
                    P.I("dve", "reciprocal", reads=["ml_rstd"], writes=["ml_rstd"], out=rstd[:], in_=rstd[:])
                    for c in range(3):
                        P.I("dve", "scalar_tensor_tensor", reads=[f"ps{c}", "ml_rstd", "ml_qn"], writes=["ml_cqnT"], out=cqnT[:, c, ls], in0=C.ps[c][:],
                            scalar=qn[:, c:c + 1], in1=rstd[:], op0=ALU.mult, op1=ALU.mult)
                    for c in range(2):
                        mmK(P, C.ps[4 + c][:], [(wkv[:, kc, c * 128:(c + 1) * 128], C.hT[:, kc, cs]) for kc in range(8)], reads=[wkvk, hk], writes=[f"ps{4 + c}"])
                        P.I("act", "activation", reads=[f"ps{4 + c}"], writes=[f"ml_sq{c % 2}"], out=sq[c % 2][:], in_=C.ps[4 + c][:], func=AF.Square)
                        P.I("pe", "matmul", reads=[f"ml_sq{c % 2}", "onesb"], writes=["ps6"], out=C.ps[6][:], lhsT=C.onesb[:], rhs=sq[c % 2][:],
                            start=(c == 0), stop=(c == 1))
                    P.I("act", "activation", reads=["ps6", "ml_eps"], writes=["ml_rstd"], out=rstd[:], in_=C.ps[6][:], func=AF.Sqrt, bias=eps6[:, 0:1], scale=1.0 / 256.0)
                    P.I("dve", "reciprocal", reads=["ml_rstd"], writes=["ml_rstd"], out=rstd[:], in_=rstd[:])
                    for c in range(2):
                        P.I("dve", "scalar_tensor_tensor", reads=[f"ps{4 + c}", "ml_rstd", "ml_kvn"], writes=["ml_ckvf"], out=ckvf[:, c, :], in0=C.ps[4 + c][:],
                            scalar=kvn[:, c:c + 1], in1=rstd[:], op0=ALU.mult, op1=ALU.mult)
                    P.I("act", "activation", reads=["ml_ckvf"], writes=["ml_ckvnT"], out=ckvnT[:, :, ls], in_=ckvf[:], func=AF.Copy)
                    mmK(P, C.ps[7][64:96, :], [(wkv[:, kc, 256:288], C.hT[:, kc, cs]) for kc in range(8)], reads=[wkvk, hk], writes=["ps7"])
                    if blk == 0:
                        P.I("act", "activation", reads=["ps7"], writes=["ml_KKpe"], out=KK[64:96, ls], in_=C.ps[7][64:96, :], func=AF.Copy)
                    else:
                        mmK(P, C.ps[3][64:96, :], [(wks[:, kc, :], C.hT[:, kc, cs]) for kc in range(8)], reads=[wkvk, hk], writes=["ps3"])
                        for hq in range(2):
                            l2 = slice(hq * 256, (hq + 1) * 256); pos = slice(gq * 512 + hq * 256, gq * 512 + (hq + 1) * 256)
                            P.I("dve", "tensor_tensor", reads=["ps7", "ml_rope"], writes=["ml_t1"], out=t1[64:96, :], in0=C.ps[7][64:96, l2], in1=rope[64:96, 0, pos], op=ALU.mult)
                            P.I("dve", "tensor_tensor", reads=["ps3", "ml_rope"], writes=["ml_t2"], out=t2[64:96, :], in0=C.ps[3][64:96, l2], in1=rope[64:96, 1, pos], op=ALU.mult)
                            P.I("dve", "tensor_tensor", reads=["ml_t1", "ml_t2"], writes=["ml_KKpe"], out=KK[64:96, pos], in0=t1[64:96, :], in1=t2[64:96, :], op=ALU.add)
                    if blk == 0:
                        for tl in range(4):
                            gt = gq * 4 + tl; b = tl % 2
                            P.G("pe", [("transpose", dict(out=C.ps[2][:, c * 128:(c + 1) * 128], in_=ckvf[:, c, tl * 128:(tl + 1) * 128], identity=C.identf[:]))
                                       for c in range(2)], reads=["ml_ckvf", "identf"], writes=["ps2"])
                            P.I("dve", "tensor_copy", reads=["ps2"], writes=[f"ml_stg{b}"], out=stg[b][:], in_=C.ps[2][:, 0:256])
                            P.D("sp", f"o_ckv{b}", C.o_ckv[gt * 128:(gt + 1) * 128, :], stg[b][:], reads=[f"ml_stg{b}"])
                            mmK(P, C.ps[2][:, 256:288], [(C.hT[:, kc, gt * 128:(gt + 1) * 128], wkv[:, kc, 256:288]) for kc in range(8)], reads=[wkvk, hk], writes=["ps2"])
                            P.I("dve", "tensor_copy", reads=["ps2"], writes=["ml_stk"], out=stk[:], in_=C.ps[2][:, 256:288])
                            P.D("sp", "o_kpe", C.o_kpe[gt * 128:(gt + 1) * 128, :], stk[:], reads=["ml_stk"])
                if blk == 1:
                    for tl in range(4):
                        b = tl % 2
                        P.D("sp", f"ml_stg{b}", stg[b][:], C.mla_ckv_ctx[tl * 128:(tl + 1) * 128, :], writes=[f"ml_stg{b}"])
                        P.G("pe", [("transpose", dict(out=C.ps[2][:, c * 128:(c + 1) * 128], in_=stg[b][:, c * 128:(c + 1) * 128], identity=C.identf[:]))
                                   for c in range(2)], reads=[f"ml_stg{b}", "identf"], writes=["ps2"])
                        P.I("act", "activation", reads=["ps2"], writes=["ml_ckvnT"], out=ckvnT[:, :, 1024 + tl * 128:1024 + (tl + 1) * 128],
                            in_=C.ps[2][:, 0:256].rearrange("p (c t) -> p c t", c=2), func=AF.Copy)
                        P.D("sp", "ml_stk", stk[:], C.mla_kpe_ctx[tl * 128:(tl + 1) * 128, :], writes=["ml_stk"])
                        P.I("dve", "tensor_copy", reads=["ml_stk"], writes=["ml_stkb"], out=stkb[:], in_=stk[:])
                        psb = C.ps[2][:].bitcast(BF16)
                        P.I("pe", "transpose", reads=["ml_stkb", "identb"], writes=["ps2"], out=psb[64:96, 512:640], in_=stkb[:], identity=C.identb[:])
                        P.I("dve", "tensor_copy", reads=["ps2"], writes=["ml_KKpe"], out=KK[64:96, 1024 + tl * 128:1024 + (tl + 1) * 128], in_=psb[64:96, 512:640])
                P.I("dve", "tensor_copy", reads=["ml_KKpe"], writes=["ml_KKpe2"], out=KK2[64:96, 0:nkeys], in_=KK[64:96, 0:nkeys])
                P.barrier()
            with ExitStack() as esB:
                KKs = [KK, KK2]
                Vh = [P.sb(f"ml_Vh{i}", [128, 12, 65], BF16, esB) for i in range(2)]
                QQ = [P.sb(f"ml_QQ{i}", [128, 1024], BF16, esB) for i in range(2)]
                PT = [P.sb(f"ml_PT{i}", [128, 512], BF16, esB) for i in range(2)]
                sgT = [P.sb(f"ml_sgT{i}", [128, 1024], BF16, esB) for i in range(2)]
                on2 = P.sb("ml_on2", [128, 8, 128], BF16, esB)
                rcp = P.sb("ml_rcp", [128, 4], F32, esB)
                u1 = P.sb("ml_u1", [128, 256], F32, esB); u2 = P.sb("ml_u2", [128, 256], F32, esB)
                for i in range(2):
                    P.I("dve", "memset", writes=[f"ml_Vh{i}"], ap=Vh[i][:, :, 64:65], constant=1.0)
                nkt = nkeys // 128
                units = [(slice(s_ * 256, (s_ + 1) * 256), [2 * s_, 2 * s_ + 1]) for s_ in range(4)] if blk == 0 else \
                        [(slice(g_ * 512, (g_ + 1) * 512), list(range(12))) for g_ in range(2)]
                sc_i = [0]
                hk = f"hT{blk}"
                W = {}

                def proj(h):
                    hp, hh = h // 2, h % 2; bsel = h % 2
                    if hh == 0:
                        W[hp] = C.ring.load_parts([wqb_all[:, :, hp * 192:(hp + 1) * 192], wqs_all[:, :, hp * 64:(hp + 1) * 64],
                                                   wkvb_all[:, :, hp * 256:(hp + 1) * 256], wv_all[:, :, 672 + hp * 128:672 + (hp + 1) * 128]])
                    (wqb, wqs, wkb, wg), wk = W[hp]
                    if hh == 0:
                        for g2 in range(2):
                            cs = slice(blk * 1024 + g2 * 512, blk * 1024 + (g2 + 1) * 512); ls = slice(g2 * 512, (g2 + 1) * 512)
                            mmK(P, C.ps[6][:], [(wg[:, kc, :], C.hT[:, kc, cs]) for kc in range(8)], reads=[wk, hk], writes=["ps6"])
                            P.I("act", "activation", reads=["ps6"], writes=[f"ml_sgT{hp % 2}"], out=sgT[hp % 2][:, ls], in_=C.ps[6][:], func=AF.Silu)
                            yield
                    KKb = KKs[bsel]
                    for kg in range(nkeys // 512):
                        ks = slice(kg * 512, (kg + 1) * 512)
                        mmK(P, C.ps[6][0:64, :], [(wkb[:, kc, hh * 128:hh * 128 + 64], ckvnT[:, kc, ks]) for kc in range(2)], reads=[wk, "ml_ckvnT"], writes=["ps6"])
                        P.I("dve", "tensor_copy", reads=["ps6"], writes=[f"ml_KKn{bsel}"], out=KKb[0:64, ks], in_=C.ps[6][0:64, :])
                        yield
                    for k8 in range((nkt + 7) // 8):
                        n8 = min(8, nkt - k8 * 8)
                        P.G("pe", [("matmul", dict(out=C.ps[7][:, j * 64:(j + 1) * 64], lhsT=ckvnT[:, kc, (k8 * 8 + j) * 128:(k8 * 8 + j + 1) * 128],
                                                   rhs=wkb[:, kc, hh * 128 + 64:hh * 128 + 128], start=(kc == 0), stop=(kc == 1)))
                                   for j in range(n8) for kc in range(2)], reads=[wk, "ml_ckvnT"], writes=["ps7"])
                        P.I("dve", "tensor_copy", reads=["ps7"], writes=[f"ml_Vh{bsel}"], out=Vh[bsel][:, k8 * 8:k8 * 8 + n8, 0:64],
                            in_=C.ps[7][:, 0:n8 * 64].rearrange("p (a b) -> p a b", a=n8))
                        yield
                    Qb = QQ[bsel]
                    for g2 in range(2):
                        ls = slice(g2 * 512, (g2 + 1) * 512)
                        mmK(P, C.ps[6][0:64, :], [(wqb[:, kc, hh * 96:hh * 96 + 64], cqnT[:, kc, ls]) for kc in range(3)], reads=[wk, "ml_cqnT"], writes=["ps6"])
                        P.I("dve", "tensor_copy", reads=["ps6"], writes=[f"ml_QQn{bsel}"], out=Qb[0:64, ls], in_=C.ps[6][0:64, :])
                        yield
                        mmK(P, C.ps[7][64:96, :], [(wqb[:, kc, hh * 96 + 64:hh * 96 + 96], cqnT[:, kc, ls]) for kc in range(3)], reads=[wk, "ml_cqnT"], writes=["ps7"])
                        if blk == 0:
                            P.I("dve", "tensor_copy", reads=["ps7"], writes=[f"ml_QQr{bsel}"], out=Qb[64:96, ls], in_=C.ps[7][64:96, :])
                        else:
                            mmK(P, C.ps[6][64:96, :], [(wqs[:, kc, hh * 32:(hh + 1) * 32], cqnT[:, kc, ls]) for kc in range(3)], reads=[wk, "ml_cqnT"], writes=["ps6"])
                            for hq in range(2):
                                l2 = slice(hq * 256, (hq + 1) * 256); pos = slice(g2 * 512 + hq * 256, g2 * 512 + (hq + 1) * 256)
                                P.I("dve", "tensor_tensor", reads=["ps7", "ml_rope"], writes=["ml_u1"], out=u1[64:96, :], in0=C.ps[7][64:96, l2], in1=rope[64:96, 0, pos], op=ALU.mult)
                                P.I("dve", "tensor_tensor", reads=["ps6", "ml_rope"], writes=["ml_u2"], out=u2[64:96, :], in0=C.ps[6][64:96, l2], in1=rope[64:96, 1, pos], op=ALU.mult)
                                P.I("dve", "tensor_tensor", reads=["ml_u1", "ml_u2"], writes=[f"ml_QQr{bsel}"], out=Qb[64:96, pos], in0=u1[64:96, :], in1=u2[64:96, :], op=ALU.add)
                        yield

                def attn(h):
                    hh = h % 2; bsel = h % 2
                    KKb = KKs[bsel]; Qb = QQ[bsel]; Vb = Vh[bsel]
                    kkeys = [f"ml_KKn{bsel}", "ml_KKpe" if bsel == 0 else "ml_KKpe2", f"ml_QQn{bsel}", f"ml_QQr{bsel}"]
                    for (qsl, kts) in units:
                        nq = qsl.stop - qsl.start; nqt = nq // 128; qt0 = qsl.start // 128
                        def qk(kt):
                            sb_ = sc_i[0] % 2; sc_i[0] += 1
                            P.I("pe", "matmul", reads=kkeys, writes=[f"ps{sb_}"], out=C.ps[sb_][:, 0:nq],
                                lhsT=KKb[0:96, kt * 128:(kt + 1) * 128], rhs=Qb[0:96, qsl], start=True, stop=True)
                            P.I("act", "activation", reads=[f"ps{sb_}"], writes=[f"ml_PT{sb_}"], out=PT[sb_][:, 0:nq], in_=C.ps[sb_][:, 0:nq], func=AF.Exp, scale=SM_SCALE)
                            return sb_
                        pend = qk(kts[0])
                        for ki, kt in enumerate(kts):
                            pi_ = pend
                            if ki + 1 < len(kts):
                                pend = qk(kts[ki + 1])
                            for qt in range(nqt):
                                P.I("pe", "matmul", reads=[f"ml_PT{pi_}", f"ml_Vh{bsel}"], writes=[f"ps{2 + qt}"], out=C.ps[2 + qt][:, 0:65],
                                    lhsT=PT[pi_][:, qt * 128:(qt + 1) * 128], rhs=Vb[:, kt, :], start=(ki == 0), stop=(ki == len(kts) - 1))
                            yield
                        for qt in range(nqt):
                            P.I("dve", "reciprocal", reads=[f"ps{2 + qt}"], writes=["ml_rcp"], out=rcp[:, qt:qt + 1], in_=C.ps[2 + qt][:, 64:65])
                            P.I("dve", "tensor_scalar", reads=[f"ps{2 + qt}", "ml_rcp"], writes=["ml_on2"], out=on2[:, qt0 + qt, hh * 64:(hh + 1) * 64],
                                in0=C.ps[2 + qt][:, 0:64], scalar1=rcp[:, qt:qt + 1], scalar2=None, op0=ALU.mult)
                        yield

                def finish_pair(hp):
                    psb = C.ps[7][:].bitcast(BF16)
                    for g2 in range(2):
                        cs = slice(blk * 1024 + g2 * 512, blk * 1024 + (g2 + 1) * 512); ls = slice(g2 * 512, (g2 + 1) * 512)
                        P.G("pe", [("transpose", dict(out=psb[:, i * 128:(i + 1) * 128], in_=on2[:, g2 * 4 + i, :], identity=C.identb[:])) for i in range(4)],
                            reads=["ml_on2", "identb"], writes=["ps7"])
                        P.I("dve", "tensor_tensor", reads=["ps7", f"ml_sgT{hp % 2}"], writes=[f"mT{blk * 2 + g2}"], out=C.mT[:, hp, cs], in0=psb[:, 0:512],
                            in1=sgT[hp % 2][:, ls], op=ALU.mult)

                _roundrobin([proj(0)])
                for h in range(16):
                    gens = [attn(h)]
                    if h + 1 < 16:
                        gens.append(proj(h + 1))
                    _roundrobin(gens)
                    if h % 2 == 1:
                        finish_pair(h // 2)
                P.barrier()
LAYERS = (0, 1, 2, 3)
_CACHE = {}

def build_nc(layers):
    nc = bass.Bass("TRN2", target_bir_lowering=False)
    C = Ctx()
    declare_io(nc, C)
    with ExitStack() as es:
        P = Prog(nc, es)
        setup_persistent(P, C)
        ada_phase(P, C, layers[0])
        input_transposes(P, C)
        for li, l in enumerate(layers):
            modulate_phase(P, C, l)
            [hgrn_layer, sconv_layer, rglru_layer, mla_layer][l % 4](P, C)
            pre = wout_prefetch(P, C)
            P.barrier()
            wout_ln_phase(P, C, l, pre, next_ada=(layers[li + 1] if li + 1 < len(layers) else None))
        output_transposes(P, C)
        outs = [n for n in P.dall if n.startswith(("yout", "o_hg", "o_rg", "o_ckv", "o_kpe"))]
        P.final_wait("sp", outs)
        with nc.Block() as block:
            P.emit(block)
        C.counts = dict(P.cnt); print('instr counts', C.counts, 'nsem', len(P.esems) + sum(len(v) for v in P.dall.values()))
    return nc, C

def colT(v, nch):
    return np.ascontiguousarray(np.asarray(v, np.float32).reshape(nch, 128).T)

def make_in_maps(I):
    f = lambda a: np.ascontiguousarray(np.asarray(a, np.float32))
    half = 8; r = np.arange(1024) // 64; cpos = np.arange(1024) % 64
    inv = (10000.0 ** (-np.arange(0, 16, 2, dtype=np.float32) / 16)).astype(np.float32)
    cs = np.zeros((128, 2, 1024), np.float32)
    for part, pos in ((0, r), (1, cpos)):
        ang = pos[None, :].astype(np.float32) * inv[:, None]
        co, si = np.cos(ang), np.sin(ang)
        b = 64 + part * 16
        cs[b:b + 8, 0] = co; cs[b + 8:b + 16, 0] = co
        cs[b:b + 8, 1] = -si; cs[b + 8:b + 16, 1] = si
    s_ = np.arange(128)[:, None]; t_ = np.arange(128)[None, :]
    same = (s_ // 64) == (t_ // 64)
    masks = np.zeros((128, 4, 128), np.float32)
    masks[:, 0] = same & (s_ <= t_); masks[:, 1] = same & (s_ >= t_)
    masks[:, 2] = np.tile((np.arange(128) % 64 != 0).astype(np.float32), (128, 1))
    masks[:, 3] = np.tile((np.arange(128) % 64 != 63).astype(np.float32), (128, 1))
    swap = np.arange(32).reshape(2, 2, 8)[:, ::-1, :].reshape(-1)
    wqb = f(I['mla_w_qb'][0])
    qb_r = wqb.reshape(384, 16, 96)[:, :, 64:]
    shared = {
        'ada_w': f(I['ada_w']), 'ada_bT': np.ascontiguousarray(f(I['ada_b']).reshape(4, 24, 128).transpose(2, 0, 1)),
        'ln_gT': np.ascontiguousarray(f(I['ln_g']).reshape(4, 8, 128).transpose(2, 0, 1)),
        'ln_bT': np.ascontiguousarray(f(I['ln_b']).reshape(4, 8, 128).transpose(2, 0, 1)),
        'ident': np.eye(128, dtype=np.float32),
        'sc_w_in': f(I['sc_w_in'][0]), 'sc_cw': np.ascontiguousarray(f(I['sc_conv_w'][0]).reshape(3, 8, 128).transpose(2, 0, 1)),
        'sc_cb': colT(I['sc_conv_b'][0], 8), 'sc_w_out': f(I['sc_w_out'][0]),
        'rg_w_in': f(I['rg_w_in'][0]), 'rg_cw': np.ascontiguousarray(f(I['rg_conv_w'][0]).reshape(4, 8, 128).transpose(2, 0, 1)),
        'rg_cb': colT(I['rg_conv_b'][0], 8), 'rg_w_gate': f(I['rg_w_gate'][0]),
        'rg_bg': np.ascontiguousarray(f(I['rg_b_gate'][0]).reshape(2, 4, 4, 128).transpose(3, 0, 1, 2)),
        'rg_lam': np.ascontiguousarray(f(I['rg_lambda'][0]).reshape(2, 8, 128).transpose(2, 0, 1)),
        'rg_w_out': f(I['rg_w_out'][0]),
        'hg_w_in': f(I['hg_w_in'][0]),
        'hg_lbl': np.ascontiguousarray(f(I['hg_lb_logits']).reshape(2, 5, 8, 128).transpose(3, 0, 2, 1).reshape(128, 16, 5)),
        'hg_ng': colT(I['hg_norm_g'][0], 8), 'hg_w_out': f(I['hg_w_out'][0]), 'hg_masks': masks,
        'mla_w_in': f(I['mla_w_in'][0]), 'mla_qn': colT(I['mla_q_norm'][0], 3), 'mla_kvn': colT(I['mla_kv_norm'][0], 2),
        'mla_w_qb': wqb, 'mla_w_qbsw': np.ascontiguousarray(qb_r[:, :, swap].reshape(384, 512)),
        'mla_w_kpesw': np.ascontiguousarray(f(I['mla_w_in'][0])[:, 640:672][:, swap]),
        'mla_w_kvb': f(I['mla_w_kvb'][0]), 'mla_w_out': f(I['mla_w_out'][0]), 'rope_cs': cs,
    }
    maps = []
    xp = f(I['x_prompt']); xs = f(I['x_sample'])
    for cid in range(8):
        k = cid // 2
        m = dict(shared)
        m['xin'] = np.ascontiguousarray(np.concatenate([xp[4 * cid:4 * cid + 4].reshape(1024, D), xs[k]], 0))
        cond = np.stack([f(I['c_ctx']), f(I['c'])[k]], 1)
        m['condT'] = np.ascontiguousarray(cond.reshape(8, 128, 2).transpose(1, 0, 2))
        m['rg_h0'] = np.ascontiguousarray(f(I['state_rglru'])[k, 0].reshape(2, 8, 128).transpose(2, 0, 1))
        m['hg_s0'] = f(I['state_hgrn'])[k, 0]
        m['mla_ckv_ctx'] = f(I['cache_mla_ckv'])[k, 0]; m['mla_kpe_ctx'] = f(I['cache_mla_kpe'])[k, 0]
        maps.append(m)
    return maps

def run_layers(I, layers, trace=False):
    key = tuple(layers)
    if key not in _CACHE:
        _CACHE[key] = build_nc(layers)
    nc, C = _CACHE[key]
    maps = make_in_maps(I)
    res = run_bass_kernel_spmd(nc, maps, core_ids=list(range(8)), trace=trace)
    R = res.results
    y_prompt = np.concatenate([R[c]['y'][:1024].reshape(4, 256, D) for c in range(8)], 0)
    y_sample = np.stack([R[2 * k]['y'][1024:] for k in range(4)], 0)
    o_hg = np.concatenate([R[c]['o_hg'] for c in range(8)], 0)[:, None]
    o_rg = np.concatenate([R[c]['o_rg'].reshape(4, 2, D) for c in range(8)], 0)[:, None]
    o_ckv = np.concatenate([R[c]['o_ckv'].reshape(4, 256, 256) for c in range(8)], 0)[:, None]
    o_kpe = np.concatenate([R[c]['o_kpe'].reshape(4, 256, 32) for c in range(8)], 0)[:, None]
    outs = tuple(np.ascontiguousarray(a, dtype=np.float32) for a in (y_prompt, y_sample, o_hg, o_rg, o_ckv, o_kpe))
    return outs, res

def kernel(**inputs):
    outs, _ = run_layers(inputs, LAYERS)
    return outs
```

```python
import numpy as np
import concourse.bass as bass
import concourse.mybir as mybir
from concourse.bass_utils import run_bass_kernel_spmd
from contextlib import ExitStack
import numpy as np
import concourse.bass as bass
import concourse.mybir as mybir
from concourse.bass_utils import run_bass_kernel_spmd
from contextlib import ExitStack
F32 = mybir.dt.float32; BF16 = mybir.dt.bfloat16
AF = mybir.ActivationFunctionType; ALU = mybir.AluOpType
AX = mybir.AxisListType

D = 1024; T = 2048; NT = 16; ALPHA = 8.0 ** 0.25
LN_EPS = 1e-5 / (ALPHA * ALPHA)

class Prog:
    ENGS = ("pe", "act", "dve", "pool", "sp")
    SEM_M = 1000
    DSEM_MAX = 1600
    def __init__(self, nc, es):
        self.nc = nc; self.es = es
        self.q = {e: [] for e in self.ENGS}
        self.cnt = {e: 0 for e in self.ENGS}
        self.esems = {}
        self.seen = {e: {} for e in self.ENGS}
        self.lastw = {}; self.readers = {}
        self.dsems = {}
        self.dall = {}
        self.semh = {}
    def sb(self, name, shape, dt, es=None):
        self._names = getattr(self, "_names", {})
        n = self._names.get(name, 0); self._names[name] = n + 1
        if n: name = f"{name}__{n}"
        return (es or self.es).enter_context(self.nc.sbuf_tensor(name, list(shape), dt))
    def ps(self, name, shape, dt):
        return self.es.enter_context(self.nc.psum_tensor(name, list(shape), dt))
    def esem(self, eng, epoch):
        k = (eng, epoch)
        if k not in self.esems:
            self.esems[k] = self.es.enter_context(self.nc.semaphore(f"s_{eng}_{epoch}"))
        return self.esems[k]
    def dsem(self, name):
        d = self.dsems.get(name)
        if d is None or d[1] + 16 > self.DSEM_MAX:
            ep = 0 if d is None else d[2] + 1
            h = self.es.enter_context(self.nc.semaphore(f"d_{name}_{ep}"))
            d = [h, 0, ep]
            self.dsems[name] = d
            self.semh[f"d_{name}#{ep}"] = h
            self.dall.setdefault(name, []).append(d)
        return d
    def _handle(self, sk, val):
        if sk in self.ENGS:
            ep = (val - 1) // self.SEM_M
            return (self.esem(sk, ep), val - ep * self.SEM_M)
        return (self.semh[sk], val)
    def _need(self, eng, waits, dep):
        if dep is None: return
        sk, val = dep
        if sk == "pe" and eng == "pe": return
        if self.seen[eng].get(sk, 0) >= val: return
        waits[sk] = max(waits.get(sk, 0), val)
    def _deps(self, eng, reads, writes):
        waits = {}
        for k in reads: self._need(eng, waits, self.lastw.get(k))
        for k in writes:
            self._need(eng, waits, self.lastw.get(k))
            for sk, v in self.readers.get(k, {}).items(): self._need(eng, waits, (sk, v))
        for sk, v in waits.items(): self.seen[eng][sk] = v
        return [self._handle(sk, v) for sk, v in waits.items()]
    def _mark(self, dep, reads, writes):
        for k in writes:
            self.lastw[k] = dep; self.readers[k] = {}
        for k in reads:
            self.readers.setdefault(k, {})[dep[0]] = dep[1]
    def op(self, eng, fns, reads=(), writes=()):
        writes = list(writes) + [k for k in reads if k.startswith("ps") and k not in writes]
        waits = self._deps(eng, reads, writes)
        self.cnt[eng] += 1
        idx = self.cnt[eng]
        h, _ = self._handle(eng, idx)
        self.q[eng].append((fns, waits, (h, 1)))
        self._mark((eng, idx), reads, writes)
    @staticmethod
    def _mk(method, kw):
        def fn(e):
            return getattr(e, method)(**kw)
        return fn
    def I(self, eng, method, reads=(), writes=(), **kw):
        self.op(eng, [self._mk(method, kw)], reads, writes)
    def G(self, eng, items, reads=(), writes=()):
        self.op(eng, [self._mk(m, kw) for (m, kw) in items], reads, writes)
    def D(self, queue, semname, out, in_, reads=(), writes=(), **kw):
        waits = self._deps(queue, reads, writes)
        d = self.dsem(semname)
        d[1] += 16
        self.q[queue].append(([self._mk("dma_start", dict(out=out, in_=in_, **kw))], waits, (d[0], 16)))
        self._mark((f"d_{semname}#{d[2]}", d[1]), reads, writes)
    def barrier(self, engs=("pe", "act", "dve", "pool", "sp")):
        targets = [(e, self.cnt[e]) for e in ("pe", "act", "dve", "pool") if self.cnt[e] > 0]
        for n, lst in self.dall.items():
            for d in lst:
                if d[1] > 0: targets.append((f"d_{n}#{d[2]}", d[1]))
        for e in engs:
            waits = {}
            for dep in targets:
                if dep[0] == e and e == "pe": continue
                if self.seen[e].get(dep[0], 0) >= dep[1]: continue
                waits[dep[0]] = dep[1]; self.seen[e][dep[0]] = dep[1]
            if waits:
                self.q[e].append((None, [self._handle(sk, v) for sk, v in waits.items()], None))
    def final_wait(self, queue, semnames):
        for n in semnames:
            for d in self.dall[n]:
                self.q[queue].append((None, [(d[0], d[1])], None))
    def emit(self, block):
        def run(e, lst):
            for fns, waits, inc in lst:
                for (h, v) in waits: e.wait_ge(h, v)
                if fns is None: continue
                for i, fn in enumerate(fns):
                    ins = fn(e)
                    if i == len(fns) - 1 and inc is not None: ins.then_inc(inc[0], inc[1])
        @block.tensor
        def _(e): run(e, self.q["pe"])
        @block.scalar
        def _(e): run(e, self.q["act"])
        @block.vector
        def _(e): run(e, self.q["dve"])
        @block.gpsimd
        def _(e): run(e, self.q["pool"])
        @block.sync
        def _(e): run(e, self.q["sp"])


class Ctx:
    pass

def mmK(P, out, pairs, reads, writes):
    n = len(pairs)
    P.G("pe", [("matmul", dict(out=out, lhsT=a, rhs=b, start=(i == 0), stop=(i == n - 1))) for i, (a, b) in enumerate(pairs)],
        reads=reads, writes=writes)

class WRing:
    def __init__(self, P, nslot=3, elems=4096):
        self.P = P; self.n = nslot; self.i = 0
        self.bufs = [P.sb(f"wr{i}", [128, elems], BF16) for i in range(nslot)]
    def load(self, dram_ap, shape_str, **dims):
        s = self.i % self.n; self.i += 1
        shp = dram_ap.shape
        n = 1
        for v in shp[1:]: n *= v
        view = self.bufs[s][:, 0:n]
        if len(shp) == 3:
            view = view.rearrange("p (a b) -> p a b", a=shp[1])
        elif len(shp) == 4:
            view = view.rearrange("p (a b c) -> p a b c", a=shp[1], b=shp[2])
        key = f"wr{s}"
        self.P.D("pool", key, view, dram_ap, writes=[key])
        return view, key

    def load_multi(self, aps):
        s = self.i % self.n; self.i += 1
        J = len(aps); K_, N_ = aps[0].shape[1], aps[0].shape[2]
        view = self.bufs[s][:, 0:K_ * J * N_].rearrange("p (a b c) -> p a b c", a=K_, b=J)
        key = f"wr{s}"
        for j, ap in enumerate(aps):
            self.P.D("pool", key, view[:, :, j, :], ap, writes=[key])
        return view, key

    def load_parts(self, aps):
        s = self.i % self.n; self.i += 1
        key = f"wr{s}"; off = 0; views = []
        for ap in aps:
            a, b = ap.shape[1], ap.shape[2]
            v = self.bufs[s][:, off:off + a * b].rearrange("p (a b) -> p a b", a=a)
            off += a * b
            self.P.D("pool", key, v, ap, writes=[key])
            views.append(v)
        assert off <= 4096
        return views, key
def declare_io(nc, C):
    def din(name, shape):
        return nc.dram_tensor(name, list(shape), F32, kind="ExternalInput").ap()
    def dout(name, shape):
        return nc.dram_tensor(name, list(shape), F32, kind="ExternalOutput").ap()
    C.xin = din("xin", [T, D]); C.condT = din("condT", [128, 8, 2])
    C.ada_w = din("ada_w", [4, D, 3 * D]); C.ada_bT = din("ada_bT", [128, 4, 24])
    C.ln_gT = din("ln_gT", [128, 4, 8]); C.ln_bT = din("ln_bT", [128, 4, 8])
    C.ident = din("ident", [128, 128])
    C.sc_w_in = din("sc_w_in", [D, 4 * D]); C.sc_cw = din("sc_cw", [128, 3, 8]); C.sc_cb = din("sc_cb", [128, 8])
    C.sc_w_out = din("sc_w_out", [D, D])
    C.rg_w_in = din("rg_w_in", [D, 2 * D]); C.rg_cw = din("rg_cw", [128, 4, 8]); C.rg_cb = din("rg_cb", [128, 8])
    C.rg_w_gate = din("rg_w_gate", [2, 4, 256, 512]); C.rg_bg = din("rg_bg", [128, 2, 4, 4])
    C.rg_lam = din("rg_lam", [128, 2, 8]); C.rg_w_out = din("rg_w_out", [D, D])
    C.rg_h0 = din("rg_h0", [128, 2, 8])
    C.hg_w_in = din("hg_w_in", [D, 5 * D]); C.hg_lbl = din("hg_lbl", [128, 16, 5]); C.hg_ng = din("hg_ng", [128, 8])
    C.hg_w_out = din("hg_w_out", [D, D]); C.hg_s0 = din("hg_s0", [2, 8, 128, 128])
    C.hg_masks = din("hg_masks", [128, 4, 128])
    C.mla_w_in = din("mla_w_in", [D, 1696]); C.mla_qn = din("mla_qn", [128, 3]); C.mla_kvn = din("mla_kvn", [128, 2])
    C.mla_w_qb = din("mla_w_qb", [384, 1536]); C.mla_w_qbsw = din("mla_w_qbsw", [384, 512])
    C.mla_w_kpesw = din("mla_w_kpesw", [D, 32])
    C.mla_w_kvb = din("mla_w_kvb", [256, 2048]); C.mla_w_out = din("mla_w_out", [D, D])
    C.mla_ckv_ctx = din("mla_ckv_ctx", [512, 256]); C.mla_kpe_ctx = din("mla_kpe_ctx", [512, 32])
    C.rope_cs = din("rope_cs", [128, 2, 1024])
    C.y = dout("y", [T, D])
    C.o_hg = dout("o_hg", [4, 2, 8, 128, 128]); C.o_rg = dout("o_rg", [8, D])
    C.o_ckv = dout("o_ckv", [1024, 256]); C.o_kpe = dout("o_kpe", [1024, 32])

def setup_persistent(P, C):
    C.xT = P.sb("xT", [128, 8, T], F32)
    C.hT = P.sb("hT", [128, 8, T], BF16)
    C.mT = P.sb("mT", [128, 8, T], BF16)
    C.ring = WRing(P, nslot=3, elems=4096)
    C.identf = P.sb("identf", [128, 128], F32)
    C.identb = P.sb("identb", [128, 128], BF16)
    C.onesb = P.sb("onesb", [128, 128], BF16)
    C.condf = P.sb("condf", [128, 8, 2], F32)
    C.scond = P.sb("scond", [128, 8, 2], BF16)
    C.adab = P.sb("adab", [128, 4, 24], F32)
    C.lng = P.sb("lng", [128, 4, 8], F32); C.lnb = P.sb("lnb", [128, 4, 8], F32)
    C.mod = P.sb("mod", [128, 24, 2], F32)
    C.colsb = [P.sb(f"cols{i}", [128, 3, 8, 2], F32) for i in range(2)]
    C.ps = [P.ps(f"ps{i}", [128, 512], F32) for i in range(8)]
    P.D("sp", "identf", C.identf[:], C.ident[:, :], writes=["identf"])
    P.D("sp", "condf", C.condf[:], C.condT[:, :, :], writes=["condf"])
    P.D("sp", "adab", C.adab[:], C.ada_bT[:, :, :], writes=["adab"])
    P.D("sp", "lng", C.lng[:], C.ln_gT[:, :, :], writes=["lng"])
    P.D("sp", "lnb", C.lnb[:], C.ln_bT[:, :, :], writes=["lnb"])
    P.I("dve", "tensor_copy", reads=["identf"], writes=["identb"], out=C.identb[:], in_=C.identf[:])
    P.I("dve", "memset", writes=["onesb"], ap=C.onesb[:], constant=1.0)
    P.I("act", "activation", reads=["condf"], writes=["scond"], out=C.scond[:], in_=C.condf[:], func=AF.Silu)

def xk(g, fc):
    return f"xT{g}_{fc}"

def input_transposes(P, C):
    with ExitStack() as es:
        st = [P.sb(f"xst{i}", [128, D], F32, es) for i in range(2)]
        for t in range(NT):
            b = t % 2
            P.D("sp", f"xst{b}", st[b][:], C.xin[t * 128:(t + 1) * 128, :], writes=[f"xst{b}"])
            for half in range(2):
                pb = C.ps[(t * 2 + half) % 4]; pk = f"ps{(t * 2 + half) % 4}"
                P.G("pe", [("transpose", dict(out=pb[:, i * 128:(i + 1) * 128], in_=st[b][:, (half * 4 + i) * 128:(half * 4 + i + 1) * 128],
                                              identity=C.identf[:])) for i in range(4)], reads=[f"xst{b}", "identf"], writes=[pk])
                eng = "act" if half == 0 else "dve"
                outap = C.xT[:, half * 4:half * 4 + 4, t * 128:(t + 1) * 128]
                inap = pb[:].rearrange("p (c t) -> p c t", c=4)
                if eng == "act":
                    P.I("act", "activation", reads=[pk], writes=[xk(t // 4, half * 4 + i) for i in range(4)], out=outap, in_=inap, func=AF.Copy)
                else:
                    P.I("dve", "tensor_copy", reads=[pk], writes=[xk(t // 4, half * 4 + i) for i in range(4)], out=outap, in_=inap)
        P.barrier()

def output_transposes(P, C):
    with ExitStack() as es:
        st = [P.sb(f"yst{i}", [128, D], F32, es) for i in range(2)]
        for t in range(NT):
            b = t % 2
            for half in range(2):
                pb = C.ps[(t * 2 + half) % 4]; pk = f"ps{(t * 2 + half) % 4}"
                P.G("pe", [("transpose", dict(out=pb[:, i * 128:(i + 1) * 128], in_=C.xT[:, half * 4 + i, t * 128:(t + 1) * 128],
                                              identity=C.identf[:])) for i in range(4)], reads=[xk(t // 4, half * 4 + i) for i in range(4)] + ["identf"], writes=[pk])
                if half == 0:
                    P.I("act", "activation", reads=[pk], writes=[f"yst{b}"], out=st[b][:, 0:512], in_=pb[:], func=AF.Copy)
                else:
                    P.I("dve", "tensor_copy", reads=[pk], writes=[f"yst{b}"], out=st[b][:, 512:1024], in_=pb[:])
            P.D("sp", f"yout{b}", C.y[t * 128:(t + 1) * 128, :], st[b][:], reads=[f"yst{b}"])
        P.barrier()

def ada_phase(P, C, l):
    cols = C.colsb[l % 2]; ck = f"cols{l % 2}"
    wv_all = C.ada_w[l].rearrange("(k p) n -> p k n", p=128)
    psA = C.ps[4]
    for piece in range(6):
        wv, wk = C.ring.load(wv_all[:, :, piece * 512:(piece + 1) * 512], "")
        for f4 in range(4):
            fc = piece * 4 + f4
            mmK(P, psA[:, fc * 2:fc * 2 + 2], [(wv[:, kc, f4 * 128:(f4 + 1) * 128], C.scond[:, kc, :]) for kc in range(8)],
                reads=[wk, "scond"], writes=["ps4"])
    P.I("dve", "tensor_tensor", reads=["ps4", "adab"], writes=["mod"], out=C.mod[:],
        in0=psA[:, 0:48].rearrange("p (f j) -> p f j", j=2), in1=C.adab[:, l, :].unsqueeze(2).to_broadcast([128, 24, 2]), op=ALU.add)
    P.I("dve", "tensor_copy", reads=["mod"], writes=[ck], out=cols[:, 0], in_=C.mod[:, 0:8, :])
    P.I("dve", "tensor_scalar_add", reads=["mod"], writes=[ck], out=cols[:, 1], in0=C.mod[:, 8:16, :], scalar1=1.0)
    P.I("dve", "tensor_scalar_mul", reads=["mod"], writes=[ck], out=cols[:, 2], in0=C.mod[:, 16:24, :], scalar1=1.0 / ALPHA)

def modulate_phase(P, C, l):
    cols = C.colsb[l % 2]; ck = f"cols{l % 2}"
    for j in range(2):
        for c in range(8):
            sl = slice(j * 1024, (j + 1) * 1024)
            rk = [xk(2 * j, c), xk(2 * j + 1, c), ck]
            if (c + j) % 2 == 0:
                P.I("dve", "tensor_scalar", reads=rk, writes=[f"hT{j}"], out=C.hT[:, c, sl], in0=C.xT[:, c, sl],
                    scalar1=cols[:, 1, c, j:j + 1], scalar2=cols[:, 0, c, j:j + 1], op0=ALU.mult, op1=ALU.add)
            else:
                P.I("act", "activation", reads=rk, writes=[f"hT{j}"], out=C.hT[:, c, sl], in_=C.xT[:, c, sl], func=AF.Identity,
                    scale=cols[:, 1, c, j:j + 1], bias=cols[:, 0, c, j:j + 1])

def load_wout(P, C, w_dram):
    C.wout_dram = w_dram

def wout_prefetch(P, C):
    wv = C.wout_dram.rearrange("(k p) n -> p k n", p=128)
    return [C.ring.load(wv[:, :, 0:512], ""), C.ring.load(wv[:, :, 512:1024], "")]

def wout_ln_phase(P, C, l, pre, next_ada=None):
    cols = C.colsb[l % 2]; ck = f"cols{l % 2}"
    with ExitStack() as es:
        zn = [P.sb(f"ln_zn{i}", [128, D], F32, es) for i in range(4)]
        st = [P.sb(f"ln_st{i}", [128, 12], F32, es) for i in range(2)]
        mv = [P.sb(f"ln_mv{i}", [128, 2], F32, es) for i in range(2)]
        rs = [P.sb(f"ln_rs{i}", [128, 2], F32, es) for i in range(2)]
        epsc = P.sb("ln_eps", [128, 1], F32, es)
        P.I("dve", "memset", writes=["ln_eps"], ap=epsc[:], constant=LN_EPS)
        yi = [0]; ti = [0]
        def zpass(g):
            j = g // 2; gs = slice(g * 512, (g + 1) * 512)
            for fc in range(8):
                wv, wk = pre[fc // 4]; f4 = fc % 4
                py = C.ps[6 + yi[0] % 2]; pyk = f"ps{6 + yi[0] % 2}"; yi[0] += 1
                mmK(P, py[:], [(wv[:, kc, f4 * 128:(f4 + 1) * 128], C.mT[:, kc, gs]) for kc in range(8)], reads=[wk, f"mT{g}"], writes=[pyk])
                P.I("dve", "scalar_tensor_tensor", reads=[pyk, xk(g, fc), ck], writes=[xk(g, fc)], out=C.xT[:, fc, gs], in0=py[:],
                    scalar=cols[:, 2, fc, j:j + 1], in1=C.xT[:, fc, gs], op0=ALU.mult, op1=ALU.add)
        def norm_tiles(g):
            for tl in range(4):
                tcols = slice(g * 512 + tl * 128, g * 512 + (tl + 1) * 128)
                i2 = ti[0] % 2; ti[0] += 1
                pb = [C.ps[2 * i2], C.ps[2 * i2 + 1]]; pbk = [f"ps{2 * i2}", f"ps{2 * i2 + 1}"]
                for half in range(2):
                    P.G("pe", [("transpose", dict(out=pb[half][:, i * 128:(i + 1) * 128], in_=C.xT[:, half * 4 + i, tcols], identity=C.identf[:]))
                               for i in range(4)], reads=[xk(g, half * 4 + i) for i in range(4)] + ["identf"], writes=[pbk[half]])
                    P.I("dve", "bn_stats", reads=[pbk[half]], writes=[f"ln_st{i2}"], out=st[i2][:, half * 6:(half + 1) * 6], in_=pb[half][:])
                P.I("dve", "bn_aggr", reads=[f"ln_st{i2}"], writes=[f"ln_mv{i2}"], out=mv[i2][:], in_=st[i2][:])
                P.I("act", "activation", reads=[f"ln_mv{i2}", "ln_eps"], writes=[f"ln_rs{i2}"], out=rs[i2][:, 0:1], in_=mv[i2][:, 1:2], func=AF.Sqrt,
                    bias=epsc[:, 0:1], scale=1.0)
                P.I("dve", "reciprocal", reads=[f"ln_rs{i2}"], writes=[f"ln_rs{i2}"], out=rs[i2][:, 0:1], in_=rs[i2][:, 0:1])
                P.I("dve", "scalar_tensor_tensor", reads=[f"ln_mv{i2}", f"ln_rs{i2}"], writes=[f"ln_rs{i2}"], out=rs[i2][:, 1:2], in0=mv[i2][:, 0:1],
                    scalar=-1.0, in1=rs[i2][:, 0:1], op0=ALU.mult, op1=ALU.mult)
                for half in range(2):
                    P.I("act", "activation", reads=[pbk[half], f"ln_rs{i2}"], writes=[f"ln_zn{tl}"], out=zn[tl][:, half * 512:(half + 1) * 512],
                        in_=pb[half][:], func=AF.Identity, scale=rs[i2][:, 0:1], bias=rs[i2][:, 1:2])
        def back(g):
            gs = slice(g * 512, (g + 1) * 512)
            for fc in range(8):
                bi = 4 + fc % 2
                P.G("pe", [("transpose", dict(out=C.ps[bi][:, tl * 128:(tl + 1) * 128], in_=zn[tl][:, fc * 128:(fc + 1) * 128], identity=C.identf[:]))
                           for tl in range(4)], reads=[f"ln_zn{tl}" for tl in range(4)] + ["identf"], writes=[f"ps{bi}"])
                if fc % 2 == 0:
                    P.I("act", "activation", reads=[f"ps{bi}", "lng", "lnb"], writes=[xk(g, fc)], out=C.xT[:, fc, gs], in_=C.ps[bi][:], func=AF.Identity,
                        scale=C.lng[:, l, fc:fc + 1], bias=C.lnb[:, l, fc:fc + 1])
                else:
                    P.I("dve", "tensor_scalar", reads=[f"ps{bi}", "lng", "lnb"], writes=[xk(g, fc)], out=C.xT[:, fc, gs], in0=C.ps[bi][:],
                        scalar1=C.lng[:, l, fc:fc + 1], scalar2=C.lnb[:, l, fc:fc + 1], op0=ALU.mult, op1=ALU.add)
        zpass(0)
        for g in range(4):
            if g + 1 < 4:
                zpass(g + 1)
            norm_tiles(g)
            if g == 3 and next_ada is not None:
                ada_phase(P, C, next_ada)
            back(g)
        P.barrier()
QS = 128.0 ** -0.5
CH = 64
NCH = 512 // CH
JT = 128 // CH

def _roundrobin(gens):
    gens = list(gens)
    while gens:
        nxt = []
        for g in gens:
            try:
                next(g); nxt.append(g)
            except StopIteration:
                pass
        gens = nxt

def hgrn_layer(P, C):
    with ExitStack() as es:
        lbl = P.sb("hg_lbl_s", [128, 16, 5], F32, es); lbm = P.sb("hg_lbm", [128, 16], F32, es)
        lb = P.sb("hg_lb", [128, 16], F32, es); oml = P.sb("hg_oml", [128, 16], F32, es)
        ng = P.sb("hg_ng_s", [128, 8], F32, es); eps6 = P.sb("hg_eps", [128, 1], F32, es)
        one = P.sb("hg_one", [128, 1], F32, es)
        mk = P.sb("hg_mk", [128, 2, 128], F32, es); rm = P.sb("hg_rm", [128, 2, 512], BF16, es)
        mstage = P.sb("hg_mst", [128, 2, 128], F32, es)
        P.D("sp", "hg_lbl", lbl[:], C.hg_lbl[:, :, :], writes=["hg_lbl"])
        P.D("sp", "hg_ng", ng[:], C.hg_ng[:, :], writes=["hg_ng"])
        P.D("sp", "hg_mk", mk[:], C.hg_masks[:, 0:2, :], writes=["hg_mk"])
        P.D("sp", "hg_mst", mstage[:], C.hg_masks[:, 2:4, :], writes=["hg_mst"])
        P.I("dve", "memset", writes=["hg_eps"], ap=eps6[:], constant=1e-6)
        P.I("dve", "memset", writes=["hg_one"], ap=one[:], constant=1.0)
        for d in range(2):
            for r in range(4):
                P.I("dve", "tensor_copy", reads=["hg_mst"], writes=["hg_rm"], out=rm[:, d, r * 128:(r + 1) * 128], in_=mstage[:, d, :])
        P.I("dve", "reduce_max", reads=["hg_lbl"], writes=["hg_lbm"], out=lbm[:], in_=lbl[:], axis=AX.X)
        P.I("dve", "tensor_tensor", reads=["hg_lbl", "hg_lbm"], writes=["hg_lbl"], out=lbl[:], in0=lbl[:],
            in1=lbm[:].unsqueeze(2).to_broadcast([128, 16, 5]), op=ALU.subtract)
        P.I("act", "activation", reads=["hg_lbl"], writes=["hg_lbl"], out=lbl[:], in_=lbl[:], func=AF.Exp)
        P.I("dve", "reduce_sum", reads=["hg_lbl"], writes=["hg_lbm"], out=lbm[:], in_=lbl[:], axis=AX.X)
        P.I("dve", "reciprocal", reads=["hg_lbm"], writes=["hg_lbm"], out=lbm[:], in_=lbm[:])
        P.I("dve", "tensor_tensor", reads=["hg_lbl", "hg_lbm"], writes=["hg_lb"], out=lb[:], in0=lbl[:, :, 0], in1=lbm[:], op=ALU.mult)
        P.I("dve", "tensor_scalar", reads=["hg_lb"], writes=["hg_oml"], out=oml[:], in0=lb[:], scalar1=-1.0, scalar2=1.0, op0=ALU.mult, op1=ALU.add)

        vtok = P.sb("hg_vtok", [128, 16, 128], BF16, es)
        qS = P.sb("hg_q", [128, 1024], F32, es)
        ob = P.sb("hg_o", [128, 1024], F32, es)
        sgB = P.sb("hg_sgb", [128, 1024], BF16, es)
        Sf = P.sb("hg_Sf", [128, 8, 128], F32, es); Sb = P.sb("hg_Sb", [128, 8, 128], BF16, es)
        attS = [P.sb(f"hg_att{i}", [128, 128], BF16, es) for i in range(2)]
        U = []
        for u in range(2):
            B_ = Ctx()
            B_.u = u
            B_.cum = P.sb(f"hg_cum{u}", [128, 512], F32, es); B_.kS = P.sb(f"hg_k{u}", [128, 512], F32, es)
            B_.A = P.sb(f"hg_A{u}", [128, 512], F32, es); B_.B = P.sb(f"hg_B{u}", [128, 512], F32, es)
            B_.qrel = P.sb(f"hg_qrel{u}", [128, 512], BF16, es); B_.krel = P.sb(f"hg_krel{u}", [128, 512], BF16, es)
            B_.qcum = P.sb(f"hg_qcum{u}", [128, 512], BF16, es); B_.kdT = P.sb(f"hg_kdT{u}", [128, 512], BF16, es)
            B_.kdtok = P.sb(f"hg_kdtok{u}", [128, 4, 128], BF16, es); B_.etot = P.sb(f"hg_etot{u}", [128, 16], F32, es)
            U.append(B_)
        wv_all = C.hg_w_in.rearrange("(k p) n -> p k n", p=128)
        P.I("dve", "memset", writes=["hg_qrel0"], ap=U[0].qrel[:], constant=0.0)
        P.I("pe", "matmul", reads=["hg_qrel0", "identb"], writes=["ps3"], out=C.ps[3][:], lhsT=C.identb[:], rhs=U[0].qrel[:], start=True, stop=True)
        cnt = {"w": 0, "att": 0, "x": 0, "y": 0, "u": 0}
        def wps():
            i = cnt["w"] % 2; cnt["w"] += 1
            return C.ps[i], f"ps{i}"
        def quarter(bank, name):
            i = cnt[name] % 4; cnt[name] += 1
            return C.ps[bank][:, i * 128:(i + 1) * 128], f"ps{bank}"
        def uslot():
            i = cnt["u"] % 2; cnt["u"] += 1
            return C.ps[6 + i][:, 0:128], f"ps{6 + i}"

        def prep(B_, wv, wk, hd, blk, d, sg_):
            u = B_.u
            K = lambda n: f"hg_{n}{u}"
            cs = slice(blk * 1024 + sg_ * 512, blk * 1024 + (sg_ + 1) * 512); ls = slice(sg_ * 512, (sg_ + 1) * 512)
            ridx = (CH // 2 - 1) if d == 0 else (CH // 2); tidx = (CH - 1) if d == 0 else 0
            cum, kS, Ab, Bb = B_.cum, B_.kS, B_.A, B_.B
            pt, pk = wps()
            mmK(P, pt[:], [(wv[:, kc, 1 + d, :], C.hT[:, kc, cs]) for kc in range(8)], reads=[wk, f"hT{blk}"], writes=[pk]); yield
            lbc = lb[:, d * 8 + hd:d * 8 + hd + 1]; omc = oml[:, d * 8 + hd:d * 8 + hd + 1]
            P.I("act", "activation", reads=[pk], writes=[K("cum")], out=cum[:], in_=pt[:], func=AF.Exp, scale=-1.0); yield
            P.I("act", "activation", reads=[K("cum"), "hg_lb", "hg_one"], writes=[K("A")], out=Ab[:], in_=cum[:], func=AF.Ln, scale=lbc, bias=one[:, 0:1]); yield
            P.I("act", "activation", reads=[K("cum"), "hg_one"], writes=[K("B")], out=Bb[:], in_=cum[:], func=AF.Ln, scale=1.0, bias=one[:, 0:1]); yield
            P.I("dve", "tensor_tensor", reads=[K("A"), K("B")], writes=[K("cum")], out=cum[:], in0=Ab[:], in1=Bb[:], op=ALU.subtract); yield
            P.I("dve", "tensor_tensor", reads=[pk, K("B")], writes=[K("B")], out=Bb[:], in0=Bb[:], in1=pt[:], op=ALU.add); yield
            P.I("act", "activation", reads=[K("B")], writes=[K("k")], out=kS[:], in_=Bb[:], func=AF.Exp, scale=-1.0); yield
            rv = slice(None) if d == 0 else slice(None, None, -1)
            P.I("dve", "tensor_tensor_scan", reads=[K("cum"), "hg_rm"], writes=[K("cum")], out=cum[:, rv], data0=rm[:, d, rv],
                data1=cum[:, rv], initial=0.0, op0=ALU.mult, op1=ALU.add); yield
            c3 = cum[:].rearrange("p (c t) -> p c t", t=CH)
            A3 = Ab[:].rearrange("p (c t) -> p c t", t=CH)
            P.I("dve", "tensor_tensor", reads=[K("cum")], writes=[K("A")], out=A3, in0=c3,
                in1=c3[:, :, ridx:ridx + 1].to_broadcast([128, NCH, CH]), op=ALU.subtract); yield
            P.I("act", "activation", reads=[K("cum")], writes=[K("B")], out=Bb[:], in_=cum[:], func=AF.Exp); yield
            P.I("dve", "scalar_tensor_tensor", reads=["hg_q", K("B")], writes=[K("qcum")], out=B_.qcum[:], in0=qS[:, ls], scalar=QS,
                in1=Bb[:], op0=ALU.mult, op1=ALU.mult); yield
            P.I("act", "activation", reads=[K("cum")], writes=[K("etot")], out=B_.etot[:, 0:NCH], in_=c3[:, :, tidx], func=AF.Exp); yield
            P.I("act", "activation", reads=[K("A")], writes=[K("B")], out=Bb[:], in_=Ab[:], func=AF.Exp); yield
            P.I("dve", "scalar_tensor_tensor", reads=["hg_q", K("B")], writes=[K("qrel")], out=B_.qrel[:], in0=qS[:, ls], scalar=QS,
                in1=Bb[:], op0=ALU.mult, op1=ALU.mult); yield
            P.I("act", "activation", reads=[K("A")], writes=[K("B")], out=Bb[:], in_=Ab[:], func=AF.Exp, scale=-1.0); yield
            P.I("dve", "scalar_tensor_tensor", reads=[K("k"), K("B"), "hg_oml"], writes=[K("krel")], out=B_.krel[:], in0=kS[:], scalar=omc, in1=Bb[:],
                op0=ALU.mult, op1=ALU.mult); yield
            P.I("dve", "tensor_tensor", reads=[K("cum")], writes=[K("A")], out=A3, in0=c3,
                in1=c3[:, :, tidx:tidx + 1].to_broadcast([128, NCH, CH]), op=ALU.subtract); yield
            P.I("act", "activation", reads=[K("A")], writes=[K("B")], out=Bb[:], in_=Ab[:], func=AF.Exp, scale=-1.0); yield
            P.I("dve", "scalar_tensor_tensor", reads=[K("k"), K("B"), "hg_oml"], writes=[K("kdT")], out=B_.kdT[:], in0=kS[:], scalar=omc, in1=Bb[:],
                op0=ALU.mult, op1=ALU.mult); yield
            psT = C.ps[2][:].bitcast(BF16)
            P.G("pe", [("transpose", dict(out=psT[:, i * 128:(i + 1) * 128], in_=B_.kdT[:, i * 128:(i + 1) * 128], identity=C.identb[:]))
                       for i in range(4)], reads=[K("kdT"), "identb"], writes=["ps2"])
            P.I("act", "activation", reads=["ps2"], writes=[K("kdtok")], out=B_.kdtok[:], in_=psT[:, 0:512].rearrange("p (a b) -> p a b", a=4),
                func=AF.Copy); yield

        for hd in range(8):
            if hd == 4:
                load_wout(P, C, C.hg_w_out)
            wv, wk = C.ring.load_multi([wv_all[:, :, j * 1024 + hd * 128:j * 1024 + (hd + 1) * 128] for j in range(4)])
            gv, gk = C.ring.load(wv_all[:, :, 4096 + hd * 128:4096 + (hd + 1) * 128], "")
            for t4 in range(4):
                vb = 2 if t4 % 2 == 0 else 4
                P.G("pe", [("matmul", dict(out=C.ps[vb][:, i * 128:(i + 1) * 128], lhsT=C.hT[:, kc, (t4 * 4 + i) * 128:(t4 * 4 + i + 1) * 128],
                                           rhs=wv[:, kc, 3, :], start=(kc == 0), stop=(kc == 7))) for i in range(4) for kc in range(8)],
                    reads=[wk, f"hT{t4 // 2}"], writes=[f"ps{vb}"])
                if t4 % 2 == 0:
                    P.I("act", "activation", reads=[f"ps{vb}"], writes=["hg_vtok"], out=vtok[:, t4 * 4:t4 * 4 + 4, :],
                        in_=C.ps[vb][:].rearrange("p (a b) -> p a b", a=4), func=AF.Copy)
                else:
                    P.I("dve", "tensor_copy", reads=[f"ps{vb}"], writes=["hg_vtok"], out=vtok[:, t4 * 4:t4 * 4 + 4, :],
                        in_=C.ps[vb][:].rearrange("p (a b) -> p a b", a=4))
            for blk in range(2):
                for g2 in range(2):
                    cs = slice(blk * 1024 + g2 * 512, blk * 1024 + (g2 + 1) * 512); ls = slice(g2 * 512, (g2 + 1) * 512)
                    pt, pk = wps()
                    mmK(P, pt[:], [(wv[:, kc, 0, :], C.hT[:, kc, cs]) for kc in range(8)], reads=[wk, f"hT{blk}"], writes=[pk])
                    P.I("act", "activation", reads=[pk], writes=["hg_q"], out=qS[:, ls], in_=pt[:], func=AF.Silu)
                for g2 in range(2):
                    cs = slice(blk * 1024 + g2 * 512, blk * 1024 + (g2 + 1) * 512); ls = slice(g2 * 512, (g2 + 1) * 512)
                    pt, pk = wps()
                    mmK(P, pt[:], [(gv[:, kc, :], C.hT[:, kc, cs]) for kc in range(8)], reads=[gk, f"hT{blk}"], writes=[pk])
                    P.I("act", "activation", reads=[pk], writes=["hg_sgb"], out=sgB[:, ls], in_=pt[:], func=AF.Silu)
                P.I("pool", "memset", writes=[f"hg_o{t_}" for t_ in range(8)], ap=ob[:], constant=0.0)
                if blk == 0:
                    P.I("pool", "memset", writes=[f"hg_Sf{c_}" for c_ in range(8)], ap=Sf[:], constant=0.0)
                    P.I("pool", "memset", writes=[f"hg_Sb{c_}" for c_ in range(8)], ap=Sb[:], constant=0.0)
                else:
                    for d in range(2):
                        P.D("sp", f"hg_s0{d}", Sf[:, d, :], C.hg_s0[d, hd], writes=[f"hg_Sf{d}"])
                        P.I("act", "activation", reads=[f"hg_Sf{d}"], writes=[f"hg_Sb{d}"], out=Sb[:, d, :], in_=Sf[:, d, :], func=AF.Copy)
                for step in range(2):
                    units = [(0, step, U[0]), (1, 1 - step, U[1])]
                    _roundrobin([prep(B_, wv, wk, hd, blk, d, sg_) for (d, sg_, B_) in units])
                    chains = []
                    for (d, sg_, B_) in units:
                        if blk == 0:
                            cl = [(d * 4 + 2 * sg_, [0, 1]), (d * 4 + 2 * sg_ + 1, [2, 3])]
                        else:
                            cl = [(d, [0, 1, 2, 3])]
                        for ch, tl in cl:
                            chains.append((d, sg_, B_, ch, tl if d == 0 else tl[::-1]))
                    npos = len(chains[0][4])
                    for pos in range(npos):
                        info = []
                        for (d, sg_, B_, ch, tl) in chains:
                            u = B_.u
                            tloc = tl[pos]; gt = blk * 8 + sg_ * 4 + tloc; ts_ = slice(tloc * 128, (tloc + 1) * 128)
                            pa, pak = quarter(3, "att")
                            t0 = tloc * 128
                            blocks = []
                            for cb in range(JT):
                                b0 = cb * CH; hC = CH // 2
                                if d == 0:
                                    blocks += [(b0, hC, b0, CH), (b0 + hC, hC, b0 + hC, hC)]
                                else:
                                    blocks += [(b0 + hC, hC, b0, CH), (b0, hC, b0, hC)]
                            P.G("pe", [("matmul", dict(out=pa[s0:s0 + sn, q0:q0 + qn], lhsT=B_.krel[:, t0 + s0:t0 + s0 + sn], rhs=B_.qrel[:, t0 + q0:t0 + q0 + qn],
                                                       start=True, stop=True, tile_position=(0, s0))) for (s0, sn, q0, qn) in blocks],
                                reads=[f"hg_krel{u}", f"hg_qrel{u}"], writes=[pak])
                            ai = cnt["att"] % 2
                            P.I("dve", "tensor_tensor", reads=[pak, "hg_mk"], writes=[f"hg_att{ai}"], out=attS[ai][:], in0=pa, in1=mk[:, d, :], op=ALU.mult)
                            px, pxk = quarter(4, "x")
                            P.I("pe", "matmul", reads=["hg_vtok", f"hg_att{ai}"], writes=[pxk], out=px, lhsT=vtok[:, gt, :], rhs=attS[ai][:], start=True, stop=True)
                            py, pyk = quarter(5, "y")
                            info.append((d, sg_, B_, ch, tloc, gt, px, pxk, py, pyk))
                        for jj in range(JT):
                            for (d, sg_, B_, ch, tloc, gt, px, pxk, py, pyk) in info:
                                u = B_.u
                                j = jj if d == 0 else JT - 1 - jj
                                qs_ = slice(tloc * 128 + j * CH, tloc * 128 + (j + 1) * CH)
                                P.I("pe", "matmul", reads=[f"hg_Sb{ch}", f"hg_qcum{u}"], writes=[pyk], out=py[:, j * CH:(j + 1) * CH], lhsT=Sb[:, ch, :],
                                    rhs=B_.qcum[:, qs_], start=True, stop=True)
                                pu, puk = uslot()
                                P.I("pe", "matmul", reads=[f"hg_kdtok{u}", "hg_vtok"], writes=[puk], out=pu, lhsT=B_.kdtok[j * CH:(j + 1) * CH, tloc, :],
                                    rhs=vtok[j * CH:(j + 1) * CH, gt, :], start=True, stop=True, tile_position=(j * CH, 0))
                                P.I("dve", "scalar_tensor_tensor", reads=[f"hg_Sf{ch}", puk, f"hg_etot{u}"], writes=[f"hg_Sf{ch}"], out=Sf[:, ch, :],
                                    in0=Sf[:, ch, :], scalar=B_.etot[:, tloc * JT + j:tloc * JT + j + 1], in1=pu, op0=ALU.mult, op1=ALU.add)
                                P.I("act", "activation", reads=[f"hg_Sf{ch}"], writes=[f"hg_Sb{ch}"], out=Sb[:, ch, :], in_=Sf[:, ch, :], func=AF.Copy)
                        for (d, sg_, B_, ch, tloc, gt, px, pxk, py, pyk) in info:
                            t8 = sg_ * 4 + tloc
                            os_ = slice(t8 * 128, (t8 + 1) * 128); ok_ = f"hg_o{t8}"
                            P.I("dve", "tensor_tensor", reads=[pxk, ok_], writes=[ok_], out=ob[:, os_], in0=ob[:, os_], in1=px, op=ALU.add)
                            P.I("dve", "tensor_tensor", reads=[pyk, ok_], writes=[ok_], out=ob[:, os_], in0=ob[:, os_], in1=py, op=ALU.add)
                    if blk == 0:
                        for (d, sg_, B_, ch, tl) in chains:
                            P.D("sp", f"o_hg{ch}", C.o_hg[ch % 4, d, hd], Sf[:, ch, :], reads=[f"hg_Sf{ch}"])
                osq = U[0].qrel; rsb = U[0].B; sgb = U[0].A
                for g2 in range(2):
                    cs = slice(blk * 1024 + g2 * 512, blk * 1024 + (g2 + 1) * 512); ls = slice(g2 * 512, (g2 + 1) * 512)
                    okeys = [f"hg_o{g2 * 4 + t_}" for t_ in range(4)]
                    P.I("act", "activation", reads=okeys, writes=["hg_qrel0"], out=osq[:], in_=ob[:, ls], func=AF.Square)
                    pt, pk = wps()
                    P.I("pe", "matmul", reads=["hg_qrel0", "onesb"], writes=[pk], out=pt[:], lhsT=C.onesb[:], rhs=osq[:], start=True, stop=True)
                    P.I("act", "activation", reads=[pk, "hg_eps"], writes=["hg_B0"], out=rsb[:], in_=pt[:], func=AF.Ln, bias=eps6[:, 0:1], scale=1.0 / 128.0)
                    P.I("act", "activation", reads=["hg_B0"], writes=["hg_B0"], out=rsb[:], in_=rsb[:], func=AF.Exp, scale=-0.5)
                    P.I("dve", "tensor_tensor", reads=["hg_B0"] + okeys, writes=["hg_B0"], out=rsb[:], in0=rsb[:], in1=ob[:, ls], op=ALU.mult)
                    P.I("dve", "scalar_tensor_tensor", reads=["hg_B0", "hg_sgb", "hg_ng"], writes=[f"mT{blk * 2 + g2}"], out=C.mT[:, hd, cs], in0=rsb[:],
                        scalar=ng[:, hd:hd + 1], in1=sgB[:, ls], op0=ALU.mult, op1=ALU.mult)
def sconv_layer(P, C):
    with ExitStack() as es:
        cw = P.sb("sc_cw_s", [128, 3, 8], F32, es); cb = P.sb("sc_cb_s", [128, 8], F32, es)
        P.D("sp", "sc_cw", cw[:], C.sc_cw[:, :, :], writes=["sc_cw"])
        P.D("sp", "sc_cb", cb[:], C.sc_cb[:, :], writes=["sc_cb"])
        pb_ = [P.sb(f"sc_p{i}", [128, 1024], F32, es) for i in range(2)]
        zb_ = [P.sb(f"sc_z{i}", [128, 1024], F32, es) for i in range(2)]
        cgS = [P.sb(f"sc_cg{i}", [128, 512], F32, es) for i in range(2)]
        sgS = [P.sb(f"sc_sg{i}", [128, 512], F32, es) for i in range(2)]
        tS = [P.sb(f"sc_t{i}", [128, 512], F32, es) for i in range(2)]
        wv_all = C.sc_w_in.rearrange("(k p) n -> p k n", p=128)
        pi = 0
        for c in range(8):
            if c == 4:
                load_wout(P, C, C.sc_w_out)
            wv, wk = C.ring.load_multi([wv_all[:, :, j * 1024 + c * 128:j * 1024 + (c + 1) * 128] for j in range(4)])
            for blk in range(2):
                p = pb_[blk]; z = zb_[blk]; pk = f"sc_p{blk}"; zk = f"sc_z{blk}"
                for g2 in range(2):
                    cs = slice(blk * 1024 + g2 * 512, blk * 1024 + (g2 + 1) * 512); ls = slice(g2 * 512, (g2 + 1) * 512)
                    pa = pi % 4; pbk = (pi + 1) % 4; pi += 2
                    mmK(P, C.ps[pa][:], [(wv[:, kc, 1, :], C.hT[:, kc, cs]) for kc in range(8)], reads=[wk, f"hT{blk}"], writes=[f"ps{pa}"])
                    mmK(P, C.ps[pbk][:], [(wv[:, kc, 2, :], C.hT[:, kc, cs]) for kc in range(8)], reads=[wk, f"hT{blk}"], writes=[f"ps{pbk}"])
                    i2 = g2
                    P.I("act", "activation", reads=[f"ps{pa}"], writes=[f"sc_cg{i2}"], out=cgS[i2][:], in_=C.ps[pa][:], func=AF.Copy)
                    P.I("dve", "tensor_tensor", reads=[f"sc_cg{i2}", f"ps{pbk}"], writes=[pk], out=p[:, ls], in0=cgS[i2][:], in1=C.ps[pbk][:], op=ALU.mult)
                P.I("dve", "tensor_scalar", reads=[pk, "sc_cw", "sc_cb"], writes=[zk], out=z[:], in0=p[:], scalar1=cw[:, 1, c:c + 1],
                    scalar2=cb[:, c:c + 1], op0=ALU.mult, op1=ALU.add)
                if blk == 0:
                    z3 = z[:].rearrange("p (s t) -> p s t", s=4); p3 = p[:].rearrange("p (s t) -> p s t", s=4)
                    zlo, plo, zhi, phi = z3[:, :, 1:], p3[:, :, :-1], z3[:, :, :-1], p3[:, :, 1:]
                else:
                    zlo, plo, zhi, phi = z[:, 1:], p[:, :-1], z[:, :-1], p[:, 1:]
                P.I("dve", "scalar_tensor_tensor", reads=[pk, zk, "sc_cw"], writes=[zk], out=zlo, in0=plo, scalar=cw[:, 0, c:c + 1], in1=zlo,
                    op0=ALU.mult, op1=ALU.add)
                P.I("dve", "scalar_tensor_tensor", reads=[pk, zk, "sc_cw"], writes=[zk], out=zhi, in0=phi, scalar=cw[:, 2, c:c + 1], in1=zhi,
                    op0=ALU.mult, op1=ALU.add)
                for g2 in range(2):
                    cs = slice(blk * 1024 + g2 * 512, blk * 1024 + (g2 + 1) * 512); ls = slice(g2 * 512, (g2 + 1) * 512)
                    g = blk * 2 + g2
                    pa = pi % 4; pbk = (pi + 1) % 4; pi += 2
                    mmK(P, C.ps[pa][:], [(wv[:, kc, 0, :], C.hT[:, kc, cs]) for kc in range(8)], reads=[wk, f"hT{blk}"], writes=[f"ps{pa}"])
                    mmK(P, C.ps[pbk][:], [(wv[:, kc, 3, :], C.hT[:, kc, cs]) for kc in range(8)], reads=[wk, f"hT{blk}"], writes=[f"ps{pbk}"])
                    i2 = g2
                    P.I("act", "activation", reads=[f"ps{pbk}"], writes=[f"sc_sg{i2}"], out=sgS[i2][:], in_=C.ps[pbk][:], func=AF.Silu)
                    P.I("dve", "tensor_tensor", reads=[f"sc_sg{i2}", f"ps{pa}"], writes=[f"sc_t{i2}"], out=tS[i2][:], in0=sgS[i2][:], in1=C.ps[pa][:], op=ALU.mult)
                    P.I("dve", "tensor_tensor", reads=[f"sc_t{i2}", zk], writes=[f"mT{g}"], out=C.mT[:, c, cs], in0=tS[i2][:], in1=z[:, ls], op=ALU.mult)
def rglru_layer(P, C):
    with ExitStack() as es:
        cw = P.sb("rg_cw_s", [128, 4, 8], F32, es); cb = P.sb("rg_cb_s", [128, 8], F32, es)
        bg = P.sb("rg_bg_s", [128, 2, 4, 4], F32, es); lam = P.sb("rg_lam_s", [128, 2, 8], F32, es)
        clam = P.sb("rg_clam", [128, 2, 8], F32, es); h0 = P.sb("rg_h0_s", [128, 2, 8], F32, es)
        one = P.sb("rg_one", [128, 1], F32, es)
        rgst = P.sb("rg_state", [128, 8, 8], F32, es)
        P.D("sp", "rg_cw", cw[:], C.rg_cw[:, :, :], writes=["rg_cw"])
        P.D("sp", "rg_cb", cb[:], C.rg_cb[:, :], writes=["rg_cb"])
        P.D("sp", "rg_bg", bg[:], C.rg_bg[:, :, :, :], writes=["rg_bg"])
        P.D("sp", "rg_lam", lam[:], C.rg_lam[:, :, :], writes=["rg_lam"])
        P.D("sp", "rg_h0", h0[:], C.rg_h0[:, :, :], writes=["rg_h0"])
        P.I("dve", "memset", writes=["rg_one"], ap=one[:], constant=1.0)
        P.I("act", "activation", reads=["rg_lam"], writes=["rg_clam"], out=clam[:], in_=lam[:], func=AF.Exp, scale=-1.0)
        P.I("act", "activation", reads=["rg_clam", "rg_one"], writes=["rg_clam"], out=clam[:], in_=clam[:], func=AF.Ln, bias=one[:, 0:1], scale=1.0)
        P.I("dve", "tensor_scalar_mul", reads=["rg_clam"], writes=["rg_clam"], out=clam[:], in0=clam[:], scalar1=-4.0)
        half = P.sb("rg_half", [128, 1], F32, es)
        P.I("dve", "memset", writes=["rg_half"], ap=half[:], constant=0.5)
        P.I("dve", "tensor_scalar_mul", reads=["rg_bg"], writes=["rg_bg"], out=bg[:], in0=bg[:], scalar1=0.5)
        P.I("dve", "tensor_scalar_mul", reads=["rg_h0"], writes=["rg_h0"], out=h0[:], in0=h0[:], scalar1=2.0)
        uraw = P.sb("rg_uraw", [128, 1024], F32, es)
        uc = P.sb("rg_uc", [128, 2, 1024], F32, es); ucb = P.sb("rg_ucb", [128, 2, 1024], BF16, es)
        abufs = [P.sb(f"rg_a{i}", [128, 1024], F32, es) for i in range(2)]
        xbs = [[P.sb(f"rg_x{o}{i}", [128, 1024], F32, es) for i in range(2)] for o in range(2)]
        sqt = [P.sb(f"rg_sq{i}", [128, 512], F32, es) for i in range(4)]
        sgS = [P.sb(f"rg_sg{i}", [128, 512], F32, es) for i in range(2)]
        wv_all = C.rg_w_in.rearrange("(k p) n -> p k n", p=128)
        pi = [0]
        def nps():
            i = pi[0] % 6; pi[0] += 1
            return C.ps[i], f"ps{i}"
        def seg(ap2d, blk, lo, hi):
            if blk == 0:
                v = ap2d.rearrange("p (s t) -> p s t", s=4)
                return v[:, :, lo:256 + hi]
            return ap2d[:, lo:1024 + hi]
        for hh in range(4):
            if hh == 2:
                load_wout(P, C, C.rg_w_out)
            wv, wk = C.ring.load_multi([wv_all[:, :, j * 1024 + c * 128:j * 1024 + (c + 1) * 128] for j in range(2) for c in (2 * hh, 2 * hh + 1)])
            gv, gk = C.ring.load_multi([C.rg_w_gate[d, hh].rearrange("(k p) n -> p k n", p=128) for d in range(2)])
            for blk in range(2):
                for cc in range(2):
                    c = 2 * hh + cc
                    for g2 in range(2):
                        cs = slice(blk * 1024 + g2 * 512, blk * 1024 + (g2 + 1) * 512); ls = slice(g2 * 512, (g2 + 1) * 512)
                        pt, pk = nps()
                        mmK(P, pt[:], [(wv[:, kc, cc, :], C.hT[:, kc, cs]) for kc in range(8)], reads=[wk, f"hT{blk}"], writes=[pk])
                        P.I("act", "activation", reads=[pk], writes=["rg_uraw"], out=uraw[:, ls], in_=pt[:], func=AF.Copy)
                    ucc = uc[:, cc, :]
                    P.I("dve", "tensor_scalar", reads=["rg_uraw", "rg_cw", "rg_cb"], writes=["rg_uc"], out=ucc, in0=uraw[:],
                        scalar1=cw[:, 2, c:c + 1], scalar2=cb[:, c:c + 1], op0=ALU.mult, op1=ALU.add)
                    for (k, lo_o, hi_o, lo_i, hi_i) in ((0, 2, 0, 0, -2), (1, 1, 0, 0, -1), (3, 0, -1, 1, 0)):
                        o_ = seg(ucc, blk, lo_o, hi_o); i_ = seg(uraw[:], blk, lo_i, hi_i)
                        P.I("dve", "scalar_tensor_tensor", reads=["rg_uraw", "rg_uc", "rg_cw"], writes=["rg_uc"], out=o_, in0=i_,
                            scalar=cw[:, k, c:c + 1], in1=o_, op0=ALU.mult, op1=ALU.add)
                    P.I("pool", "tensor_copy", reads=["rg_uc"], writes=["rg_ucb"], out=ucb[:, cc, :], in_=ucc)
                for oc in range(2):
                    c = 2 * hh + oc
                    xb = xbs[oc]
                    for d in range(2):
                        xin = xb[d]; xk = f"rg_x{oc}{d}"; abuf = abufs[d]; ak = f"rg_a{d}"
                        for g2 in range(2):
                            ls = slice(g2 * 512, (g2 + 1) * 512)
                            pr, prk = nps(); pq, pqk = nps()
                            mmK(P, pr[:], [(gv[:, kc, d, oc * 128:(oc + 1) * 128], ucb[:, kc, ls]) for kc in range(2)], reads=[gk, "rg_ucb"], writes=[prk])
                            mmK(P, pq[:], [(gv[:, kc, d, 256 + oc * 128:256 + (oc + 1) * 128], ucb[:, kc, ls]) for kc in range(2)], reads=[gk, "rg_ucb"], writes=[pqk])
                            P.I("act", "activation", reads=[prk, "rg_bg"], writes=[ak], out=abuf[:, ls], in_=pr[:], func=AF.Tanh,
                                bias=bg[:, d, hh, oc:oc + 1], scale=0.5)
                            P.I("act", "activation", reads=[ak, "rg_clam"], writes=[ak], out=abuf[:, ls], in_=abuf[:, ls], func=AF.Exp,
                                scale=clam[:, d, c:c + 1], bias=clam[:, d, c:c + 1])
                            P.I("act", "activation", reads=[pqk, "rg_bg"], writes=[xk], out=xin[:, ls], in_=pq[:], func=AF.Tanh,
                                bias=bg[:, d, hh, 2 + oc:3 + oc], scale=0.5)
                            sq = sqt[d * 2 + g2]; sk = f"rg_sq{d * 2 + g2}"
                            P.I("act", "activation", reads=[ak], writes=[sk], out=sq[:], in_=abuf[:, ls], func=AF.Square)
                        for g2 in range(2):
                            ls = slice(g2 * 512, (g2 + 1) * 512)
                            sq = sqt[d * 2 + g2]; sk = f"rg_sq{d * 2 + g2}"
                            P.I("act", "activation", reads=[sk, "rg_one"], writes=[sk], out=sq[:], in_=sq[:], func=AF.Sqrt, bias=one[:, 0:1], scale=-1.0)
                            P.I("dve", "scalar_tensor_tensor", reads=[xk, sk], writes=[xk], out=xin[:, ls], in0=xin[:, ls], scalar=1.0, in1=sq[:],
                                op0=ALU.add, op1=ALU.mult)
                            P.I("dve", "tensor_tensor", reads=[xk, "rg_uc"], writes=[xk], out=xin[:, ls], in0=xin[:, ls], in1=uc[:, oc, ls], op=ALU.mult)
                        seqs = [(s * 256, 256) for s in range(4)] if blk == 0 else [(0, 1024)]
                        for (o0, L) in seqs:
                            sl = slice(o0, o0 + L) if d == 0 else slice(o0 + L - 1, (o0 - 1) if o0 > 0 else None, -1)
                            init = 0.0 if blk == 0 else h0[:, d, c:c + 1]
                            P.I("dve", "tensor_tensor_scan", reads=[ak, xk, "rg_h0"], writes=[xk], out=xin[:, sl], data0=abuf[:, sl],
                                data1=xin[:, sl], initial=init, op0=ALU.mult, op1=ALU.add)
                        if blk == 0:
                            x3 = xin[:].rearrange("p (s t) -> p s t", s=4)
                            src = x3[:, :, 255:256] if d == 0 else x3[:, :, 0:1]
                            dst = rgst[:, c, :].rearrange("p (s d) -> p s d", d=2)[:, :, d:d + 1]
                            P.I("dve", "tensor_scalar_mul", reads=[xk], writes=["rg_state"], out=dst, in0=src, scalar1=0.5)
                    P.I("dve", "tensor_tensor", reads=[f"rg_x{oc}0", f"rg_x{oc}1"], writes=[f"rg_x{oc}0"], out=xb[0][:], in0=xb[0][:], in1=xb[1][:], op=ALU.add)
                    for g2 in range(2):
                        cs = slice(blk * 1024 + g2 * 512, blk * 1024 + (g2 + 1) * 512); ls = slice(g2 * 512, (g2 + 1) * 512)
                        pt, pk = nps()
                        mmK(P, pt[:], [(wv[:, kc, 2 + oc, :], C.hT[:, kc, cs]) for kc in range(8)], reads=[wk, f"hT{blk}"], writes=[pk])
                        P.I("act", "activation", reads=[pk], writes=[f"rg_sg{g2}"], out=sgS[g2][:], in_=pt[:], func=AF.Tanh, scale=0.5)
                        P.I("dve", "scalar_tensor_tensor", reads=[f"rg_sg{g2}", pk], writes=[f"rg_sg{g2}"], out=sgS[g2][:], in0=sgS[g2][:], scalar=1.0, in1=pt[:],
                            op0=ALU.add, op1=ALU.mult)
                        P.I("dve", "scalar_tensor_tensor", reads=[f"rg_sg{g2}", f"rg_x{oc}0"], writes=[f"mT{blk * 2 + g2}"], out=C.mT[:, c, cs], in0=sgS[g2][:],
                            scalar=0.25, in1=xb[0][:, ls], op0=ALU.mult, op1=ALU.mult)
        srow = uraw[0:8, :]
        for half in range(2):
            pt, pk = nps()
            P.G("pe", [("transpose", dict(out=pt[0:8, i * 128:(i + 1) * 128], in_=rgst[:, half * 4 + i, :], identity=C.identf[:])) for i in range(4)],
                reads=["rg_state", "identf"], writes=[pk])
            P.I("dve", "tensor_copy", reads=[pk], writes=["rg_uraw"], out=srow[:, half * 512:(half + 1) * 512], in_=pt[0:8, :])
        P.D("sp", "o_rg", C.o_rg[:, :], srow, reads=["rg_uraw"])
SM_SCALE = 96.0 ** -0.5

def mla_layer(P, C):
    with ExitStack() as es:
        qn = P.sb("ml_qn", [128, 3], F32, es); kvn = P.sb("ml_kvn", [128, 2], F32, es)
        eps6 = P.sb("ml_eps", [128, 1], F32, es)
        rope = P.sb("ml_rope", [128, 2, 1024], F32, es)
        cqnT = P.sb("ml_cqnT", [128, 3, 1024], BF16, es)
        ckvnT = P.sb("ml_ckvnT", [128, 2, 1536], BF16, es)
        KK = P.sb("ml_KK", [128, 1536], BF16, es)
        KK2 = P.sb("ml_KK2", [128, 1536], BF16, es)
        P.D("sp", "ml_qn", qn[:], C.mla_qn[:, :], writes=["ml_qn"])
        P.D("sp", "ml_kvn", kvn[:], C.mla_kvn[:, :], writes=["ml_kvn"])
        P.D("sp", "ml_rope", rope[64:96], C.rope_cs[64:96, :, :], writes=["ml_rope"])
        P.I("dve", "memset", writes=["ml_eps"], ap=eps6[:], constant=1e-6)
        wv_all = C.mla_w_in.rearrange("(k p) n -> p k n", p=128)
        wqb_all = C.mla_w_qb.rearrange("(k p) n -> p k n", p=128)
        wqs_all = C.mla_w_qbsw.rearrange("(k p) n -> p k n", p=128)
        wkvb_all = C.mla_w_kvb.rearrange("(k p) n -> p k n", p=128)
        wks_all = C.mla_w_kpesw.rearrange("(k p) n -> p k n", p=128)
        load_wout(P, C, C.mla_w_out)
        for blk in range(2):
            nkeys = 1024 if blk == 0 else 1536
            with ExitStack() as esA:
                sq = [P.sb(f"ml_sq{i}", [128, 512], BF16, esA) for i in range(2)]
                rstd = P.sb("ml_rstd", [128, 512], F32, esA)
                ckvf = P.sb("ml_ckvf", [128, 2, 512], F32, esA)
                t1 = P.sb("ml_t1", [128, 256], F32, esA); t2 = P.sb("ml_t2", [128, 256], F32, esA)
                stg = [P.sb(f"ml_stg{i}", [128, 256], F32, esA) for i in range(2)]
                stk = P.sb("ml_stk", [128, 32], F32, esA); stkb = P.sb("ml_stkb", [128, 32], BF16, esA)
                (wcq,), wcqk = C.ring.load_parts([wv_all[:, :, 0:384]])
                (wkv, wks), wkvk = C.ring.load_parts([wv_all[:, :, 384:672], wks_all[:, :, :]])
                for gq in range(2):
                    cs = slice(blk * 1024 + gq * 512, blk * 1024 + (gq + 1) * 512); ls = slice(gq * 512, (gq + 1) * 512)
                    hk = f"hT{blk}"
                    for c in range(3):
                        mmK(P, C.ps[c][:], [(wcq[:, kc, c * 128:(c + 1) * 128], C.hT[:, kc, cs]) for kc in range(8)], reads=[wcqk, hk], writes=[f"ps{c}"])
                        P.I("act", "activation", reads=[f"ps{c}"], writes=[f"ml_sq{c % 2}"], out=sq[c % 2][:], in_=C.ps[c][:], func=AF.Square)
                        P.I("pe", "matmul", reads=[f"ml_sq{c % 2}", "onesb"], writes=["ps3"], out=C.ps[3][:], lhsT=C.onesb[:], rhs=sq[c % 2][:],
                            start=(c == 0), stop=(c == 2))
                    P.I("act", "activation", reads=["ps3", "ml_eps"], writes=["ml_rstd"], out=rstd[:], in_=C.ps[3][:], func=AF.Sqrt, bias=eps6[:, 0:1], scale=1.0 / 384.0)
                    P.I("dve", "reciprocal", reads=["ml_rstd"], writes=["ml_rstd"], out=rstd[:], in_=rstd[:])
                    for c in range(3):
                        P.I("dve", "scalar_tensor_tensor", reads=[f"ps{c}", "ml_rstd", "ml_qn"], writes=["ml_cqnT"], out=cqnT[:, c, ls], in0=C.ps[c][:],
                            scalar=qn[:, c:c + 1], in1=rstd[:], op0=ALU.mult, op1=ALU.mult)
                    for c in range(2):
                        mmK(P, C.ps[4 + c][:], [(wkv[:, kc, c * 128:(c + 1) * 128], C.hT[:, kc, cs]) for kc in range(8)], reads=[wkvk, hk], writes=[f"ps{4 + c}"])
                        P.I("act", "activation", reads=[f"ps{4 + c}"], writes=[f"ml_sq{c % 2}"], out=sq[c % 2][:], in_=C.ps[4 + c][:], func=AF.Square)
                        P.I("pe", "matmul", reads=[f"ml_sq{c % 2}", "onesb"], writes=["ps6"], out=C.ps[6][:], lhsT=C.onesb[:], rhs=sq[c % 2][:],
                            start=(c == 0), stop=(c == 1))
                    P.I("act", "activation", reads=["ps6", "ml_eps"], writes=["ml_rstd"], out=rstd[:], in_=C.ps[6][:], func=AF.Sqrt, bias=eps6[:, 0:1], scale=1.0 / 256.0)
                    P.I("dve", "reciprocal", reads=["ml_rstd"], writes=["ml_rstd"], out=rstd[:], in_=rstd[:])
                    for c in range(2):
                        P.I("dve", "scalar_tensor_tensor", reads=[f"ps{4 + c}", "ml_rstd", "ml_kvn"], writes=["ml_ckvf"], out=ckvf[:, c, :], in0=C.ps[4 + c][:],
                            scalar=kvn[:, c:c + 1], in1=rstd[:], op0=ALU.mult, op1=ALU.mult)
                    P.I("act", "activation", reads=["ml_ckvf"], writes=["ml_ckvnT"], out=ckvnT[:, :, ls], in_=ckvf[:], func=AF.Copy)
                    mmK(P, C.ps[7][64:96, :], [(wkv[:, kc, 256:288], C.hT[:, kc, cs]) for kc in range(8)], reads=[wkvk, hk], writes=["ps7"])
                    if blk == 0:
                        P.I("act", "activation", reads=["ps7"], writes=["ml_KKpe"], out=KK[64:96, ls], in_=C.ps[7][64:96, :], func=AF.Copy)
                    else:
                        mmK(P, C.ps[3][64:96, :], [(wks[:, kc, :], C.hT[:, kc, cs]) for kc in range(8)], reads=[wkvk, hk], writes=["ps3"])
                        for hq in range(2):
                            l2 = slice(hq * 256, (hq + 1) * 256); pos = slice(gq * 512 + hq * 256, gq * 512 + (hq + 1) * 256)
                            P.I("dve", "tensor_tensor", reads=["ps7", "ml_rope"], writes=["ml_t1"], out=t1[64:96, :], in0=C.ps[7][64:96, l2], in1=rope[64:96, 0, pos], op=ALU.mult)
                            P.I("dve", "tensor_tensor", reads=["ps3", "ml_rope"], writes=["ml_t2"], out=t2[64:96, :], in0=C.ps[3][64:96, l2], in1=rope[64:96, 1, pos], op=ALU.mult)
                            P.I("dve", "tensor_tensor", reads=["ml_t1", "ml_t2"], writes=["ml_KKpe"], out=KK[64:96, pos], in0=t1[64:96, :], in1=t2[64:96, :], op=ALU.add)
                    if blk == 0:
                        for tl in range(4):
                            gt = gq * 4 + tl; b = tl % 2
                            P.G("pe", [("transpose", dict(out=C.ps[2][:, c * 128:(c + 1) * 128], in_=ckvf[:, c, tl * 128:(tl + 1) * 128], identity=C.identf[:]))
                                       for c in range(2)], reads=["ml_ckvf", "identf"], writes=["ps2"])
                            P.I("dve", "tensor_copy", reads=["ps2"], writes=[f"ml_stg{b}"], out=stg[b][:], in_=C.ps[2][:, 0:256])
                            P.D("sp", f"o_ckv{b}", C.o_ckv[gt * 128:(gt + 1) * 128, :], stg[b][:], reads=[f"ml_stg{b}"])
                            mmK(P, C.ps[2][:, 256:288], [(C.hT[:, kc, gt * 128:(gt + 1) * 128], wkv[:, kc, 256:288]) for kc in range(8)], reads=[wkvk, hk], writes=["ps2"])
                            P.I("dve", "tensor_copy", reads=["ps2"], writes=["ml_stk"], out=stk[:], in_=C.ps[2][:, 256:288])
                            P.D("sp", "o_kpe", C.o_kpe[gt * 128:(gt + 1) * 128, :], stk[:], reads=["ml_stk"])
                if blk == 1:
                    for tl in range(4):
                        b = tl % 2
                        P.D("sp", f"ml_stg{b}", stg[b][:], C.mla_ckv_ctx[tl * 128:(tl + 1) * 128, :], writes=[f"ml_stg{b}"])
                        P.G("pe", [("transpose", dict(out=C.ps[2][:, c * 128:(c + 1) * 128], in_=stg[b][:, c * 128:(c + 1) * 128], identity=C.identf[:]))
                                   for c in range(2)], reads=[f"ml_stg{b}", "identf"], writes=["ps2"])
                        P.I("act", "activation", reads=["ps2"], writes=["ml_ckvnT"], out=ckvnT[:, :, 1024 + tl * 128:1024 + (tl + 1) * 128],
                            in_=C.ps[2][:, 0:256].rearrange("p (c t) -> p c t", c=2), func=AF.Copy)
                        P.D("sp", "ml_stk", stk[:], C.mla_kpe_ctx[tl * 128:(tl + 1) * 128, :], writes=["ml_stk"])
                        P.I("dve", "tensor_copy", reads=["ml_stk"], writes=["ml_stkb"], out=stkb[:], in_=stk[:])
                        psb = C.ps[2][:].bitcast(BF16)
                        P.I("pe", "transpose", reads=["ml_stkb", "identb"], writes=["ps2"], out=psb[64:96, 512:640], in_=stkb[:], identity=C.identb[:])
                        P.I("dve", "tensor_copy", reads=["ps2"], writes=["ml_KKpe"], out=KK[64:96, 1024 + tl * 128:1024 + (tl + 1) * 128], in_=psb[64:96, 512:640])
                P.I("dve", "tensor_copy", reads=["ml_KKpe"], writes=["ml_KKpe2"], out=KK2[64:96, 0:nkeys], in_=KK[64:96, 0:nkeys])
                P.barrier()
            with ExitStack() as esB:
                KKs = [KK, KK2]
                Vh = [P.sb(f"ml_Vh{i}", [128, 12, 65], BF16, esB) for i in range(2)]
                QQ = [P.sb(f"ml_QQ{i}", [128, 1024], BF16, esB) for i in range(2)]
                PT = [P.sb(f"ml_PT{i}", [128, 512], BF16, esB) for i in range(2)]
                sgT = [P.sb(f"ml_sgT{i}", [128, 1024], BF16, esB) for i in range(2)]
                on2 = P.sb("ml_on2", [128, 8, 128], BF16, esB)
                rcp = P.sb("ml_rcp", [128, 4], F32, esB)
                u1 = P.sb("ml_u1", [128, 256], F32, esB); u2 = P.sb("ml_u2", [128, 256], F32, esB)
                tg = P.sb("ml_tg", [128, 512], F32, esB)
                for i in range(2):
                    P.I("dve", "memset", writes=[f"ml_Vh{i}"], ap=Vh[i][:, :, 64:65], constant=1.0)
                nkt = nkeys // 128
                units = [(slice(s_ * 256, (s_ + 1) * 256), [2 * s_, 2 * s_ + 1]) for s_ in range(4)] if blk == 0 else \
                        [(slice(g_ * 512, (g_ + 1) * 512), list(range(12))) for g_ in range(2)]
                sc_i = [0]
                hk = f"hT{blk}"
                W = {}

                def proj(h):
                    hp, hh = h // 2, h % 2; bsel = h % 2
                    if hh == 0:
                        W[hp] = C.ring.load_parts([wqb_all[:, :, hp * 192:(hp + 1) * 192], wqs_all[:, :, hp * 64:(hp + 1) * 64],
                                                   wkvb_all[:, :, hp * 256:(hp + 1) * 256], wv_all[:, :, 672 + hp * 128:672 + (hp + 1) * 128]])
                    (wqb, wqs, wkb, wg), wk = W[hp]
                    if hh == 0:
                        for g2 in range(2):
                            cs = slice(blk * 1024 + g2 * 512, blk * 1024 + (g2 + 1) * 512); ls = slice(g2 * 512, (g2 + 1) * 512)
                            mmK(P, C.ps[6][:], [(wg[:, kc, :], C.hT[:, kc, cs]) for kc in range(8)], reads=[wk, hk], writes=["ps6"])
                            P.I("act", "activation", reads=["ps6"], writes=["ml_tg"], out=tg[:], in_=C.ps[6][:], func=AF.Tanh, scale=0.5)
                            P.I("dve", "scalar_tensor_tensor", reads=["ml_tg", "ps6"], writes=[f"ml_sgT{hp % 2}"], out=sgT[hp % 2][:, ls], in0=tg[:], scalar=1.0,
                                in1=C.ps[6][:], op0=ALU.add, op1=ALU.mult)
                            yield
                    KKb = KKs[bsel]
                    for kg in range(nkeys // 512):
                        ks = slice(kg * 512, (kg + 1) * 512)
                        mmK(P, C.ps[6][0:64, :], [(wkb[:, kc, hh * 128:hh * 128 + 64], ckvnT[:, kc, ks]) for kc in range(2)], reads=[wk, "ml_ckvnT"], writes=["ps6"])
                        P.I("dve", "tensor_copy", reads=["ps6"], writes=[f"ml_KKn{bsel}"], out=KKb[0:64, ks], in_=C.ps[6][0:64, :])
                        yield
                    for k8 in range((nkt + 7) // 8):
                        n8 = min(8, nkt - k8 * 8)
                        P.G("pe", [("matmul", dict(out=C.ps[7][:, j * 64:(j + 1) * 64], lhsT=ckvnT[:, kc, (k8 * 8 + j) * 128:(k8 * 8 + j + 1) * 128],
                                                   rhs=wkb[:, kc, hh * 128 + 64:hh * 128 + 128], start=(kc == 0), stop=(kc == 1)))
                                   for j in range(n8) for kc in range(2)], reads=[wk, "ml_ckvnT"], writes=["ps7"])
                        P.I("dve", "tensor_copy", reads=["ps7"], writes=[f"ml_Vh{bsel}"], out=Vh[bsel][:, k8 * 8:k8 * 8 + n8, 0:64],
                            in_=C.ps[7][:, 0:n8 * 64].rearrange("p (a b) -> p a b", a=n8))
                        yield
                    Qb = QQ[bsel]
                    for g2 in range(2):
                        ls = slice(g2 * 512, (g2 + 1) * 512)
                        mmK(P, C.ps[6][0:64, :], [(wqb[:, kc, hh * 96:hh * 96 + 64], cqnT[:, kc, ls]) for kc in range(3)], reads=[wk, "ml_cqnT"], writes=["ps6"])
                        P.I("dve", "tensor_copy", reads=["ps6"], writes=[f"ml_QQn{bsel}"], out=Qb[0:64, ls], in_=C.ps[6][0:64, :])
                        yield
                        mmK(P, C.ps[7][64:96, :], [(wqb[:, kc, hh * 96 + 64:hh * 96 + 96], cqnT[:, kc, ls]) for kc in range(3)], reads=[wk, "ml_cqnT"], writes=["ps7"])
                        if blk == 0:
                            P.I("dve", "tensor_copy", reads=["ps7"], writes=[f"ml_QQr{bsel}"], out=Qb[64:96, ls], in_=C.ps[7][64:96, :])
                        else:
                            mmK(P, C.ps[6][64:96, :], [(wqs[:, kc, hh * 32:(hh + 1) * 32], cqnT[:, kc, ls]) for kc in range(3)], reads=[wk, "ml_cqnT"], writes=["ps6"])
                            for hq in range(2):
                                l2 = slice(hq * 256, (hq + 1) * 256); pos = slice(g2 * 512 + hq * 256, g2 * 512 + (hq + 1) * 256)
                                P.I("dve", "tensor_tensor", reads=["ps7", "ml_rope"], writes=["ml_u1"], out=u1[64:96, :], in0=C.ps[7][64:96, l2], in1=rope[64:96, 0, pos], op=ALU.mult)
                                P.I("dve", "tensor_tensor", reads=["ps6", "ml_rope"], writes=["ml_u2"], out=u2[64:96, :], in0=C.ps[6][64:96, l2], in1=rope[64:96, 1, pos], op=ALU.mult)
                                P.I("dve", "tensor_tensor", reads=["ml_u1", "ml_u2"], writes=[f"ml_QQr{bsel}"], out=Qb[64:96, pos], in0=u1[64:96, :], in1=u2[64:96, :], op=ALU.add)
                        yield

                def attn(h):
                    hh = h % 2; bsel = h % 2
                    KKb = KKs[bsel]; Qb = QQ[bsel]; Vb = Vh[bsel]
                    kkeys = [f"ml_KKn{bsel}", "ml_KKpe" if bsel == 0 else "ml_KKpe2", f"ml_QQn{bsel}", f"ml_QQr{bsel}"]
                    for (qsl, kts) in units:
                        nq = qsl.stop - qsl.start; nqt = nq // 128; qt0 = qsl.start // 128
                        def qk(kt):
                            sb_ = sc_i[0] % 2; sc_i[0] += 1
                            P.I("pe", "matmul", reads=kkeys, writes=[f"ps{sb_}"], out=C.ps[sb_][:, 0:nq],
                                lhsT=KKb[0:96, kt * 128:(kt + 1) * 128], rhs=Qb[0:96, qsl], start=True, stop=True)
                            P.I("act", "activation", reads=[f"ps{sb_}"], writes=[f"ml_PT{sb_}"], out=PT[sb_][:, 0:nq], in_=C.ps[sb_][:, 0:nq], func=AF.Exp, scale=SM_SCALE)
                            return sb_
                        pend = qk(kts[0])
                        for ki, kt in enumerate(kts):
                            pi_ = pend
                            if ki + 1 < len(kts):
                                pend = qk(kts[ki + 1])
                            for qt in range(nqt):
                                P.I("pe", "matmul", reads=[f"ml_PT{pi_}", f"ml_Vh{bsel}"], writes=[f"ps{2 + qt}"], out=C.ps[2 + qt][:, 0:65],
                                    lhsT=PT[pi_][:, qt * 128:(qt + 1) * 128], rhs=Vb[:, kt, :], start=(ki == 0), stop=(ki == len(kts) - 1))
                            yield
                        for qt in range(nqt):
                            P.I("dve", "reciprocal", reads=[f"ps{2 + qt}"], writes=["ml_rcp"], out=rcp[:, qt:qt + 1], in_=C.ps[2 + qt][:, 64:65])
                            P.I("dve", "tensor_scalar", reads=[f"ps{2 + qt}", "ml_rcp"], writes=["ml_on2"], out=on2[:, qt0 + qt, hh * 64:(hh + 1) * 64],
                                in0=C.ps[2 + qt][:, 0:64], scalar1=rcp[:, qt:qt + 1], scalar2=None, op0=ALU.mult)
                        yield

                def finish_pair(hp):
                    psb = C.ps[7][:].bitcast(BF16)
                    for g2 in range(2):
                        cs = slice(blk * 1024 + g2 * 512, blk * 1024 + (g2 + 1) * 512); ls = slice(g2 * 512, (g2 + 1) * 512)
                        P.G("pe", [("transpose", dict(out=psb[:, i * 128:(i + 1) * 128], in_=on2[:, g2 * 4 + i, :], identity=C.identb[:])) for i in range(4)],
                            reads=["ml_on2", "identb"], writes=["ps7"])
                        P.I("dve", "scalar_tensor_tensor", reads=["ps7", f"ml_sgT{hp % 2}"], writes=[f"mT{blk * 2 + g2}"], out=C.mT[:, hp, cs], in0=psb[:, 0:512],
                            scalar=0.5, in1=sgT[hp % 2][:, ls], op0=ALU.mult, op1=ALU.mult)

                _roundrobin([proj(0)])
                for h in range(16):
                    gens = [attn(h)]
                    if h + 1 < 16:
                        gens.append(proj(h + 1))
                    _roundrobin(gens)
                    if h % 2 == 1:
                        finish_pair(h // 2)
                P.barrier()
LAYERS = (0, 1, 2, 3)
_CACHE = {}

def build_nc(layers):
    nc = bass.Bass("TRN2", target_bir_lowering=False)
    C = Ctx()
    declare_io(nc, C)
    with ExitStack() as es:
        P = Prog(nc, es)
        setup_persistent(P, C)
        ada_phase(P, C, layers[0])
        input_transposes(P, C)
        for li, l in enumerate(layers):
            modulate_phase(P, C, l)
            [hgrn_layer, sconv_layer, rglru_layer, mla_layer][l % 4](P, C)
            pre = wout_prefetch(P, C)
            P.barrier()
            wout_ln_phase(P, C, l, pre, next_ada=(layers[li + 1] if li + 1 < len(layers) else None))
        output_transposes(P, C)
        outs = [n for n in P.dall if n.startswith(("yout", "o_hg", "o_rg", "o_ckv", "o_kpe"))]
        P.final_wait("sp", outs)
        with nc.Block() as block:
            P.emit(block)
        C.counts = dict(P.cnt); print('instr counts', C.counts, 'nsem', len(P.esems) + sum(len(v) for v in P.dall.values()))
    return nc, C

def colT(v, nch):
    return np.ascontiguousarray(np.asarray(v, np.float32).reshape(nch, 128).T)

def make_in_maps(I):
    f = lambda a: np.ascontiguousarray(np.asarray(a, np.float32))
    half = 8; r = np.arange(1024) // 64; cpos = np.arange(1024) % 64
    inv = (10000.0 ** (-np.arange(0, 16, 2, dtype=np.float32) / 16)).astype(np.float32)
    cs = np.zeros((128, 2, 1024), np.float32)
    for part, pos in ((0, r), (1, cpos)):
        ang = pos[None, :].astype(np.float32) * inv[:, None]
        co, si = np.cos(ang), np.sin(ang)
        b = 64 + part * 16
        cs[b:b + 8, 0] = co; cs[b + 8:b + 16, 0] = co
        cs[b:b + 8, 1] = -si; cs[b + 8:b + 16, 1] = si
    s_ = np.arange(128)[:, None]; t_ = np.arange(128)[None, :]
    same = (s_ // 64) == (t_ // 64)
    masks = np.zeros((128, 4, 128), np.float32)
    masks[:, 0] = same & (s_ <= t_); masks[:, 1] = same & (s_ >= t_)
    masks[:, 2] = np.tile((np.arange(128) % 64 != 0).astype(np.float32), (128, 1))
    masks[:, 3] = np.tile((np.arange(128) % 64 != 63).astype(np.float32), (128, 1))
    swap = np.arange(32).reshape(2, 2, 8)[:, ::-1, :].reshape(-1)
    wqb = f(I['mla_w_qb'][0])
    qb_r = wqb.reshape(384, 16, 96)[:, :, 64:]
    shared = {
        'ada_w': f(I['ada_w']), 'ada_bT': np.ascontiguousarray(f(I['ada_b']).reshape(4, 24, 128).transpose(2, 0, 1)),
        'ln_gT': np.ascontiguousarray(f(I['ln_g']).reshape(4, 8, 128).transpose(2, 0, 1)),
        'ln_bT': np.ascontiguousarray(f(I['ln_b']).reshape(4, 8, 128).transpose(2, 0, 1)),
        'ident': np.eye(128, dtype=np.float32),
        'sc_w_in': f(I['sc_w_in'][0]), 'sc_cw': np.ascontiguousarray(f(I['sc_conv_w'][0]).reshape(3, 8, 128).transpose(2, 0, 1)),
        'sc_cb': colT(I['sc_conv_b'][0], 8), 'sc_w_out': f(I['sc_w_out'][0]),
        'rg_w_in': f(I['rg_w_in'][0]), 'rg_cw': np.ascontiguousarray(f(I['rg_conv_w'][0]).reshape(4, 8, 128).transpose(2, 0, 1)),
        'rg_cb': colT(I['rg_conv_b'][0], 8), 'rg_w_gate': f(I['rg_w_gate'][0]),
        'rg_bg': np.ascontiguousarray(f(I['rg_b_gate'][0]).reshape(2, 4, 4, 128).transpose(3, 0, 1, 2)),
        'rg_lam': np.ascontiguousarray(f(I['rg_lambda'][0]).reshape(2, 8, 128).transpose(2, 0, 1)),
        'rg_w_out': f(I['rg_w_out'][0]),
        'hg_w_in': f(I['hg_w_in'][0]),
        'hg_lbl': np.ascontiguousarray(f(I['hg_lb_logits']).reshape(2, 5, 8, 128).transpose(3, 0, 2, 1).reshape(128, 16, 5)),
        'hg_ng': colT(I['hg_norm_g'][0], 8), 'hg_w_out': f(I['hg_w_out'][0]), 'hg_masks': masks,
        'mla_w_in': f(I['mla_w_in'][0]), 'mla_qn': colT(I['mla_q_norm'][0], 3), 'mla_kvn': colT(I['mla_kv_norm'][0], 2),
        'mla_w_qb': wqb, 'mla_w_qbsw': np.ascontiguousarray(qb_r[:, :, swap].reshape(384, 512)),
        'mla_w_kpesw': np.ascontiguousarray(f(I['mla_w_in'][0])[:, 640:672][:, swap]),
        'mla_w_kvb': f(I['mla_w_kvb'][0]), 'mla_w_out': f(I['mla_w_out'][0]), 'rope_cs': cs,
    }
    maps = []
    xp = f(I['x_prompt']); xs = f(I['x_sample'])
    for cid in range(8):
        k = cid // 2
        m = dict(shared)
        m['xin'] = np.ascontiguousarray(np.concatenate([xp[4 * cid:4 * cid + 4].reshape(1024, D), xs[k]], 0))
        cond = np.stack([f(I['c_ctx']), f(I['c'])[k]], 1)
        m['condT'] = np.ascontiguousarray(cond.reshape(8, 128, 2).transpose(1, 0, 2))
        m['rg_h0'] = np.ascontiguousarray(f(I['state_rglru'])[k, 0].reshape(2, 8, 128).transpose(2, 0, 1))
        m['hg_s0'] = f(I['state_hgrn'])[k, 0]
        m['mla_ckv_ctx'] = f(I['cache_mla_ckv'])[k, 0]; m['mla_kpe_ctx'] = f(I['cache_mla_kpe'])[k, 0]
        maps.append(m)
    return maps

def run_layers(I, layers, trace=False):
    key = tuple(layers)
    if key not in _CACHE:
        _CACHE[key] = build_nc(layers)
    nc, C = _CACHE[key]
    maps = make_in_maps(I)
    res = run_bass_kernel_spmd(nc, maps, core_ids=list(range(8)), trace=trace)
    R = res.results
    y_prompt = np.concatenate([R[c]['y'][:1024].reshape(4, 256, D) for c in range(8)], 0)
    y_sample = np.stack([R[2 * k]['y'][1024:] for k in range(4)], 0)
    o_hg = np.concatenate([R[c]['o_hg'] for c in range(8)], 0)[:, None]
    o_rg = np.concatenate([R[c]['o_rg'].reshape(4, 2, D) for c in range(8)], 0)[:, None]
    o_ckv = np.concatenate([R[c]['o_ckv'].reshape(4, 256, 256) for c in range(8)], 0)[:, None]
    o_kpe = np.concatenate([R[c]['o_kpe'].reshape(4, 256, 32) for c in range(8)], 0)[:, None]
    outs = tuple(np.ascontiguousarray(a, dtype=np.float32) for a in (y_prompt, y_sample, o_hg, o_rg, o_ckv, o_kpe))
    return outs, res

def kernel(**inputs):
    outs, _ = run_layers(inputs, LAYERS)
    return outs
```

```python
import numpy as np
import concourse.bass as bass
import concourse.mybir as mybir
from concourse.bass_utils import run_bass_kernel_spmd
from contextlib import ExitStack
import numpy as np
import concourse.bass as bass
import concourse.mybir as mybir
from concourse.bass_utils import run_bass_kernel_spmd
from contextlib import ExitStack
F32 = mybir.dt.float32; BF16 = mybir.dt.bfloat16
AF = mybir.ActivationFunctionType; ALU = mybir.AluOpType
AX = mybir.AxisListType

D = 1024; T = 2048; NT = 16; ALPHA = 8.0 ** 0.25
LN_EPS = 1e-5 / (ALPHA * ALPHA)

class Prog:
    ENGS = ("pe", "act", "dve", "pool", "sp")
    SEM_M = 1000
    DSEM_MAX = 1600
    def __init__(self, nc, es):
        self.nc = nc; self.es = es
        self.q = {e: [] for e in self.ENGS}
        self.cnt = {e: 0 for e in self.ENGS}
        self.esems = {}
        self.seen = {e: {} for e in self.ENGS}
        self.lastw = {}; self.readers = {}
        self.dsems = {}
        self.dall = {}
        self.semh = {}
    def sb(self, name, shape, dt, es=None):
        self._names = getattr(self, "_names", {})
        n = self._names.get(name, 0); self._names[name] = n + 1
        if n: name = f"{name}__{n}"
        return (es or self.es).enter_context(self.nc.sbuf_tensor(name, list(shape), dt))
    def ps(self, name, shape, dt):
        return self.es.enter_context(self.nc.psum_tensor(name, list(shape), dt))
    def esem(self, eng, epoch):
        k = (eng, epoch)
        if k not in self.esems:
            self.esems[k] = self.es.enter_context(self.nc.semaphore(f"s_{eng}_{epoch}"))
        return self.esems[k]
    def dsem(self, name):
        d = self.dsems.get(name)
        if d is None or d[1] + 16 > self.DSEM_MAX:
            ep = 0 if d is None else d[2] + 1
            h = self.es.enter_context(self.nc.semaphore(f"d_{name}_{ep}"))
            d = [h, 0, ep]
            self.dsems[name] = d
            self.semh[f"d_{name}#{ep}"] = h
            self.dall.setdefault(name, []).append(d)
        return d
    def _handle(self, sk, val):
        if sk in self.ENGS:
            ep = (val - 1) // self.SEM_M
            return (self.esem(sk, ep), val - ep * self.SEM_M)
        return (self.semh[sk], val)
    def _need(self, eng, waits, dep):
        if dep is None: return
        sk, val = dep
        if sk == "pe" and eng == "pe": return
        if self.seen[eng].get(sk, 0) >= val: return
        waits[sk] = max(waits.get(sk, 0), val)
    def _deps(self, eng, reads, writes):
        waits = {}
        for k in reads: self._need(eng, waits, self.lastw.get(k))
        for k in writes:
            self._need(eng, waits, self.lastw.get(k))
            for sk, v in self.readers.get(k, {}).items(): self._need(eng, waits, (sk, v))
        for sk, v in waits.items(): self.seen[eng][sk] = v
        return [self._handle(sk, v) for sk, v in waits.items()]
    def _mark(self, dep, reads, writes):
        for k in writes:
            self.lastw[k] = dep; self.readers[k] = {}
        for k in reads:
            self.readers.setdefault(k, {})[dep[0]] = dep[1]
    def op(self, eng, fns, reads=(), writes=()):
        writes = list(writes) + [k for k in reads if k.startswith("ps") and k not in writes]
        waits = self._deps(eng, reads, writes)
        self.cnt[eng] += 1
        idx = self.cnt[eng]
        h, _ = self._handle(eng, idx)
        self.q[eng].append((fns, waits, (h, 1)))
        self._mark((eng, idx), reads, writes)
    @staticmethod
    def _mk(method, kw):
        def fn(e):
            return getattr(e, method)(**kw)
        return fn
    def I(self, eng, method, reads=(), writes=(), **kw):
        self.op(eng, [self._mk(method, kw)], reads, writes)
    def G(self, eng, items, reads=(), writes=()):
        self.op(eng, [self._mk(m, kw) for (m, kw) in items], reads, writes)
    def D(self, queue, semname, out, in_, reads=(), writes=(), **kw):
        waits = self._deps(queue, reads, writes)
        d = self.dsem(semname)
        d[1] += 16
        self.q[queue].append(([self._mk("dma_start", dict(out=out, in_=in_, **kw))], waits, (d[0], 16)))
        self._mark((f"d_{semname}#{d[2]}", d[1]), reads, writes)
    def barrier(self, engs=("pe", "act", "dve", "pool", "sp")):
        targets = [(e, self.cnt[e]) for e in ("pe", "act", "dve", "pool") if self.cnt[e] > 0]
        for n, lst in self.dall.items():
            for d in lst:
                if d[1] > 0: targets.append((f"d_{n}#{d[2]}", d[1]))
        for e in engs:
            waits = {}
            for dep in targets:
                if dep[0] == e and e == "pe": continue
                if self.seen[e].get(dep[0], 0) >= dep[1]: continue
                waits[dep[0]] = dep[1]; self.seen[e][dep[0]] = dep[1]
            if waits:
                self.q[e].append((None, [self._handle(sk, v) for sk, v in waits.items()], None))
    def final_wait(self, queue, semnames):
        for n in semnames:
            for d in self.dall[n]:
                self.q[queue].append((None, [(d[0], d[1])], None))
    def emit(self, block):
        def run(e, lst):
            for fns, waits, inc in lst:
                for (h, v) in waits: e.wait_ge(h, v)
                if fns is None: continue
                for i, fn in enumerate(fns):
                    ins = fn(e)
                    if i == len(fns) - 1 and inc is not None: ins.then_inc(inc[0], inc[1])
        @block.tensor
        def _(e): run(e, self.q["pe"])
        @block.scalar
        def _(e): run(e, self.q["act"])
        @block.vector
        def _(e): run(e, self.q["dve"])
        @block.gpsimd
        def _(e): run(e, self.q["pool"])
        @block.sync
        def _(e): run(e, self.q["sp"])


class Ctx:
    pass

def mmK(P, out, pairs, reads, writes):
    n = len(pairs)
    P.G("pe", [("matmul", dict(out=out, lhsT=a, rhs=b, start=(i == 0), stop=(i == n - 1))) for i, (a, b) in enumerate(pairs)],
        reads=reads, writes=writes)

class WRing:
    def __init__(self, P, nslot=3, elems=4096):
        self.P = P; self.n = nslot; self.i = 0
        self.bufs = [P.sb(f"wr{i}", [128, elems], BF16) for i in range(nslot)]
    def load(self, dram_ap, shape_str, **dims):
        s = self.i % self.n; self.i += 1
        shp = dram_ap.shape
        n = 1
        for v in shp[1:]: n *= v
        view = self.bufs[s][:, 0:n]
        if len(shp) == 3:
            view = view.rearrange("p (a b) -> p a b", a=shp[1])
        elif len(shp) == 4:
            view = view.rearrange("p (a b c) -> p a b c", a=shp[1], b=shp[2])
        key = f"wr{s}"
        self.P.D("pool", key, view, dram_ap, writes=[key])
        return view, key

    def load_multi(self, aps):
        s = self.i % self.n; self.i += 1
        J = len(aps); K_, N_ = aps[0].shape[1], aps[0].shape[2]
        view = self.bufs[s][:, 0:K_ * J * N_].rearrange("p (a b c) -> p a b c", a=K_, b=J)
        key = f"wr{s}"
        for j, ap in enumerate(aps):
            self.P.D("pool", key, view[:, :, j, :], ap, writes=[key])
        return view, key

    def load_parts(self, aps):
        s = self.i % self.n; self.i += 1
        key = f"wr{s}"; off = 0; views = []
        for ap in aps:
            a, b = ap.shape[1], ap.shape[2]
            v = self.bufs[s][:, off:off + a * b].rearrange("p (a b) -> p a b", a=a)
            off += a * b
            self.P.D("pool", key, v, ap, writes=[key])
            views.append(v)
        assert off <= 4096
        return views, key
def declare_io(nc, C):
    def din(name, shape):
        return nc.dram_tensor(name, list(shape), F32, kind="ExternalInput").ap()
    def dout(name, shape):
        return nc.dram_tensor(name, list(shape), F32, kind="ExternalOutput").ap()
    C.xin = din("xin", [T, D]); C.condT = din("condT", [128, 8, 2])
    C.ada_w = din("ada_w", [4, D, 3 * D]); C.ada_bT = din("ada_bT", [128, 4, 24])
    C.ln_gT = din("ln_gT", [128, 4, 8]); C.ln_bT = din("ln_bT", [128, 4, 8])
    C.ident = din("ident", [128, 128])
    C.sc_w_in = din("sc_w_in", [D, 4 * D]); C.sc_cw = din("sc_cw", [128, 3, 8]); C.sc_cb = din("sc_cb", [128, 8])
    C.sc_w_out = din("sc_w_out", [D, D])
    C.rg_w_in = din("rg_w_in", [D, 2 * D]); C.rg_cw = din("rg_cw", [128, 4, 8]); C.rg_cb = din("rg_cb", [128, 8])
    C.rg_w_gate = din("rg_w_gate", [2, 4, 256, 512]); C.rg_bg = din("rg_bg", [128, 2, 4, 4])
    C.rg_lam = din("rg_lam", [128, 2, 8]); C.rg_w_out = din("rg_w_out", [D, D])
    C.rg_h0 = din("rg_h0", [128, 2, 8])
    C.hg_w_in = din("hg_w_in", [D, 5 * D]); C.hg_lbl = din("hg_lbl", [128, 16, 5]); C.hg_ng = din("hg_ng", [128, 8])
    C.hg_w_out = din("hg_w_out", [D, D]); C.hg_s0 = din("hg_s0", [2, 8, 128, 128])
    C.hg_masks = din("hg_masks", [128, 4, 128])
    C.mla_w_in = din("mla_w_in", [D, 1696]); C.mla_qn = din("mla_qn", [128, 3]); C.mla_kvn = din("mla_kvn", [128, 2])
    C.mla_w_qb = din("mla_w_qb", [384, 1536]); C.mla_w_qbsw = din("mla_w_qbsw", [384, 512])
    C.mla_w_kpesw = din("mla_w_kpesw", [D, 32])
    C.mla_w_kvb = din("mla_w_kvb", [256, 2048]); C.mla_w_out = din("mla_w_out", [D, D])
    C.mla_ckv_ctx = din("mla_ckv_ctx", [512, 256]); C.mla_kpe_ctx = din("mla_kpe_ctx", [512, 32])
    C.rope_cs = din("rope_cs", [128, 2, 1024])
    C.y = dout("y", [T, D])
    C.o_hg = dout("o_hg", [4, 2, 8, 128, 128]); C.o_rg = dout("o_rg", [8, D])
    C.o_ckv = dout("o_ckv", [1024, 256]); C.o_kpe = dout("o_kpe", [1024, 32])

def setup_persistent(P, C):
    C.xT = P.sb("xT", [128, 8, T], F32)
    C.hT = P.sb("hT", [128, 8, T], BF16)
    C.mT = P.sb("mT", [128, 8, T], BF16)
    C.ring = WRing(P, nslot=3, elems=4096)
    C.identf = P.sb("identf", [128, 128], F32)
    C.identb = P.sb("identb", [128, 128], BF16)
    C.onesb = P.sb("onesb", [128, 128], BF16)
    C.condf = P.sb("condf", [128, 8, 2], F32)
    C.scond = P.sb("scond", [128, 8, 2], BF16)
    C.adab = P.sb("adab", [128, 4, 24], F32)
    C.lng = P.sb("lng", [128, 4, 8], F32); C.lnb = P.sb("lnb", [128, 4, 8], F32)
    C.mod = P.sb("mod", [128, 24, 2], F32)
    C.colsb = [P.sb(f"cols{i}", [128, 3, 8, 2], F32) for i in range(2)]
    C.ps = [P.ps(f"ps{i}", [128, 512], F32) for i in range(8)]
    P.D("sp", "identf", C.identf[:], C.ident[:, :], writes=["identf"])
    P.D("sp", "condf", C.condf[:], C.condT[:, :, :], writes=["condf"])
    P.D("sp", "adab", C.adab[:], C.ada_bT[:, :, :], writes=["adab"])
    P.D("sp", "lng", C.lng[:], C.ln_gT[:, :, :], writes=["lng"])
    P.D("sp", "lnb", C.lnb[:], C.ln_bT[:, :, :], writes=["lnb"])
    P.I("dve", "tensor_copy", reads=["identf"], writes=["identb"], out=C.identb[:], in_=C.identf[:])
    P.I("dve", "memset", writes=["onesb"], ap=C.onesb[:], constant=1.0)
    P.I("act", "activation", reads=["condf"], writes=["scond"], out=C.scond[:], in_=C.condf[:], func=AF.Silu)

def xk(g, fc):
    return f"xT{g}_{fc}"

def input_transposes(P, C):
    with ExitStack() as es:
        st = [P.sb(f"xst{i}", [128, D], F32, es) for i in range(2)]
        for t in range(NT):
            b = t % 2
            P.D("sp", f"xst{b}", st[b][:], C.xin[t * 128:(t + 1) * 128, :], writes=[f"xst{b}"])
            for half in range(2):
                pb = C.ps[(t * 2 + half) % 4]; pk = f"ps{(t * 2 + half) % 4}"
                P.G("pe", [("transpose", dict(out=pb[:, i * 128:(i + 1) * 128], in_=st[b][:, (half * 4 + i) * 128:(half * 4 + i + 1) * 128],
                                              identity=C.identf[:])) for i in range(4)], reads=[f"xst{b}", "identf"], writes=[pk])
                eng = "act" if half == 0 else "dve"
                outap = C.xT[:, half * 4:half * 4 + 4, t * 128:(t + 1) * 128]
                inap = pb[:].rearrange("p (c t) -> p c t", c=4)
                if eng == "act":
                    P.I("act", "activation", reads=[pk], writes=[xk(t // 4, half * 4 + i) for i in range(4)], out=outap, in_=inap, func=AF.Copy)
                else:
                    P.I("dve", "tensor_copy", reads=[pk], writes=[xk(t // 4, half * 4 + i) for i in range(4)], out=outap, in_=inap)
        P.barrier()

def output_transposes(P, C):
    with ExitStack() as es:
        st = [P.sb(f"yst{i}", [128, D], F32, es) for i in range(2)]
        for t in range(NT):
            b = t % 2
            for half in range(2):
                pb = C.ps[(t * 2 + half) % 4]; pk = f"ps{(t * 2 + half) % 4}"
                P.G("pe", [("transpose", dict(out=pb[:, i * 128:(i + 1) * 128], in_=C.xT[:, half * 4 + i, t * 128:(t + 1) * 128],
                                              identity=C.identf[:])) for i in range(4)], reads=[xk(t // 4, half * 4 + i) for i in range(4)] + ["identf"], writes=[pk])
                if half == 0:
                    P.I("act", "activation", reads=[pk], writes=[f"yst{b}"], out=st[b][:, 0:512], in_=pb[:], func=AF.Copy)
                else:
                    P.I("dve", "tensor_copy", reads=[pk], writes=[f"yst{b}"], out=st[b][:, 512:1024], in_=pb[:])
            P.D("sp", f"yout{b}", C.y[t * 128:(t + 1) * 128, :], st[b][:], reads=[f"yst{b}"])
        P.barrier()

def ada_phase(P, C, l):
    cols = C.colsb[l % 2]; ck = f"cols{l % 2}"
    wv_all = C.ada_w[l].rearrange("(k p) n -> p k n", p=128)
    psA = C.ps[4]
    for piece in range(6):
        wv, wk = C.ring.load(wv_all[:, :, piece * 512:(piece + 1) * 512], "")
        for f4 in range(4):
            fc = piece * 4 + f4
            mmK(P, psA[:, fc * 2:fc * 2 + 2], [(wv[:, kc, f4 * 128:(f4 + 1) * 128], C.scond[:, kc, :]) for kc in range(8)],
                reads=[wk, "scond"], writes=["ps4"])
    P.I("dve", "tensor_tensor", reads=["ps4", "adab"], writes=["mod"], out=C.mod[:],
        in0=psA[:, 0:48].rearrange("p (f j) -> p f j", j=2), in1=C.adab[:, l, :].unsqueeze(2).to_broadcast([128, 24, 2]), op=ALU.add)
    P.I("dve", "tensor_copy", reads=["mod"], writes=[ck], out=cols[:, 0], in_=C.mod[:, 0:8, :])
    P.I("dve", "tensor_scalar_add", reads=["mod"], writes=[ck], out=cols[:, 1], in0=C.mod[:, 8:16, :], scalar1=1.0)
    P.I("dve", "tensor_scalar_mul", reads=["mod"], writes=[ck], out=cols[:, 2], in0=C.mod[:, 16:24, :], scalar1=1.0 / ALPHA)

def modulate_phase(P, C, l):
    cols = C.colsb[l % 2]; ck = f"cols{l % 2}"
    for j in range(2):
        for c in range(8):
            sl = slice(j * 1024, (j + 1) * 1024)
            rk = [xk(2 * j, c), xk(2 * j + 1, c), ck]
            if (c + j) % 2 == 0:
                P.I("dve", "tensor_scalar", reads=rk, writes=[f"hT{j}"], out=C.hT[:, c, sl], in0=C.xT[:, c, sl],
                    scalar1=cols[:, 1, c, j:j + 1], scalar2=cols[:, 0, c, j:j + 1], op0=ALU.mult, op1=ALU.add)
            else:
                P.I("act", "activation", reads=rk, writes=[f"hT{j}"], out=C.hT[:, c, sl], in_=C.xT[:, c, sl], func=AF.Identity,
                    scale=cols[:, 1, c, j:j + 1], bias=cols[:, 0, c, j:j + 1])

def load_wout(P, C, w_dram):
    C.wout_dram = w_dram

def wout_prefetch(P, C):
    wv = C.wout_dram.rearrange("(k p) n -> p k n", p=128)
    return [C.ring.load(wv[:, :, 0:512], ""), C.ring.load(wv[:, :, 512:1024], "")]

def wout_ln_phase(P, C, l, pre, next_ada=None):
    cols = C.colsb[l % 2]; ck = f"cols{l % 2}"
    with ExitStack() as es:
        zn = [P.sb(f"ln_zn{i}", [128, D], F32, es) for i in range(4)]
        st = [P.sb(f"ln_st{i}", [128, 12], F32, es) for i in range(2)]
        mv = [P.sb(f"ln_mv{i}", [128, 2], F32, es) for i in range(2)]
        rs = [P.sb(f"ln_rs{i}", [128, 2], F32, es) for i in range(2)]
        epsc = P.sb("ln_eps", [128, 1], F32, es)
        P.I("dve", "memset", writes=["ln_eps"], ap=epsc[:], constant=LN_EPS)
        yi = [0]; ti = [0]
        def zpass(g):
            j = g // 2; gs = slice(g * 512, (g + 1) * 512)
            for fc in range(8):
                wv, wk = pre[fc // 4]; f4 = fc % 4
                py = C.ps[6 + yi[0] % 2]; pyk = f"ps{6 + yi[0] % 2}"; yi[0] += 1
                mmK(P, py[:], [(wv[:, kc, f4 * 128:(f4 + 1) * 128], C.mT[:, kc, gs]) for kc in range(8)], reads=[wk, f"mT{g}"], writes=[pyk])
                P.I("dve", "scalar_tensor_tensor", reads=[pyk, xk(g, fc), ck], writes=[xk(g, fc)], out=C.xT[:, fc, gs], in0=py[:],
                    scalar=cols[:, 2, fc, j:j + 1], in1=C.xT[:, fc, gs], op0=ALU.mult, op1=ALU.add)
        def norm_tiles(g):
            for tl in range(4):
                tcols = slice(g * 512 + tl * 128, g * 512 + (tl + 1) * 128)
                i2 = ti[0] % 2; ti[0] += 1
                pb = [C.ps[2 * i2], C.ps[2 * i2 + 1]]; pbk = [f"ps{2 * i2}", f"ps{2 * i2 + 1}"]
                for half in range(2):
                    P.G("pe", [("transpose", dict(out=pb[half][:, i * 128:(i + 1) * 128], in_=C.xT[:, half * 4 + i, tcols], identity=C.identf[:]))
                               for i in range(4)], reads=[xk(g, half * 4 + i) for i in range(4)] + ["identf"], writes=[pbk[half]])
                    P.I("dve", "bn_stats", reads=[pbk[half]], writes=[f"ln_st{i2}"], out=st[i2][:, half * 6:(half + 1) * 6], in_=pb[half][:])
                P.I("dve", "bn_aggr", reads=[f"ln_st{i2}"], writes=[f"ln_mv{i2}"], out=mv[i2][:], in_=st[i2][:])
                P.I("act", "activation", reads=[f"ln_mv{i2}", "ln_eps"], writes=[f"ln_rs{i2}"], out=rs[i2][:, 0:1], in_=mv[i2][:, 1:2], func=AF.Sqrt,
                    bias=epsc[:, 0:1], scale=1.0)
                P.I("dve", "reciprocal", reads=[f"ln_rs{i2}"], writes=[f"ln_rs{i2}"], out=rs[i2][:, 0:1], in_=rs[i2][:, 0:1])
                P.I("dve", "scalar_tensor_tensor", reads=[f"ln_mv{i2}", f"ln_rs{i2}"], writes=[f"ln_rs{i2}"], out=rs[i2][:, 1:2], in0=mv[i2][:, 0:1],
                    scalar=-1.0, in1=rs[i2][:, 0:1], op0=ALU.mult, op1=ALU.mult)
                for half in range(2):
                    P.I("act", "activation", reads=[pbk[half], f"ln_rs{i2}"], writes=[f"ln_zn{tl}"], out=zn[tl][:, half * 512:(half + 1) * 512],
                        in_=pb[half][:], func=AF.Identity, scale=rs[i2][:, 0:1], bias=rs[i2][:, 1:2])
        def back(g):
            gs = slice(g * 512, (g + 1) * 512)
            for fc in range(8):
                bi = 4 + fc % 2
                P.G("pe", [("transpose", dict(out=C.ps[bi][:, tl * 128:(tl + 1) * 128], in_=zn[tl][:, fc * 128:(fc + 1) * 128], identity=C.identf[:]))
                           for tl in range(4)], reads=[f"ln_zn{tl}" for tl in range(4)] + ["identf"], writes=[f"ps{bi}"])
                if fc % 2 == 0:
                    P.I("act", "activation", reads=[f"ps{bi}", "lng", "lnb"], writes=[xk(g, fc)], out=C.xT[:, fc, gs], in_=C.ps[bi][:], func=AF.Identity,
                        scale=C.lng[:, l, fc:fc + 1], bias=C.lnb[:, l, fc:fc + 1])
                else:
                    P.I("dve", "tensor_scalar", reads=[f"ps{bi}", "lng", "lnb"], writes=[xk(g, fc)], out=C.xT[:, fc, gs], in0=C.ps[bi][:],
                        scalar1=C.lng[:, l, fc:fc + 1], scalar2=C.lnb[:, l, fc:fc + 1], op0=ALU.mult, op1=ALU.add)
        zpass(0)
        for g in range(4):
            if g + 1 < 4:
                zpass(g + 1)
            norm_tiles(g)
            if g == 3 and next_ada is not None:
                ada_phase(P, C, next_ada)
            back(g)
        P.barrier()
QS = 128.0 ** -0.5
CH = 64
NCH = 512 // CH
JT = 128 // CH

def _roundrobin(gens):
    gens = list(gens)
    while gens:
        nxt = []
        for g in gens:
            try:
                next(g); nxt.append(g)
            except StopIteration:
                pass
        gens = nxt

def hgrn_layer(P, C):
    with ExitStack() as es:
        lbl = P.sb("hg_lbl_s", [128, 16, 5], F32, es); lbm = P.sb("hg_lbm", [128, 16], F32, es)
        lb = P.sb("hg_lb", [128, 16], F32, es); oml = P.sb("hg_oml", [128, 16], F32, es)
        ng = P.sb("hg_ng_s", [128, 8], F32, es); eps6 = P.sb("hg_eps", [128, 1], F32, es)
        one = P.sb("hg_one", [128, 1], F32, es)
        mk = P.sb("hg_mk", [128, 2, 128], F32, es); rm = P.sb("hg_rm", [128, 2, 512], BF16, es)
        mstage = P.sb("hg_mst", [128, 2, 128], F32, es)
        P.D("sp", "hg_lbl", lbl[:], C.hg_lbl[:, :, :], writes=["hg_lbl"])
        P.D("sp", "hg_ng", ng[:], C.hg_ng[:, :], writes=["hg_ng"])
        P.D("sp", "hg_mk", mk[:], C.hg_masks[:, 0:2, :], writes=["hg_mk"])
        P.D("sp", "hg_mst", mstage[:], C.hg_masks[:, 2:4, :], writes=["hg_mst"])
        P.I("dve", "memset", writes=["hg_eps"], ap=eps6[:], constant=1e-6)
        P.I("dve", "memset", writes=["hg_one"], ap=one[:], constant=1.0)
        for d in range(2):
            for r in range(4):
                P.I("dve", "tensor_copy", reads=["hg_mst"], writes=["hg_rm"], out=rm[:, d, r * 128:(r + 1) * 128], in_=mstage[:, d, :])
        P.I("dve", "reduce_max", reads=["hg_lbl"], writes=["hg_lbm"], out=lbm[:], in_=lbl[:], axis=AX.X)
        P.I("dve", "tensor_tensor", reads=["hg_lbl", "hg_lbm"], writes=["hg_lbl"], out=lbl[:], in0=lbl[:],
            in1=lbm[:].unsqueeze(2).to_broadcast([128, 16, 5]), op=ALU.subtract)
        P.I("act", "activation", reads=["hg_lbl"], writes=["hg_lbl"], out=lbl[:], in_=lbl[:], func=AF.Exp)
        P.I("dve", "reduce_sum", reads=["hg_lbl"], writes=["hg_lbm"], out=lbm[:], in_=lbl[:], axis=AX.X)
        P.I("dve", "reciprocal", reads=["hg_lbm"], writes=["hg_lbm"], out=lbm[:], in_=lbm[:])
        P.I("dve", "tensor_tensor", reads=["hg_lbl", "hg_lbm"], writes=["hg_lb"], out=lb[:], in0=lbl[:, :, 0], in1=lbm[:], op=ALU.mult)
        P.I("dve", "tensor_scalar", reads=["hg_lb"], writes=["hg_oml"], out=oml[:], in0=lb[:], scalar1=-1.0, scalar2=1.0, op0=ALU.mult, op1=ALU.add)

        vtok = P.sb("hg_vtok", [128, 16, 128], BF16, es)
        qS = P.sb("hg_q", [128, 1024], F32, es)
        ob = P.sb("hg_o", [128, 1024], F32, es)
        sgB = P.sb("hg_sgb", [128, 1024], BF16, es)
        Sf = P.sb("hg_Sf", [128, 8, 128], F32, es); Sb = P.sb("hg_Sb", [128, 8, 128], BF16, es)
        attS = [P.sb(f"hg_att{i}", [128, 128], BF16, es) for i in range(2)]
        U = []
        for u in range(2):
            B_ = Ctx()
            B_.u = u
            B_.cum = P.sb(f"hg_cum{u}", [128, 512], F32, es); B_.kS = P.sb(f"hg_k{u}", [128, 512], F32, es)
            B_.A = P.sb(f"hg_A{u}", [128, 512], F32, es); B_.B = P.sb(f"hg_B{u}", [128, 512], F32, es)
            B_.qrel = P.sb(f"hg_qrel{u}", [128, 512], BF16, es); B_.krel = P.sb(f"hg_krel{u}", [128, 512], BF16, es)
            B_.qcum = P.sb(f"hg_qcum{u}", [128, 512], BF16, es); B_.kdT = P.sb(f"hg_kdT{u}", [128, 512], BF16, es)
            B_.kdtok = P.sb(f"hg_kdtok{u}", [128, 4, 128], BF16, es); B_.etot = P.sb(f"hg_etot{u}", [128, 16], F32, es)
            U.append(B_)
        wv_all = C.hg_w_in.rearrange("(k p) n -> p k n", p=128)
        P.I("dve", "memset", writes=["hg_qrel0"], ap=U[0].qrel[:], constant=0.0)
        P.I("pe", "matmul", reads=["hg_qrel0", "identb"], writes=["ps3"], out=C.ps[3][:], lhsT=C.identb[:], rhs=U[0].qrel[:], start=True, stop=True)
        cnt = {"w": 0, "att": 0, "x": 0, "y": 0, "u": 0}
        def wps():
            i = cnt["w"] % 2; cnt["w"] += 1
            return C.ps[i], f"ps{i}"
        def quarter(bank, name):
            i = cnt[name] % 4; cnt[name] += 1
            return C.ps[bank][:, i * 128:(i + 1) * 128], f"ps{bank}"
        def uslot():
            i = cnt["u"] % 2; cnt["u"] += 1
            return C.ps[6 + i][:, 0:128], f"ps{6 + i}"

        def prep(B_, wv, wk, hd, blk, d, sg_):
            u = B_.u
            K = lambda n: f"hg_{n}{u}"
            cs = slice(blk * 1024 + sg_ * 512, blk * 1024 + (sg_ + 1) * 512); ls = slice(sg_ * 512, (sg_ + 1) * 512)
            ridx = (CH // 2 - 1) if d == 0 else (CH // 2); tidx = (CH - 1) if d == 0 else 0
            cum, kS, Ab, Bb = B_.cum, B_.kS, B_.A, B_.B
            pt, pk = wps()
            mmK(P, pt[:], [(wv[:, kc, 1 + d, :], C.hT[:, kc, cs]) for kc in range(8)], reads=[wk, f"hT{blk}"], writes=[pk]); yield
            lbc = lb[:, d * 8 + hd:d * 8 + hd + 1]; omc = oml[:, d * 8 + hd:d * 8 + hd + 1]
            P.I("act", "activation", reads=[pk], writes=[K("cum")], out=cum[:], in_=pt[:], func=AF.Exp, scale=-1.0); yield
            P.I("act", "activation", reads=[K("cum"), "hg_lb", "hg_one"], writes=[K("A")], out=Ab[:], in_=cum[:], func=AF.Ln, scale=lbc, bias=one[:, 0:1]); yield
            P.I("act", "activation", reads=[K("cum"), "hg_one"], writes=[K("B")], out=Bb[:], in_=cum[:], func=AF.Ln, scale=1.0, bias=one[:, 0:1]); yield
            P.I("dve", "tensor_tensor", reads=[K("A"), K("B")], writes=[K("cum")], out=cum[:], in0=Ab[:], in1=Bb[:], op=ALU.subtract); yield
            P.I("dve", "tensor_tensor", reads=[pk, K("B")], writes=[K("B")], out=Bb[:], in0=Bb[:], in1=pt[:], op=ALU.add); yield
            P.I("act", "activation", reads=[K("B")], writes=[K("k")], out=kS[:], in_=Bb[:], func=AF.Exp, scale=-1.0); yield
            rv = slice(None) if d == 0 else slice(None, None, -1)
            P.I("dve", "tensor_tensor_scan", reads=[K("cum"), "hg_rm"], writes=[K("cum")], out=cum[:, rv], data0=rm[:, d, rv],
                data1=cum[:, rv], initial=0.0, op0=ALU.mult, op1=ALU.add); yield
            c3 = cum[:].rearrange("p (c t) -> p c t", t=CH)
            A3 = Ab[:].rearrange("p (c t) -> p c t", t=CH)
            P.I("dve", "tensor_tensor", reads=[K("cum")], writes=[K("A")], out=A3, in0=c3,
                in1=c3[:, :, ridx:ridx + 1].to_broadcast([128, NCH, CH]), op=ALU.subtract); yield
            P.I("act", "activation", reads=[K("cum")], writes=[K("B")], out=Bb[:], in_=cum[:], func=AF.Exp); yield
            P.I("dve", "scalar_tensor_tensor", reads=["hg_q", K("B")], writes=[K("qcum")], out=B_.qcum[:], in0=qS[:, ls], scalar=QS,
                in1=Bb[:], op0=ALU.mult, op1=ALU.mult); yield
            P.I("act", "activation", reads=[K("cum")], writes=[K("etot")], out=B_.etot[:, 0:NCH], in_=c3[:, :, tidx], func=AF.Exp); yield
            P.I("act", "activation", reads=[K("A")], writes=[K("B")], out=Bb[:], in_=Ab[:], func=AF.Exp); yield
            P.I("dve", "scalar_tensor_tensor", reads=["hg_q", K("B")], writes=[K("qrel")], out=B_.qrel[:], in0=qS[:, ls], scalar=QS,
                in1=Bb[:], op0=ALU.mult, op1=ALU.mult); yield
            P.I("act", "activation", reads=[K("A")], writes=[K("B")], out=Bb[:], in_=Ab[:], func=AF.Exp, scale=-1.0); yield
            P.I("dve", "scalar_tensor_tensor", reads=[K("k"), K("B"), "hg_oml"], writes=[K("krel")], out=B_.krel[:], in0=kS[:], scalar=omc, in1=Bb[:],
                op0=ALU.mult, op1=ALU.mult); yield
            P.I("dve", "tensor_tensor", reads=[K("cum")], writes=[K("A")], out=A3, in0=c3,
                in1=c3[:, :, tidx:tidx + 1].to_broadcast([128, NCH, CH]), op=ALU.subtract); yield
            P.I("act", "activation", reads=[K("A")], writes=[K("B")], out=Bb[:], in_=Ab[:], func=AF.Exp, scale=-1.0); yield
            P.I("dve", "scalar_tensor_tensor", reads=[K("k"), K("B"), "hg_oml"], writes=[K("kdT")], out=B_.kdT[:], in0=kS[:], scalar=omc, in1=Bb[:],
                op0=ALU.mult, op1=ALU.mult); yield
            psT = C.ps[2][:].bitcast(BF16)
            P.G("pe", [("transpose", dict(out=psT[:, i * 128:(i + 1) * 128], in_=B_.kdT[:, i * 128:(i + 1) * 128], identity=C.identb[:]))
                       for i in range(4)], reads=[K("kdT"), "identb"], writes=["ps2"])
            P.I("act", "activation", reads=["ps2"], writes=[K("kdtok")], out=B_.kdtok[:], in_=psT[:, 0:512].rearrange("p (a b) -> p a b", a=4),
                func=AF.Copy); yield

        for hd in range(8):
            if hd == 4:
                load_wout(P, C, C.hg_w_out)
            wv, wk = C.ring.load_multi([wv_all[:, :, j * 1024 + hd * 128:j * 1024 + (hd + 1) * 128] for j in range(4)])
            gv, gk = C.ring.load(wv_all[:, :, 4096 + hd * 128:4096 + (hd + 1) * 128], "")
            for t4 in range(4):
                vb = 2 if t4 % 2 == 0 else 4
                P.G("pe", [("matmul", dict(out=C.ps[vb][:, i * 128:(i + 1) * 128], lhsT=C.hT[:, kc, (t4 * 4 + i) * 128:(t4 * 4 + i + 1) * 128],
                                           rhs=wv[:, kc, 3, :], start=(kc == 0), stop=(kc == 7))) for i in range(4) for kc in range(8)],
                    reads=[wk, f"hT{t4 // 2}"], writes=[f"ps{vb}"])
                if t4 % 2 == 0:
                    P.I("act", "activation", reads=[f"ps{vb}"], writes=["hg_vtok"], out=vtok[:, t4 * 4:t4 * 4 + 4, :],
                        in_=C.ps[vb][:].rearrange("p (a b) -> p a b", a=4), func=AF.Copy)
                else:
                    P.I("dve", "tensor_copy", reads=[f"ps{vb}"], writes=["hg_vtok"], out=vtok[:, t4 * 4:t4 * 4 + 4, :],
                        in_=C.ps[vb][:].rearrange("p (a b) -> p a b", a=4))
            for blk in range(2):
                for g2 in range(2):
                    cs = slice(blk * 1024 + g2 * 512, blk * 1024 + (g2 + 1) * 512); ls = slice(g2 * 512, (g2 + 1) * 512)
                    pt, pk = wps()
                    mmK(P, pt[:], [(wv[:, kc, 0, :], C.hT[:, kc, cs]) for kc in range(8)], reads=[wk, f"hT{blk}"], writes=[pk])
                    P.I("act", "activation", reads=[pk], writes=["hg_q"], out=qS[:, ls], in_=pt[:], func=AF.Silu)
                for g2 in range(2):
                    cs = slice(blk * 1024 + g2 * 512, blk * 1024 + (g2 + 1) * 512); ls = slice(g2 * 512, (g2 + 1) * 512)
                    pt, pk = wps()
                    mmK(P, pt[:], [(gv[:, kc, :], C.hT[:, kc, cs]) for kc in range(8)], reads=[gk, f"hT{blk}"], writes=[pk])
                    P.I("act", "activation", reads=[pk], writes=["hg_sgb"], out=sgB[:, ls], in_=pt[:], func=AF.Silu)
                P.I("pool", "memset", writes=[f"hg_o{t_}" for t_ in range(8)], ap=ob[:], constant=0.0)
                if blk == 0:
                    P.I("pool", "memset", writes=[f"hg_Sf{c_}" for c_ in range(8)], ap=Sf[:], constant=0.0)
                    P.I("pool", "memset", writes=[f"hg_Sb{c_}" for c_ in range(8)], ap=Sb[:], constant=0.0)
                else:
                    for d in range(2):
                        P.D("sp", f"hg_s0{d}", Sf[:, d, :], C.hg_s0[d, hd], writes=[f"hg_Sf{d}"])
                        P.I("act", "activation", reads=[f"hg_Sf{d}"], writes=[f"hg_Sb{d}"], out=Sb[:, d, :], in_=Sf[:, d, :], func=AF.Copy)
                for step in range(2):
                    units = [(0, step, U[0]), (1, 1 - step, U[1])]
                    _roundrobin([prep(B_, wv, wk, hd, blk, d, sg_) for (d, sg_, B_) in units])
                    chains = []
                    for (d, sg_, B_) in units:
                        if blk == 0:
                            cl = [(d * 4 + 2 * sg_, [0, 1]), (d * 4 + 2 * sg_ + 1, [2, 3])]
                        else:
                            cl = [(d, [0, 1, 2, 3])]
                        for ch, tl in cl:
                            chains.append((d, sg_, B_, ch, tl if d == 0 else tl[::-1]))
                    npos = len(chains[0][4])
                    xy_banks = [4, 5, 0, 1]
                    for pos in range(npos):
                        info = []
                        for ci, (d, sg_, B_, ch, tl) in enumerate(chains):
                            u = B_.u
                            tloc = tl[pos]; gt = blk * 8 + sg_ * 4 + tloc; ts_ = slice(tloc * 128, (tloc + 1) * 128)
                            pa, pak = quarter(3, "att")
                            t0 = tloc * 128
                            blocks = []
                            for cb in range(JT):
                                b0 = cb * CH; hC = CH // 2
                                if d == 0:
                                    blocks += [(b0, hC, b0, CH), (b0 + hC, hC, b0 + hC, hC)]
                                else:
                                    blocks += [(b0 + hC, hC, b0, CH), (b0, hC, b0, hC)]
                            P.G("pe", [("matmul", dict(out=pa[s0:s0 + sn, q0:q0 + qn], lhsT=B_.krel[:, t0 + s0:t0 + s0 + sn], rhs=B_.qrel[:, t0 + q0:t0 + q0 + qn],
                                                       start=True, stop=True, tile_position=(0, s0))) for (s0, sn, q0, qn) in blocks],
                                reads=[f"hg_krel{u}", f"hg_qrel{u}"], writes=[pak])
                            ai = cnt["att"] % 2
                            P.I("dve", "tensor_tensor", reads=[pak, "hg_mk"], writes=[f"hg_att{ai}"], out=attS[ai][:], in0=pa, in1=mk[:, d, :], op=ALU.mult)
                            xb_ = xy_banks[ci]
                            px = C.ps[xb_][:, 0:128]; pxk = f"ps{xb_}"; py = px; pyk = pxk
                            P.I("pe", "matmul", reads=["hg_vtok", f"hg_att{ai}"], writes=[pxk], out=px, lhsT=vtok[:, gt, :], rhs=attS[ai][:], start=True, stop=False)
                            info.append((d, sg_, B_, ch, tloc, gt, px, pxk, py, pyk))
                        for jj in range(JT):
                            for (d, sg_, B_, ch, tloc, gt, px, pxk, py, pyk) in info:
                                u = B_.u
                                j = jj if d == 0 else JT - 1 - jj
                                qs_ = slice(tloc * 128 + j * CH, tloc * 128 + (j + 1) * CH)
                                P.I("pe", "matmul", reads=[f"hg_Sb{ch}", f"hg_qcum{u}"], writes=[pyk], out=py[:, j * CH:(j + 1) * CH], lhsT=Sb[:, ch, :],
                                    rhs=B_.qcum[:, qs_], start=False, stop=(jj == JT - 1))
                                pu, puk = uslot()
                                P.I("pe", "matmul", reads=[f"hg_kdtok{u}", "hg_vtok"], writes=[puk], out=pu, lhsT=B_.kdtok[j * CH:(j + 1) * CH, tloc, :],
                                    rhs=vtok[j * CH:(j + 1) * CH, gt, :], start=True, stop=True, tile_position=(j * CH, 0))
                                P.I("dve", "scalar_tensor_tensor", reads=[f"hg_Sf{ch}", puk, f"hg_etot{u}"], writes=[f"hg_Sf{ch}"], out=Sf[:, ch, :],
                                    in0=Sf[:, ch, :], scalar=B_.etot[:, tloc * JT + j:tloc * JT + j + 1], in1=pu, op0=ALU.mult, op1=ALU.add)
                                P.I("act", "activation", reads=[f"hg_Sf{ch}"], writes=[f"hg_Sb{ch}"], out=Sb[:, ch, :], in_=Sf[:, ch, :], func=AF.Copy)
                        for (d, sg_, B_, ch, tloc, gt, px, pxk, py, pyk) in info:
                            t8 = sg_ * 4 + tloc
                            os_ = slice(t8 * 128, (t8 + 1) * 128); ok_ = f"hg_o{t8}"
                            P.I("dve", "tensor_tensor", reads=[pxk, ok_], writes=[ok_], out=ob[:, os_], in0=ob[:, os_], in1=px, op=ALU.add)
                    if blk == 0:
                        for (d, sg_, B_, ch, tl) in chains:
                            P.D("sp", f"o_hg{ch}", C.o_hg[ch % 4, d, hd], Sf[:, ch, :], reads=[f"hg_Sf{ch}"])
                osq = U[0].qrel; rsb = U[0].B; sgb = U[0].A
                for g2 in range(2):
                    cs = slice(blk * 1024 + g2 * 512, blk * 1024 + (g2 + 1) * 512); ls = slice(g2 * 512, (g2 + 1) * 512)
                    okeys = [f"hg_o{g2 * 4 + t_}" for t_ in range(4)]
                    P.I("act", "activation", reads=okeys, writes=["hg_qrel0"], out=osq[:], in_=ob[:, ls], func=AF.Square)
                    pt, pk = wps()
                    P.I("pe", "matmul", reads=["hg_qrel0", "onesb"], writes=[pk], out=pt[:], lhsT=C.onesb[:], rhs=osq[:], start=True, stop=True)
                    P.I("act", "activation", reads=[pk, "hg_eps"], writes=["hg_B0"], out=rsb[:], in_=pt[:], func=AF.Ln, bias=eps6[:, 0:1], scale=1.0 / 128.0)
                    P.I("act", "activation", reads=["hg_B0"], writes=["hg_B0"], out=rsb[:], in_=rsb[:], func=AF.Exp, scale=-0.5)
                    P.I("dve", "tensor_tensor", reads=["hg_B0"] + okeys, writes=["hg_B0"], out=rsb[:], in0=rsb[:], in1=ob[:, ls], op=ALU.mult)
                    P.I("dve", "scalar_tensor_tensor", reads=["hg_B0", "hg_sgb", "hg_ng"], writes=[f"mT{blk * 2 + g2}"], out=C.mT[:, hd, cs], in0=rsb[:],
                        scalar=ng[:, hd:hd + 1], in1=sgB[:, ls], op0=ALU.mult, op1=ALU.mult)
def sconv_layer(P, C):
    with ExitStack() as es:
        cw = P.sb("sc_cw_s", [128, 3, 8], F32, es); cb = P.sb("sc_cb_s", [128, 8], F32, es)
        P.D("sp", "sc_cw", cw[:], C.sc_cw[:, :, :], writes=["sc_cw"])
        P.D("sp", "sc_cb", cb[:], C.sc_cb[:, :], writes=["sc_cb"])
        pb_ = [P.sb(f"sc_p{i}", [128, 1024], F32, es) for i in range(2)]
        zb_ = [P.sb(f"sc_z{i}", [128, 1024], F32, es) for i in range(2)]
        cgS = [P.sb(f"sc_cg{i}", [128, 512], F32, es) for i in range(2)]
        sgS = [P.sb(f"sc_sg{i}", [128, 512], F32, es) for i in range(2)]
        tS = [P.sb(f"sc_t{i}", [128, 512], F32, es) for i in range(2)]
        wv_all = C.sc_w_in.rearrange("(k p) n -> p k n", p=128)
        pi = 0
        for c in range(8):
            if c == 4:
                load_wout(P, C, C.sc_w_out)
            wv, wk = C.ring.load_multi([wv_all[:, :, j * 1024 + c * 128:j * 1024 + (c + 1) * 128] for j in range(4)])
            for blk in range(2):
                p = pb_[blk]; z = zb_[blk]; pk = f"sc_p{blk}"; zk = f"sc_z{blk}"
                for g2 in range(2):
                    cs = slice(blk * 1024 + g2 * 512, blk * 1024 + (g2 + 1) * 512); ls = slice(g2 * 512, (g2 + 1) * 512)
                    pa = pi % 4; pbk = (pi + 1) % 4; pi += 2
                    mmK(P, C.ps[pa][:], [(wv[:, kc, 1, :], C.hT[:, kc, cs]) for kc in range(8)], reads=[wk, f"hT{blk}"], writes=[f"ps{pa}"])
                    mmK(P, C.ps[pbk][:], [(wv[:, kc, 2, :], C.hT[:, kc, cs]) for kc in range(8)], reads=[wk, f"hT{blk}"], writes=[f"ps{pbk}"])
                    i2 = g2
                    P.I("act", "activation", reads=[f"ps{pa}"], writes=[f"sc_cg{i2}"], out=cgS[i2][:], in_=C.ps[pa][:], func=AF.Copy)
                    P.I("dve", "tensor_tensor", reads=[f"sc_cg{i2}", f"ps{pbk}"], writes=[pk], out=p[:, ls], in0=cgS[i2][:], in1=C.ps[pbk][:], op=ALU.mult)
                P.I("dve", "tensor_scalar", reads=[pk, "sc_cw", "sc_cb"], writes=[zk], out=z[:], in0=p[:], scalar1=cw[:, 1, c:c + 1],
                    scalar2=cb[:, c:c + 1], op0=ALU.mult, op1=ALU.add)
                if blk == 0:
                    z3 = z[:].rearrange("p (s t) -> p s t", s=4); p3 = p[:].rearrange("p (s t) -> p s t", s=4)
                    zlo, plo, zhi, phi = z3[:, :, 1:], p3[:, :, :-1], z3[:, :, :-1], p3[:, :, 1:]
                else:
                    zlo, plo, zhi, phi = z[:, 1:], p[:, :-1], z[:, :-1], p[:, 1:]
                P.I("dve", "scalar_tensor_tensor", reads=[pk, zk, "sc_cw"], writes=[zk], out=zlo, in0=plo, scalar=cw[:, 0, c:c + 1], in1=zlo,
                    op0=ALU.mult, op1=ALU.add)
                P.I("dve", "scalar_tensor_tensor", reads=[pk, zk, "sc_cw"], writes=[zk], out=zhi, in0=phi, scalar=cw[:, 2, c:c + 1], in1=zhi,
                    op0=ALU.mult, op1=ALU.add)
                for g2 in range(2):
                    cs = slice(blk * 1024 + g2 * 512, blk * 1024 + (g2 + 1) * 512); ls = slice(g2 * 512, (g2 + 1) * 512)
                    g = blk * 2 + g2
                    pa = pi % 4; pbk = (pi + 1) % 4; pi += 2
                    mmK(P, C.ps[pa][:], [(wv[:, kc, 0, :], C.hT[:, kc, cs]) for kc in range(8)], reads=[wk, f"hT{blk}"], writes=[f"ps{pa}"])
                    mmK(P, C.ps[pbk][:], [(wv[:, kc, 3, :], C.hT[:, kc, cs]) for kc in range(8)], reads=[wk, f"hT{blk}"], writes=[f"ps{pbk}"])
                    i2 = g2
                    P.I("act", "activation", reads=[f"ps{pbk}"], writes=[f"sc_sg{i2}"], out=sgS[i2][:], in_=C.ps[pbk][:], func=AF.Silu)
                    P.I("dve", "tensor_tensor", reads=[f"sc_sg{i2}", f"ps{pa}"], writes=[f"sc_t{i2}"], out=tS[i2][:], in0=sgS[i2][:], in1=C.ps[pa][:], op=ALU.mult)
                    P.I("dve", "tensor_tensor", reads=[f"sc_t{i2}", zk], writes=[f"mT{g}"], out=C.mT[:, c, cs], in0=tS[i2][:], in1=z[:, ls], op=ALU.mult)
def rglru_layer(P, C):
    with ExitStack() as es:
        cw = P.sb("rg_cw_s", [128, 4, 8], F32, es); cb = P.sb("rg_cb_s", [128, 8], F32, es)
        bg = P.sb("rg_bg_s", [128, 2, 4, 4], F32, es); lam = P.sb("rg_lam_s", [128, 2, 8], F32, es)
        clam = P.sb("rg_clam", [128, 2, 8], F32, es); h0 = P.sb("rg_h0_s", [128, 2, 8], F32, es)
        one = P.sb("rg_one", [128, 1], F32, es)
        rgst = P.sb("rg_state", [128, 8, 8], F32, es)
        P.D("sp", "rg_cw", cw[:], C.rg_cw[:, :, :], writes=["rg_cw"])
        P.D("sp", "rg_cb", cb[:], C.rg_cb[:, :], writes=["rg_cb"])
        P.D("sp", "rg_bg", bg[:], C.rg_bg[:, :, :, :], writes=["rg_bg"])
        P.D("sp", "rg_lam", lam[:], C.rg_lam[:, :, :], writes=["rg_lam"])
        P.D("sp", "rg_h0", h0[:], C.rg_h0[:, :, :], writes=["rg_h0"])
        P.I("dve", "memset", writes=["rg_one"], ap=one[:], constant=1.0)
        P.I("act", "activation", reads=["rg_lam"], writes=["rg_clam"], out=clam[:], in_=lam[:], func=AF.Exp, scale=-1.0)
        P.I("act", "activation", reads=["rg_clam", "rg_one"], writes=["rg_clam"], out=clam[:], in_=clam[:], func=AF.Ln, bias=one[:, 0:1], scale=1.0)
        P.I("dve", "tensor_scalar_mul", reads=["rg_clam"], writes=["rg_clam"], out=clam[:], in0=clam[:], scalar1=-4.0)
        half = P.sb("rg_half", [128, 1], F32, es)
        P.I("dve", "memset", writes=["rg_half"], ap=half[:], constant=0.5)
        P.I("dve", "tensor_scalar_mul", reads=["rg_bg"], writes=["rg_bg"], out=bg[:], in0=bg[:], scalar1=0.5)
        P.I("dve", "tensor_scalar_mul", reads=["rg_h0"], writes=["rg_h0"], out=h0[:], in0=h0[:], scalar1=2.0)
        uraw = P.sb("rg_uraw", [128, 1024], F32, es)
        uc = P.sb("rg_uc", [128, 2, 1024], F32, es); ucb = P.sb("rg_ucb", [128, 2, 1024], BF16, es)
        abufs = [P.sb(f"rg_a{i}", [128, 1024], F32, es) for i in range(2)]
        xbs = [[P.sb(f"rg_x{o}{i}", [128, 1024], F32, es) for i in range(2)] for o in range(2)]
        sqt = [P.sb(f"rg_sq{i}", [128, 512], F32, es) for i in range(4)]
        sgS = [P.sb(f"rg_sg{i}", [128, 512], F32, es) for i in range(2)]
        wv_all = C.rg_w_in.rearrange("(k p) n -> p k n", p=128)
        pi = [0]
        def nps():
            i = pi[0] % 6; pi[0] += 1
            return C.ps[i], f"ps{i}"
        def seg(ap2d, blk, lo, hi):
            if blk == 0:
                v = ap2d.rearrange("p (s t) -> p s t", s=4)
                return v[:, :, lo:256 + hi]
            return ap2d[:, lo:1024 + hi]
        for hh in range(4):
            if hh == 2:
                load_wout(P, C, C.rg_w_out)
            wv, wk = C.ring.load_multi([wv_all[:, :, j * 1024 + c * 128:j * 1024 + (c + 1) * 128] for j in range(2) for c in (2 * hh, 2 * hh + 1)])
            gv, gk = C.ring.load_multi([C.rg_w_gate[d, hh].rearrange("(k p) n -> p k n", p=128) for d in range(2)])
            for blk in range(2):
                for cc in range(2):
                    c = 2 * hh + cc
                    for g2 in range(2):
                        cs = slice(blk * 1024 + g2 * 512, blk * 1024 + (g2 + 1) * 512); ls = slice(g2 * 512, (g2 + 1) * 512)
                        pt, pk = nps()
                        mmK(P, pt[:], [(wv[:, kc, cc, :], C.hT[:, kc, cs]) for kc in range(8)], reads=[wk, f"hT{blk}"], writes=[pk])
                        P.I("act", "activation", reads=[pk], writes=["rg_uraw"], out=uraw[:, ls], in_=pt[:], func=AF.Copy)
                    ucc = uc[:, cc, :]
                    P.I("dve", "tensor_scalar", reads=["rg_uraw", "rg_cw", "rg_cb"], writes=["rg_uc"], out=ucc, in0=uraw[:],
                        scalar1=cw[:, 2, c:c + 1], scalar2=cb[:, c:c + 1], op0=ALU.mult, op1=ALU.add)
                    for (k, lo_o, hi_o, lo_i, hi_i) in ((0, 2, 0, 0, -2), (1, 1, 0, 0, -1), (3, 0, -1, 1, 0)):
                        o_ = seg(ucc, blk, lo_o, hi_o); i_ = seg(uraw[:], blk, lo_i, hi_i)
                        P.I("dve", "scalar_tensor_tensor", reads=["rg_uraw", "rg_uc", "rg_cw"], writes=["rg_uc"], out=o_, in0=i_,
                            scalar=cw[:, k, c:c + 1], in1=o_, op0=ALU.mult, op1=ALU.add)
                    P.I("pool", "tensor_copy", reads=["rg_uc"], writes=["rg_ucb"], out=ucb[:, cc, :], in_=ucc)
                for oc in range(2):
                    c = 2 * hh + oc
                    xb = xbs[oc]
                    for d in range(2):
                        xin = xb[d]; xk = f"rg_x{oc}{d}"; abuf = abufs[d]; ak = f"rg_a{d}"
                        for g2 in range(2):
                            ls = slice(g2 * 512, (g2 + 1) * 512)
                            pr, prk = nps(); pq, pqk = nps()
                            mmK(P, pr[:], [(gv[:, kc, d, oc * 128:(oc + 1) * 128], ucb[:, kc, ls]) for kc in range(2)], reads=[gk, "rg_ucb"], writes=[prk])
                            mmK(P, pq[:], [(gv[:, kc, d, 256 + oc * 128:256 + (oc + 1) * 128], ucb[:, kc, ls]) for kc in range(2)], reads=[gk, "rg_ucb"], writes=[pqk])
                            P.I("act", "activation", reads=[prk, "rg_bg"], writes=[ak], out=abuf[:, ls], in_=pr[:], func=AF.Tanh,
                                bias=bg[:, d, hh, oc:oc + 1], scale=0.5)
                            P.I("act", "activation", reads=[ak, "rg_clam"], writes=[ak], out=abuf[:, ls], in_=abuf[:, ls], func=AF.Exp,
                                scale=clam[:, d, c:c + 1], bias=clam[:, d, c:c + 1])
                            P.I("act", "activation", reads=[pqk, "rg_bg"], writes=[xk], out=xin[:, ls], in_=pq[:], func=AF.Tanh,
                                bias=bg[:, d, hh, 2 + oc:3 + oc], scale=0.5)
                            sq = sqt[d * 2 + g2]; sk = f"rg_sq{d * 2 + g2}"
                            P.I("act", "activation", reads=[ak], writes=[sk], out=sq[:], in_=abuf[:, ls], func=AF.Square)
                        for g2 in range(2):
                            ls = slice(g2 * 512, (g2 + 1) * 512)
                            sq = sqt[d * 2 + g2]; sk = f"rg_sq{d * 2 + g2}"
                            P.I("act", "activation", reads=[sk, "rg_one"], writes=[sk], out=sq[:], in_=sq[:], func=AF.Sqrt, bias=one[:, 0:1], scale=-1.0)
                            P.I("dve", "scalar_tensor_tensor", reads=[xk, sk], writes=[xk], out=xin[:, ls], in0=xin[:, ls], scalar=1.0, in1=sq[:],
                                op0=ALU.add, op1=ALU.mult)
                            P.I("dve", "tensor_tensor", reads=[xk, "rg_uc"], writes=[xk], out=xin[:, ls], in0=xin[:, ls], in1=uc[:, oc, ls], op=ALU.mult)
                        seqs = [(s * 256, 256) for s in range(4)] if blk == 0 else [(0, 1024)]
                        for (o0, L) in seqs:
                            sl = slice(o0, o0 + L) if d == 0 else slice(o0 + L - 1, (o0 - 1) if o0 > 0 else None, -1)
                            init = 0.0 if blk == 0 else h0[:, d, c:c + 1]
                            P.I("dve", "tensor_tensor_scan", reads=[ak, xk, "rg_h0"], writes=[xk], out=xin[:, sl], data0=abuf[:, sl],
                                data1=xin[:, sl], initial=init, op0=ALU.mult, op1=ALU.add)
                        if blk == 0:
                            x3 = xin[:].rearrange("p (s t) -> p s t", s=4)
                            src = x3[:, :, 255:256] if d == 0 else x3[:, :, 0:1]
                            dst = rgst[:, c, :].rearrange("p (s d) -> p s d", d=2)[:, :, d:d + 1]
                            P.I("dve", "tensor_scalar_mul", reads=[xk], writes=["rg_state"], out=dst, in0=src, scalar1=0.5)
                    P.I("dve", "tensor_tensor", reads=[f"rg_x{oc}0", f"rg_x{oc}1"], writes=[f"rg_x{oc}0"], out=xb[0][:], in0=xb[0][:], in1=xb[1][:], op=ALU.add)
                    for g2 in range(2):
                        cs = slice(blk * 1024 + g2 * 512, blk * 1024 + (g2 + 1) * 512); ls = slice(g2 * 512, (g2 + 1) * 512)
                        pt, pk = nps()
                        mmK(P, pt[:], [(wv[:, kc, 2 + oc, :], C.hT[:, kc, cs]) for kc in range(8)], reads=[wk, f"hT{blk}"], writes=[pk])
                        P.I("act", "activation", reads=[pk], writes=[f"rg_sg{g2}"], out=sgS[g2][:], in_=pt[:], func=AF.Tanh, scale=0.5)
                        P.I("dve", "scalar_tensor_tensor", reads=[f"rg_sg{g2}", pk], writes=[f"rg_sg{g2}"], out=sgS[g2][:], in0=sgS[g2][:], scalar=1.0, in1=pt[:],
                            op0=ALU.add, op1=ALU.mult)
                        P.I("dve", "scalar_tensor_tensor", reads=[f"rg_sg{g2}", f"rg_x{oc}0"], writes=[f"mT{blk * 2 + g2}"], out=C.mT[:, c, cs], in0=sgS[g2][:],
                            scalar=0.25, in1=xb[0][:, ls], op0=ALU.mult, op1=ALU.mult)
        srow = uraw[0:8, :]
        for half in range(2):
            pt, pk = nps()
            P.G("pe", [("transpose", dict(out=pt[0:8, i * 128:(i + 1) * 128], in_=rgst[:, half * 4 + i, :], identity=C.identf[:])) for i in range(4)],
                reads=["rg_state", "identf"], writes=[pk])
            P.I("dve", "tensor_copy", reads=[pk], writes=["rg_uraw"], out=srow[:, half * 512:(half + 1) * 512], in_=pt[0:8, :])
        P.D("sp", "o_rg", C.o_rg[:, :], srow, reads=["rg_uraw"])
SM_SCALE = 96.0 ** -0.5

def mla_layer(P, C):
    with ExitStack() as es:
        qn = P.sb("ml_qn", [128, 3], F32, es); kvn = P.sb("ml_kvn", [128, 2], F32, es)
        eps6 = P.sb("ml_eps", [128, 1], F32, es)
        rope = P.sb("ml_rope", [128, 2, 1024], F32, es)
        cqnT = P.sb("ml_cqnT", [128, 3, 1024], BF16, es)
        ckvnT = P.sb("ml_ckvnT", [128, 2, 1536], BF16, es)
        KK = P.sb("ml_KK", [128, 1536], BF16, es)
        KK2 = P.sb("ml_KK2", [128, 1536], BF16, es)
        P.D("sp", "ml_qn", qn[:], C.mla_qn[:, :], writes=["ml_qn"])
        P.D("sp", "ml_kvn", kvn[:], C.mla_kvn[:, :], writes=["ml_kvn"])
        P.D("sp", "ml_rope", rope[64:96], C.rope_cs[64:96, :, :], writes=["ml_rope"])
        P.I("dve", "memset", writes=["ml_eps"], ap=eps6[:], constant=1e-6)
        wv_all = C.mla_w_in.rearrange("(k p) n -> p k n", p=128)
        wqb_all = C.mla_w_qb.rearrange("(k p) n -> p k n", p=128)
        wqs_all = C.mla_w_qbsw.rearrange("(k p) n -> p k n", p=128)
        wkvb_all = C.mla_w_kvb.rearrange("(k p) n -> p k n", p=128)
        wks_all = C.mla_w_kpesw.rearrange("(k p) n -> p k n", p=128)
        load_wout(P, C, C.mla_w_out)
        for blk in range(2):
            nkeys = 1024 if blk == 0 else 1536
            with ExitStack() as esA:
                sq = [P.sb(f"ml_sq{i}", [128, 512], BF16, esA) for i in range(2)]
                rstd = P.sb("ml_rstd", [128, 512], F32, esA)
                ckvf = P.sb("ml_ckvf", [128, 2, 512], F32, esA)
                t1 = P.sb("ml_t1", [128, 256], F32, esA); t2 = P.sb("ml_t2", [128, 256], F32, esA)
                stg = [P.sb(f"ml_stg{i}", [128, 256], F32, esA) for i in range(2)]
                stk = P.sb("ml_stk", [128, 32], F32, esA); stkb = P.sb("ml_stkb", [128, 32], BF16, esA)
                (wcq,), wcqk = C.ring.load_parts([wv_all[:, :, 0:384]])
                (wkv, wks), wkvk = C.ring.load_parts([wv_all[:, :, 384:672], wks_all[:, :, :]])
                for gq in range(2):
                    cs = slice(blk * 1024 + gq * 512, blk * 1024 + (gq + 1) * 512); ls = slice(gq * 512, (gq + 1) * 512)
                    hk = f"hT{blk}"
                    for c in range(3):
                        mmK(P, C.ps[c][:], [(wcq[:, kc, c * 128:(c + 1) * 128], C.hT[:, kc, cs]) for kc in range(8)], reads=[wcqk, hk], writes=[f"ps{c}"])
                        P.I("act", "activation", reads=[f"ps{c}"], writes=[f"ml_sq{c % 2}"], out=sq[c % 2][:], in_=C.ps[c][:], func=AF.Square)
                        P.I("pe", "matmul", reads=[f"ml_sq{c % 2}", "onesb"], writes=["ps3"], out=C.ps[3][:], lhsT=C.onesb[:], rhs=sq[c % 2][:],
                            start=(c == 0), stop=(c == 2))
                    P.I("act", "activation", reads=["ps3", "ml_eps"], writes=["ml_rstd"], out=rstd[:], in_=C.ps[3][:], func=AF.Sqrt, bias=eps6[:, 0:1], scale=1.0 / 384.0)
                    P.I("dve", "reciprocal", reads=["ml_rstd"], writes=["ml_rstd"], out=rstd[:], in_=rstd[:])
                    for c in range(3):
                        P.I("dve", "scalar_tensor_tensor", reads=[f"ps{c}", "ml_rstd", "ml_qn"], writes=["ml_cqnT"], out=cqnT[:, c, ls], in0=C.ps[c][:],
                            scalar=qn[:, c:c + 1], in1=rstd[:], op0=ALU.mult, op1=ALU.mult)
                    for c in range(2):
                        mmK(P, C.ps[4 + c][:], [(wkv[:, kc, c * 128:(c + 1) * 128], C.hT[:, kc, cs]) for kc in range(8)], reads=[wkvk, hk], writes=[f"ps{4 + c}"])
                        P.I("act", "activation", reads=[f"ps{4 + c}"], writes=[f"ml_sq{c % 2}"], out=sq[c % 2][:], in_=C.ps[4 + c][:], func=AF.Square)
                        P.I("pe", "matmul", reads=[f"ml_sq{c % 2}", "onesb"], writes=["ps6"], out=C.ps[6][:], lhsT=C.onesb[:], rhs=sq[c % 2][:],
                            start=(c == 0), stop=(c == 1))
                    P.I("act", "activation", reads=["ps6", "ml_eps"], writes=["ml_rstd"], out=rstd[:], in_=C.ps[6][:], func=AF.Sqrt, bias=eps6[:, 0:1], scale=1.0 / 256.0)
                    P.I("dve", "reciprocal", reads=["ml_rstd"], writes=["ml_rstd"], out=rstd[:], in_=rstd[:])
                    for c in range(2):
                        P.I("dve", "scalar_tensor_tensor", reads=[f"ps{4 + c}", "ml_rstd", "ml_kvn"], writes=["ml_ckvf"], out=ckvf[:, c, :], in0=C.ps[4 + c][:],
                            scalar=kvn[:, c:c + 1], in1=rstd[:], op0=ALU.mult, op1=ALU.mult)
                    P.I("act", "activation", reads=["ml_ckvf"], writes=["ml_ckvnT"], out=ckvnT[:, :, ls], in_=ckvf[:], func=AF.Copy)
                    mmK(P, C.ps[7][64:96, :], [(wkv[:, kc, 256:288], C.hT[:, kc, cs]) for kc in range(8)], reads=[wkvk, hk], writes=["ps7"])
                    if blk == 0:
                        P.I("act", "activation", reads=["ps7"], writes=["ml_KKpe"], out=KK[64:96, ls], in_=C.ps[7][64:96, :], func=AF.Copy)
                    else:
                        mmK(P, C.ps[3][64:96, :], [(wks[:, kc, :], C.hT[:, kc, cs]) for kc in range(8)], reads=[wkvk, hk], writes=["ps3"])
                        for hq in range(2):
                            l2 = slice(hq * 256, (hq + 1) * 256); pos = slice(gq * 512 + hq * 256, gq * 512 + (hq + 1) * 256)
                            P.I("dve", "tensor_tensor", reads=["ps7", "ml_rope"], writes=["ml_t1"], out=t1[64:96, :], in0=C.ps[7][64:96, l2], in1=rope[64:96, 0, pos], op=ALU.mult)
                            P.I("dve", "tensor_tensor", reads=["ps3", "ml_rope"], writes=["ml_t2"], out=t2[64:96, :], in0=C.ps[3][64:96, l2], in1=rope[64:96, 1, pos], op=ALU.mult)
                            P.I("dve", "tensor_tensor", reads=["ml_t1", "ml_t2"], writes=["ml_KKpe"], out=KK[64:96, pos], in0=t1[64:96, :], in1=t2[64:96, :], op=ALU.add)
                    if blk == 0:
                        for tl in range(4):
                            gt = gq * 4 + tl; b = tl % 2
                            P.G("pe", [("transpose", dict(out=C.ps[2][:, c * 128:(c + 1) * 128], in_=ckvf[:, c, tl * 128:(tl + 1) * 128], identity=C.identf[:]))
                                       for c in range(2)], reads=["ml_ckvf", "identf"], writes=["ps2"])
                            P.I("dve", "tensor_copy", reads=["ps2"], writes=[f"ml_stg{b}"], out=stg[b][:], in_=C.ps[2][:, 0:256])
                            P.D("sp", f"o_ckv{b}", C.o_ckv[gt * 128:(gt + 1) * 128, :], stg[b][:], reads=[f"ml_stg{b}"])
                            mmK(P, C.ps[2][:, 256:288], [(C.hT[:, kc, gt * 128:(gt + 1) * 128], wkv[:, kc, 256:288]) for kc in range(8)], reads=[wkvk, hk], writes=["ps2"])
                            P.I("dve", "tensor_copy", reads=["ps2"], writes=["ml_stk"], out=stk[:], in_=C.ps[2][:, 256:288])
                            P.D("sp", "o_kpe", C.o_kpe[gt * 128:(gt + 1) * 128, :], stk[:], reads=["ml_stk"])
                if blk == 1:
                    for tl in range(4):
                        b = tl % 2
                        P.D("sp", f"ml_stg{b}", stg[b][:], C.mla_ckv_ctx[tl * 128:(tl + 1) * 128, :], writes=[f"ml_stg{b}"])
                        P.G("pe", [("transpose", dict(out=C.ps[2][:, c * 128:(c + 1) * 128], in_=stg[b][:, c * 128:(c + 1) * 128], identity=C.identf[:]))
                                   for c in range(2)], reads=[f"ml_stg{b}", "identf"], writes=["ps2"])
                        P.I("act", "activation", reads=["ps2"], writes=["ml_ckvnT"], out=ckvnT[:, :, 1024 + tl * 128:1024 + (tl + 1) * 128],
                            in_=C.ps[2][:, 0:256].rearrange("p (c t) -> p c t", c=2), func=AF.Copy)
                        P.D("sp", "ml_stk", stk[:], C.mla_kpe_ctx[tl * 128:(tl + 1) * 128, :], writes=["ml_stk"])
                        P.I("dve", "tensor_copy", reads=["ml_stk"], writes=["ml_stkb"], out=stkb[:], in_=stk[:])
                        psb = C.ps[2][:].bitcast(BF16)
                        P.I("pe", "transpose", reads=["ml_stkb", "identb"], writes=["ps2"], out=psb[64:96, 512:640], in_=stkb[:], identity=C.identb[:])
                        P.I("dve", "tensor_copy", reads=["ps2"], writes=["ml_KKpe"], out=KK[64:96, 1024 + tl * 128:1024 + (tl + 1) * 128], in_=psb[64:96, 512:640])
                P.I("dve", "tensor_copy", reads=["ml_KKpe"], writes=["ml_KKpe2"], out=KK2[64:96, 0:nkeys], in_=KK[64:96, 0:nkeys])
                P.barrier()
            with ExitStack() as esB:
                KKs = [KK, KK2]
                Vh = [P.sb(f"ml_Vh{i}", [128, 12, 65], BF16, esB) for i in range(2)]
                QQ = [P.sb(f"ml_QQ{i}", [128, 1024], BF16, esB) for i in range(2)]
                PT = [P.sb(f"ml_PT{i}", [128, 512], BF16, esB) for i in range(2)]
                sgT = [P.sb(f"ml_sgT{i}", [128, 1024], BF16, esB) for i in range(2)]
                on2 = P.sb("ml_on2", [128, 8, 128], BF16, esB)
                rcp = P.sb("ml_rcp", [128, 4], F32, esB)
                u1 = P.sb("ml_u1", [128, 256], F32, esB); u2 = P.sb("ml_u2", [128, 256], F32, esB)
                tg = P.sb("ml_tg", [128, 512], F32, esB)
                for i in range(2):
                    P.I("dve", "memset", writes=[f"ml_Vh{i}"], ap=Vh[i][:, :, 64:65], constant=1.0)
                nkt = nkeys // 128
                units = [(slice(s_ * 256, (s_ + 1) * 256), [2 * s_, 2 * s_ + 1]) for s_ in range(4)] if blk == 0 else \
                        [(slice(g_ * 512, (g_ + 1) * 512), list(range(12))) for g_ in range(2)]
                sc_i = [0]
                hk = f"hT{blk}"
                W = {}

                def proj(h):
                    hp, hh = h // 2, h % 2; bsel = h % 2
                    if hh == 0:
                        W[hp] = C.ring.load_parts([wqb_all[:, :, hp * 192:(hp + 1) * 192], wqs_all[:, :, hp * 64:(hp + 1) * 64],
                                                   wkvb_all[:, :, hp * 256:(hp + 1) * 256], wv_all[:, :, 672 + hp * 128:672 + (hp + 1) * 128]])
                    (wqb, wqs, wkb, wg), wk = W[hp]
                    if hh == 0:
                        for g2 in range(2):
                            cs = slice(blk * 1024 + g2 * 512, blk * 1024 + (g2 + 1) * 512); ls = slice(g2 * 512, (g2 + 1) * 512)
                            mmK(P, C.ps[6][:], [(wg[:, kc, :], C.hT[:, kc, cs]) for kc in range(8)], reads=[wk, hk], writes=["ps6"])
                            P.I("act", "activation", reads=["ps6"], writes=["ml_tg"], out=tg[:], in_=C.ps[6][:], func=AF.Tanh, scale=0.5)
                            P.I("dve", "scalar_tensor_tensor", reads=["ml_tg", "ps6"], writes=[f"ml_sgT{hp % 2}"], out=sgT[hp % 2][:, ls], in0=tg[:], scalar=1.0,
                                in1=C.ps[6][:], op0=ALU.add, op1=ALU.mult)
                            yield
                    KKb = KKs[bsel]
                    for kg in range(nkeys // 512):
                        ks = slice(kg * 512, (kg + 1) * 512)
                        mmK(P, C.ps[6][0:64, :], [(wkb[:, kc, hh * 128:hh * 128 + 64], ckvnT[:, kc, ks]) for kc in range(2)], reads=[wk, "ml_ckvnT"], writes=["ps6"])
                        P.I("dve", "tensor_copy", reads=["ps6"], writes=[f"ml_KKn{bsel}"], out=KKb[0:64, ks], in_=C.ps[6][0:64, :])
                        yield
                    for k8 in range((nkt + 7) // 8):
                        n8 = min(8, nkt - k8 * 8)
                        P.G("pe", [("matmul", dict(out=C.ps[7][:, j * 64:(j + 1) * 64], lhsT=ckvnT[:, kc, (k8 * 8 + j) * 128:(k8 * 8 + j + 1) * 128],
                                                   rhs=wkb[:, kc, hh * 128 + 64:hh * 128 + 128], start=(kc == 0), stop=(kc == 1)))
                                   for j in range(n8) for kc in range(2)], reads=[wk, "ml_ckvnT"], writes=["ps7"])
                        P.I("dve", "tensor_copy", reads=["ps7"], writes=[f"ml_Vh{bsel}"], out=Vh[bsel][:, k8 * 8:k8 * 8 + n8, 0:64],
                            in_=C.ps[7][:, 0:n8 * 64].rearrange("p (a b) -> p a b", a=n8))
                        yield
                    Qb = QQ[bsel]
                    for g2 in range(2):
                        ls = slice(g2 * 512, (g2 + 1) * 512)
                        mmK(P, C.ps[6][0:64, :], [(wqb[:, kc, hh * 96:hh * 96 + 64], cqnT[:, kc, ls]) for kc in range(3)], reads=[wk, "ml_cqnT"], writes=["ps6"])
                        P.I("dve", "tensor_copy", reads=["ps6"], writes=[f"ml_QQn{bsel}"], out=Qb[0:64, ls], in_=C.ps[6][0:64, :])
                        yield
                        mmK(P, C.ps[7][64:96, :], [(wqb[:, kc, hh * 96 + 64:hh * 96 + 96], cqnT[:, kc, ls]) for kc in range(3)], reads=[wk, "ml_cqnT"], writes=["ps7"])
                        if blk == 0:
                            P.I("dve", "tensor_copy", reads=["ps7"], writes=[f"ml_QQr{bsel}"], out=Qb[64:96, ls], in_=C.ps[7][64:96, :])
                        else:
                            mmK(P, C.ps[6][64:96, :], [(wqs[:, kc, hh * 32:(hh + 1) * 32], cqnT[:, kc, ls]) for kc in range(3)], reads=[wk, "ml_cqnT"], writes=["ps6"])
                            for hq in range(2):
                                l2 = slice(hq * 256, (hq + 1) * 256); pos = slice(g2 * 512 + hq * 256, g2 * 512 + (hq + 1) * 256)
                                P.I("dve", "tensor_tensor", reads=["ps7", "ml_rope"], writes=["ml_u1"], out=u1[64:96, :], in0=C.ps[7][64:96, l2], in1=rope[64:96, 0, pos], op=ALU.mult)
                                P.I("dve", "tensor_tensor", reads=["ps6", "ml_rope"], writes=["ml_u2"], out=u2[64:96, :], in0=C.ps[6][64:96, l2], in1=rope[64:96, 1, pos], op=ALU.mult)
                                P.I("dve", "tensor_tensor", reads=["ml_u1", "ml_u2"], writes=[f"ml_QQr{bsel}"], out=Qb[64:96, pos], in0=u1[64:96, :], in1=u2[64:96, :], op=ALU.add)
                        yield

                def attn(h):
                    hh = h % 2; bsel = h % 2
                    KKb = KKs[bsel]; Qb = QQ[bsel]; Vb = Vh[bsel]
                    kkeys = [f"ml_KKn{bsel}", "ml_KKpe" if bsel == 0 else "ml_KKpe2", f"ml_QQn{bsel}", f"ml_QQr{bsel}"]
                    for (qsl, kts) in units:
                        nq = qsl.stop - qsl.start; nqt = nq // 128; qt0 = qsl.start // 128
                        def qk(kt):
                            sb_ = sc_i[0] % 2; sc_i[0] += 1
                            P.I("pe", "matmul", reads=kkeys, writes=[f"ps{sb_}"], out=C.ps[sb_][:, 0:nq],
                                lhsT=KKb[0:96, kt * 128:(kt + 1) * 128], rhs=Qb[0:96, qsl], start=True, stop=True)
                            P.I("act", "activation", reads=[f"ps{sb_}"], writes=[f"ml_PT{sb_}"], out=PT[sb_][:, 0:nq], in_=C.ps[sb_][:, 0:nq], func=AF.Exp, scale=SM_SCALE)
                            return sb_
                        pend = qk(kts[0])
                        for ki, kt in enumerate(kts):
                            pi_ = pend
                            if ki + 1 < len(kts):
                                pend = qk(kts[ki + 1])
                            for qt in range(nqt):
                                P.I("pe", "matmul", reads=[f"ml_PT{pi_}", f"ml_Vh{bsel}"], writes=[f"ps{2 + qt}"], out=C.ps[2 + qt][:, 0:65],
                                    lhsT=PT[pi_][:, qt * 128:(qt + 1) * 128], rhs=Vb[:, kt, :], start=(ki == 0), stop=(ki == len(kts) - 1))
                            yield
                        for qt in range(nqt):
                            P.I("dve", "reciprocal", reads=[f"ps{2 + qt}"], writes=["ml_rcp"], out=rcp[:, qt:qt + 1], in_=C.ps[2 + qt][:, 64:65])
                            P.I("dve", "tensor_scalar", reads=[f"ps{2 + qt}", "ml_rcp"], writes=["ml_on2"], out=on2[:, qt0 + qt, hh * 64:(hh + 1) * 64],
                                in0=C.ps[2 + qt][:, 0:64], scalar1=rcp[:, qt:qt + 1], scalar2=None, op0=ALU.mult)
                        yield

                def finish_pair(hp):
                    psb = C.ps[7][:].bitcast(BF16)
                    for g2 in range(2):
                        cs = slice(blk * 1024 + g2 * 512, blk * 1024 + (g2 + 1) * 512); ls = slice(g2 * 512, (g2 + 1) * 512)
                        P.G("pe", [("transpose", dict(out=psb[:, i * 128:(i + 1) * 128], in_=on2[:, g2 * 4 + i, :], identity=C.identb[:])) for i in range(4)],
                            reads=["ml_on2", "identb"], writes=["ps7"])
                        P.I("dve", "scalar_tensor_tensor", reads=["ps7", f"ml_sgT{hp % 2}"], writes=[f"mT{blk * 2 + g2}"], out=C.mT[:, hp, cs], in0=psb[:, 0:512],
                            scalar=0.5, in1=sgT[hp % 2][:, ls], op0=ALU.mult, op1=ALU.mult)

                _roundrobin([proj(0)])
                for h in range(16):
                    gens = [attn(h)]
                    if h + 1 < 16:
                        gens.append(proj(h + 1))
                    _roundrobin(gens)
                    if h % 2 == 1:
                        finish_pair(h // 2)
                P.barrier()
LAYERS = (0, 1, 2, 3)
_CACHE = {}

def build_nc(layers):
    nc = bass.Bass("TRN2", target_bir_lowering=False)
    C = Ctx()
    declare_io(nc, C)
    with ExitStack() as es:
        P = Prog(nc, es)
        setup_persistent(P, C)
        ada_phase(P, C, layers[0])
        input_transposes(P, C)
        for li, l in enumerate(layers):
            modulate_phase(P, C, l)
            [hgrn_layer, sconv_layer, rglru_layer, mla_layer][l % 4](P, C)
            pre = wout_prefetch(P, C)
            P.barrier()
            wout_ln_phase(P, C, l, pre, next_ada=(layers[li + 1] if li + 1 < len(layers) else None))
        output_transposes(P, C)
        outs = [n for n in P.dall if n.startswith(("yout", "o_hg", "o_rg", "o_ckv", "o_kpe"))]
        P.final_wait("sp", outs)
        with nc.Block() as block:
            P.emit(block)
        C.counts = dict(P.cnt); print('instr counts', C.counts, 'nsem', len(P.esems) + sum(len(v) for v in P.dall.values()))
    return nc, C

def colT(v, nch):
    return np.ascontiguousarray(np.asarray(v, np.float32).reshape(nch, 128).T)

def make_in_maps(I):
    f = lambda a: np.ascontiguousarray(np.asarray(a, np.float32))
    half = 8; r = np.arange(1024) // 64; cpos = np.arange(1024) % 64
    inv = (10000.0 ** (-np.arange(0, 16, 2, dtype=np.float32) / 16)).astype(np.float32)
    cs = np.zeros((128, 2, 1024), np.float32)
    for part, pos in ((0, r), (1, cpos)):
        ang = pos[None, :].astype(np.float32) * inv[:, None]
        co, si = np.cos(ang), np.sin(ang)
        b = 64 + part * 16
        cs[b:b + 8, 0] = co; cs[b + 8:b + 16, 0] = co
        cs[b:b + 8, 1] = -si; cs[b + 8:b + 16, 1] = si
    s_ = np.arange(128)[:, None]; t_ = np.arange(128)[None, :]
    same = (s_ // 64) == (t_ // 64)
    masks = np.zeros((128, 4, 128), np.float32)
    masks[:, 0] = same & (s_ <= t_); masks[:, 1] = same & (s_ >= t_)
    masks[:, 2] = np.tile((np.arange(128) % 64 != 0).astype(np.float32), (128, 1))
    masks[:, 3] = np.tile((np.arange(128) % 64 != 63).astype(np.float32), (128, 1))
    swap = np.arange(32).reshape(2, 2, 8)[:, ::-1, :].reshape(-1)
    wqb = f(I['mla_w_qb'][0])
    qb_r = wqb.reshape(384, 16, 96)[:, :, 64:]
    shared = {
        'ada_w': f(I['ada_w']), 'ada_bT': np.ascontiguousarray(f(I['ada_b']).reshape(4, 24, 128).transpose(2, 0, 1)),
        'ln_gT': np.ascontiguousarray(f(I['ln_g']).reshape(4, 8, 128).transpose(2, 0, 1)),
        'ln_bT': np.ascontiguousarray(f(I['ln_b']).reshape(4, 8, 128).transpose(2, 0, 1)),
        'ident': np.eye(128, dtype=np.float32),
        'sc_w_in': f(I['sc_w_in'][0]), 'sc_cw': np.ascontiguousarray(f(I['sc_conv_w'][0]).reshape(3, 8, 128).transpose(2, 0, 1)),
        'sc_cb': colT(I['sc_conv_b'][0], 8), 'sc_w_out': f(I['sc_w_out'][0]),
        'rg_w_in': f(I['rg_w_in'][0]), 'rg_cw': np.ascontiguousarray(f(I['rg_conv_w'][0]).reshape(4, 8, 128).transpose(2, 0, 1)),
        'rg_cb': colT(I['rg_conv_b'][0], 8), 'rg_w_gate': f(I['rg_w_gate'][0]),
        'rg_bg': np.ascontiguousarray(f(I['rg_b_gate'][0]).reshape(2, 4, 4, 128).transpose(3, 0, 1, 2)),
        'rg_lam': np.ascontiguousarray(f(I['rg_lambda'][0]).reshape(2, 8, 128).transpose(2, 0, 1)),
        'rg_w_out': f(I['rg_w_out'][0]),
        'hg_w_in': f(I['hg_w_in'][0]),
        'hg_lbl': np.ascontiguousarray(f(I['hg_lb_logits']).reshape(2, 5, 8, 128).transpose(3, 0, 2, 1).reshape(128, 16, 5)),
        'hg_ng': colT(I['hg_norm_g'][0], 8), 'hg_w_out': f(I['hg_w_out'][0]), 'hg_masks': masks,
        'mla_w_in': f(I['mla_w_in'][0]), 'mla_qn': colT(I['mla_q_norm'][0], 3), 'mla_kvn': colT(I['mla_kv_norm'][0], 2),
        'mla_w_qb': wqb, 'mla_w_qbsw': np.ascontiguousarray(qb_r[:, :, swap].reshape(384, 512)),
        'mla_w_kpesw': np.ascontiguousarray(f(I['mla_w_in'][0])[:, 640:672][:, swap]),
        'mla_w_kvb': f(I['mla_w_kvb'][0]), 'mla_w_out': f(I['mla_w_out'][0]), 'rope_cs': cs,
    }
    maps = []
    xp = f(I['x_prompt']); xs = f(I['x_sample'])
    for cid in range(8):
        k = cid // 2
        m = dict(shared)
        m['xin'] = np.ascontiguousarray(np.concatenate([xp[4 * cid:4 * cid + 4].reshape(1024, D), xs[k]], 0))
        cond = np.stack([f(I['c_ctx']), f(I['c'])[k]], 1)
        m['condT'] = np.ascontiguousarray(cond.reshape(8, 128, 2).transpose(1, 0, 2))
        m['rg_h0'] = np.ascontiguousarray(f(I['state_rglru'])[k, 0].reshape(2, 8, 128).transpose(2, 0, 1))
        m['hg_s0'] = f(I['state_hgrn'])[k, 0]
        m['mla_ckv_ctx'] = f(I['cache_mla_ckv'])[k, 0]; m['mla_kpe_ctx'] = f(I['cache_mla_kpe'])[k, 0]
        maps.append(m)
    return maps

def run_layers(I, layers, trace=False):
    key = tuple(layers)
    if key not in _CACHE:
        _CACHE[key] = build_nc(layers)
    nc, C = _CACHE[key]
    maps = make_in_maps(I)
    res = run_bass_kernel_spmd(nc, maps, core_ids=list(range(8)), trace=trace)
    R = res.results
    y_prompt = np.concatenate([R[c]['y'][:1024].reshape(4, 256, D) for c in range(8)], 0)
    y_sample = np.stack([R[2 * k]['y'][1024:] for k in range(4)], 0)
    o_hg = np.concatenate([R[c]['o_hg'] for c in range(8)], 0)[:, None]
    o_rg = np.concatenate([R[c]['o_rg'].reshape(4, 2, D) for c in range(8)], 0)[:, None]
    o_ckv = np.concatenate([R[c]['o_ckv'].reshape(4, 256, 256) for c in range(8)], 0)[:, None]
    o_kpe = np.concatenate([R[c]['o_kpe'].reshape(4, 256, 32) for c in range(8)], 0)[:, None]
    outs = tuple(np.ascontiguousarray(a, dtype=np.float32) for a in (y_prompt, y_sample, o_hg, o_rg, o_ckv, o_kpe))
    return outs, res

def kernel(**inputs):
    outs, _ = run_layers(inputs, LAYERS)
    return outs
```

```python
import numpy as np
import concourse.bass as bass
import concourse.mybir as mybir
from concourse.bass_utils import run_bass_kernel_spmd
from contextlib import ExitStack
import numpy as np
import concourse.bass as bass
import concourse.mybir as mybir
from concourse.bass_utils import run_bass_kernel_spmd
from contextlib import ExitStack
F32 = mybir.dt.float32; BF16 = mybir.dt.bfloat16
AF = mybir.ActivationFunctionType; ALU = mybir.AluOpType
AX = mybir.AxisListType

D = 1024; T = 2048; NT = 16; ALPHA = 8.0 ** 0.25
LN_EPS = 1e-5 / (ALPHA * ALPHA)

class Prog:
    ENGS = ("pe", "act", "dve", "pool", "sp")
    SEM_M = 1000
    DSEM_MAX = 1600
    def __init__(self, nc, es):
        self.nc = nc; self.es = es
        self.q = {e: [] for e in self.ENGS}
        self.cnt = {e: 0 for e in self.ENGS}
        self.esems = {}
        self.seen = {e: {} for e in self.ENGS}
        self.lastw = {}; self.readers = {}
        self.dsems = {}
        self.dall = {}
        self.semh = {}
    def sb(self, name, shape, dt, es=None):
        self._names = getattr(self, "_names", {})
        n = self._names.get(name, 0); self._names[name] = n + 1
        if n: name = f"{name}__{n}"
        return (es or self.es).enter_context(self.nc.sbuf_tensor(name, list(shape), dt))
    def ps(self, name, shape, dt):
        return self.es.enter_context(self.nc.psum_tensor(name, list(shape), dt))
    def esem(self, eng, epoch):
        k = (eng, epoch)
        if k not in self.esems:
            self.esems[k] = self.es.enter_context(self.nc.semaphore(f"s_{eng}_{epoch}"))
        return self.esems[k]
    def dsem(self, name):
        d = self.dsems.get(name)
        if d is None or d[1] + 16 > self.DSEM_MAX:
            ep = 0 if d is None else d[2] + 1
            h = self.es.enter_context(self.nc.semaphore(f"d_{name}_{ep}"))
            d = [h, 0, ep]
            self.dsems[name] = d
            self.semh[f"d_{name}#{ep}"] = h
            self.dall.setdefault(name, []).append(d)
        return d
    def _handle(self, sk, val):
        if sk in self.ENGS:
            ep = (val - 1) // self.SEM_M
            return (self.esem(sk, ep), val - ep * self.SEM_M)
        return (self.semh[sk], val)
    def _need(self, eng, waits, dep):
        if dep is None: return
        sk, val = dep
        if sk == "pe" and eng == "pe": return
        if self.seen[eng].get(sk, 0) >= val: return
        waits[sk] = max(waits.get(sk, 0), val)
    def _deps(self, eng, reads, writes):
        waits = {}
        for k in reads: self._need(eng, waits, self.lastw.get(k))
        for k in writes:
            self._need(eng, waits, self.lastw.get(k))
            for sk, v in self.readers.get(k, {}).items(): self._need(eng, waits, (sk, v))
        for sk, v in waits.items(): self.seen[eng][sk] = v
        return [self._handle(sk, v) for sk, v in waits.items()]
    def _mark(self, dep, reads, writes):
        for k in writes:
            self.lastw[k] = dep; self.readers[k] = {}
        for k in reads:
            self.readers.setdefault(k, {})[dep[0]] = dep[1]
    def op(self, eng, fns, reads=(), writes=()):
        writes = list(writes) + [k for k in reads if k.startswith("ps") and k not in writes]
        waits = self._deps(eng, reads, writes)
        self.cnt[eng] += 1
        idx = self.cnt[eng]
        h, _ = self._handle(eng, idx)
        self.q[eng].append((fns, waits, (h, 1)))
        self._mark((eng, idx), reads, writes)
    @staticmethod
    def _mk(method, kw):
        def fn(e):
            return getattr(e, method)(**kw)
        return fn
    def I(self, eng, method, reads=(), writes=(), **kw):
        self.op(eng, [self._mk(method, kw)], reads, writes)
    def G(self, eng, items, reads=(), writes=()):
        self.op(eng, [self._mk(m, kw) for (m, kw) in items], reads, writes)
    def D(self, queue, semname, out, in_, reads=(), writes=(), **kw):
        waits = self._deps(queue, reads, writes)
        d = self.dsem(semname)
        d[1] += 16
        self.q[queue].append(([self._mk("dma_start", dict(out=out, in_=in_, **kw))], waits, (d[0], 16)))
        self._mark((f"d_{semname}#{d[2]}", d[1]), reads, writes)
    def barrier(self, engs=("pe", "act", "dve", "pool", "sp")):
        targets = [(e, self.cnt[e]) for e in ("pe", "act", "dve", "pool") if self.cnt[e] > 0]
        for n, lst in self.dall.items():
            for d in lst:
                if d[1] > 0: targets.append((f"d_{n}#{d[2]}", d[1]))
        for e in engs:
            waits = {}
            for dep in targets:
                if dep[0] == e and e == "pe": continue
                if self.seen[e].get(dep[0], 0) >= dep[1]: continue
                waits[dep[0]] = dep[1]; self.seen[e][dep[0]] = dep[1]
            if waits:
                self.q[e].append((None, [self._handle(sk, v) for sk, v in waits.items()], None))
    def final_wait(self, queue, semnames):
        for n in semnames:
            for d in self.dall[n]:
                self.q[queue].append((None, [(d[0], d[1])], None))
    def emit(self, block):
        def run(e, lst):
            for fns, waits, inc in lst:
                for (h, v) in waits: e.wait_ge(h, v)
                if fns is None: continue
                for i, fn in enumerate(fns):
                    ins = fn(e)
                    if i == len(fns) - 1 and inc is not None: ins.then_inc(inc[0], inc[1])
        @block.tensor
        def _(e): run(e, self.q["pe"])
        @block.scalar
        def _(e): run(e, self.q["act"])
        @block.vector
        def _(e): run(e, self.q["dve"])
        @block.gpsimd
        def _(e): run(e, self.q["pool"])
        @block.sync
        def _(e): run(e, self.q["sp"])


class Ctx:
    pass

def mmK(P, out, pairs, reads, writes):
    n = len(pairs)
    P.G("pe", [("matmul", dict(out=out, lhsT=a, rhs=b, start=(i == 0), stop=(i == n - 1))) for i, (a, b) in enumerate(pairs)],
        reads=reads, writes=writes)

class WRing:
    def __init__(self, P, nslot=3, elems=4096):
        self.P = P; self.n = nslot; self.i = 0
        self.bufs = [P.sb(f"wr{i}", [128, elems], BF16) for i in range(nslot)]
    def load(self, dram_ap, shape_str, **dims):
        s = self.i % self.n; self.i += 1
        shp = dram_ap.shape
        n = 1
        for v in shp[1:]: n *= v
        view = self.bufs[s][:, 0:n]
        if len(shp) == 3:
            view = view.rearrange("p (a b) -> p a b", a=shp[1])
        elif len(shp) == 4:
            view = view.rearrange("p (a b c) -> p a b c", a=shp[1], b=shp[2])
        key = f"wr{s}"
        self.P.D("pool", key, view, dram_ap, writes=[key])
        return view, key

    def load_multi(self, aps):
        s = self.i % self.n; self.i += 1
        J = len(aps); K_, N_ = aps[0].shape[1], aps[0].shape[2]
        view = self.bufs[s][:, 0:K_ * J * N_].rearrange("p (a b c) -> p a b c", a=K_, b=J)
        key = f"wr{s}"
        for j, ap in enumerate(aps):
            self.P.D("pool", key, view[:, :, j, :], ap, writes=[key])
        return view, key

    def load_parts(self, aps):
        s = self.i % self.n; self.i += 1
        key = f"wr{s}"; off = 0; views = []
        for ap in aps:
            a, b = ap.shape[1], ap.shape[2]
            v = self.bufs[s][:, off:off + a * b].rearrange("p (a b) -> p a b", a=a)
            off += a * b
            self.P.D("pool", key, v, ap, writes=[key])
            views.append(v)
        assert off <= 4096
        return views, key
def declare_io(nc, C):
    def din(name, shape):
        return nc.dram_tensor(name, list(shape), F32, kind="ExternalInput").ap()
    def dout(name, shape):
        return nc.dram_tensor(name, list(shape), F32, kind="ExternalOutput").ap()
    C.xin = din("xin", [T, D]); C.condT = din("condT", [128, 8, 2])
    C.ada_w = din("ada_w", [4, D, 3 * D]); C.ada_bT = din("ada_bT", [128, 4, 24])
    C.ln_gT = din("ln_gT", [128, 4, 8]); C.ln_bT = din("ln_bT", [128, 4, 8])
    C.ident = din("ident", [128, 128])
    C.sc_w_in = din("sc_w_in", [D, 4 * D]); C.sc_cw = din("sc_cw", [128, 3, 8]); C.sc_cb = din("sc_cb", [128, 8])
    C.sc_w_out = din("sc_w_out", [D, D])
    C.rg_w_in = din("rg_w_in", [D, 2 * D]); C.rg_cw = din("rg_cw", [128, 4, 8]); C.rg_cb = din("rg_cb", [128, 8])
    C.rg_w_gate = din("rg_w_gate", [2, 4, 256, 512]); C.rg_bg = din("rg_bg", [128, 2, 4, 4])
    C.rg_lam = din("rg_lam", [128, 2, 8]); C.rg_w_out = din("rg_w_out", [D, D])
    C.rg_h0 = din("rg_h0", [128, 2, 8])
    C.hg_w_in = din("hg_w_in", [D, 5 * D]); C.hg_lbl = din("hg_lbl", [128, 16, 5]); C.hg_ng = din("hg_ng", [128, 8])
    C.hg_w_out = din("hg_w_out", [D, D]); C.hg_s0 = din("hg_s0", [2, 8, 128, 128])
    C.hg_masks = din("hg_masks", [128, 4, 128])
    C.mla_w_in = din("mla_w_in", [D, 1696]); C.mla_qn = din("mla_qn", [128, 3]); C.mla_kvn = din("mla_kvn", [128, 2])
    C.mla_w_qb = din("mla_w_qb", [384, 1536]); C.mla_w_qbsw = din("mla_w_qbsw", [384, 512])
    C.mla_w_kpesw = din("mla_w_kpesw", [D, 32])
    C.mla_w_kvb = din("mla_w_kvb", [256, 2048]); C.mla_w_out = din("mla_w_out", [D, D])
    C.mla_ckv_ctx = din("mla_ckv_ctx", [512, 256]); C.mla_kpe_ctx = din("mla_kpe_ctx", [512, 32])
    C.rope_cs = din("rope_cs", [128, 2, 1024])
    C.y = dout("y", [T, D])
    C.o_hg = dout("o_hg", [4, 2, 8, 128, 128]); C.o_rg = dout("o_rg", [8, D])
    C.o_ckv = dout("o_ckv", [1024, 256]); C.o_kpe = dout("o_kpe", [1024, 32])

def setup_persistent(P, C):
    C.xT = P.sb("xT", [128, 8, T], F32)
    C.hT = P.sb("hT", [128, 8, T], BF16)
    C.mT = P.sb("mT", [128, 8, T], BF16)
    C.ring = WRing(P, nslot=3, elems=4096)
    C.identf = P.sb("identf", [128, 128], F32)
    C.identb = P.sb("identb", [128, 128], BF16)
    C.onesb = P.sb("onesb", [128, 128], BF16)
    C.condf = P.sb("condf", [128, 8, 2], F32)
    C.scond = P.sb("scond", [128, 8, 2], BF16)
    C.adab = P.sb("adab", [128, 4, 24], F32)
    C.lng = P.sb("lng", [128, 4, 8], F32); C.lnb = P.sb("lnb", [128, 4, 8], F32)
    C.mod = P.sb("mod", [128, 24, 2], F32)
    C.colsb = [P.sb(f"cols{i}", [128, 3, 8, 2], F32) for i in range(2)]
    C.ps = [P.ps(f"ps{i}", [128, 512], F32) for i in range(8)]
    P.D("sp", "identf", C.identf[:], C.ident[:, :], writes=["identf"])
    P.D("sp", "condf", C.condf[:], C.condT[:, :, :], writes=["condf"])
    P.D("sp", "adab", C.adab[:], C.ada_bT[:, :, :], writes=["adab"])
    P.D("sp", "lng", C.lng[:], C.ln_gT[:, :, :], writes=["lng"])
    P.D("sp", "lnb", C.lnb[:], C.ln_bT[:, :, :], writes=["lnb"])
    P.I("dve", "tensor_copy", reads=["identf"], writes=["identb"], out=C.identb[:], in_=C.identf[:])
    P.I("dve", "memset", writes=["onesb"], ap=C.onesb[:], constant=1.0)
    P.I("act", "activation", reads=["condf"], writes=["scond"], out=C.scond[:], in_=C.condf[:], func=AF.Silu)

def xk(g, fc):
    return f"xT{g}_{fc}"

def input_transposes(P, C):
    with ExitStack() as es:
        st = [P.sb(f"xst{i}", [128, D], F32, es) for i in range(2)]
        for t in range(NT):
            b = t % 2
            P.D("sp", f"xst{b}", st[b][:], C.xin[t * 128:(t + 1) * 128, :], writes=[f"xst{b}"])
            for half in range(2):
                pb = C.ps[(t * 2 + half) % 4]; pk = f"ps{(t * 2 + half) % 4}"
                P.G("pe", [("transpose", dict(out=pb[:, i * 128:(i + 1) * 128], in_=st[b][:, (half * 4 + i) * 128:(half * 4 + i + 1) * 128],
                                              identity=C.identf[:])) for i in range(4)], reads=[f"xst{b}", "identf"], writes=[pk])
                eng = "act" if half == 0 else "dve"
                outap = C.xT[:, half * 4:half * 4 + 4, t * 128:(t + 1) * 128]
                inap = pb[:].rearrange("p (c t) -> p c t", c=4)
                if eng == "act":
                    P.I("act", "activation", reads=[pk], writes=[xk(t // 4, half * 4 + i) for i in range(4)], out=outap, in_=inap, func=AF.Copy)
                else:
                    P.I("dve", "tensor_copy", reads=[pk], writes=[xk(t // 4, half * 4 + i) for i in range(4)], out=outap, in_=inap)
        P.barrier()

def output_transposes(P, C):
    with ExitStack() as es:
        st = [P.sb(f"yst{i}", [128, D], F32, es) for i in range(2)]
        for t in range(NT):
            b = t % 2
            for half in range(2):
                pb = C.ps[(t * 2 + half) % 4]; pk = f"ps{(t * 2 + half) % 4}"
                P.G("pe", [("transpose", dict(out=pb[:, i * 128:(i + 1) * 128], in_=C.xT[:, half * 4 + i, t * 128:(t + 1) * 128],
                                              identity=C.identf[:])) for i in range(4)], reads=[xk(t // 4, half * 4 + i) for i in range(4)] + ["identf"], writes=[pk])
                if half == 0:
                    P.I("act", "activation", reads=[pk], writes=[f"yst{b}"], out=st[b][:, 0:512], in_=pb[:], func=AF.Copy)
                else:
                    P.I("dve", "tensor_copy", reads=[pk], writes=[f"yst{b}"], out=st[b][:, 512:1024], in_=pb[:])
            P.D("sp", f"yout{b}", C.y[t * 128:(t + 1) * 128, :], st[b][:], reads=[f"yst{b}"])
        P.barrier()

def ada_phase(P, C, l):
    cols = C.colsb[l % 2]; ck = f"cols{l % 2}"
    wv_all = C.ada_w[l].rearrange("(k p) n -> p k n", p=128)
    psA = C.ps[4]
    for piece in range(6):
        wv, wk = C.ring.load(wv_all[:, :, piece * 512:(piece + 1) * 512], "")
        for f4 in range(4):
            fc = piece * 4 + f4
            mmK(P, psA[:, fc * 2:fc * 2 + 2], [(wv[:, kc, f4 * 128:(f4 + 1) * 128], C.scond[:, kc, :]) for kc in range(8)],
                reads=[wk, "scond"], writes=["ps4"])
    P.I("dve", "tensor_tensor", reads=["ps4", "adab"], writes=["mod"], out=C.mod[:],
        in0=psA[:, 0:48].rearrange("p (f j) -> p f j", j=2), in1=C.adab[:, l, :].unsqueeze(2).to_broadcast([128, 24, 2]), op=ALU.add)
    P.I("dve", "tensor_copy", reads=["mod"], writes=[ck], out=cols[:, 0], in_=C.mod[:, 0:8, :])
    P.I("dve", "tensor_scalar_add", reads=["mod"], writes=[ck], out=cols[:, 1], in0=C.mod[:, 8:16, :], scalar1=1.0)
    P.I("dve", "tensor_scalar_mul", reads=["mod"], writes=[ck], out=cols[:, 2], in0=C.mod[:, 16:24, :], scalar1=1.0 / ALPHA)

def modulate_phase(P, C, l):
    cols = C.colsb[l % 2]; ck = f"cols{l % 2}"
    for j in range(2):
        for c in range(8):
            sl = slice(j * 1024, (j + 1) * 1024)
            rk = [xk(2 * j, c), xk(2 * j + 1, c), ck]
            if (c + j) % 2 == 0:
                P.I("dve", "tensor_scalar", reads=rk, writes=[f"hT{j}"], out=C.hT[:, c, sl], in0=C.xT[:, c, sl],
                    scalar1=cols[:, 1, c, j:j + 1], scalar2=cols[:, 0, c, j:j + 1], op0=ALU.mult, op1=ALU.add)
            else:
                P.I("act", "activation", reads=rk, writes=[f"hT{j}"], out=C.hT[:, c, sl], in_=C.xT[:, c, sl], func=AF.Identity,
                    scale=cols[:, 1, c, j:j + 1], bias=cols[:, 0, c, j:j + 1])

def load_wout(P, C, w_dram):
    C.wout_dram = w_dram

def wout_prefetch(P, C):
    wv = C.wout_dram.rearrange("(k p) n -> p k n", p=128)
    return [C.ring.load(wv[:, :, 0:512], ""), C.ring.load(wv[:, :, 512:1024], "")]

def wout_ln_phase(P, C, l, pre, next_ada=None):
    cols = C.colsb[l % 2]; ck = f"cols{l % 2}"
    with ExitStack() as es:
        zn = [P.sb(f"ln_zn{i}", [128, D], F32, es) for i in range(4)]
        st = [P.sb(f"ln_st{i}", [128, 12], F32, es) for i in range(2)]
        mv = [P.sb(f"ln_mv{i}", [128, 2], F32, es) for i in range(2)]
        rs = [P.sb(f"ln_rs{i}", [128, 2], F32, es) for i in range(2)]
        epsc = P.sb("ln_eps", [128, 1], F32, es)
        P.I("dve", "memset", writes=["ln_eps"], ap=epsc[:], constant=LN_EPS)
        yi = [0]; ti = [0]
        def zpass(g):
            j = g // 2; gs = slice(g * 512, (g + 1) * 512)
            for fc in range(8):
                wv, wk = pre[fc // 4]; f4 = fc % 4
                py = C.ps[6 + yi[0] % 2]; pyk = f"ps{6 + yi[0] % 2}"; yi[0] += 1
                mmK(P, py[:], [(wv[:, kc, f4 * 128:(f4 + 1) * 128], C.mT[:, kc, gs]) for kc in range(8)], reads=[wk, f"mT{g}"], writes=[pyk])
                P.I("dve", "scalar_tensor_tensor", reads=[pyk, xk(g, fc), ck], writes=[xk(g, fc)], out=C.xT[:, fc, gs], in0=py[:],
                    scalar=cols[:, 2, fc, j:j + 1], in1=C.xT[:, fc, gs], op0=ALU.mult, op1=ALU.add)
        def norm_tiles(g):
            for tl in range(4):
                tcols = slice(g * 512 + tl * 128, g * 512 + (tl + 1) * 128)
                i2 = ti[0] % 2; ti[0] += 1
                pb = [C.ps[2 * i2], C.ps[2 * i2 + 1]]; pbk = [f"ps{2 * i2}", f"ps{2 * i2 + 1}"]
                for half in range(2):
                    P.G("pe", [("transpose", dict(out=pb[half][:, i * 128:(i + 1) * 128], in_=C.xT[:, half * 4 + i, tcols], identity=C.identf[:]))
                               for i in range(4)], reads=[xk(g, half * 4 + i) for i in range(4)] + ["identf"], writes=[pbk[half]])
                    P.I("dve", "bn_stats", reads=[pbk[half]], writes=[f"ln_st{i2}"], out=st[i2][:, half * 6:(half + 1) * 6], in_=pb[half][:])
                P.I("dve", "bn_aggr", reads=[f"ln_st{i2}"], writes=[f"ln_mv{i2}"], out=mv[i2][:], in_=st[i2][:])
                P.I("act", "activation", reads=[f"ln_mv{i2}", "ln_eps"], writes=[f"ln_rs{i2}"], out=rs[i2][:, 0:1], in_=mv[i2][:, 1:2], func=AF.Sqrt,
                    bias=epsc[:, 0:1], scale=1.0)
                P.I("dve", "reciprocal", reads=[f"ln_rs{i2}"], writes=[f"ln_rs{i2}"], out=rs[i2][:, 0:1], in_=rs[i2][:, 0:1])
                P.I("dve", "scalar_tensor_tensor", reads=[f"ln_mv{i2}", f"ln_rs{i2}"], writes=[f"ln_rs{i2}"], out=rs[i2][:, 1:2], in0=mv[i2][:, 0:1],
                    scalar=-1.0, in1=rs[i2][:, 0:1], op0=ALU.mult, op1=ALU.mult)
                for half in range(2):
                    P.I("act", "activation", reads=[pbk[half], f"ln_rs{i2}"], writes=[f"ln_zn{tl}"], out=zn[tl][:, half * 512:(half + 1) * 512],
                        in_=pb[half][:], func=AF.Identity, scale=rs[i2][:, 0:1], bias=rs[i2][:, 1:2])
        def back(g):
            gs = slice(g * 512, (g + 1) * 512)
            for fc in range(8):
                bi = 4 + fc % 2
                P.G("pe", [("transpose", dict(out=C.ps[bi][:, tl * 128:(tl + 1) * 128], in_=zn[tl][:, fc * 128:(fc + 1) * 128], identity=C.identf[:]))
                           for tl in range(4)], reads=[f"ln_zn{tl}" for tl in range(4)] + ["identf"], writes=[f"ps{bi}"])
                if fc % 2 == 0:
                    P.I("act", "activation", reads=[f"ps{bi}", "lng", "lnb"], writes=[xk(g, fc)], out=C.xT[:, fc, gs], in_=C.ps[bi][:], func=AF.Identity,
                        scale=C.lng[:, l, fc:fc + 1], bias=C.lnb[:, l, fc:fc + 1])
                else:
                    P.I("dve", "tensor_scalar", reads=[f"ps{bi}", "lng", "lnb"], writes=[xk(g, fc)], out=C.xT[:, fc, gs], in0=C.ps[bi][:],
                        scalar1=C.lng[:, l, fc:fc + 1], scalar2=C.lnb[:, l, fc:fc + 1], op0=ALU.mult, op1=ALU.add)
        zpass(0)
        for g in range(4):
            if g + 1 < 4:
                zpass(g + 1)
            norm_tiles(g)
            if g == 3 and next_ada is not None:
                ada_phase(P, C, next_ada)
            back(g)
        P.barrier()
QS = 128.0 ** -0.5
CH = 64
NCH = 512 // CH
JT = 128 // CH

def _roundrobin(gens):
    gens = list(gens)
    while gens:
        nxt = []
        for g in gens:
            try:
                next(g); nxt.append(g)
            except StopIteration:
                pass
        gens = nxt

def hgrn_layer(P, C):
    with ExitStack() as es:
        lbl = P.sb("hg_lbl_s", [128, 16, 5], F32, es); lbm = P.sb("hg_lbm", [128, 16], F32, es)
        lb = P.sb("hg_lb", [128, 16], F32, es); oml = P.sb("hg_oml", [128, 16], F32, es)
        ng = P.sb("hg_ng_s", [128, 8], F32, es); eps6 = P.sb("hg_eps", [128, 1], F32, es)
        one = P.sb("hg_one", [128, 1], F32, es)
        mk = P.sb("hg_mk", [128, 2, 128], F32, es); rm = P.sb("hg_rm", [128, 2, 512], BF16, es)
        mstage = P.sb("hg_mst", [128, 2, 128], F32, es)
        P.D("sp", "hg_lbl", lbl[:], C.hg_lbl[:, :, :], writes=["hg_lbl"])
        P.D("sp", "hg_ng", ng[:], C.hg_ng[:, :], writes=["hg_ng"])
        P.D("sp", "hg_mk", mk[:], C.hg_masks[:, 0:2, :], writes=["hg_mk"])
        P.D("sp", "hg_mst", mstage[:], C.hg_masks[:, 2:4, :], writes=["hg_mst"])
        P.I("dve", "memset", writes=["hg_eps"], ap=eps6[:], constant=1e-6)
        P.I("dve", "memset", writes=["hg_one"], ap=one[:], constant=1.0)
        for d in range(2):
            for r in range(4):
                P.I("dve", "tensor_copy", reads=["hg_mst"], writes=["hg_rm"], out=rm[:, d, r * 128:(r + 1) * 128], in_=mstage[:, d, :])
        P.I("dve", "reduce_max", reads=["hg_lbl"], writes=["hg_lbm"], out=lbm[:], in_=lbl[:], axis=AX.X)
        P.I("dve", "tensor_tensor", reads=["hg_lbl", "hg_lbm"], writes=["hg_lbl"], out=lbl[:], in0=lbl[:],
            in1=lbm[:].unsqueeze(2).to_broadcast([128, 16, 5]), op=ALU.subtract)
        P.I("act", "activation", reads=["hg_lbl"], writes=["hg_lbl"], out=lbl[:], in_=lbl[:], func=AF.Exp)
        P.I("dve", "reduce_sum", reads=["hg_lbl"], writes=["hg_lbm"], out=lbm[:], in_=lbl[:], axis=AX.X)
        P.I("dve", "reciprocal", reads=["hg_lbm"], writes=["hg_lbm"], out=lbm[:], in_=lbm[:])
        P.I("dve", "tensor_tensor", reads=["hg_lbl", "hg_lbm"], writes=["hg_lb"], out=lb[:], in0=lbl[:, :, 0], in1=lbm[:], op=ALU.mult)
        P.I("dve", "tensor_scalar", reads=["hg_lb"], writes=["hg_oml"], out=oml[:], in0=lb[:], scalar1=-1.0, scalar2=1.0, op0=ALU.mult, op1=ALU.add)

        vtok = P.sb("hg_vtok", [128, 16, 128], BF16, es)
        qS = P.sb("hg_q", [128, 1024], F32, es)
        ob = P.sb("hg_o", [128, 1024], F32, es)
        sgB = P.sb("hg_sgb", [128, 1024], BF16, es)
        Sf = P.sb("hg_Sf", [128, 8, 128], F32, es); Sb = P.sb("hg_Sb", [128, 8, 128], BF16, es)
        attS = [P.sb(f"hg_att{i}", [128, 128], BF16, es) for i in range(2)]
        U = []
        for u in range(2):
            B_ = Ctx()
            B_.u = u
            B_.cum = P.sb(f"hg_cum{u}", [128, 512], F32, es); B_.kS = P.sb(f"hg_k{u}", [128, 512], F32, es)
            B_.A = P.sb(f"hg_A{u}", [128, 512], F32, es); B_.B = P.sb(f"hg_B{u}", [128, 512], F32, es)
            B_.qrel = P.sb(f"hg_qrel{u}", [128, 512], BF16, es); B_.krel = P.sb(f"hg_krel{u}", [128, 512], BF16, es)
            B_.qcum = P.sb(f"hg_qcum{u}", [128, 512], BF16, es); B_.kdT = P.sb(f"hg_kdT{u}", [128, 512], BF16, es)
            B_.kdtok = P.sb(f"hg_kdtok{u}", [128, 4, 128], BF16, es); B_.etot = P.sb(f"hg_etot{u}", [128, 16], F32, es)
            U.append(B_)
        wv_all = C.hg_w_in.rearrange("(k p) n -> p k n", p=128)
        P.I("dve", "memset", writes=["hg_qrel0"], ap=U[0].qrel[:], constant=0.0)
        P.I("pe", "matmul", reads=["hg_qrel0", "identb"], writes=["ps3"], out=C.ps[3][:], lhsT=C.identb[:], rhs=U[0].qrel[:], start=True, stop=True)
        cnt = {"w": 0, "att": 0, "x": 0, "y": 0, "u": 0}
        def wps():
            i = cnt["w"] % 2; cnt["w"] += 1
            return C.ps[i], f"ps{i}"
        def quarter(bank, name):
            i = cnt[name] % 4; cnt[name] += 1
            return C.ps[bank][:, i * 128:(i + 1) * 128], f"ps{bank}"
        def uslot():
            i = cnt["u"] % 2; cnt["u"] += 1
            return C.ps[6 + i][:, 0:128], f"ps{6 + i}"

        def prep(B_, wv, wk, hd, blk, d, sg_):
            u = B_.u
            K = lambda n: f"hg_{n}{u}"
            cs = slice(blk * 1024 + sg_ * 512, blk * 1024 + (sg_ + 1) * 512); ls = slice(sg_ * 512, (sg_ + 1) * 512)
            ridx = (CH // 2 - 1) if d == 0 else (CH // 2); tidx = (CH - 1) if d == 0 else 0
            cum, kS, Ab, Bb = B_.cum, B_.kS, B_.A, B_.B
            pt, pk = wps()
            mmK(P, pt[:], [(wv[:, kc, 1 + d, :], C.hT[:, kc, cs]) for kc in range(8)], reads=[wk, f"hT{blk}"], writes=[pk]); yield
            lbc = lb[:, d * 8 + hd:d * 8 + hd + 1]; omc = oml[:, d * 8 + hd:d * 8 + hd + 1]
            P.I("act", "activation", reads=[pk], writes=[K("cum")], out=cum[:], in_=pt[:], func=AF.Exp, scale=-1.0); yield
            P.I("act", "activation", reads=[K("cum"), "hg_lb", "hg_one"], writes=[K("A")], out=Ab[:], in_=cum[:], func=AF.Ln, scale=lbc, bias=one[:, 0:1]); yield
            P.I("act", "activation", reads=[K("cum"), "hg_one"], writes=[K("B")], out=Bb[:], in_=cum[:], func=AF.Ln, scale=1.0, bias=one[:, 0:1]); yield
            P.I("dve", "tensor_tensor", reads=[K("A"), K("B")], writes=[K("cum")], out=cum[:], in0=Ab[:], in1=Bb[:], op=ALU.subtract); yield
            P.I("dve", "tensor_tensor", reads=[pk, K("B")], writes=[K("B")], out=Bb[:], in0=Bb[:], in1=pt[:], op=ALU.add); yield
            P.I("act", "activation", reads=[K("B")], writes=[K("k")], out=kS[:], in_=Bb[:], func=AF.Exp, scale=-1.0); yield
            rv = slice(None) if d == 0 else slice(None, None, -1)
            P.I("dve", "tensor_tensor_scan", reads=[K("cum"), "hg_rm"], writes=[K("cum")], out=cum[:, rv], data0=rm[:, d, rv],
                data1=cum[:, rv], initial=0.0, op0=ALU.mult, op1=ALU.add); yield
            c3 = cum[:].rearrange("p (c t) -> p c t", t=CH)
            A3 = Ab[:].rearrange("p (c t) -> p c t", t=CH)
            P.I("dve", "tensor_tensor", reads=[K("cum")], writes=[K("A")], out=A3, in0=c3,
                in1=c3[:, :, ridx:ridx + 1].to_broadcast([128, NCH, CH]), op=ALU.subtract); yield
            P.I("act", "activation", reads=[K("cum")], writes=[K("B")], out=Bb[:], in_=cum[:], func=AF.Exp); yield
            P.I("dve", "scalar_tensor_tensor", reads=["hg_q", K("B")], writes=[K("qcum")], out=B_.qcum[:], in0=qS[:, ls], scalar=QS,
                in1=Bb[:], op0=ALU.mult, op1=ALU.mult); yield
            P.I("act", "activation", reads=[K("cum")], writes=[K("etot")], out=B_.etot[:, 0:NCH], in_=c3[:, :, tidx], func=AF.Exp); yield
            P.I("act", "activation", reads=[K("A")], writes=[K("B")], out=Bb[:], in_=Ab[:], func=AF.Exp); yield
            P.I("dve", "scalar_tensor_tensor", reads=["hg_q", K("B")], writes=[K("qrel")], out=B_.qrel[:], in0=qS[:, ls], scalar=QS,
                in1=Bb[:], op0=ALU.mult, op1=ALU.mult); yield
            P.I("act", "activation", reads=[K("A")], writes=[K("B")], out=Bb[:], in_=Ab[:], func=AF.Exp, scale=-1.0); yield
            P.I("dve", "scalar_tensor_tensor", reads=[K("k"), K("B"), "hg_oml"], writes=[K("krel")], out=B_.krel[:], in0=kS[:], scalar=omc, in1=Bb[:],
                op0=ALU.mult, op1=ALU.mult); yield
            P.I("dve", "tensor_tensor", reads=[K("cum")], writes=[K("A")], out=A3, in0=c3,
                in1=c3[:, :, tidx:tidx + 1].to_broadcast([128, NCH, CH]), op=ALU.subtract); yield
            P.I("act", "activation", reads=[K("A")], writes=[K("B")], out=Bb[:], in_=Ab[:], func=AF.Exp, scale=-1.0); yield
            P.I("dve", "scalar_tensor_tensor", reads=[K("k"), K("B"), "hg_oml"], writes=[K("kdT")], out=B_.kdT[:], in0=kS[:], scalar=omc, in1=Bb[:],
                op0=ALU.mult, op1=ALU.mult); yield
            psT = C.ps[2][:].bitcast(BF16)
            P.G("pe", [("transpose", dict(out=psT[:, i * 128:(i + 1) * 128], in_=B_.kdT[:, i * 128:(i + 1) * 128], identity=C.identb[:]))
                       for i in range(4)], reads=[K("kdT"), "identb"], writes=["ps2"])
            P.I("act", "activation", reads=["ps2"], writes=[K("kdtok")], out=B_.kdtok[:], in_=psT[:, 0:512].rearrange("p (a b) -> p a b", a=4),
                func=AF.Copy); yield

        for hd in range(8):
            if hd == 4:
                load_wout(P, C, C.hg_w_out)
            wv, wk = C.ring.load_multi([wv_all[:, :, j * 1024 + hd * 128:j * 1024 + (hd + 1) * 128] for j in range(4)])
            gv, gk = C.ring.load(wv_all[:, :, 4096 + hd * 128:4096 + (hd + 1) * 128], "")
            for t4 in range(4):
                vb = 2 if t4 % 2 == 0 else 4
                P.G("pe", [("matmul", dict(out=C.ps[vb][:, i * 128:(i + 1) * 128], lhsT=C.hT[:, kc, (t4 * 4 + i) * 128:(t4 * 4 + i + 1) * 128],
                                           rhs=wv[:, kc, 3, :], start=(kc == 0), stop=(kc == 7))) for i in range(4) for kc in range(8)],
                    reads=[wk, f"hT{t4 // 2}"], writes=[f"ps{vb}"])
                if t4 % 2 == 0:
                    P.I("act", "activation", reads=[f"ps{vb}"], writes=["hg_vtok"], out=vtok[:, t4 * 4:t4 * 4 + 4, :],
                        in_=C.ps[vb][:].rearrange("p (a b) -> p a b", a=4), func=AF.Copy)
                else:
                    P.I("dve", "tensor_copy", reads=[f"ps{vb}"], writes=["hg_vtok"], out=vtok[:, t4 * 4:t4 * 4 + 4, :],
                        in_=C.ps[vb][:].rearrange("p (a b) -> p a b", a=4))
            for blk in range(2):
                for g2 in range(2):
                    cs = slice(blk * 1024 + g2 * 512, blk * 1024 + (g2 + 1) * 512); ls = slice(g2 * 512, (g2 + 1) * 512)
                    pt, pk = wps()
                    mmK(P, pt[:], [(wv[:, kc, 0, :], C.hT[:, kc, cs]) for kc in range(8)], reads=[wk, f"hT{blk}"], writes=[pk])
                    P.I("act", "activation", reads=[pk], writes=["hg_q"], out=qS[:, ls], in_=pt[:], func=AF.Silu)
                for g2 in range(2):
                    cs = slice(blk * 1024 + g2 * 512, blk * 1024 + (g2 + 1) * 512); ls = slice(g2 * 512, (g2 + 1) * 512)
                    pt, pk = wps()
                    mmK(P, pt[:], [(gv[:, kc, :], C.hT[:, kc, cs]) for kc in range(8)], reads=[gk, f"hT{blk}"], writes=[pk])
                    P.I("act", "activation", reads=[pk], writes=["hg_sgb"], out=sgB[:, ls], in_=pt[:], func=AF.Silu)
                P.I("pool", "memset", writes=[f"hg_o{t_}" for t_ in range(8)], ap=ob[:], constant=0.0)
                if blk == 0:
                    P.I("pool", "memset", writes=[f"hg_Sf{c_}" for c_ in range(8)], ap=Sf[:], constant=0.0)
                    P.I("pool", "memset", writes=[f"hg_Sb{c_}" for c_ in range(8)], ap=Sb[:], constant=0.0)
                else:
                    for d in range(2):
                        P.D("sp", f"hg_s0{d}", Sf[:, d, :], C.hg_s0[d, hd], writes=[f"hg_Sf{d}"])
                        P.I("act", "activation", reads=[f"hg_Sf{d}"], writes=[f"hg_Sb{d}"], out=Sb[:, d, :], in_=Sf[:, d, :], func=AF.Copy)
                for step in range(2):
                    units = [(0, step, U[0]), (1, 1 - step, U[1])]
                    _roundrobin([prep(B_, wv, wk, hd, blk, d, sg_) for (d, sg_, B_) in units])
                    chains = []
                    for (d, sg_, B_) in units:
                        if blk == 0:
                            cl = [(d * 4 + 2 * sg_, [0, 1]), (d * 4 + 2 * sg_ + 1, [2, 3])]
                        else:
                            cl = [(d, [0, 1, 2, 3])]
                        for ch, tl in cl:
                            chains.append((d, sg_, B_, ch, tl if d == 0 else tl[::-1]))
                    npos = len(chains[0][4])
                    xy_banks = [4, 5, 0, 1]
                    for pos in range(npos):
                        info = []
                        for ci, (d, sg_, B_, ch, tl) in enumerate(chains):
                            u = B_.u
                            tloc = tl[pos]; gt = blk * 8 + sg_ * 4 + tloc; ts_ = slice(tloc * 128, (tloc + 1) * 128)
                            pa, pak = quarter(3, "att")
                            t0 = tloc * 128
                            blocks = []
                            for cb in range(JT):
                                b0 = cb * CH; hC = CH // 2
                                if d == 0:
                                    blocks += [(b0, hC, b0, CH), (b0 + hC, hC, b0 + hC, hC)]
                                else:
                                    blocks += [(b0 + hC, hC, b0, CH), (b0, hC, b0, hC)]
                            P.G("pe", [("matmul", dict(out=pa[s0:s0 + sn, q0:q0 + qn], lhsT=B_.krel[:, t0 + s0:t0 + s0 + sn], rhs=B_.qrel[:, t0 + q0:t0 + q0 + qn],
                                                       start=True, stop=True, tile_position=(0, s0))) for (s0, sn, q0, qn) in blocks],
                                reads=[f"hg_krel{u}", f"hg_qrel{u}"], writes=[pak])
                            ai = cnt["att"] % 2
                            P.I("dve", "tensor_tensor", reads=[pak, "hg_mk"], writes=[f"hg_att{ai}"], out=attS[ai][:], in0=pa, in1=mk[:, d, :], op=ALU.mult)
                            xb_ = xy_banks[ci]
                            px = C.ps[xb_][:, 0:128]; pxk = f"ps{xb_}"; py = px; pyk = pxk
                            P.I("pe", "matmul", reads=["hg_vtok", f"hg_att{ai}"], writes=[pxk], out=px, lhsT=vtok[:, gt, :], rhs=attS[ai][:], start=True, stop=False)
                            info.append((d, sg_, B_, ch, tloc, gt, px, pxk, py, pyk))
                        for jj in range(JT):
                            for (d, sg_, B_, ch, tloc, gt, px, pxk, py, pyk) in info:
                                u = B_.u
                                j = jj if d == 0 else JT - 1 - jj
                                qs_ = slice(tloc * 128 + j * CH, tloc * 128 + (j + 1) * CH)
                                P.I("pe", "matmul", reads=[f"hg_Sb{ch}", f"hg_qcum{u}"], writes=[pyk], out=py[:, j * CH:(j + 1) * CH], lhsT=Sb[:, ch, :],
                                    rhs=B_.qcum[:, qs_], start=False, stop=(jj == JT - 1))
                                pu, puk = uslot()
                                P.I("pe", "matmul", reads=[f"hg_kdtok{u}", "hg_vtok"], writes=[puk], out=pu, lhsT=B_.kdtok[j * CH:(j + 1) * CH, tloc, :],
                                    rhs=vtok[j * CH:(j + 1) * CH, gt, :], start=True, stop=True, tile_position=(j * CH, 0))
                                P.I("dve", "scalar_tensor_tensor", reads=[f"hg_Sf{ch}", puk, f"hg_etot{u}"], writes=[f"hg_Sf{ch}"], out=Sf[:, ch, :],
                                    in0=Sf[:, ch, :], scalar=B_.etot[:, tloc * JT + j:tloc * JT + j + 1], in1=pu, op0=ALU.mult, op1=ALU.add)
                                P.I("act", "activation", reads=[f"hg_Sf{ch}"], writes=[f"hg_Sb{ch}"], out=Sb[:, ch, :], in_=Sf[:, ch, :], func=AF.Copy)
                        for (d, sg_, B_, ch, tloc, gt, px, pxk, py, pyk) in info:
                            t8 = sg_ * 4 + tloc
                            os_ = slice(t8 * 128, (t8 + 1) * 128); ok_ = f"hg_o{t8}"
                            P.I("dve", "tensor_tensor", reads=[pxk, ok_], writes=[ok_], out=ob[:, os_], in0=ob[:, os_], in1=px, op=ALU.add)
                    if blk == 0:
                        for (d, sg_, B_, ch, tl) in chains:
                            P.D("sp", f"o_hg{ch}", C.o_hg[ch % 4, d, hd], Sf[:, ch, :], reads=[f"hg_Sf{ch}"])
                osq = U[0].qrel; rsb = U[0].B; sgb = U[0].A
                for g2 in range(2):
                    cs = slice(blk * 1024 + g2 * 512, blk * 1024 + (g2 + 1) * 512); ls = slice(g2 * 512, (g2 + 1) * 512)
                    okeys = [f"hg_o{g2 * 4 + t_}" for t_ in range(4)]
                    P.I("act", "activation", reads=okeys, writes=["hg_qrel0"], out=osq[:], in_=ob[:, ls], func=AF.Square)
                    pt, pk = wps()
                    P.I("pe", "matmul", reads=["hg_qrel0", "onesb"], writes=[pk], out=pt[:], lhsT=C.onesb[:], rhs=osq[:], start=True, stop=True)
                    P.I("act", "activation", reads=[pk, "hg_eps"], writes=["hg_B0"], out=rsb[:], in_=pt[:], func=AF.Ln, bias=eps6[:, 0:1], scale=1.0 / 128.0)
                    P.I("act", "activation", reads=["hg_B0"], writes=["hg_B0"], out=rsb[:], in_=rsb[:], func=AF.Exp, scale=-0.5)
                    P.I("dve", "tensor_tensor", reads=["hg_B0"] + okeys, writes=["hg_B0"], out=rsb[:], in0=rsb[:], in1=ob[:, ls], op=ALU.mult)
                    P.I("dve", "scalar_tensor_tensor", reads=["hg_B0", "hg_sgb", "hg_ng"], writes=[f"mT{blk * 2 + g2}"], out=C.mT[:, hd, cs], in0=rsb[:],
                        scalar=ng[:, hd:hd + 1], in1=sgB[:, ls], op0=ALU.mult, op1=ALU.mult)
def sconv_layer(P, C):
    with ExitStack() as es:
        cw = P.sb("sc_cw_s", [128, 3, 8], F32, es); cb = P.sb("sc_cb_s", [128, 8], F32, es)
        P.D("sp", "sc_cw", cw[:], C.sc_cw[:, :, :], writes=["sc_cw"])
        P.D("sp", "sc_cb", cb[:], C.sc_cb[:, :], writes=["sc_cb"])
        pb_ = [P.sb(f"sc_p{i}", [128, 1024], F32, es) for i in range(2)]
        zb_ = [P.sb(f"sc_z{i}", [128, 1024], F32, es) for i in range(2)]
        cgS = [P.sb(f"sc_cg{i}", [128, 512], F32, es) for i in range(2)]
        sgS = [P.sb(f"sc_sg{i}", [128, 512], F32, es) for i in range(2)]
        tS = [P.sb(f"sc_t{i}", [128, 512], F32, es) for i in range(2)]
        wv_all = C.sc_w_in.rearrange("(k p) n -> p k n", p=128)
        pi = 0
        for c in range(8):
            if c == 4:
                load_wout(P, C, C.sc_w_out)
            wv, wk = C.ring.load_multi([wv_all[:, :, j * 1024 + c * 128:j * 1024 + (c + 1) * 128] for j in range(4)])
            for blk in range(2):
                p = pb_[blk]; z = zb_[blk]; pk = f"sc_p{blk}"; zk = f"sc_z{blk}"
                for g2 in range(2):
                    cs = slice(blk * 1024 + g2 * 512, blk * 1024 + (g2 + 1) * 512); ls = slice(g2 * 512, (g2 + 1) * 512)
                    pa = pi % 4; pbk = (pi + 1) % 4; pi += 2
                    mmK(P, C.ps[pa][:], [(wv[:, kc, 1, :], C.hT[:, kc, cs]) for kc in range(8)], reads=[wk, f"hT{blk}"], writes=[f"ps{pa}"])
                    mmK(P, C.ps[pbk][:], [(wv[:, kc, 2, :], C.hT[:, kc, cs]) for kc in range(8)], reads=[wk, f"hT{blk}"], writes=[f"ps{pbk}"])
                    i2 = g2
                    P.I("act", "activation", reads=[f"ps{pa}"], writes=[f"sc_cg{i2}"], out=cgS[i2][:], in_=C.ps[pa][:], func=AF.Copy)
                    P.I("dve", "tensor_tensor", reads=[f"sc_cg{i2}", f"ps{pbk}"], writes=[pk], out=p[:, ls], in0=cgS[i2][:], in1=C.ps[pbk][:], op=ALU.mult)
                P.I("dve", "tensor_scalar", reads=[pk, "sc_cw", "sc_cb"], writes=[zk], out=z[:], in0=p[:], scalar1=cw[:, 1, c:c + 1],
                    scalar2=cb[:, c:c + 1], op0=ALU.mult, op1=ALU.add)
                if blk == 0:
                    z3 = z[:].rearrange("p (s t) -> p s t", s=4); p3 = p[:].rearrange("p (s t) -> p s t", s=4)
                    zlo, plo, zhi, phi = z3[:, :, 1:], p3[:, :, :-1], z3[:, :, :-1], p3[:, :, 1:]
                else:
                    zlo, plo, zhi, phi = z[:, 1:], p[:, :-1], z[:, :-1], p[:, 1:]
                P.I("dve", "scalar_tensor_tensor", reads=[pk, zk, "sc_cw"], writes=[zk], out=zlo, in0=plo, scalar=cw[:, 0, c:c + 1], in1=zlo,
                    op0=ALU.mult, op1=ALU.add)
                P.I("dve", "scalar_tensor_tensor", reads=[pk, zk, "sc_cw"], writes=[zk], out=zhi, in0=phi, scalar=cw[:, 2, c:c + 1], in1=zhi,
                    op0=ALU.mult, op1=ALU.add)
                for g2 in range(2):
                    cs = slice(blk * 1024 + g2 * 512, blk * 1024 + (g2 + 1) * 512); ls = slice(g2 * 512, (g2 + 1) * 512)
                    g = blk * 2 + g2
                    pa = pi % 4; pbk = (pi + 1) % 4; pi += 2
                    mmK(P, C.ps[pa][:], [(wv[:, kc, 0, :], C.hT[:, kc, cs]) for kc in range(8)], reads=[wk, f"hT{blk}"], writes=[f"ps{pa}"])
                    mmK(P, C.ps[pbk][:], [(wv[:, kc, 3, :], C.hT[:, kc, cs]) for kc in range(8)], reads=[wk, f"hT{blk}"], writes=[f"ps{pbk}"])
                    i2 = g2
                    P.I("act", "activation", reads=[f"ps{pbk}"], writes=[f"sc_sg{i2}"], out=sgS[i2][:], in_=C.ps[pbk][:], func=AF.Silu)
                    P.I("dve", "tensor_tensor", reads=[f"sc_sg{i2}", f"ps{pa}"], writes=[f"sc_t{i2}"], out=tS[i2][:], in0=sgS[i2][:], in1=C.ps[pa][:], op=ALU.mult)
                    P.I("dve", "tensor_tensor", reads=[f"sc_t{i2}", zk], writes=[f"mT{g}"], out=C.mT[:, c, cs], in0=tS[i2][:], in1=z[:, ls], op=ALU.mult)
def rglru_layer(P, C):
    with ExitStack() as es:
        cw = P.sb("rg_cw_s", [128, 4, 8], F32, es); cb = P.sb("rg_cb_s", [128, 8], F32, es)
        bg = P.sb("rg_bg_s", [128, 2, 4, 4], F32, es); lam = P.sb("rg_lam_s", [128, 2, 8], F32, es)
        clam = P.sb("rg_clam", [128, 2, 8], F32, es); h0 = P.sb("rg_h0_s", [128, 2, 8], F32, es)
        one = P.sb("rg_one", [128, 1], F32, es)
        rgst = P.sb("rg_state", [128, 8, 8], F32, es)
        P.D("sp", "rg_cw", cw[:], C.rg_cw[:, :, :], writes=["rg_cw"])
        P.D("sp", "rg_cb", cb[:], C.rg_cb[:, :], writes=["rg_cb"])
        P.D("sp", "rg_bg", bg[:], C.rg_bg[:, :, :, :], writes=["rg_bg"])
        P.D("sp", "rg_lam", lam[:], C.rg_lam[:, :, :], writes=["rg_lam"])
        P.D("sp", "rg_h0", h0[:], C.rg_h0[:, :, :], writes=["rg_h0"])
        P.I("dve", "memset", writes=["rg_one"], ap=one[:], constant=1.0)
        P.I("act", "activation", reads=["rg_lam"], writes=["rg_clam"], out=clam[:], in_=lam[:], func=AF.Exp, scale=-1.0)
        P.I("act", "activation", reads=["rg_clam", "rg_one"], writes=["rg_clam"], out=clam[:], in_=clam[:], func=AF.Ln, bias=one[:, 0:1], scale=1.0)
        P.I("dve", "tensor_scalar_mul", reads=["rg_clam"], writes=["rg_clam"], out=clam[:], in0=clam[:], scalar1=-4.0)
        half = P.sb("rg_half", [128, 1], F32, es)
        P.I("dve", "memset", writes=["rg_half"], ap=half[:], constant=0.5)
        P.I("dve", "tensor_scalar_mul", reads=["rg_bg"], writes=["rg_bg"], out=bg[:], in0=bg[:], scalar1=0.5)
        P.I("dve", "tensor_scalar_mul", reads=["rg_h0"], writes=["rg_h0"], out=h0[:], in0=h0[:], scalar1=2.0)
        uraw = P.sb("rg_uraw", [128, 1024], F32, es)
        uc = P.sb("rg_uc", [128, 2, 1024], F32, es); ucb = P.sb("rg_ucb", [128, 2, 1024], BF16, es)
        abufs = [P.sb(f"rg_a{i}", [128, 1024], F32, es) for i in range(2)]
        xbs = [[P.sb(f"rg_x{o}{i}", [128, 1024], F32, es) for i in range(2)] for o in range(2)]
        sqt = [P.sb(f"rg_sq{i}", [128, 512], F32, es) for i in range(4)]
        sgS = [P.sb(f"rg_sg{i}", [128, 512], F32, es) for i in range(2)]
        wv_all = C.rg_w_in.rearrange("(k p) n -> p k n", p=128)
        pi = [0]
        def nps():
            i = pi[0] % 6; pi[0] += 1
            return C.ps[i], f"ps{i}"
        def seg(ap2d, blk, lo, hi):
            if blk == 0:
                v = ap2d.rearrange("p (s t) -> p s t", s=4)
                return v[:, :, lo:256 + hi]
            return ap2d[:, lo:1024 + hi]
        for hh in range(4):
            if hh == 2:
                load_wout(P, C, C.rg_w_out)
            wv, wk = C.ring.load_multi([wv_all[:, :, j * 1024 + c * 128:j * 1024 + (c + 1) * 128] for j in range(2) for c in (2 * hh, 2 * hh + 1)])
            gv, gk = C.ring.load_multi([C.rg_w_gate[d, hh].rearrange("(k p) n -> p k n", p=128) for d in range(2)])
            for blk in range(2):
                for cc in range(2):
                    c = 2 * hh + cc
                    for g2 in range(2):
                        cs = slice(blk * 1024 + g2 * 512, blk * 1024 + (g2 + 1) * 512); ls = slice(g2 * 512, (g2 + 1) * 512)
                        pt, pk = nps()
                        mmK(P, pt[:], [(wv[:, kc, cc, :], C.hT[:, kc, cs]) for kc in range(8)], reads=[wk, f"hT{blk}"], writes=[pk])
                        P.I("act", "activation", reads=[pk], writes=["rg_uraw"], out=uraw[:, ls], in_=pt[:], func=AF.Copy)
                    ucc = uc[:, cc, :]
                    P.I("dve", "tensor_scalar", reads=["rg_uraw", "rg_cw", "rg_cb"], writes=["rg_uc"], out=ucc, in0=uraw[:],
                        scalar1=cw[:, 2, c:c + 1], scalar2=cb[:, c:c + 1], op0=ALU.mult, op1=ALU.add)
                    for (k, lo_o, hi_o, lo_i, hi_i) in ((0, 2, 0, 0, -2), (1, 1, 0, 0, -1), (3, 0, -1, 1, 0)):
                        o_ = seg(ucc, blk, lo_o, hi_o); i_ = seg(uraw[:], blk, lo_i, hi_i)
                        P.I("dve", "scalar_tensor_tensor", reads=["rg_uraw", "rg_uc", "rg_cw"], writes=["rg_uc"], out=o_, in0=i_,
                            scalar=cw[:, k, c:c + 1], in1=o_, op0=ALU.mult, op1=ALU.add)
                    P.I("act", "activation", reads=["rg_uc"], writes=["rg_ucb"], out=ucb[:, cc, :], in_=ucc, func=AF.Copy)
                for oc in range(2):
                    c = 2 * hh + oc
                    xb = xbs[oc]
                    for d in range(2):
                        xin = xb[d]; xk = f"rg_x{oc}{d}"; abuf = abufs[d]; ak = f"rg_a{d}"
                        for g2 in range(2):
                            ls = slice(g2 * 512, (g2 + 1) * 512)
                            pr, prk = nps(); pq, pqk = nps()
                            mmK(P, pr[:], [(gv[:, kc, d, oc * 128:(oc + 1) * 128], ucb[:, kc, ls]) for kc in range(2)], reads=[gk, "rg_ucb"], writes=[prk])
                            mmK(P, pq[:], [(gv[:, kc, d, 256 + oc * 128:256 + (oc + 1) * 128], ucb[:, kc, ls]) for kc in range(2)], reads=[gk, "rg_ucb"], writes=[pqk])
                            P.I("act", "activation", reads=[prk, "rg_bg"], writes=[ak], out=abuf[:, ls], in_=pr[:], func=AF.Tanh,
                                bias=bg[:, d, hh, oc:oc + 1], scale=0.5)
                            P.I("act", "activation", reads=[ak, "rg_clam"], writes=[ak], out=abuf[:, ls], in_=abuf[:, ls], func=AF.Exp,
                                scale=clam[:, d, c:c + 1], bias=clam[:, d, c:c + 1])
                            P.I("act", "activation", reads=[pqk, "rg_bg"], writes=[xk], out=xin[:, ls], in_=pq[:], func=AF.Tanh,
                                bias=bg[:, d, hh, 2 + oc:3 + oc], scale=0.5)
                            sq = sqt[d * 2 + g2]; sk = f"rg_sq{d * 2 + g2}"
                            P.I("act", "activation", reads=[ak], writes=[sk], out=sq[:], in_=abuf[:, ls], func=AF.Square)
                        for g2 in range(2):
                            ls = slice(g2 * 512, (g2 + 1) * 512)
                            sq = sqt[d * 2 + g2]; sk = f"rg_sq{d * 2 + g2}"
                            P.I("act", "activation", reads=[sk, "rg_one"], writes=[sk], out=sq[:], in_=sq[:], func=AF.Sqrt, bias=one[:, 0:1], scale=-1.0)
                            P.I("dve", "scalar_tensor_tensor", reads=[xk, sk], writes=[xk], out=xin[:, ls], in0=xin[:, ls], scalar=1.0, in1=sq[:],
                                op0=ALU.add, op1=ALU.mult)
                            P.I("dve", "tensor_tensor", reads=[xk, "rg_uc"], writes=[xk], out=xin[:, ls], in0=xin[:, ls], in1=uc[:, oc, ls], op=ALU.mult)
                        seqs = [(s * 256, 256) for s in range(4)] if blk == 0 else [(0, 1024)]
                        for (o0, L) in seqs:
                            sl = slice(o0, o0 + L) if d == 0 else slice(o0 + L - 1, (o0 - 1) if o0 > 0 else None, -1)
                            init = 0.0 if blk == 0 else h0[:, d, c:c + 1]
                            P.I("dve", "tensor_tensor_scan", reads=[ak, xk, "rg_h0"], writes=[xk], out=xin[:, sl], data0=abuf[:, sl],
                                data1=xin[:, sl], initial=init, op0=ALU.mult, op1=ALU.add)
                        if blk == 0:
                            x3 = xin[:].rearrange("p (s t) -> p s t", s=4)
                            src = x3[:, :, 255:256] if d == 0 else x3[:, :, 0:1]
                            dst = rgst[:, c, :].rearrange("p (s d) -> p s d", d=2)[:, :, d:d + 1]
                            P.I("dve", "tensor_scalar_mul", reads=[xk], writes=["rg_state"], out=dst, in0=src, scalar1=0.5)
                    P.I("dve", "tensor_tensor", reads=[f"rg_x{oc}0", f"rg_x{oc}1"], writes=[f"rg_x{oc}0"], out=xb[0][:], in0=xb[0][:], in1=xb[1][:], op=ALU.add)
                    for g2 in range(2):
                        cs = slice(blk * 1024 + g2 * 512, blk * 1024 + (g2 + 1) * 512); ls = slice(g2 * 512, (g2 + 1) * 512)
                        pt, pk = nps()
                        mmK(P, pt[:], [(wv[:, kc, 2 + oc, :], C.hT[:, kc, cs]) for kc in range(8)], reads=[wk, f"hT{blk}"], writes=[pk])
                        P.I("act", "activation", reads=[pk], writes=[f"rg_sg{g2}"], out=sgS[g2][:], in_=pt[:], func=AF.Tanh, scale=0.5)
                        P.I("dve", "scalar_tensor_tensor", reads=[f"rg_sg{g2}", pk], writes=[f"rg_sg{g2}"], out=sgS[g2][:], in0=sgS[g2][:], scalar=1.0, in1=pt[:],
                            op0=ALU.add, op1=ALU.mult)
                        P.I("dve", "scalar_tensor_tensor", reads=[f"rg_sg{g2}", f"rg_x{oc}0"], writes=[f"mT{blk * 2 + g2}"], out=C.mT[:, c, cs], in0=sgS[g2][:],
                            scalar=0.25, in1=xb[0][:, ls], op0=ALU.mult, op1=ALU.mult)
        srow = uraw[0:8, :]
        for half in range(2):
            pt, pk = nps()
            P.G("pe", [("transpose", dict(out=pt[0:8, i * 128:(i + 1) * 128], in_=rgst[:, half * 4 + i, :], identity=C.identf[:])) for i in range(4)],
                reads=["rg_state", "identf"], writes=[pk])
            P.I("dve", "tensor_copy", reads=[pk], writes=["rg_uraw"], out=srow[:, half * 512:(half + 1) * 512], in_=pt[0:8, :])
        P.D("sp", "o_rg", C.o_rg[:, :], srow, reads=["rg_uraw"])
SM_SCALE = 96.0 ** -0.5

def mla_layer(P, C):
    with ExitStack() as es:
        qn = P.sb("ml_qn", [128, 3], F32, es); kvn = P.sb("ml_kvn", [128, 2], F32, es)
        eps6 = P.sb("ml_eps", [128, 1], F32, es)
        rope = P.sb("ml_rope", [128, 2, 1024], F32, es)
        cqnT = P.sb("ml_cqnT", [128, 3, 1024], BF16, es)
        ckvnT = P.sb("ml_ckvnT", [128, 2, 1536], BF16, es)
        KK = P.sb("ml_KK", [128, 1536], BF16, es)
        KK2 = P.sb("ml_KK2", [128, 1536], BF16, es)
        P.D("sp", "ml_qn", qn[:], C.mla_qn[:, :], writes=["ml_qn"])
        P.D("sp", "ml_kvn", kvn[:], C.mla_kvn[:, :], writes=["ml_kvn"])
        P.D("sp", "ml_rope", rope[64:96], C.rope_cs[64:96, :, :], writes=["ml_rope"])
        P.I("dve", "memset", writes=["ml_eps"], ap=eps6[:], constant=1e-6)
        wv_all = C.mla_w_in.rearrange("(k p) n -> p k n", p=128)
        wqb_all = C.mla_w_qb.rearrange("(k p) n -> p k n", p=128)
        wqs_all = C.mla_w_qbsw.rearrange("(k p) n -> p k n", p=128)
        wkvb_all = C.mla_w_kvb.rearrange("(k p) n -> p k n", p=128)
        wks_all = C.mla_w_kpesw.rearrange("(k p) n -> p k n", p=128)
        load_wout(P, C, C.mla_w_out)
        for blk in range(2):
            nkeys = 1024 if blk == 0 else 1536
            with ExitStack() as esA:
                sq = [P.sb(f"ml_sq{i}", [128, 512], BF16, esA) for i in range(2)]
                rstd = P.sb("ml_rstd", [128, 512], F32, esA)
                ckvf = P.sb("ml_ckvf", [128, 2, 512], F32, esA)
                t1 = P.sb("ml_t1", [128, 256], F32, esA); t2 = P.sb("ml_t2", [128, 256], F32, esA)
                stg = [P.sb(f"ml_stg{i}", [128, 256], F32, esA) for i in range(2)]
                stk = P.sb("ml_stk", [128, 32], F32, esA); stkb = P.sb("ml_stkb", [128, 32], BF16, esA)
                (wcq,), wcqk = C.ring.load_parts([wv_all[:, :, 0:384]])
                (wkv, wks), wkvk = C.ring.load_parts([wv_all[:, :, 384:672], wks_all[:, :, :]])
                for gq in range(2):
                    cs = slice(blk * 1024 + gq * 512, blk * 1024 + (gq + 1) * 512); ls = slice(gq * 512, (gq + 1) * 512)
                    hk = f"hT{blk}"
                    for c in range(3):
                        mmK(P, C.ps[c][:], [(wcq[:, kc, c * 128:(c + 1) * 128], C.hT[:, kc, cs]) for kc in range(8)], reads=[wcqk, hk], writes=[f"ps{c}"])
                        P.I("act", "activation", reads=[f"ps{c}"], writes=[f"ml_sq{c % 2}"], out=sq[c % 2][:], in_=C.ps[c][:], func=AF.Square)
                        P.I("pe", "matmul", reads=[f"ml_sq{c % 2}", "onesb"], writes=["ps3"], out=C.ps[3][:], lhsT=C.onesb[:], rhs=sq[c % 2][:],
                            start=(c == 0), stop=(c == 2))
                    P.I("act", "activation", reads=["ps3", "ml_eps"], writes=["ml_rstd"], out=rstd[:], in_=C.ps[3][:], func=AF.Sqrt, bias=eps6[:, 0:1], scale=1.0 / 384.0)
                    P.I("dve", "reciprocal", reads=["ml_rstd"], writes=["ml_rstd"], out=rstd[:], in_=rstd[:])
                    for c in range(3):
                        P.I("dve", "scalar_tensor_tensor", reads=[f"ps{c}", "ml_rstd", "ml_qn"], writes=["ml_cqnT"], out=cqnT[:, c, ls], in0=C.ps[c][:],
                            scalar=qn[:, c:c + 1], in1=rstd[:], op0=ALU.mult, op1=ALU.mult)
                    for c in range(2):
                        mmK(P, C.ps[4 + c][:], [(wkv[:, kc, c * 128:(c + 1) * 128], C.hT[:, kc, cs]) for kc in range(8)], reads=[wkvk, hk], writes=[f"ps{4 + c}"])
                        P.I("act", "activation", reads=[f"ps{4 + c}"], writes=[f"ml_sq{c % 2}"], out=sq[c % 2][:], in_=C.ps[4 + c][:], func=AF.Square)
                        P.I("pe", "matmul", reads=[f"ml_sq{c % 2}", "onesb"], writes=["ps6"], out=C.ps[6][:], lhsT=C.onesb[:], rhs=sq[c % 2][:],
                            start=(c == 0), stop=(c == 1))
                    P.I("act", "activation", reads=["ps6", "ml_eps"], writes=["ml_rstd"], out=rstd[:], in_=C.ps[6][:], func=AF.Sqrt, bias=eps6[:, 0:1], scale=1.0 / 256.0)
                    P.I("dve", "reciprocal", reads=["ml_rstd"], writes=["ml_rstd"], out=rstd[:], in_=rstd[:])
                    for c in range(2):
                        P.I("dve", "scalar_tensor_tensor", reads=[f"ps{4 + c}", "ml_rstd", "ml_kvn"], writes=["ml_ckvf"], out=ckvf[:, c, :], in0=C.ps[4 + c][:],
                            scalar=kvn[:, c:c + 1], in1=rstd[:], op0=ALU.mult, op1=ALU.mult)
                    P.I("act", "activation", reads=["ml_ckvf"], writes=["ml_ckvnT"], out=ckvnT[:, :, ls], in_=ckvf[:], func=AF.Copy)
                    mmK(P, C.ps[7][64:96, :], [(wkv[:, kc, 256:288], C.hT[:, kc, cs]) for kc in range(8)], reads=[wkvk, hk], writes=["ps7"])
                    if blk == 0:
                        P.I("act", "activation", reads=["ps7"], writes=["ml_KKpe"], out=KK[64:96, ls], in_=C.ps[7][64:96, :], func=AF.Copy)
                    else:
                        mmK(P, C.ps[3][64:96, :], [(wks[:, kc, :], C.hT[:, kc, cs]) for kc in range(8)], reads=[wkvk, hk], writes=["ps3"])
                        for hq in range(2):
                            l2 = slice(hq * 256, (hq + 1) * 256); pos = slice(gq * 512 + hq * 256, gq * 512 + (hq + 1) * 256)
                            P.I("dve", "tensor_tensor", reads=["ps7", "ml_rope"], writes=["ml_t1"], out=t1[64:96, :], in0=C.ps[7][64:96, l2], in1=rope[64:96, 0, pos], op=ALU.mult)
                            P.I("dve", "tensor_tensor", reads=["ps3", "ml_rope"], writes=["ml_t2"], out=t2[64:96, :], in0=C.ps[3][64:96, l2], in1=rope[64:96, 1, pos], op=ALU.mult)
                            P.I("dve", "tensor_tensor", reads=["ml_t1", "ml_t2"], writes=["ml_KKpe"], out=KK[64:96, pos], in0=t1[64:96, :], in1=t2[64:96, :], op=ALU.add)
                    if blk == 0:
                        for tl in range(4):
                            gt = gq * 4 + tl; b = tl % 2
                            P.G("pe", [("transpose", dict(out=C.ps[2][:, c * 128:(c + 1) * 128], in_=ckvf[:, c, tl * 128:(tl + 1) * 128], identity=C.identf[:]))
                                       for c in range(2)], reads=["ml_ckvf", "identf"], writes=["ps2"])
                            P.I("dve", "tensor_copy", reads=["ps2"], writes=[f"ml_stg{b}"], out=stg[b][:], in_=C.ps[2][:, 0:256])
                            P.D("sp", f"o_ckv{b}", C.o_ckv[gt * 128:(gt + 1) * 128, :], stg[b][:], reads=[f"ml_stg{b}"])
                            mmK(P, C.ps[2][:, 256:288], [(C.hT[:, kc, gt * 128:(gt + 1) * 128], wkv[:, kc, 256:288]) for kc in range(8)], reads=[wkvk, hk], writes=["ps2"])
                            P.I("dve", "tensor_copy", reads=["ps2"], writes=["ml_stk"], out=stk[:], in_=C.ps[2][:, 256:288])
                            P.D("sp", "o_kpe", C.o_kpe[gt * 128:(gt + 1) * 128, :], stk[:], reads=["ml_stk"])
                if blk == 1:
                    for tl in range(4):
                        b = tl % 2
                        P.D("sp", f"ml_stg{b}", stg[b][:], C.mla_ckv_ctx[tl * 128:(tl + 1) * 128, :], writes=[f"ml_stg{b}"])
                        P.G("pe", [("transpose", dict(out=C.ps[2][:, c * 128:(c + 1) * 128], in_=stg[b][:, c * 128:(c + 1) * 128], identity=C.identf[:]))
                                   for c in range(2)], reads=[f"ml_stg{b}", "identf"], writes=["ps2"])
                        P.I("act", "activation", reads=["ps2"], writes=["ml_ckvnT"], out=ckvnT[:, :, 1024 + tl * 128:1024 + (tl + 1) * 128],
                            in_=C.ps[2][:, 0:256].rearrange("p (c t) -> p c t", c=2), func=AF.Copy)
                        P.D("sp", "ml_stk", stk[:], C.mla_kpe_ctx[tl * 128:(tl + 1) * 128, :], writes=["ml_stk"])
                        P.I("dve", "tensor_copy", reads=["ml_stk"], writes=["ml_stkb"], out=stkb[:], in_=stk[:])
                        psb = C.ps[2][:].bitcast(BF16)
                        P.I("pe", "transpose", reads=["ml_stkb", "identb"], writes=["ps2"], out=psb[64:96, 512:640], in_=stkb[:], identity=C.identb[:])
                        P.I("dve", "tensor_copy", reads=["ps2"], writes=["ml_KKpe"], out=KK[64:96, 1024 + tl * 128:1024 + (tl + 1) * 128], in_=psb[64:96, 512:640])
                P.I("dve", "tensor_copy", reads=["ml_KKpe"], writes=["ml_KKpe2"], out=KK2[64:96, 0:nkeys], in_=KK[64:96, 0:nkeys])
                P.barrier()
            with ExitStack() as esB:
                KKs = [KK, KK2]
                Vh = [P.sb(f"ml_Vh{i}", [128, 12, 65], BF16, esB) for i in range(2)]
                QQ = [P.sb(f"ml_QQ{i}", [128, 1024], BF16, esB) for i in range(2)]
                PT = [P.sb(f"ml_PT{i}", [128, 512], BF16, esB) for i in range(2)]
                sgT = [P.sb(f"ml_sgT{i}", [128, 1024], BF16, esB) for i in range(2)]
                on2 = P.sb("ml_on2", [128, 8, 128], BF16, esB)
                rcp = P.sb("ml_rcp", [128, 4], F32, esB)
                u1 = P.sb("ml_u1", [128, 256], F32, esB); u2 = P.sb("ml_u2", [128, 256], F32, esB)
                tg = P.sb("ml_tg", [128, 512], F32, esB)
                for i in range(2):
                    P.I("dve", "memset", writes=[f"ml_Vh{i}"], ap=Vh[i][:, :, 64:65], constant=1.0)
                nkt = nkeys // 128
                units = [(slice(s_ * 256, (s_ + 1) * 256), [2 * s_, 2 * s_ + 1]) for s_ in range(4)] if blk == 0 else \
                        [(slice(g_ * 512, (g_ + 1) * 512), list(range(12))) for g_ in range(2)]
                sc_i = [0]
                hk = f"hT{blk}"
                W = {}

                def proj(h):
                    hp, hh = h // 2, h % 2; bsel = h % 2
                    if hh == 0:
                        W[hp] = C.ring.load_parts([wqb_all[:, :, hp * 192:(hp + 1) * 192], wqs_all[:, :, hp * 64:(hp + 1) * 64],
                                                   wkvb_all[:, :, hp * 256:(hp + 1) * 256], wv_all[:, :, 672 + hp * 128:672 + (hp + 1) * 128]])
                    (wqb, wqs, wkb, wg), wk = W[hp]
                    if hh == 0:
                        for g2 in range(2):
                            cs = slice(blk * 1024 + g2 * 512, blk * 1024 + (g2 + 1) * 512); ls = slice(g2 * 512, (g2 + 1) * 512)
                            mmK(P, C.ps[6][:], [(wg[:, kc, :], C.hT[:, kc, cs]) for kc in range(8)], reads=[wk, hk], writes=["ps6"])
                            P.I("act", "activation", reads=["ps6"], writes=["ml_tg"], out=tg[:], in_=C.ps[6][:], func=AF.Tanh, scale=0.5)
                            P.I("dve", "scalar_tensor_tensor", reads=["ml_tg", "ps6"], writes=[f"ml_sgT{hp % 2}"], out=sgT[hp % 2][:, ls], in0=tg[:], scalar=1.0,
                                in1=C.ps[6][:], op0=ALU.add, op1=ALU.mult)
                            yield
                    KKb = KKs[bsel]
                    for kg in range(nkeys // 512):
                        ks = slice(kg * 512, (kg + 1) * 512)
                        mmK(P, C.ps[6][0:64, :], [(wkb[:, kc, hh * 128:hh * 128 + 64], ckvnT[:, kc, ks]) for kc in range(2)], reads=[wk, "ml_ckvnT"], writes=["ps6"])
                        P.I("dve", "tensor_copy", reads=["ps6"], writes=[f"ml_KKn{bsel}"], out=KKb[0:64, ks], in_=C.ps[6][0:64, :])
                        yield
                    for k8 in range((nkt + 7) // 8):
                        n8 = min(8, nkt - k8 * 8)
                        P.G("pe", [("matmul", dict(out=C.ps[7][:, j * 64:(j + 1) * 64], lhsT=ckvnT[:, kc, (k8 * 8 + j) * 128:(k8 * 8 + j + 1) * 128],
                                                   rhs=wkb[:, kc, hh * 128 + 64:hh * 128 + 128], start=(kc == 0), stop=(kc == 1)))
                                   for j in range(n8) for kc in range(2)], reads=[wk, "ml_ckvnT"], writes=["ps7"])
                        P.I("dve", "tensor_copy", reads=["ps7"], writes=[f"ml_Vh{bsel}"], out=Vh[bsel][:, k8 * 8:k8 * 8 + n8, 0:64],
                            in_=C.ps[7][:, 0:n8 * 64].rearrange("p (a b) -> p a b", a=n8))
                        yield
                    Qb = QQ[bsel]
                    for g2 in range(2):
                        ls = slice(g2 * 512, (g2 + 1) * 512)
                        mmK(P, C.ps[6][0:64, :], [(wqb[:, kc, hh * 96:hh * 96 + 64], cqnT[:, kc, ls]) for kc in range(3)], reads=[wk, "ml_cqnT"], writes=["ps6"])
                        P.I("dve", "tensor_copy", reads=["ps6"], writes=[f"ml_QQn{bsel}"], out=Qb[0:64, ls], in_=C.ps[6][0:64, :])
                        yield
                        mmK(P, C.ps[7][64:96, :], [(wqb[:, kc, hh * 96 + 64:hh * 96 + 96], cqnT[:, kc, ls]) for kc in range(3)], reads=[wk, "ml_cqnT"], writes=["ps7"])
                        if blk == 0:
                            P.I("dve", "tensor_copy", reads=["ps7"], writes=[f"ml_QQr{bsel}"], out=Qb[64:96, ls], in_=C.ps[7][64:96, :])
                        else:
                            mmK(P, C.ps[6][64:96, :], [(wqs[:, kc, hh * 32:(hh + 1) * 32], cqnT[:, kc, ls]) for kc in range(3)], reads=[wk, "ml_cqnT"], writes=["ps6"])
                            for hq in range(2):
                                l2 = slice(hq * 256, (hq + 1) * 256); pos = slice(g2 * 512 + hq * 256, g2 * 512 + (hq + 1) * 256)
                                P.I("dve", "tensor_tensor", reads=["ps7", "ml_rope"], writes=["ml_u1"], out=u1[64:96, :], in0=C.ps[7][64:96, l2], in1=rope[64:96, 0, pos], op=ALU.mult)
                                P.I("dve", "tensor_tensor", reads=["ps6", "ml_rope"], writes=["ml_u2"], out=u2[64:96, :], in0=C.ps[6][64:96, l2], in1=rope[64:96, 1, pos], op=ALU.mult)
                                P.I("dve", "tensor_tensor", reads=["ml_u1", "ml_u2"], writes=[f"ml_QQr{bsel}"], out=Qb[64:96, pos], in0=u1[64:96, :], in1=u2[64:96, :], op=ALU.add)
                        yield

                def attn(h):
                    hh = h % 2; bsel = h % 2
                    KKb = KKs[bsel]; Qb = QQ[bsel]; Vb = Vh[bsel]
                    kkeys = [f"ml_KKn{bsel}", "ml_KKpe" if bsel == 0 else "ml_KKpe2", f"ml_QQn{bsel}", f"ml_QQr{bsel}"]
                    for (qsl, kts) in units:
                        nq = qsl.stop - qsl.start; nqt = nq // 128; qt0 = qsl.start // 128
                        def qk(kt):
                            sb_ = sc_i[0] % 2; sc_i[0] += 1
                            P.I("pe", "matmul", reads=kkeys, writes=[f"ps{sb_}"], out=C.ps[sb_][:, 0:nq],
                                lhsT=KKb[0:96, kt * 128:(kt + 1) * 128], rhs=Qb[0:96, qsl], start=True, stop=True)
                            P.I("act", "activation", reads=[f"ps{sb_}"], writes=[f"ml_PT{sb_}"], out=PT[sb_][:, 0:nq], in_=C.ps[sb_][:, 0:nq], func=AF.Exp, scale=SM_SCALE)
                            return sb_
                        pend = qk(kts[0])
                        for ki, kt in enumerate(kts):
                            pi_ = pend
                            if ki + 1 < len(kts):
                                pend = qk(kts[ki + 1])
                            for qt in range(nqt):
                                P.I("pe", "matmul", reads=[f"ml_PT{pi_}", f"ml_Vh{bsel}"], writes=[f"ps{2 + qt}"], out=C.ps[2 + qt][:, 0:65],
                                    lhsT=PT[pi_][:, qt * 128:(qt + 1) * 128], rhs=Vb[:, kt, :], start=(ki == 0), stop=(ki == len(kts) - 1))
                            yield
                        for qt in range(nqt):
                            P.I("dve", "reciprocal", reads=[f"ps{2 + qt}"], writes=["ml_rcp"], out=rcp[:, qt:qt + 1], in_=C.ps[2 + qt][:, 64:65])
                            P.I("dve", "tensor_scalar", reads=[f"ps{2 + qt}", "ml_rcp"], writes=["ml_on2"], out=on2[:, qt0 + qt, hh * 64:(hh + 1) * 64],
                                in0=C.ps[2 + qt][:, 0:64], scalar1=rcp[:, qt:qt + 1], scalar2=None, op0=ALU.mult)
                        yield

                def finish_pair(hp):
                    psb = C.ps[7][:].bitcast(BF16)
                    for g2 in range(2):
                        cs = slice(blk * 1024 + g2 * 512, blk * 1024 + (g2 + 1) * 512); ls = slice(g2 * 512, (g2 + 1) * 512)
                        P.G("pe", [("transpose", dict(out=psb[:, i * 128:(i + 1) * 128], in_=on2[:, g2 * 4 + i, :], identity=C.identb[:])) for i in range(4)],
                            reads=["ml_on2", "identb"], writes=["ps7"])
                        P.I("dve", "scalar_tensor_tensor", reads=["ps7", f"ml_sgT{hp % 2}"], writes=[f"mT{blk * 2 + g2}"], out=C.mT[:, hp, cs], in0=psb[:, 0:512],
                            scalar=0.5, in1=sgT[hp % 2][:, ls], op0=ALU.mult, op1=ALU.mult)

                _roundrobin([proj(0)])
                for h in range(16):
                    gens = [attn(h)]
                    if h + 1 < 16:
                        gens.append(proj(h + 1))
                    _roundrobin(gens)
                    if h % 2 == 1:
                        finish_pair(h // 2)
                P.barrier()
LAYERS = (0, 1, 2, 3)
_CACHE = {}

def build_nc(layers):
    nc = bass.Bass("TRN2", target_bir_lowering=False)
    C = Ctx()
    declare_io(nc, C)
    with ExitStack() as es:
        P = Prog(nc, es)
        setup_persistent(P, C)
        ada_phase(P, C, layers[0])
        input_transposes(P, C)
        for li, l in enumerate(layers):
            modulate_phase(P, C, l)
            [hgrn_layer, sconv_layer, rglru_layer, mla_layer][l % 4](P, C)
            pre = wout_prefetch(P, C)
            P.barrier()
            wout_ln_phase(P, C, l, pre, next_ada=(layers[li + 1] if li + 1 < len(layers) else None))
        output_transposes(P, C)
        outs = [n for n in P.dall if n.startswith(("yout", "o_hg", "o_rg", "o_ckv", "o_kpe"))]
        P.final_wait("sp", outs)
        with nc.Block() as block:
            P.emit(block)
        C.counts = dict(P.cnt); print('instr counts', C.counts, 'nsem', len(P.esems) + sum(len(v) for v in P.dall.values()))
    return nc, C

def colT(v, nch):
    return np.ascontiguousarray(np.asarray(v, np.float32).reshape(nch, 128).T)

def make_in_maps(I):
    f = lambda a: np.ascontiguousarray(np.asarray(a, np.float32))
    half = 8; r = np.arange(1024) // 64; cpos = np.arange(1024) % 64
    inv = (10000.0 ** (-np.arange(0, 16, 2, dtype=np.float32) / 16)).astype(np.float32)
    cs = np.zeros((128, 2, 1024), np.float32)
    for part, pos in ((0, r), (1, cpos)):
        ang = pos[None, :].astype(np.float32) * inv[:, None]
        co, si = np.cos(ang), np.sin(ang)
        b = 64 + part * 16
        cs[b:b + 8, 0] = co; cs[b + 8:b + 16, 0] = co
        cs[b:b + 8, 1] = -si; cs[b + 8:b + 16, 1] = si
    s_ = np.arange(128)[:, None]; t_ = np.arange(128)[None, :]
    same = (s_ // 64) == (t_ // 64)
    masks = np.zeros((128, 4, 128), np.float32)
    masks[:, 0] = same & (s_ <= t_); masks[:, 1] = same & (s_ >= t_)
    masks[:, 2] = np.tile((np.arange(128) % 64 != 0).astype(np.float32), (128, 1))
    masks[:, 3] = np.tile((np.arange(128) % 64 != 63).astype(np.float32), (128, 1))
    swap = np.arange(32).reshape(2, 2, 8)[:, ::-1, :].reshape(-1)
    wqb = f(I['mla_w_qb'][0])
    qb_r = wqb.reshape(384, 16, 96)[:, :, 64:]
    shared = {
        'ada_w': f(I['ada_w']), 'ada_bT': np.ascontiguousarray(f(I['ada_b']).reshape(4, 24, 128).transpose(2, 0, 1)),
        'ln_gT': np.ascontiguousarray(f(I['ln_g']).reshape(4, 8, 128).transpose(2, 0, 1)),
        'ln_bT': np.ascontiguousarray(f(I['ln_b']).reshape(4, 8, 128).transpose(2, 0, 1)),
        'ident': np.eye(128, dtype=np.float32),
        'sc_w_in': f(I['sc_w_in'][0]), 'sc_cw': np.ascontiguousarray(f(I['sc_conv_w'][0]).reshape(3, 8, 128).transpose(2, 0, 1)),
        'sc_cb': colT(I['sc_conv_b'][0], 8), 'sc_w_out': f(I['sc_w_out'][0]),
        'rg_w_in': f(I['rg_w_in'][0]), 'rg_cw': np.ascontiguousarray(f(I['rg_conv_w'][0]).reshape(4, 8, 128).transpose(2, 0, 1)),
        'rg_cb': colT(I['rg_conv_b'][0], 8), 'rg_w_gate': f(I['rg_w_gate'][0]),
        'rg_bg': np.ascontiguousarray(f(I['rg_b_gate'][0]).reshape(2, 4, 4, 128).transpose(3, 0, 1, 2)),
        'rg_lam': np.ascontiguousarray(f(I['rg_lambda'][0]).reshape(2, 8, 128).transpose(2, 0, 1)),
        'rg_w_out': f(I['rg_w_out'][0]),
        'hg_w_in': f(I['hg_w_in'][0]),
        'hg_lbl': np.ascontiguousarray(f(I['hg_lb_logits']).reshape(2, 5, 8, 128).transpose(3, 0, 2, 1).reshape(128, 16, 5)),
        'hg_ng': colT(I['hg_norm_g'][0], 8), 'hg_w_out': f(I['hg_w_out'][0]), 'hg_masks': masks,
        'mla_w_in': f(I['mla_w_in'][0]), 'mla_qn': colT(I['mla_q_norm'][0], 3), 'mla_kvn': colT(I['mla_kv_norm'][0], 2),
        'mla_w_qb': wqb, 'mla_w_qbsw': np.ascontiguousarray(qb_r[:, :, swap].reshape(384, 512)),
        'mla_w_kpesw': np.ascontiguousarray(f(I['mla_w_in'][0])[:, 640:672][:, swap]),
        'mla_w_kvb': f(I['mla_w_kvb'][0]), 'mla_w_out': f(I['mla_w_out'][0]), 'rope_cs': cs,
    }
    maps = []
    xp = f(I['x_prompt']); xs = f(I['x_sample'])
    for cid in range(8):
        k = cid // 2
        m = dict(shared)
        m['xin'] = np.ascontiguousarray(np.concatenate([xp[4 * cid:4 * cid + 4].reshape(1024, D), xs[k]], 0))
        cond = np.stack([f(I['c_ctx']), f(I['c'])[k]], 1)
        m['condT'] = np.ascontiguousarray(cond.reshape(8, 128, 2).transpose(1, 0, 2))
        m['rg_h0'] = np.ascontiguousarray(f(I['state_rglru'])[k, 0].reshape(2, 8, 128).transpose(2, 0, 1))
        m['hg_s0'] = f(I['state_hgrn'])[k, 0]
        m['mla_ckv_ctx'] = f(I['cache_mla_ckv'])[k, 0]; m['mla_kpe_ctx'] = f(I['cache_mla_kpe'])[k, 0]
        maps.append(m)
    return maps

def run_layers(I, layers, trace=False):
    key = tuple(layers)
    if key not in _CACHE:
        _CACHE[key] = build_nc(layers)
    nc, C = _CACHE[key]
    maps = make_in_maps(I)
    res = run_bass_kernel_spmd(nc, maps, core_ids=list(range(8)), trace=trace)
    R = res.results
    y_prompt = np.concatenate([R[c]['y'][:1024].reshape(4, 256, D) for c in range(8)], 0)
    y_sample = np.stack([R[2 * k]['y'][1024:] for k in range(4)], 0)
    o_hg = np.concatenate([R[c]['o_hg'] for c in range(8)], 0)[:, None]
    o_rg = np.concatenate([R[c]['o_rg'].reshape(4, 2, D) for c in range(8)], 0)[:, None]
    o_ckv = np.concatenate([R[c]['o_ckv'].reshape(4, 256, 256) for c in range(8)], 0)[:, None]
    o_kpe = np.concatenate([R[c]['o_kpe'].reshape(4, 256, 32) for c in range(8)], 0)[:, None]
    outs = tuple(np.ascontiguousarray(a, dtype=np.float32) for a in (y_prompt, y_sample, o_hg, o_rg, o_ckv, o_kpe))
    return outs, res

def kernel(**inputs):
    outs, _ = run_layers(inputs, LAYERS)
    return outs
```

```python
import numpy as np
import concourse.bass as bass
import concourse.mybir as mybir
from concourse.bass_utils import run_bass_kernel_spmd
from contextlib import ExitStack
import numpy as np
import concourse.bass as bass
import concourse.mybir as mybir
from concourse.bass_utils import run_bass_kernel_spmd
from contextlib import ExitStack
F32 = mybir.dt.float32; BF16 = mybir.dt.bfloat16
AF = mybir.ActivationFunctionType; ALU = mybir.AluOpType
AX = mybir.AxisListType

D = 1024; T = 2048; NT = 16; ALPHA = 8.0 ** 0.25
LN_EPS = 1e-5 / (ALPHA * ALPHA)

class Prog:
    ENGS = ("pe", "act", "dve", "pool", "sp")
    SEM_M = 1000
    DSEM_MAX = 1600
    def __init__(self, nc, es):
        self.nc = nc; self.es = es
        self.q = {e: [] for e in self.ENGS}
        self.cnt = {e: 0 for e in self.ENGS}
        self.esems = {}
        self.seen = {e: {} for e in self.ENGS}
        self.lastw = {}; self.readers = {}
        self.dsems = {}
        self.dall = {}
        self.semh = {}
    def sb(self, name, shape, dt, es=None):
        self._names = getattr(self, "_names", {})
        n = self._names.get(name, 0); self._names[name] = n + 1
        if n: name = f"{name}__{n}"
        return (es or self.es).enter_context(self.nc.sbuf_tensor(name, list(shape), dt))
    def ps(self, name, shape, dt):
        return self.es.enter_context(self.nc.psum_tensor(name, list(shape), dt))
    def esem(self, eng, epoch):
        k = (eng, epoch)
        if k not in self.esems:
            self.esems[k] = self.es.enter_context(self.nc.semaphore(f"s_{eng}_{epoch}"))
        return self.esems[k]
    def dsem(self, name):
        d = self.dsems.get(name)
        if d is None or d[1] + 16 > self.DSEM_MAX:
            ep = 0 if d is None else d[2] + 1
            h = self.es.enter_context(self.nc.semaphore(f"d_{name}_{ep}"))
            d = [h, 0, ep]
            self.dsems[name] = d
            self.semh[f"d_{name}#{ep}"] = h
            self.dall.setdefault(name, []).append(d)
        return d
    def _handle(self, sk, val):
        if sk in self.ENGS:
            ep = (val - 1) // self.SEM_M
            return (self.esem(sk, ep), val - ep * self.SEM_M)
        return (self.semh[sk], val)
    def _need(self, eng, waits, dep):
        if dep is None: return
        sk, val = dep
        if sk == "pe" and eng == "pe": return
        if self.seen[eng].get(sk, 0) >= val: return
        waits[sk] = max(waits.get(sk, 0), val)
    def _deps(self, eng, reads, writes):
        waits = {}
        for k in reads: self._need(eng, waits, self.lastw.get(k))
        for k in writes:
            self._need(eng, waits, self.lastw.get(k))
            for sk, v in self.readers.get(k, {}).items(): self._need(eng, waits, (sk, v))
        for sk, v in waits.items(): self.seen[eng][sk] = v
        return [self._handle(sk, v) for sk, v in waits.items()]
    def _mark(self, dep, reads, writes):
        for k in writes:
            self.lastw[k] = dep; self.readers[k] = {}
        for k in reads:
            self.readers.setdefault(k, {})[dep[0]] = dep[1]
    def op(self, eng, fns, reads=(), writes=()):
        writes = list(writes) + [k for k in reads if k.startswith("ps") and k not in writes]
        waits = self._deps(eng, reads, writes)
        self.cnt[eng] += 1
        idx = self.cnt[eng]
        h, _ = self._handle(eng, idx)
        self.q[eng].append((fns, waits, (h, 1)))
        self._mark((eng, idx), reads, writes)
    @staticmethod
    def _mk(method, kw):
        def fn(e):
            return getattr(e, method)(**kw)
        return fn
    def I(self, eng, method, reads=(), writes=(), **kw):
        self.op(eng, [self._mk(method, kw)], reads, writes)
    def G(self, eng, items, reads=(), writes=()):
        self.op(eng, [self._mk(m, kw) for (m, kw) in items], reads, writes)
    def D(self, queue, semname, out, in_, reads=(), writes=(), **kw):
        waits = self._deps(queue, reads, writes)
        d = self.dsem(semname)
        d[1] += 16
        self.q[queue].append(([self._mk("dma_start", dict(out=out, in_=in_, **kw))], waits, (d[0], 16)))
        self._mark((f"d_{semname}#{d[2]}", d[1]), reads, writes)
    def barrier(self, engs=("pe", "act", "dve", "pool", "sp")):
        targets = [(e, self.cnt[e]) for e in ("pe", "act", "dve", "pool") if self.cnt[e] > 0]
        for n, lst in self.dall.items():
            for d in lst:
                if d[1] > 0: targets.append((f"d_{n}#{d[2]}", d[1]))
        for e in engs:
            waits = {}
            for dep in targets:
                if dep[0] == e and e == "pe": continue
                if self.seen[e].get(dep[0], 0) >= dep[1]: continue
                waits[dep[0]] = dep[1]; self.seen[e][dep[0]] = dep[1]
            if waits:
                self.q[e].append((None, [self._handle(sk, v) for sk, v in waits.items()], None))
    def final_wait(self, queue, semnames):
        for n in semnames:
            for d in self.dall[n]:
                self.q[queue].append((None, [(d[0], d[1])], None))
    def emit(self, block):
        def run(e, lst):
            for fns, waits, inc in lst:
                for (h, v) in waits: e.wait_ge(h, v)
                if fns is None: continue
                for i, fn in enumerate(fns):
                    ins = fn(e)
                    if i == len(fns) - 1 and inc is not None: ins.then_inc(inc[0], inc[1])
        @block.tensor
        def _(e): run(e, self.q["pe"])
        @block.scalar
        def _(e): run(e, self.q["act"])
        @block.vector
        def _(e): run(e, self.q["dve"])
        @block.gpsimd
        def _(e): run(e, self.q["pool"])
        @block.sync
        def _(e): run(e, self.q["sp"])


class Ctx:
    pass

def mmK(P, out, pairs, reads, writes):
    n = len(pairs)
    P.G("pe", [("matmul", dict(out=out, lhsT=a, rhs=b, start=(i == 0), stop=(i == n - 1))) for i, (a, b) in enumerate(pairs)],
        reads=reads, writes=writes)

class WRing:
    def __init__(self, P, nslot=3, elems=4096):
        self.P = P; self.n = nslot; self.i = 0
        self.bufs = [P.sb(f"wr{i}", [128, elems], BF16) for i in range(nslot)]
    def load(self, dram_ap, shape_str, **dims):
        s = self.i % self.n; self.i += 1
        shp = dram_ap.shape
        n = 1
        for v in shp[1:]: n *= v
        view = self.bufs[s][:, 0:n]
        if len(shp) == 3:
            view = view.rearrange("p (a b) -> p a b", a=shp[1])
        elif len(shp) == 4:
            view = view.rearrange("p (a b c) -> p a b c", a=shp[1], b=shp[2])
        key = f"wr{s}"
        self.P.D("pool", key, view, dram_ap, writes=[key])
        return view, key

    def load_multi(self, aps):
        s = self.i % self.n; self.i += 1
        J = len(aps); K_, N_ = aps[0].shape[1], aps[0].shape[2]
        view = self.bufs[s][:, 0:K_ * J * N_].rearrange("p (a b c) -> p a b c", a=K_, b=J)
        key = f"wr{s}"
        for j, ap in enumerate(aps):
            self.P.D("pool", key, view[:, :, j, :], ap, writes=[key])
        return view, key

    def load_parts(self, aps):
        s = self.i % self.n; self.i += 1
        key = f"wr{s}"; off = 0; views = []
        for ap in aps:
            a, b = ap.shape[1], ap.shape[2]
            v = self.bufs[s][:, off:off + a * b].rearrange("p (a b) -> p a b", a=a)
            off += a * b
            self.P.D("pool", key, v, ap, writes=[key])
            views.append(v)
        assert off <= 4096
        return views, key
def declare_io(nc, C):
    def din(name, shape):
        return nc.dram_tensor(name, list(shape), F32, kind="ExternalInput").ap()
    def dout(name, shape):
        return nc.dram_tensor(name, list(shape), F32, kind="ExternalOutput").ap()
    C.xin = din("xin", [T, D]); C.condT = din("condT", [128, 8, 2])
    C.ada_w = din("ada_w", [4, D, 3 * D]); C.ada_bT = din("ada_bT", [128, 4, 24])
    C.ln_gT = din("ln_gT", [128, 4, 8]); C.ln_bT = din("ln_bT", [128, 4, 8])
    C.ident = din("ident", [128, 128])
    C.sc_w_in = din("sc_w_in", [D, 4 * D]); C.sc_cw = din("sc_cw", [128, 3, 8]); C.sc_cb = din("sc_cb", [128, 8])
    C.sc_w_out = din("sc_w_out", [D, D])
    C.rg_w_in = din("rg_w_in", [D, 2 * D]); C.rg_cw = din("rg_cw", [128, 4, 8]); C.rg_cb = din("rg_cb", [128, 8])
    C.rg_w_gate = din("rg_w_gate", [2, 4, 256, 512]); C.rg_bg = din("rg_bg", [128, 2, 4, 4])
    C.rg_lam = din("rg_lam", [128, 2, 8]); C.rg_w_out = din("rg_w_out", [D, D])
    C.rg_h0 = din("rg_h0", [128, 2, 8])
    C.hg_w_in = din("hg_w_in", [D, 5 * D]); C.hg_lbl = din("hg_lbl", [128, 16, 5]); C.hg_ng = din("hg_ng", [128, 8])
    C.hg_w_out = din("hg_w_out", [D, D]); C.hg_s0 = din("hg_s0", [2, 8, 128, 128])
    C.hg_masks = din("hg_masks", [128, 4, 128])
    C.mla_w_in = din("mla_w_in", [D, 1696]); C.mla_qn = din("mla_qn", [128, 3]); C.mla_kvn = din("mla_kvn", [128, 2])
    C.mla_w_qb = din("mla_w_qb", [384, 1536]); C.mla_w_qbsw = din("mla_w_qbsw", [384, 512])
    C.mla_w_kpesw = din("mla_w_kpesw", [D, 32])
    C.mla_w_kvb = din("mla_w_kvb", [256, 2048]); C.mla_w_out = din("mla_w_out", [D, D])
    C.mla_ckv_ctx = din("mla_ckv_ctx", [512, 256]); C.mla_kpe_ctx = din("mla_kpe_ctx", [512, 32])
    C.rope_cs = din("rope_cs", [128, 2, 1024])
    C.y = dout("y", [T, D])
    C.o_hg = dout("o_hg", [4, 2, 8, 128, 128]); C.o_rg = dout("o_rg", [8, D])
    C.o_ckv = dout("o_ckv", [1024, 256]); C.o_kpe = dout("o_kpe", [1024, 32])

def setup_persistent(P, C):
    C.xT = P.sb("xT", [128, 8, T], F32)
    C.hT = P.sb("hT", [128, 8, T], BF16)
    C.mT = P.sb("mT", [128, 8, T], BF16)
    C.ring = WRing(P, nslot=3, elems=4096)
    C.identf = P.sb("identf", [128, 128], F32)
    C.identb = P.sb("identb", [128, 128], BF16)
    C.onesb = P.sb("onesb", [128, 128], BF16)
    C.condf = P.sb("condf", [128, 8, 2], F32)
    C.scond = P.sb("scond", [128, 8, 2], BF16)
    C.adab = P.sb("adab", [128, 4, 24], F32)
    C.lng = P.sb("lng", [128, 4, 8], F32); C.lnb = P.sb("lnb", [128, 4, 8], F32)
    C.mod = P.sb("mod", [128, 24, 2], F32)
    C.colsb = [P.sb(f"cols{i}", [128, 3, 8, 2], F32) for i in range(2)]
    C.ps = [P.ps(f"ps{i}", [128, 512], F32) for i in range(8)]
    P.D("sp", "identf", C.identf[:], C.ident[:, :], writes=["identf"])
    P.D("sp", "condf", C.condf[:], C.condT[:, :, :], writes=["condf"])
    P.D("sp", "adab", C.adab[:], C.ada_bT[:, :, :], writes=["adab"])
    P.D("sp", "lng", C.lng[:], C.ln_gT[:, :, :], writes=["lng"])
    P.D("sp", "lnb", C.lnb[:], C.ln_bT[:, :, :], writes=["lnb"])
    P.I("dve", "tensor_copy", reads=["identf"], writes=["identb"], out=C.identb[:], in_=C.identf[:])
    P.I("dve", "memset", writes=["onesb"], ap=C.onesb[:], constant=1.0)
    P.I("act", "activation", reads=["condf"], writes=["scond"], out=C.scond[:], in_=C.condf[:], func=AF.Silu)

def xk(g, fc):
    return f"xT{g}_{fc}"

def input_transposes(P, C):
    with ExitStack() as es:
        st = [P.sb(f"xst{i}", [128, D], F32, es) for i in range(2)]
        for t in range(NT):
            b = t % 2
            P.D("sp", f"xst{b}", st[b][:], C.xin[t * 128:(t + 1) * 128, :], writes=[f"xst{b}"])
            for half in range(2):
                pb = C.ps[(t * 2 + half) % 4]; pk = f"ps{(t * 2 + half) % 4}"
                P.G("pe", [("transpose", dict(out=pb[:, i * 128:(i + 1) * 128], in_=st[b][:, (half * 4 + i) * 128:(half * 4 + i + 1) * 128],
                                              identity=C.identf[:])) for i in range(4)], reads=[f"xst{b}", "identf"], writes=[pk])
                eng = "act" if half == 0 else "dve"
                outap = C.xT[:, half * 4:half * 4 + 4, t * 128:(t + 1) * 128]
                inap = pb[:].rearrange("p (c t) -> p c t", c=4)
                if eng == "act":
                    P.I("act", "activation", reads=[pk], writes=[xk(t // 4, half * 4 + i) for i in range(4)], out=outap, in_=inap, func=AF.Copy)
                else:
                    P.I("dve", "tensor_copy", reads=[pk], writes=[xk(t // 4, half * 4 + i) for i in range(4)], out=outap, in_=inap)
        P.barrier()

def output_transposes(P, C):
    with ExitStack() as es:
        st = [P.sb(f"yst{i}", [128, D], F32, es) for i in range(2)]
        for t in range(NT):
            b = t % 2
            for half in range(2):
                pb = C.ps[(t * 2 + half) % 4]; pk = f"ps{(t * 2 + half) % 4}"
                P.G("pe", [("transpose", dict(out=pb[:, i * 128:(i + 1) * 128], in_=C.xT[:, half * 4 + i, t * 128:(t + 1) * 128],
                                              identity=C.identf[:])) for i in range(4)], reads=[xk(t // 4, half * 4 + i) for i in range(4)] + ["identf"], writes=[pk])
                if half == 0:
                    P.I("act", "activation", reads=[pk], writes=[f"yst{b}"], out=st[b][:, 0:512], in_=pb[:], func=AF.Copy)
                else:
                    P.I("dve", "tensor_copy", reads=[pk], writes=[f"yst{b}"], out=st[b][:, 512:1024], in_=pb[:])
            P.D("sp", f"yout{b}", C.y[t * 128:(t + 1) * 128, :], st[b][:], reads=[f"yst{b}"])
        P.barrier()

def ada_phase(P, C, l):
    cols = C.colsb[l % 2]; ck = f"cols{l % 2}"
    wv_all = C.ada_w[l].rearrange("(k p) n -> p k n", p=128)
    psA = C.ps[4]
    for piece in range(6):
        wv, wk = C.ring.load(wv_all[:, :, piece * 512:(piece + 1) * 512], "")
        for f4 in range(4):
            fc = piece * 4 + f4
            mmK(P, psA[:, fc * 2:fc * 2 + 2], [(wv[:, kc, f4 * 128:(f4 + 1) * 128], C.scond[:, kc, :]) for kc in range(8)],
                reads=[wk, "scond"], writes=["ps4"])
    P.I("dve", "tensor_tensor", reads=["ps4", "adab"], writes=["mod"], out=C.mod[:],
        in0=psA[:, 0:48].rearrange("p (f j) -> p f j", j=2), in1=C.adab[:, l, :].unsqueeze(2).to_broadcast([128, 24, 2]), op=ALU.add)
    P.I("dve", "tensor_copy", reads=["mod"], writes=[ck], out=cols[:, 0], in_=C.mod[:, 0:8, :])
    P.I("dve", "tensor_scalar_add", reads=["mod"], writes=[ck], out=cols[:, 1], in0=C.mod[:, 8:16, :], scalar1=1.0)
    P.I("dve", "tensor_scalar_mul", reads=["mod"], writes=[ck], out=cols[:, 2], in0=C.mod[:, 16:24, :], scalar1=1.0 / ALPHA)

def modulate_phase(P, C, l):
    cols = C.colsb[l % 2]; ck = f"cols{l % 2}"
    for j in range(2):
        for c in range(8):
            sl = slice(j * 1024, (j + 1) * 1024)
            rk = [xk(2 * j, c), xk(2 * j + 1, c), ck]
            if (c + j) % 2 == 0:
                P.I("dve", "tensor_scalar", reads=rk, writes=[f"hT{j}"], out=C.hT[:, c, sl], in0=C.xT[:, c, sl],
                    scalar1=cols[:, 1, c, j:j + 1], scalar2=cols[:, 0, c, j:j + 1], op0=ALU.mult, op1=ALU.add)
            else:
                P.I("act", "activation", reads=rk, writes=[f"hT{j}"], out=C.hT[:, c, sl], in_=C.xT[:, c, sl], func=AF.Identity,
                    scale=cols[:, 1, c, j:j + 1], bias=cols[:, 0, c, j:j + 1])

def load_wout(P, C, w_dram):
    C.wout_dram = w_dram

def wout_prefetch(P, C):
    wv = C.wout_dram.rearrange("(k p) n -> p k n", p=128)
    return [C.ring.load(wv[:, :, 0:512], ""), C.ring.load(wv[:, :, 512:1024], "")]

def wout_ln_phase(P, C, l, pre, next_ada=None):
    cols = C.colsb[l % 2]; ck = f"cols{l % 2}"
    with ExitStack() as es:
        zn = [P.sb(f"ln_zn{i}", [128, D], F32, es) for i in range(4)]
        st = [P.sb(f"ln_st{i}", [128, 12], F32, es) for i in range(2)]
        mv = [P.sb(f"ln_mv{i}", [128, 2], F32, es) for i in range(2)]
        rs = [P.sb(f"ln_rs{i}", [128, 2], F32, es) for i in range(2)]
        epsc = P.sb("ln_eps", [128, 1], F32, es)
        P.I("dve", "memset", writes=["ln_eps"], ap=epsc[:], constant=LN_EPS)
        yi = [0]; ti = [0]
        def zpass(g):
            j = g // 2; gs = slice(g * 512, (g + 1) * 512)
            for fc in range(8):
                wv, wk = pre[fc // 4]; f4 = fc % 4
                py = C.ps[6 + yi[0] % 2]; pyk = f"ps{6 + yi[0] % 2}"; yi[0] += 1
                mmK(P, py[:], [(wv[:, kc, f4 * 128:(f4 + 1) * 128], C.mT[:, kc, gs]) for kc in range(8)], reads=[wk, f"mT{g}"], writes=[pyk])
                P.I("dve", "scalar_tensor_tensor", reads=[pyk, xk(g, fc), ck], writes=[xk(g, fc)], out=C.xT[:, fc, gs], in0=py[:],
                    scalar=cols[:, 2, fc, j:j + 1], in1=C.xT[:, fc, gs], op0=ALU.mult, op1=ALU.add)
        def norm_tiles(g):
            for tl in range(4):
                tcols = slice(g * 512 + tl * 128, g * 512 + (tl + 1) * 128)
                i2 = ti[0] % 2; ti[0] += 1
                pb = [C.ps[2 * i2], C.ps[2 * i2 + 1]]; pbk = [f"ps{2 * i2}", f"ps{2 * i2 + 1}"]
                for half in range(2):
                    P.G("pe", [("transpose", dict(out=pb[half][:, i * 128:(i + 1) * 128], in_=C.xT[:, half * 4 + i, tcols], identity=C.identf[:]))
                               for i in range(4)], reads=[xk(g, half * 4 + i) for i in range(4)] + ["identf"], writes=[pbk[half]])
                    P.I("dve", "bn_stats", reads=[pbk[half]], writes=[f"ln_st{i2}"], out=st[i2][:, half * 6:(half + 1) * 6], in_=pb[half][:])
                P.I("dve", "bn_aggr", reads=[f"ln_st{i2}"], writes=[f"ln_mv{i2}"], out=mv[i2][:], in_=st[i2][:])
                P.I("act", "activation", reads=[f"ln_mv{i2}", "ln_eps"], writes=[f"ln_rs{i2}"], out=rs[i2][:, 0:1], in_=mv[i2][:, 1:2], func=AF.Sqrt,
                    bias=epsc[:, 0:1], scale=1.0)
                P.I("dve", "reciprocal", reads=[f"ln_rs{i2}"], writes=[f"ln_rs{i2}"], out=rs[i2][:, 0:1], in_=rs[i2][:, 0:1])
                P.I("dve", "scalar_tensor_tensor", reads=[f"ln_mv{i2}", f"ln_rs{i2}"], writes=[f"ln_rs{i2}"], out=rs[i2][:, 1:2], in0=mv[i2][:, 0:1],
                    scalar=-1.0, in1=rs[i2][:, 0:1], op0=ALU.mult, op1=ALU.mult)
                for half in range(2):
                    P.I("act", "activation", reads=[pbk[half], f"ln_rs{i2}"], writes=[f"ln_zn{tl}"], out=zn[tl][:, half * 512:(half + 1) * 512],
                        in_=pb[half][:], func=AF.Identity, scale=rs[i2][:, 0:1], bias=rs[i2][:, 1:2])
        def back(g):
            gs = slice(g * 512, (g + 1) * 512)
            for fc in range(8):
                bi = 4 + fc % 2
                P.G("pe", [("transpose", dict(out=C.ps[bi][:, tl * 128:(tl + 1) * 128], in_=zn[tl][:, fc * 128:(fc + 1) * 128], identity=C.identf[:]))
                           for tl in range(4)], reads=[f"ln_zn{tl}" for tl in range(4)] + ["identf"], writes=[f"ps{bi}"])
                if fc % 2 == 0:
                    P.I("act", "activation", reads=[f"ps{bi}", "lng", "lnb"], writes=[xk(g, fc)], out=C.xT[:, fc, gs], in_=C.ps[bi][:], func=AF.Identity,
                        scale=C.lng[:, l, fc:fc + 1], bias=C.lnb[:, l, fc:fc + 1])
                else:
                    P.I("dve", "tensor_scalar", reads=[f"ps{bi}", "lng", "lnb"], writes=[xk(g, fc)], out=C.xT[:, fc, gs], in0=C.ps[bi][:],
                        scalar1=C.lng[:, l, fc:fc + 1], scalar2=C.lnb[:, l, fc:fc + 1], op0=ALU.mult, op1=ALU.add)
        zpass(0)
        for g in range(4):
            if g + 1 < 4:
                zpass(g + 1)
            norm_tiles(g)
            if g == 3 and next_ada is not None:
                ada_phase(P, C, next_ada)
            back(g)
        P.barrier()
QS = 128.0 ** -0.5
CH = 64
NCH = 512 // CH
JT = 128 // CH

def _roundrobin(gens):
    gens = list(gens)
    while gens:
        nxt = []
        for g in gens:
            try:
                next(g); nxt.append(g)
            except StopIteration:
                pass
        gens = nxt

def hgrn_layer(P, C):
    with ExitStack() as es:
        lbl = P.sb("hg_lbl_s", [128, 16, 5], F32, es); lbm = P.sb("hg_lbm", [128, 16], F32, es)
        lb = P.sb("hg_lb", [128, 16], F32, es); oml = P.sb("hg_oml", [128, 16], F32, es)
        ng = P.sb("hg_ng_s", [128, 8], F32, es); eps6 = P.sb("hg_eps", [128, 1], F32, es)
        one = P.sb("hg_one", [128, 1], F32, es)
        mk = P.sb("hg_mk", [128, 2, 128], F32, es); rm = P.sb("hg_rm", [128, 2, 512], BF16, es)
        mstage = P.sb("hg_mst", [128, 2, 128], F32, es)
        P.D("sp", "hg_lbl", lbl[:], C.hg_lbl[:, :, :], writes=["hg_lbl"])
        P.D("sp", "hg_ng", ng[:], C.hg_ng[:, :], writes=["hg_ng"])
        P.D("sp", "hg_mk", mk[:], C.hg_masks[:, 0:2, :], writes=["hg_mk"])
        P.D("sp", "hg_mst", mstage[:], C.hg_masks[:, 2:4, :], writes=["hg_mst"])
        P.I("dve", "memset", writes=["hg_eps"], ap=eps6[:], constant=1e-6)
        P.I("dve", "memset", writes=["hg_one"], ap=one[:], constant=1.0)
        for d in range(2):
            for r in range(4):
                P.I("dve", "tensor_copy", reads=["hg_mst"], writes=["hg_rm"], out=rm[:, d, r * 128:(r + 1) * 128], in_=mstage[:, d, :])
        P.I("dve", "reduce_max", reads=["hg_lbl"], writes=["hg_lbm"], out=lbm[:], in_=lbl[:], axis=AX.X)
        P.I("dve", "tensor_tensor", reads=["hg_lbl", "hg_lbm"], writes=["hg_lbl"], out=lbl[:], in0=lbl[:],
            in1=lbm[:].unsqueeze(2).to_broadcast([128, 16, 5]), op=ALU.subtract)
        P.I("act", "activation", reads=["hg_lbl"], writes=["hg_lbl"], out=lbl[:], in_=lbl[:], func=AF.Exp)
        P.I("dve", "reduce_sum", reads=["hg_lbl"], writes=["hg_lbm"], out=lbm[:], in_=lbl[:], axis=AX.X)
        P.I("dve", "reciprocal", reads=["hg_lbm"], writes=["hg_lbm"], out=lbm[:], in_=lbm[:])
        P.I("dve", "tensor_tensor", reads=["hg_lbl", "hg_lbm"], writes=["hg_lb"], out=lb[:], in0=lbl[:, :, 0], in1=lbm[:], op=ALU.mult)
        P.I("dve", "tensor_scalar", reads=["hg_lb"], writes=["hg_oml"], out=oml[:], in0=lb[:], scalar1=-1.0, scalar2=1.0, op0=ALU.mult, op1=ALU.add)

        vtok = P.sb("hg_vtok", [128, 16, 128], BF16, es)
        qS = P.sb("hg_q", [128, 1024], F32, es)
        ob = P.sb("hg_o", [128, 1024], F32, es)
        sgB = P.sb("hg_sgb", [128, 1024], BF16, es)
        Sf = P.sb("hg_Sf", [128, 8, 128], F32, es); Sb = P.sb("hg_Sb", [128, 8, 128], BF16, es)
        attS = [P.sb(f"hg_att{i}", [128, 128], BF16, es) for i in range(4)]
        U = []
        for u in range(2):
            B_ = Ctx()
            B_.u = u
            B_.cum = P.sb(f"hg_cum{u}", [128, 512], F32, es); B_.kS = P.sb(f"hg_k{u}", [128, 512], F32, es)
            B_.A = P.sb(f"hg_A{u}", [128, 512], F32, es); B_.B = P.sb(f"hg_B{u}", [128, 512], F32, es)
            B_.qrel = P.sb(f"hg_qrel{u}", [128, 512], BF16, es); B_.krel = P.sb(f"hg_krel{u}", [128, 512], BF16, es)
            B_.qcum = P.sb(f"hg_qcum{u}", [128, 512], BF16, es); B_.kdT = P.sb(f"hg_kdT{u}", [128, 512], BF16, es)
            B_.kdtok = P.sb(f"hg_kdtok{u}", [128, 4, 128], BF16, es); B_.etot = P.sb(f"hg_etot{u}", [128, 16], F32, es)
            U.append(B_)
        wv_all = C.hg_w_in.rearrange("(k p) n -> p k n", p=128)
        P.I("dve", "memset", writes=["hg_qrel0"], ap=U[0].qrel[:], constant=0.0)
        P.I("pe", "matmul", reads=["hg_qrel0", "identb"], writes=["ps3"], out=C.ps[3][:], lhsT=C.identb[:], rhs=U[0].qrel[:], start=True, stop=True)
        P.I("pe", "matmul", reads=["hg_qrel0", "identb"], writes=["ps2"], out=C.ps[2][:], lhsT=C.identb[:], rhs=U[0].qrel[:], start=True, stop=True)
        cnt = {"w": 0, "att": 0, "x": 0, "y": 0, "u": 0}
        def wps():
            i = cnt["w"] % 2; cnt["w"] += 1
            return C.ps[i], f"ps{i}"
        def quarter(bank, name):
            i = cnt[name] % 4; cnt[name] += 1
            return C.ps[bank][:, i * 128:(i + 1) * 128], f"ps{bank}"
        def uslot():
            i = cnt["u"] % 2; cnt["u"] += 1
            return C.ps[6 + i][:, 0:128], f"ps{6 + i}"

        def prep(B_, wv, wk, hd, blk, d, sg_):
            u = B_.u
            K = lambda n: f"hg_{n}{u}"
            cs = slice(blk * 1024 + sg_ * 512, blk * 1024 + (sg_ + 1) * 512); ls = slice(sg_ * 512, (sg_ + 1) * 512)
            ridx = (CH // 2 - 1) if d == 0 else (CH // 2); tidx = (CH - 1) if d == 0 else 0
            cum, kS, Ab, Bb = B_.cum, B_.kS, B_.A, B_.B
            pt, pk = wps()
            mmK(P, pt[:], [(wv[:, kc, 1 + d, :], C.hT[:, kc, cs]) for kc in range(8)], reads=[wk, f"hT{blk}"], writes=[pk]); yield
            lbc = lb[:, d * 8 + hd:d * 8 + hd + 1]; omc = oml[:, d * 8 + hd:d * 8 + hd + 1]
            P.I("act", "activation", reads=[pk], writes=[K("cum")], out=cum[:], in_=pt[:], func=AF.Exp, scale=-1.0); yield
            P.I("act", "activation", reads=[K("cum"), "hg_lb", "hg_one"], writes=[K("A")], out=Ab[:], in_=cum[:], func=AF.Ln, scale=lbc, bias=one[:, 0:1]); yield
            P.I("act", "activation", reads=[K("cum"), "hg_one"], writes=[K("B")], out=Bb[:], in_=cum[:], func=AF.Ln, scale=1.0, bias=one[:, 0:1]); yield
            P.I("dve", "tensor_tensor", reads=[K("A"), K("B")], writes=[K("cum")], out=cum[:], in0=Ab[:], in1=Bb[:], op=ALU.subtract); yield
            P.I("dve", "tensor_tensor", reads=[pk, K("B")], writes=[K("B")], out=Bb[:], in0=Bb[:], in1=pt[:], op=ALU.add); yield
            P.I("act", "activation", reads=[K("B")], writes=[K("k")], out=kS[:], in_=Bb[:], func=AF.Exp, scale=-1.0); yield
            rv = slice(None) if d == 0 else slice(None, None, -1)
            P.I("dve", "tensor_tensor_scan", reads=[K("cum"), "hg_rm"], writes=[K("cum")], out=cum[:, rv], data0=rm[:, d, rv],
                data1=cum[:, rv], initial=0.0, op0=ALU.mult, op1=ALU.add); yield
            c3 = cum[:].rearrange("p (c t) -> p c t", t=CH)
            A3 = Ab[:].rearrange("p (c t) -> p c t", t=CH)
            P.I("dve", "tensor_tensor", reads=[K("cum")], writes=[K("A")], out=A3, in0=c3,
                in1=c3[:, :, ridx:ridx + 1].to_broadcast([128, NCH, CH]), op=ALU.subtract); yield
            P.I("act", "activation", reads=[K("cum")], writes=[K("B")], out=Bb[:], in_=cum[:], func=AF.Exp); yield
            P.I("dve", "scalar_tensor_tensor", reads=["hg_q", K("B")], writes=[K("qcum")], out=B_.qcum[:], in0=qS[:, ls], scalar=QS,
                in1=Bb[:], op0=ALU.mult, op1=ALU.mult); yield
            P.I("act", "activation", reads=[K("cum")], writes=[K("etot")], out=B_.etot[:, 0:NCH], in_=c3[:, :, tidx], func=AF.Exp); yield
            P.I("act", "activation", reads=[K("A")], writes=[K("B")], out=Bb[:], in_=Ab[:], func=AF.Exp); yield
            P.I("dve", "scalar_tensor_tensor", reads=["hg_q", K("B")], writes=[K("qrel")], out=B_.qrel[:], in0=qS[:, ls], scalar=QS,
                in1=Bb[:], op0=ALU.mult, op1=ALU.mult); yield
            P.I("act", "activation", reads=[K("A")], writes=[K("B")], out=Bb[:], in_=Ab[:], func=AF.Exp, scale=-1.0); yield
            P.I("dve", "scalar_tensor_tensor", reads=[K("k"), K("B"), "hg_oml"], writes=[K("krel")], out=B_.krel[:], in0=kS[:], scalar=omc, in1=Bb[:],
                op0=ALU.mult, op1=ALU.mult); yield
            P.I("dve", "tensor_tensor", reads=[K("cum")], writes=[K("A")], out=A3, in0=c3,
                in1=c3[:, :, tidx:tidx + 1].to_broadcast([128, NCH, CH]), op=ALU.subtract); yield
            P.I("act", "activation", reads=[K("A")], writes=[K("B")], out=Bb[:], in_=Ab[:], func=AF.Exp, scale=-1.0); yield
            P.I("dve", "scalar_tensor_tensor", reads=[K("k"), K("B"), "hg_oml"], writes=[K("kdT")], out=B_.kdT[:], in0=kS[:], scalar=omc, in1=Bb[:],
                op0=ALU.mult, op1=ALU.mult); yield
            psT = C.ps[2][:].bitcast(BF16)
            P.G("pe", [("transpose", dict(out=psT[:, i * 128:(i + 1) * 128], in_=B_.kdT[:, i * 128:(i + 1) * 128], identity=C.identb[:]))
                       for i in range(4)], reads=[K("kdT"), "identb"], writes=["ps2"])
            P.I("act", "activation", reads=["ps2"], writes=[K("kdtok")], out=B_.kdtok[:], in_=psT[:, 0:512].rearrange("p (a b) -> p a b", a=4),
                func=AF.Copy); yield

        for hd in range(8):
            if hd == 4:
                load_wout(P, C, C.hg_w_out)
            wv, wk = C.ring.load_multi([wv_all[:, :, j * 1024 + hd * 128:j * 1024 + (hd + 1) * 128] for j in range(4)])
            gv, gk = C.ring.load(wv_all[:, :, 4096 + hd * 128:4096 + (hd + 1) * 128], "")
            for t4 in range(4):
                vb = 2 if t4 % 2 == 0 else 4
                P.G("pe", [("matmul", dict(out=C.ps[vb][:, i * 128:(i + 1) * 128], lhsT=C.hT[:, kc, (t4 * 4 + i) * 128:(t4 * 4 + i + 1) * 128],
                                           rhs=wv[:, kc, 3, :], start=(kc == 0), stop=(kc == 7))) for i in range(4) for kc in range(8)],
                    reads=[wk, f"hT{t4 // 2}"], writes=[f"ps{vb}"])
                if t4 % 2 == 0:
                    P.I("act", "activation", reads=[f"ps{vb}"], writes=["hg_vtok"], out=vtok[:, t4 * 4:t4 * 4 + 4, :],
                        in_=C.ps[vb][:].rearrange("p (a b) -> p a b", a=4), func=AF.Copy)
                else:
                    P.I("dve", "tensor_copy", reads=[f"ps{vb}"], writes=["hg_vtok"], out=vtok[:, t4 * 4:t4 * 4 + 4, :],
                        in_=C.ps[vb][:].rearrange("p (a b) -> p a b", a=4))
            for blk in range(2):
                for g2 in range(2):
                    cs = slice(blk * 1024 + g2 * 512, blk * 1024 + (g2 + 1) * 512); ls = slice(g2 * 512, (g2 + 1) * 512)
                    pt, pk = wps()
                    mmK(P, pt[:], [(wv[:, kc, 0, :], C.hT[:, kc, cs]) for kc in range(8)], reads=[wk, f"hT{blk}"], writes=[pk])
                    P.I("act", "activation", reads=[pk], writes=["hg_q"], out=qS[:, ls], in_=pt[:], func=AF.Silu)
                for g2 in range(2):
                    cs = slice(blk * 1024 + g2 * 512, blk * 1024 + (g2 + 1) * 512); ls = slice(g2 * 512, (g2 + 1) * 512)
                    pt, pk = wps()
                    mmK(P, pt[:], [(gv[:, kc, :], C.hT[:, kc, cs]) for kc in range(8)], reads=[gk, f"hT{blk}"], writes=[pk])
                    P.I("act", "activation", reads=[pk], writes=["hg_sgb"], out=sgB[:, ls], in_=pt[:], func=AF.Silu)
                P.I("pool", "memset", writes=[f"hg_o{t_}" for t_ in range(8)], ap=ob[:], constant=0.0)
                if blk == 0:
                    P.I("pool", "memset", writes=[f"hg_Sf{c_}" for c_ in range(8)], ap=Sf[:], constant=0.0)
                    P.I("pool", "memset", writes=[f"hg_Sb{c_}" for c_ in range(8)], ap=Sb[:], constant=0.0)
                else:
                    for d in range(2):
                        P.D("sp", f"hg_s0{d}", Sf[:, d, :], C.hg_s0[d, hd], writes=[f"hg_Sf{d}"])
                        P.I("act", "activation", reads=[f"hg_Sf{d}"], writes=[f"hg_Sb{d}"], out=Sb[:, d, :], in_=Sf[:, d, :], func=AF.Copy)
                for step in range(2):
                    units = [(0, step, U[0]), (1, 1 - step, U[1])]
                    _roundrobin([prep(B_, wv, wk, hd, blk, d, sg_) for (d, sg_, B_) in units])
                    chains = []
                    for (d, sg_, B_) in units:
                        if blk == 0:
                            cl = [(d * 4 + 2 * sg_, [0, 1]), (d * 4 + 2 * sg_ + 1, [2, 3])]
                        else:
                            cl = [(d, [0, 1, 2, 3])]
                        for ch, tl in cl:
                            chains.append((d, sg_, B_, ch, tl if d == 0 else tl[::-1]))
                    npos = len(chains[0][4])
                    xy_banks = [4, 5, 0, 1]
                    for pos in range(npos):
                        info = []
                        for ci, (d, sg_, B_, ch, tl) in enumerate(chains):
                            u = B_.u
                            tloc = tl[pos]; gt = blk * 8 + sg_ * 4 + tloc; ts_ = slice(tloc * 128, (tloc + 1) * 128)
                            pa, pak = quarter(3 if ci % 2 == 0 else 2, "att")
                            t0 = tloc * 128
                            blocks = []
                            for cb in range(JT):
                                b0 = cb * CH; hC = CH // 2
                                if d == 0:
                                    blocks += [(b0, hC, b0, CH), (b0 + hC, hC, b0 + hC, hC)]
                                else:
                                    blocks += [(b0 + hC, hC, b0, CH), (b0, hC, b0, hC)]
                            P.G("pe", [("matmul", dict(out=pa[s0:s0 + sn, q0:q0 + qn], lhsT=B_.krel[:, t0 + s0:t0 + s0 + sn], rhs=B_.qrel[:, t0 + q0:t0 + q0 + qn],
                                                       start=True, stop=True, tile_position=(0, s0))) for (s0, sn, q0, qn) in blocks],
                                reads=[f"hg_krel{u}", f"hg_qrel{u}"], writes=[pak])
                            ai = ci
                            P.I("dve", "tensor_tensor", reads=[pak, "hg_mk"], writes=[f"hg_att{ai}"], out=attS[ai][:], in0=pa, in1=mk[:, d, :], op=ALU.mult)
                            xb_ = xy_banks[ci]
                            px = C.ps[xb_][:, 0:128]; pxk = f"ps{xb_}"; py = px; pyk = pxk
                            P.I("pe", "matmul", reads=["hg_vtok", f"hg_att{ai}"], writes=[pxk], out=px, lhsT=vtok[:, gt, :], rhs=attS[ai][:], start=True, stop=False)
                            info.append((d, sg_, B_, ch, tloc, gt, px, pxk, py, pyk))
                        for jj in range(JT):
                            for (d, sg_, B_, ch, tloc, gt, px, pxk, py, pyk) in info:
                                u = B_.u
                                j = jj if d == 0 else JT - 1 - jj
                                qs_ = slice(tloc * 128 + j * CH, tloc * 128 + (j + 1) * CH)
                                P.I("pe", "matmul", reads=[f"hg_Sb{ch}", f"hg_qcum{u}"], writes=[pyk], out=py[:, j * CH:(j + 1) * CH], lhsT=Sb[:, ch, :],
                                    rhs=B_.qcum[:, qs_], start=False, stop=(jj == JT - 1))
                                pu, puk = uslot()
                                P.I("pe", "matmul", reads=[f"hg_kdtok{u}", "hg_vtok"], writes=[puk], out=pu, lhsT=B_.kdtok[j * CH:(j + 1) * CH, tloc, :],
                                    rhs=vtok[j * CH:(j + 1) * CH, gt, :], start=True, stop=True, tile_position=(j * CH, 0))
                                P.I("dve", "scalar_tensor_tensor", reads=[f"hg_Sf{ch}", puk, f"hg_etot{u}"], writes=[f"hg_Sf{ch}"], out=Sf[:, ch, :],
                                    in0=Sf[:, ch, :], scalar=B_.etot[:, tloc * JT + j:tloc * JT + j + 1], in1=pu, op0=ALU.mult, op1=ALU.add)
                                P.I("act", "activation", reads=[f"hg_Sf{ch}"], writes=[f"hg_Sb{ch}"], out=Sb[:, ch, :], in_=Sf[:, ch, :], func=AF.Copy)
                        for (d, sg_, B_, ch, tloc, gt, px, pxk, py, pyk) in info:
                            t8 = sg_ * 4 + tloc
                            os_ = slice(t8 * 128, (t8 + 1) * 128); ok_ = f"hg_o{t8}"
                            P.I("dve", "tensor_tensor", reads=[pxk, ok_], writes=[ok_], out=ob[:, os_], in0=ob[:, os_], in1=px, op=ALU.add)
                    if blk == 0:
                        for (d, sg_, B_, ch, tl) in chains:
                            P.D("sp", f"o_hg{ch}", C.o_hg[ch % 4, d, hd], Sf[:, ch, :], reads=[f"hg_Sf{ch}"])
                osq = U[0].qrel; rsb = U[0].B; sgb = U[0].A
                for g2 in range(2):
                    cs = slice(blk * 1024 + g2 * 512, blk * 1024 + (g2 + 1) * 512); ls = slice(g2 * 512, (g2 + 1) * 512)
                    okeys = [f"hg_o{g2 * 4 + t_}" for t_ in range(4)]
                    P.I("act", "activation", reads=okeys, writes=["hg_qrel0"], out=osq[:], in_=ob[:, ls], func=AF.Square)
                    pt, pk = wps()
                    P.I("pe", "matmul", reads=["hg_qrel0", "onesb"], writes=[pk], out=pt[:], lhsT=C.onesb[:], rhs=osq[:], start=True, stop=True)
                    P.I("act", "activation", reads=[pk, "hg_eps"], writes=["hg_B0"], out=rsb[:], in_=pt[:], func=AF.Ln, bias=eps6[:, 0:1], scale=1.0 / 128.0)
                    P.I("act", "activation", reads=["hg_B0"], writes=["hg_B0"], out=rsb[:], in_=rsb[:], func=AF.Exp, scale=-0.5)
                    P.I("dve", "tensor_tensor", reads=["hg_B0"] + okeys, writes=["hg_B0"], out=rsb[:], in0=rsb[:], in1=ob[:, ls], op=ALU.mult)
                    P.I("dve", "scalar_tensor_tensor", reads=["hg_B0", "hg_sgb", "hg_ng"], writes=[f"mT{blk * 2 + g2}"], out=C.mT[:, hd, cs], in0=rsb[:],
                        scalar=ng[:, hd:hd + 1], in1=sgB[:, ls], op0=ALU.mult, op1=ALU.mult)
def sconv_layer(P, C):
    with ExitStack() as es:
        cw = P.sb("sc_cw_s", [128, 3, 8], F32, es); cb = P.sb("sc_cb_s", [128, 8], F32, es)
        P.D("sp", "sc_cw", cw[:], C.sc_cw[:, :, :], writes=["sc_cw"])
        P.D("sp", "sc_cb", cb[:], C.sc_cb[:, :], writes=["sc_cb"])
        pb_ = [P.sb(f"sc_p{i}", [128, 1024], F32, es) for i in range(2)]
        zb_ = [P.sb(f"sc_z{i}", [128, 1024], F32, es) for i in range(2)]
        cgS = [P.sb(f"sc_cg{i}", [128, 512], F32, es) for i in range(2)]
        sgS = [P.sb(f"sc_sg{i}", [128, 512], F32, es) for i in range(2)]
        tS = [P.sb(f"sc_t{i}", [128, 512], F32, es) for i in range(2)]
        wv_all = C.sc_w_in.rearrange("(k p) n -> p k n", p=128)
        pi = 0
        for c in range(8):
            if c == 4:
                load_wout(P, C, C.sc_w_out)
            wv, wk = C.ring.load_multi([wv_all[:, :, j * 1024 + c * 128:j * 1024 + (c + 1) * 128] for j in range(4)])
            for blk in range(2):
                p = pb_[blk]; z = zb_[blk]; pk = f"sc_p{blk}"; zk = f"sc_z{blk}"
                for g2 in range(2):
                    cs = slice(blk * 1024 + g2 * 512, blk * 1024 + (g2 + 1) * 512); ls = slice(g2 * 512, (g2 + 1) * 512)
                    pa = pi % 4; pbk = (pi + 1) % 4; pi += 2
                    mmK(P, C.ps[pa][:], [(wv[:, kc, 1, :], C.hT[:, kc, cs]) for kc in range(8)], reads=[wk, f"hT{blk}"], writes=[f"ps{pa}"])
                    mmK(P, C.ps[pbk][:], [(wv[:, kc, 2, :], C.hT[:, kc, cs]) for kc in range(8)], reads=[wk, f"hT{blk}"], writes=[f"ps{pbk}"])
                    i2 = g2
                    P.I("act", "activation", reads=[f"ps{pa}"], writes=[f"sc_cg{i2}"], out=cgS[i2][:], in_=C.ps[pa][:], func=AF.Copy)
                    P.I("dve", "tensor_tensor", reads=[f"sc_cg{i2}", f"ps{pbk}"], writes=[pk], out=p[:, ls], in0=cgS[i2][:], in1=C.ps[pbk][:], op=ALU.mult)
                P.I("dve", "tensor_scalar", reads=[pk, "sc_cw", "sc_cb"], writes=[zk], out=z[:], in0=p[:], scalar1=cw[:, 1, c:c + 1],
                    scalar2=cb[:, c:c + 1], op0=ALU.mult, op1=ALU.add)
                if blk == 0:
                    z3 = z[:].rearrange("p (s t) -> p s t", s=4); p3 = p[:].rearrange("p (s t) -> p s t", s=4)
                    zlo, plo, zhi, phi = z3[:, :, 1:], p3[:, :, :-1], z3[:, :, :-1], p3[:, :, 1:]
                else:
                    zlo, plo, zhi, phi = z[:, 1:], p[:, :-1], z[:, :-1], p[:, 1:]
                P.I("dve", "scalar_tensor_tensor", reads=[pk, zk, "sc_cw"], writes=[zk], out=zlo, in0=plo, scalar=cw[:, 0, c:c + 1], in1=zlo,
                    op0=ALU.mult, op1=ALU.add)
                P.I("dve", "scalar_tensor_tensor", reads=[pk, zk, "sc_cw"], writes=[zk], out=zhi, in0=phi, scalar=cw[:, 2, c:c + 1], in1=zhi,
                    op0=ALU.mult, op1=ALU.add)
                for g2 in range(2):
                    cs = slice(blk * 1024 + g2 * 512, blk * 1024 + (g2 + 1) * 512); ls = slice(g2 * 512, (g2 + 1) * 512)
                    g = blk * 2 + g2
                    pa = pi % 4; pbk = (pi + 1) % 4; pi += 2
                    mmK(P, C.ps[pa][:], [(wv[:, kc, 0, :], C.hT[:, kc, cs]) for kc in range(8)], reads=[wk, f"hT{blk}"], writes=[f"ps{pa}"])
                    mmK(P, C.ps[pbk][:], [(wv[:, kc, 3, :], C.hT[:, kc, cs]) for kc in range(8)], reads=[wk, f"hT{blk}"], writes=[f"ps{pbk}"])
                    i2 = g2
                    P.I("act", "activation", reads=[f"ps{pbk}"], writes=[f"sc_sg{i2}"], out=sgS[i2][:], in_=C.ps[pbk][:], func=AF.Silu)
                    P.I("dve", "tensor_tensor", reads=[f"sc_sg{i2}", f"ps{pa}"], writes=[f"sc_t{i2}"], out=tS[i2][:], in0=sgS[i2][:], in1=C.ps[pa][:], op=ALU.mult)
                    P.I("dve", "tensor_tensor", reads=[f"sc_t{i2}", zk], writes=[f"mT{g}"], out=C.mT[:, c, cs], in0=tS[i2][:], in1=z[:, ls], op=ALU.mult)
def rglru_layer(P, C):
    with ExitStack() as es:
        cw = P.sb("rg_cw_s", [128, 4, 8], F32, es); cb = P.sb("rg_cb_s", [128, 8], F32, es)
        bg = P.sb("rg_bg_s", [128, 2, 4, 4], F32, es); lam = P.sb("rg_lam_s", [128, 2, 8], F32, es)
        clam = P.sb("rg_clam", [128, 2, 8], F32, es); h0 = P.sb("rg_h0_s", [128, 2, 8], F32, es)
        one = P.sb("rg_one", [128, 1], F32, es)
        rgst = P.sb("rg_state", [128, 8, 8], F32, es)
        P.D("sp", "rg_cw", cw[:], C.rg_cw[:, :, :], writes=["rg_cw"])
        P.D("sp", "rg_cb", cb[:], C.rg_cb[:, :], writes=["rg_cb"])
        P.D("sp", "rg_bg", bg[:], C.rg_bg[:, :, :, :], writes=["rg_bg"])
        P.D("sp", "rg_lam", lam[:], C.rg_lam[:, :, :], writes=["rg_lam"])
        P.D("sp", "rg_h0", h0[:], C.rg_h0[:, :, :], writes=["rg_h0"])
        P.I("dve", "memset", writes=["rg_one"], ap=one[:], constant=1.0)
        P.I("act", "activation", reads=["rg_lam"], writes=["rg_clam"], out=clam[:], in_=lam[:], func=AF.Exp, scale=-1.0)
        P.I("act", "activation", reads=["rg_clam", "rg_one"], writes=["rg_clam"], out=clam[:], in_=clam[:], func=AF.Ln, bias=one[:, 0:1], scale=1.0)
        P.I("dve", "tensor_scalar_mul", reads=["rg_clam"], writes=["rg_clam"], out=clam[:], in0=clam[:], scalar1=-4.0)
        half = P.sb("rg_half", [128, 1], F32, es)
        P.I("dve", "memset", writes=["rg_half"], ap=half[:], constant=0.5)
        P.I("dve", "tensor_scalar_mul", reads=["rg_bg"], writes=["rg_bg"], out=bg[:], in0=bg[:], scalar1=0.5)
        P.I("dve", "tensor_scalar_mul", reads=["rg_h0"], writes=["rg_h0"], out=h0[:], in0=h0[:], scalar1=2.0)
        uraw = P.sb("rg_uraw", [128, 1024], F32, es)
        uc = P.sb("rg_uc", [128, 2, 1024], F32, es); ucb = P.sb("rg_ucb", [128, 2, 1024], BF16, es)
        abufs = [P.sb(f"rg_a{i}", [128, 1024], F32, es) for i in range(2)]
        xbs = [[P.sb(f"rg_x{o}{i}", [128, 1024], F32, es) for i in range(2)] for o in range(2)]
        sqt = [P.sb(f"rg_sq{i}", [128, 512], F32, es) for i in range(4)]
        sgS = [P.sb(f"rg_sg{i}", [128, 512], F32, es) for i in range(2)]
        wv_all = C.rg_w_in.rearrange("(k p) n -> p k n", p=128)
        pi = [0]
        def nps():
            i = pi[0] % 6; pi[0] += 1
            return C.ps[i], f"ps{i}"
        def seg(ap2d, blk, lo, hi):
            if blk == 0:
                v = ap2d.rearrange("p (s t) -> p s t", s=4)
                return v[:, :, lo:256 + hi]
            return ap2d[:, lo:1024 + hi]
        for hh in range(4):
            if hh == 2:
                load_wout(P, C, C.rg_w_out)
            wv, wk = C.ring.load_multi([wv_all[:, :, j * 1024 + c * 128:j * 1024 + (c + 1) * 128] for j in range(2) for c in (2 * hh, 2 * hh + 1)])
            gv, gk = C.ring.load_multi([C.rg_w_gate[d, hh].rearrange("(k p) n -> p k n", p=128) for d in range(2)])
            for blk in range(2):
                for cc in range(2):
                    c = 2 * hh + cc
                    for g2 in range(2):
                        cs = slice(blk * 1024 + g2 * 512, blk * 1024 + (g2 + 1) * 512); ls = slice(g2 * 512, (g2 + 1) * 512)
                        pt, pk = nps()
                        mmK(P, pt[:], [(wv[:, kc, cc, :], C.hT[:, kc, cs]) for kc in range(8)], reads=[wk, f"hT{blk}"], writes=[pk])
                        P.I("act", "activation", reads=[pk], writes=["rg_uraw"], out=uraw[:, ls], in_=pt[:], func=AF.Copy)
                    ucc = uc[:, cc, :]
                    P.I("dve", "tensor_scalar", reads=["rg_uraw", "rg_cw", "rg_cb"], writes=["rg_uc"], out=ucc, in0=uraw[:],
                        scalar1=cw[:, 2, c:c + 1], scalar2=cb[:, c:c + 1], op0=ALU.mult, op1=ALU.add)
                    for (k, lo_o, hi_o, lo_i, hi_i) in ((0, 2, 0, 0, -2), (1, 1, 0, 0, -1), (3, 0, -1, 1, 0)):
                        o_ = seg(ucc, blk, lo_o, hi_o); i_ = seg(uraw[:], blk, lo_i, hi_i)
                        P.I("dve", "scalar_tensor_tensor", reads=["rg_uraw", "rg_uc", "rg_cw"], writes=["rg_uc"], out=o_, in0=i_,
                            scalar=cw[:, k, c:c + 1], in1=o_, op0=ALU.mult, op1=ALU.add)
                    P.I("act", "activation", reads=["rg_uc"], writes=["rg_ucb"], out=ucb[:, cc, :], in_=ucc, func=AF.Copy)
                for oc in range(2):
                    c = 2 * hh + oc
                    xb = xbs[oc]
                    for d in range(2):
                        xin = xb[d]; xk = f"rg_x{oc}{d}"; abuf = abufs[d]; ak = f"rg_a{d}"
                        for g2 in range(2):
                            ls = slice(g2 * 512, (g2 + 1) * 512)
                            pr, prk = nps(); pq, pqk = nps()
                            mmK(P, pr[:], [(gv[:, kc, d, oc * 128:(oc + 1) * 128], ucb[:, kc, ls]) for kc in range(2)], reads=[gk, "rg_ucb"], writes=[prk])
                            mmK(P, pq[:], [(gv[:, kc, d, 256 + oc * 128:256 + (oc + 1) * 128], ucb[:, kc, ls]) for kc in range(2)], reads=[gk, "rg_ucb"], writes=[pqk])
                            P.I("act", "activation", reads=[prk, "rg_bg"], writes=[ak], out=abuf[:, ls], in_=pr[:], func=AF.Tanh,
                                bias=bg[:, d, hh, oc:oc + 1], scale=0.5)
                            P.I("act", "activation", reads=[ak, "rg_clam"], writes=[ak], out=abuf[:, ls], in_=abuf[:, ls], func=AF.Exp,
                                scale=clam[:, d, c:c + 1], bias=clam[:, d, c:c + 1])
                            P.I("act", "activation", reads=[pqk, "rg_bg"], writes=[xk], out=xin[:, ls], in_=pq[:], func=AF.Tanh,
                                bias=bg[:, d, hh, 2 + oc:3 + oc], scale=0.5)
                            sq = sqt[d * 2 + g2]; sk = f"rg_sq{d * 2 + g2}"
                            P.I("act", "activation", reads=[ak], writes=[sk], out=sq[:], in_=abuf[:, ls], func=AF.Square)
                        for g2 in range(2):
                            ls = slice(g2 * 512, (g2 + 1) * 512)
                            sq = sqt[d * 2 + g2]; sk = f"rg_sq{d * 2 + g2}"
                            P.I("act", "activation", reads=[sk, "rg_one"], writes=[sk], out=sq[:], in_=sq[:], func=AF.Sqrt, bias=one[:, 0:1], scale=-1.0)
                            P.I("dve", "scalar_tensor_tensor", reads=[xk, sk], writes=[xk], out=xin[:, ls], in0=xin[:, ls], scalar=1.0, in1=sq[:],
                                op0=ALU.add, op1=ALU.mult)
                            P.I("dve", "tensor_tensor", reads=[xk, "rg_uc"], writes=[xk], out=xin[:, ls], in0=xin[:, ls], in1=uc[:, oc, ls], op=ALU.mult)
                        seqs = [(s * 256, 256) for s in range(4)] if blk == 0 else [(0, 1024)]
                        for (o0, L) in seqs:
                            sl = slice(o0, o0 + L) if d == 0 else slice(o0 + L - 1, (o0 - 1) if o0 > 0 else None, -1)
                            init = 0.0 if blk == 0 else h0[:, d, c:c + 1]
                            P.I("dve", "tensor_tensor_scan", reads=[ak, xk, "rg_h0"], writes=[xk], out=xin[:, sl], data0=abuf[:, sl],
                                data1=xin[:, sl], initial=init, op0=ALU.mult, op1=ALU.add)
                        if blk == 0:
                            x3 = xin[:].rearrange("p (s t) -> p s t", s=4)
                            src = x3[:, :, 255:256] if d == 0 else x3[:, :, 0:1]
                            dst = rgst[:, c, :].rearrange("p (s d) -> p s d", d=2)[:, :, d:d + 1]
                            P.I("dve", "tensor_scalar_mul", reads=[xk], writes=["rg_state"], out=dst, in0=src, scalar1=0.5)
                    P.I("dve", "tensor_tensor", reads=[f"rg_x{oc}0", f"rg_x{oc}1"], writes=[f"rg_x{oc}0"], out=xb[0][:], in0=xb[0][:], in1=xb[1][:], op=ALU.add)
                    for g2 in range(2):
                        cs = slice(blk * 1024 + g2 * 512, blk * 1024 + (g2 + 1) * 512); ls = slice(g2 * 512, (g2 + 1) * 512)
                        pt, pk = nps()
                        mmK(P, pt[:], [(wv[:, kc, 2 + oc, :], C.hT[:, kc, cs]) for kc in range(8)], reads=[wk, f"hT{blk}"], writes=[pk])
                        P.I("act", "activation", reads=[pk], writes=[f"rg_sg{g2}"], out=sgS[g2][:], in_=pt[:], func=AF.Tanh, scale=0.5)
                        P.I("dve", "scalar_tensor_tensor", reads=[f"rg_sg{g2}", pk], writes=[f"rg_sg{g2}"], out=sgS[g2][:], in0=sgS[g2][:], scalar=1.0, in1=pt[:],
                            op0=ALU.add, op1=ALU.mult)
                        P.I("dve", "scalar_tensor_tensor", reads=[f"rg_sg{g2}", f"rg_x{oc}0"], writes=[f"mT{blk * 2 + g2}"], out=C.mT[:, c, cs], in0=sgS[g2][:],
                            scalar=0.25, in1=xb[0][:, ls], op0=ALU.mult, op1=ALU.mult)
        srow = uraw[0:8, :]
        for half in range(2):
            pt, pk = nps()
            P.G("pe", [("transpose", dict(out=pt[0:8, i * 128:(i + 1) * 128], in_=rgst[:, half * 4 + i, :], identity=C.identf[:])) for i in range(4)],
                reads=["rg_state", "identf"], writes=[pk])
            P.I("dve", "tensor_copy", reads=[pk], writes=["rg_uraw"], out=srow[:, half * 512:(half + 1) * 512], in_=pt[0:8, :])
        P.D("sp", "o_rg", C.o_rg[:, :], srow, reads=["rg_uraw"])
SM_SCALE = 96.0 ** -0.5

def mla_layer(P, C):
    with ExitStack() as es:
        qn = P.sb("ml_qn", [128, 3], F32, es); kvn = P.sb("ml_kvn", [128, 2], F32, es)
        eps6 = P.sb("ml_eps", [128, 1], F32, es)
        rope = P.sb("ml_rope", [128, 2, 1024], F32, es)
        cqnT = P.sb("ml_cqnT", [128, 3, 1024], BF16, es)
        ckvnT = P.sb("ml_ckvnT", [128, 2, 1536], BF16, es)
        KK = P.sb("ml_KK", [128, 1536], BF16, es)
        KK2 = P.sb("ml_KK2", [128, 1536], BF16, es)
        P.D("sp", "ml_qn", qn[:], C.mla_qn[:, :], writes=["ml_qn"])
        P.D("sp", "ml_kvn", kvn[:], C.mla_kvn[:, :], writes=["ml_kvn"])
        P.D("sp", "ml_rope", rope[64:96], C.rope_cs[64:96, :, :], writes=["ml_rope"])
        P.I("dve", "memset", writes=["ml_eps"], ap=eps6[:], constant=1e-6)
        wv_all = C.mla_w_in.rearrange("(k p) n -> p k n", p=128)
        wqb_all = C.mla_w_qb.rearrange("(k p) n -> p k n", p=128)
        wqs_all = C.mla_w_qbsw.rearrange("(k p) n -> p k n", p=128)
        wkvb_all = C.mla_w_kvb.rearrange("(k p) n -> p k n", p=128)
        wks_all = C.mla_w_kpesw.rearrange("(k p) n -> p k n", p=128)
        load_wout(P, C, C.mla_w_out)
        for blk in range(2):
            nkeys = 1024 if blk == 0 else 1536
            with ExitStack() as esA:
                sq = [P.sb(f"ml_sq{i}", [128, 512], BF16, esA) for i in range(2)]
                rstd = P.sb("ml_rstd", [128, 512], F32, esA)
                ckvf = P.sb("ml_ckvf", [128, 2, 512], F32, esA)
                t1 = P.sb("ml_t1", [128, 256], F32, esA); t2 = P.sb("ml_t2", [128, 256], F32, esA)
                stg = [P.sb(f"ml_stg{i}", [128, 256], F32, esA) for i in range(2)]
                stk = P.sb("ml_stk", [128, 32], F32, esA); stkb = P.sb("ml_stkb", [128, 32], BF16, esA)
                (wcq,), wcqk = C.ring.load_parts([wv_all[:, :, 0:384]])
                (wkv, wks), wkvk = C.ring.load_parts([wv_all[:, :, 384:672], wks_all[:, :, :]])
                for gq in range(2):
                    cs = slice(blk * 1024 + gq * 512, blk * 1024 + (gq + 1) * 512); ls = slice(gq * 512, (gq + 1) * 512)
                    hk = f"hT{blk}"
                    for c in range(3):
                        mmK(P, C.ps[c][:], [(wcq[:, kc, c * 128:(c + 1) * 128], C.hT[:, kc, cs]) for kc in range(8)], reads=[wcqk, hk], writes=[f"ps{c}"])
                        P.I("act", "activation", reads=[f"ps{c}"], writes=[f"ml_sq{c % 2}"], out=sq[c % 2][:], in_=C.ps[c][:], func=AF.Square)
                        P.I("pe", "matmul", reads=[f"ml_sq{c % 2}", "onesb"], writes=["ps3"], out=C.ps[3][:], lhsT=C.onesb[:], rhs=sq[c % 2][:],
                            start=(c == 0), stop=(c == 2))
                    P.I("act", "activation", reads=["ps3", "ml_eps"], writes=["ml_rstd"], out=rstd[:], in_=C.ps[3][:], func=AF.Sqrt, bias=eps6[:, 0:1], scale=1.0 / 384.0)
                    P.I("dve", "reciprocal", reads=["ml_rstd"], writes=["ml_rstd"], out=rstd[:], in_=rstd[:])
                    for c in range(3):
                        P.I("dve", "scalar_tensor_tensor", reads=[f"ps{c}", "ml_rstd", "ml_qn"], writes=["ml_cqnT"], out=cqnT[:, c, ls], in0=C.ps[c][:],
                            scalar=qn[:, c:c + 1], in1=rstd[:], op0=ALU.mult, op1=ALU.mult)
                    for c in range(2):
                        mmK(P, C.ps[4 + c][:], [(wkv[:, kc, c * 128:(c + 1) * 128], C.hT[:, kc, cs]) for kc in range(8)], reads=[wkvk, hk], writes=[f"ps{4 + c}"])
                        P.I("act", "activation", reads=[f"ps{4 + c}"], writes=[f"ml_sq{c % 2}"], out=sq[c % 2][:], in_=C.ps[4 + c][:], func=AF.Square)
                        P.I("pe", "matmul", reads=[f"ml_sq{c % 2}", "onesb"], writes=["ps6"], out=C.ps[6][:], lhsT=C.onesb[:], rhs=sq[c % 2][:],
                            start=(c == 0), stop=(c == 1))
                    P.I("act", "activation", reads=["ps6", "ml_eps"], writes=["ml_rstd"], out=rstd[:], in_=C.ps[6][:], func=AF.Sqrt, bias=eps6[:, 0:1], scale=1.0 / 256.0)
                    P.I("dve", "reciprocal", reads=["ml_rstd"], writes=["ml_rstd"], out=rstd[:], in_=rstd[:])
                    for c in range(2):
                        P.I("dve", "scalar_tensor_tensor", reads=[f"ps{4 + c}", "ml_rstd", "ml_kvn"], writes=["ml_ckvf"], out=ckvf[:, c, :], in0=C.ps[4 + c][:],
                            scalar=kvn[:, c:c + 1], in1=rstd[:], op0=ALU.mult, op1=ALU.mult)
                    P.I("act", "activation", reads=["ml_ckvf"], writes=["ml_ckvnT"], out=ckvnT[:, :, ls], in_=ckvf[:], func=AF.Copy)
                    mmK(P, C.ps[7][64:96, :], [(wkv[:, kc, 256:288], C.hT[:, kc, cs]) for kc in range(8)], reads=[wkvk, hk], writes=["ps7"])
                    if blk == 0:
                        P.I("act", "activation", reads=["ps7"], writes=["ml_KKpe"], out=KK[64:96, ls], in_=C.ps[7][64:96, :], func=AF.Copy)
                    else:
                        mmK(P, C.ps[3][64:96, :], [(wks[:, kc, :], C.hT[:, kc, cs]) for kc in range(8)], reads=[wkvk, hk], writes=["ps3"])
                        for hq in range(2):
                            l2 = slice(hq * 256, (hq + 1) * 256); pos = slice(gq * 512 + hq * 256, gq * 512 + (hq + 1) * 256)
                            P.I("dve", "tensor_tensor", reads=["ps7", "ml_rope"], writes=["ml_t1"], out=t1[64:96, :], in0=C.ps[7][64:96, l2], in1=rope[64:96, 0, pos], op=ALU.mult)
                            P.I("dve", "tensor_tensor", reads=["ps3", "ml_rope"], writes=["ml_t2"], out=t2[64:96, :], in0=C.ps[3][64:96, l2], in1=rope[64:96, 1, pos], op=ALU.mult)
                            P.I("dve", "tensor_tensor", reads=["ml_t1", "ml_t2"], writes=["ml_KKpe"], out=KK[64:96, pos], in0=t1[64:96, :], in1=t2[64:96, :], op=ALU.add)
                    if blk == 0:
                        for tl in range(4):
                            gt = gq * 4 + tl; b = tl % 2
                            P.G("pe", [("transpose", dict(out=C.ps[2][:, c * 128:(c + 1) * 128], in_=ckvf[:, c, tl * 128:(tl + 1) * 128], identity=C.identf[:]))
                                       for c in range(2)], reads=["ml_ckvf", "identf"], writes=["ps2"])
                            P.I("dve", "tensor_copy", reads=["ps2"], writes=[f"ml_stg{b}"], out=stg[b][:], in_=C.ps[2][:, 0:256])
                            P.D("sp", f"o_ckv{b}", C.o_ckv[gt * 128:(gt + 1) * 128, :], stg[b][:], reads=[f"ml_stg{b}"])
                            mmK(P, C.ps[2][:, 256:288], [(C.hT[:, kc, gt * 128:(gt + 1) * 128], wkv[:, kc, 256:288]) for kc in range(8)], reads=[wkvk, hk], writes=["ps2"])
                            P.I("dve", "tensor_copy", reads=["ps2"], writes=["ml_stk"], out=stk[:], in_=C.ps[2][:, 256:288])
                            P.D("sp", "o_kpe", C.o_kpe[gt * 128:(gt + 1) * 128, :], stk[:], reads=["ml_stk"])
                if blk == 1:
                    for tl in range(4):
                        b = tl % 2
                        P.D("sp", f"ml_stg{b}", stg[b][:], C.mla_ckv_ctx[tl * 128:(tl + 1) * 128, :], writes=[f"ml_stg{b}"])
                        P.G("pe", [("transpose", dict(out=C.ps[2][:, c * 128:(c + 1) * 128], in_=stg[b][:, c * 128:(c + 1) * 128], identity=C.identf[:]))
                                   for c in range(2)], reads=[f"ml_stg{b}", "identf"], writes=["ps2"])
                        P.I("act", "activation", reads=["ps2"], writes=["ml_ckvnT"], out=ckvnT[:, :, 1024 + tl * 128:1024 + (tl + 1) * 128],
                            in_=C.ps[2][:, 0:256].rearrange("p (c t) -> p c t", c=2), func=AF.Copy)
                        P.D("sp", "ml_stk", stk[:], C.mla_kpe_ctx[tl * 128:(tl + 1) * 128, :], writes=["ml_stk"])
                        P.I("dve", "tensor_copy", reads=["ml_stk"], writes=["ml_stkb"], out=stkb[:], in_=stk[:])
                        psb = C.ps[2][:].bitcast(BF16)
                        P.I("pe", "transpose", reads=["ml_stkb", "identb"], writes=["ps2"], out=psb[64:96, 512:640], in_=stkb[:], identity=C.identb[:])
                        P.I("dve", "tensor_copy", reads=["ps2"], writes=["ml_KKpe"], out=KK[64:96, 1024 + tl * 128:1024 + (tl + 1) * 128], in_=psb[64:96, 512:640])
                P.I("dve", "tensor_copy", reads=["ml_KKpe"], writes=["ml_KKpe2"], out=KK2[64:96, 0:nkeys], in_=KK[64:96, 0:nkeys])
                P.barrier()
            with ExitStack() as esB:
                KKs = [KK, KK2]
                Vh = [P.sb(f"ml_Vh{i}", [128, 12, 65], BF16, esB) for i in range(2)]
                QQ = [P.sb(f"ml_QQ{i}", [128, 1024], BF16, esB) for i in range(2)]
                PT = [P.sb(f"ml_PT{i}", [128, 512], BF16, esB) for i in range(2)]
                sgT = [P.sb(f"ml_sgT{i}", [128, 1024], BF16, esB) for i in range(2)]
                on2 = P.sb("ml_on2", [128, 8, 128], BF16, esB)
                rcp = P.sb("ml_rcp", [128, 4], F32, esB)
                u1 = P.sb("ml_u1", [128, 256], F32, esB); u2 = P.sb("ml_u2", [128, 256], F32, esB)
                tg = P.sb("ml_tg", [128, 512], F32, esB)
                for i in range(2):
                    P.I("dve", "memset", writes=[f"ml_Vh{i}"], ap=Vh[i][:, :, 64:65], constant=1.0)
                nkt = nkeys // 128
                units = [(slice(s_ * 256, (s_ + 1) * 256), [2 * s_, 2 * s_ + 1]) for s_ in range(4)] if blk == 0 else \
                        [(slice(g_ * 512, (g_ + 1) * 512), list(range(12))) for g_ in range(2)]
                sc_i = [0]
                hk = f"hT{blk}"
                W = {}

                def proj(h):
                    hp, hh = h // 2, h % 2; bsel = h % 2
                    if hh == 0:
                        W[hp] = C.ring.load_parts([wqb_all[:, :, hp * 192:(hp + 1) * 192], wqs_all[:, :, hp * 64:(hp + 1) * 64],
                                                   wkvb_all[:, :, hp * 256:(hp + 1) * 256], wv_all[:, :, 672 + hp * 128:672 + (hp + 1) * 128]])
                    (wqb, wqs, wkb, wg), wk = W[hp]
                    if hh == 0:
                        for g2 in range(2):
                            cs = slice(blk * 1024 + g2 * 512, blk * 1024 + (g2 + 1) * 512); ls = slice(g2 * 512, (g2 + 1) * 512)
                            mmK(P, C.ps[6][:], [(wg[:, kc, :], C.hT[:, kc, cs]) for kc in range(8)], reads=[wk, hk], writes=["ps6"])
                            P.I("act", "activation", reads=["ps6"], writes=["ml_tg"], out=tg[:], in_=C.ps[6][:], func=AF.Tanh, scale=0.5)
                            P.I("dve", "scalar_tensor_tensor", reads=["ml_tg", "ps6"], writes=[f"ml_sgT{hp % 2}"], out=sgT[hp % 2][:, ls], in0=tg[:], scalar=1.0,
                                in1=C.ps[6][:], op0=ALU.add, op1=ALU.mult)
                            yield
                    KKb = KKs[bsel]
                    for kg in range(nkeys // 512):
                        ks = slice(kg * 512, (kg + 1) * 512)
                        mmK(P, C.ps[6][0:64, :], [(wkb[:, kc, hh * 128:hh * 128 + 64], ckvnT[:, kc, ks]) for kc in range(2)], reads=[wk, "ml_ckvnT"], writes=["ps6"])
                        P.I("dve", "tensor_copy", reads=["ps6"], writes=[f"ml_KKn{bsel}"], out=KKb[0:64, ks], in_=C.ps[6][0:64, :])
                        yield
                    for k8 in range((nkt + 7) // 8):
                        n8 = min(8, nkt - k8 * 8)
                        P.G("pe", [("matmul", dict(out=C.ps[7][:, j * 64:(j + 1) * 64], lhsT=ckvnT[:, kc, (k8 * 8 + j) * 128:(k8 * 8 + j + 1) * 128],
                                                   rhs=wkb[:, kc, hh * 128 + 64:hh * 128 + 128], start=(kc == 0), stop=(kc == 1)))
                                   for j in range(n8) for kc in range(2)], reads=[wk, "ml_ckvnT"], writes=["ps7"])
                        P.I("dve", "tensor_copy", reads=["ps7"], writes=[f"ml_Vh{bsel}"], out=Vh[bsel][:, k8 * 8:k8 * 8 + n8, 0:64],
                            in_=C.ps[7][:, 0:n8 * 64].rearrange("p (a b) -> p a b", a=n8))
                        yield
                    Qb = QQ[bsel]
                    for g2 in range(2):
                        ls = slice(g2 * 512, (g2 + 1) * 512)
                        mmK(P, C.ps[6][0:64, :], [(wqb[:, kc, hh * 96:hh * 96 + 64], cqnT[:, kc, ls]) for kc in range(3)], reads=[wk, "ml_cqnT"], writes=["ps6"])
                        P.I("dve", "tensor_copy", reads=["ps6"], writes=[f"ml_QQn{bsel}"], out=Qb[0:64, ls], in_=C.ps[6][0:64, :])
                        yield
                        mmK(P, C.ps[7][64:96, :], [(wqb[:, kc, hh * 96 + 64:hh * 96 + 96], cqnT[:, kc, ls]) for kc in range(3)], reads=[wk, "ml_cqnT"], writes=["ps7"])
                        if blk == 0:
                            P.I("dve", "tensor_copy", reads=["ps7"], writes=[f"ml_QQr{bsel}"], out=Qb[64:96, ls], in_=C.ps[7][64:96, :])
                        else:
                            mmK(P, C.ps[6][64:96, :], [(wqs[:, kc, hh * 32:(hh + 1) * 32], cqnT[:, kc, ls]) for kc in range(3)], reads=[wk, "ml_cqnT"], writes=["ps6"])
                            for hq in range(2):
                                l2 = slice(hq * 256, (hq + 1) * 256); pos = slice(g2 * 512 + hq * 256, g2 * 512 + (hq + 1) * 256)
                                P.I("dve", "tensor_tensor", reads=["ps7", "ml_rope"], writes=["ml_u1"], out=u1[64:96, :], in0=C.ps[7][64:96, l2], in1=rope[64:96, 0, pos], op=ALU.mult)
                                P.I("dve", "tensor_tensor", reads=["ps6", "ml_rope"], writes=["ml_u2"], out=u2[64:96, :], in0=C.ps[6][64:96, l2], in1=rope[64:96, 1, pos], op=ALU.mult)
                                P.I("dve", "tensor_tensor", reads=["ml_u1", "ml_u2"], writes=[f"ml_QQr{bsel}"], out=Qb[64:96, pos], in0=u1[64:96, :], in1=u2[64:96, :], op=ALU.add)
                        yield

                def attn(h):
                    hh = h % 2; bsel = h % 2
                    KKb = KKs[bsel]; Qb = QQ[bsel]; Vb = Vh[bsel]
                    kkeys = [f"ml_KKn{bsel}", "ml_KKpe" if bsel == 0 else "ml_KKpe2", f"ml_QQn{bsel}", f"ml_QQr{bsel}"]
                    for (qsl, kts) in units:
                        nq = qsl.stop - qsl.start; nqt = nq // 128; qt0 = qsl.start // 128
                        def qk(kt):
                            sb_ = sc_i[0] % 2; sc_i[0] += 1
                            P.I("pe", "matmul", reads=kkeys, writes=[f"ps{sb_}"], out=C.ps[sb_][:, 0:nq],
                                lhsT=KKb[0:96, kt * 128:(kt + 1) * 128], rhs=Qb[0:96, qsl], start=True, stop=True)
                            P.I("act", "activation", reads=[f"ps{sb_}"], writes=[f"ml_PT{sb_}"], out=PT[sb_][:, 0:nq], in_=C.ps[sb_][:, 0:nq], func=AF.Exp, scale=SM_SCALE)
                            return sb_
                        pend = qk(kts[0])
                        for ki, kt in enumerate(kts):
                            pi_ = pend
                            if ki + 1 < len(kts):
                                pend = qk(kts[ki + 1])
                            for qt in range(nqt):
                                P.I("pe", "matmul", reads=[f"ml_PT{pi_}", f"ml_Vh{bsel}"], writes=[f"ps{2 + qt}"], out=C.ps[2 + qt][:, 0:65],
                                    lhsT=PT[pi_][:, qt * 128:(qt + 1) * 128], rhs=Vb[:, kt, :], start=(ki == 0), stop=(ki == len(kts) - 1))
                            yield
                        for qt in range(nqt):
                            P.I("dve", "reciprocal", reads=[f"ps{2 + qt}"], writes=["ml_rcp"], out=rcp[:, qt:qt + 1], in_=C.ps[2 + qt][:, 64:65])
                            P.I("dve", "tensor_scalar", reads=[f"ps{2 + qt}", "ml_rcp"], writes=["ml_on2"], out=on2[:, qt0 + qt, hh * 64:(hh + 1) * 64],
                                in0=C.ps[2 + qt][:, 0:64], scalar1=rcp[:, qt:qt + 1], scalar2=None, op0=ALU.mult)
                        yield

                def finish_pair(hp):
                    psb = C.ps[7][:].bitcast(BF16)
                    for g2 in range(2):
                        cs = slice(blk * 1024 + g2 * 512, blk * 1024 + (g2 + 1) * 512); ls = slice(g2 * 512, (g2 + 1) * 512)
                        P.G("pe", [("transpose", dict(out=psb[:, i * 128:(i + 1) * 128], in_=on2[:, g2 * 4 + i, :], identity=C.identb[:])) for i in range(4)],
                            reads=["ml_on2", "identb"], writes=["ps7"])
                        P.I("dve", "scalar_tensor_tensor", reads=["ps7", f"ml_sgT{hp % 2}"], writes=[f"mT{blk * 2 + g2}"], out=C.mT[:, hp, cs], in0=psb[:, 0:512],
                            scalar=0.5, in1=sgT[hp % 2][:, ls], op0=ALU.mult, op1=ALU.mult)

                _roundrobin([proj(0)])
                for h in range(16):
                    gens = [attn(h)]
                    if h + 1 < 16:
                        gens.append(proj(h + 1))
                    _roundrobin(gens)
                    if h % 2 == 1:
                        finish_pair(h // 2)
                P.barrier()
LAYERS = (0, 1, 2, 3)
_CACHE = {}

def build_nc(layers):
    nc = bass.Bass("TRN2", target_bir_lowering=False)
    C = Ctx()
    declare_io(nc, C)
    with ExitStack() as es:
        P = Prog(nc, es)
        setup_persistent(P, C)
        ada_phase(P, C, layers[0])
        input_transposes(P, C)
        for li, l in enumerate(layers):
            modulate_phase(P, C, l)
            [hgrn_layer, sconv_layer, rglru_layer, mla_layer][l % 4](P, C)
            pre = wout_prefetch(P, C)
            P.barrier()
            wout_ln_phase(P, C, l, pre, next_ada=(layers[li + 1] if li + 1 < len(layers) else None))
        output_transposes(P, C)
        outs = [n for n in P.dall if n.startswith(("yout", "o_hg", "o_rg", "o_ckv", "o_kpe"))]
        P.final_wait("sp", outs)
        with nc.Block() as block:
            P.emit(block)
        C.counts = dict(P.cnt); print('instr counts', C.counts, 'nsem', len(P.esems) + sum(len(v) for v in P.dall.values()))
    return nc, C

def colT(v, nch):
    return np.ascontiguousarray(np.asarray(v, np.float32).reshape(nch, 128).T)

def make_in_maps(I):
    f = lambda a: np.ascontiguousarray(np.asarray(a, np.float32))
    half = 8; r = np.arange(1024) // 64; cpos = np.arange(1024) % 64
    inv = (10000.0 ** (-np.arange(0, 16, 2, dtype=np.float32) / 16)).astype(np.float32)
    cs = np.zeros((128, 2, 1024), np.float32)
    for part, pos in ((0, r), (1, cpos)):
        ang = pos[None, :].astype(np.float32) * inv[:, None]
        co, si = np.cos(ang), np.sin(ang)
        b = 64 + part * 16
        cs[b:b + 8, 0] = co; cs[b + 8:b + 16, 0] = co
        cs[b:b + 8, 1] = -si; cs[b + 8:b + 16, 1] = si
    s_ = np.arange(128)[:, None]; t_ = np.arange(128)[None, :]
    same = (s_ // 64) == (t_ // 64)
    masks = np.zeros((128, 4, 128), np.float32)
    masks[:, 0] = same & (s_ <= t_); masks[:, 1] = same & (s_ >= t_)
    masks[:, 2] = np.tile((np.arange(128) % 64 != 0).astype(np.float32), (128, 1))
    masks[:, 3] = np.tile((np.arange(128) % 64 != 63).astype(np.float32), (128, 1))
    swap = np.arange(32).reshape(2, 2, 8)[:, ::-1, :].reshape(-1)
    wqb = f(I['mla_w_qb'][0])
    qb_r = wqb.reshape(384, 16, 96)[:, :, 64:]
    shared = {
        'ada_w': f(I['ada_w']), 'ada_bT': np.ascontiguousarray(f(I['ada_b']).reshape(4, 24, 128).transpose(2, 0, 1)),
        'ln_gT': np.ascontiguousarray(f(I['ln_g']).reshape(4, 8, 128).transpose(2, 0, 1)),
        'ln_bT': np.ascontiguousarray(f(I['ln_b']).reshape(4, 8, 128).transpose(2, 0, 1)),
        'ident': np.eye(128, dtype=np.float32),
        'sc_w_in': f(I['sc_w_in'][0]), 'sc_cw': np.ascontiguousarray(f(I['sc_conv_w'][0]).reshape(3, 8, 128).transpose(2, 0, 1)),
        'sc_cb': colT(I['sc_conv_b'][0], 8), 'sc_w_out': f(I['sc_w_out'][0]),
        'rg_w_in': f(I['rg_w_in'][0]), 'rg_cw': np.ascontiguousarray(f(I['rg_conv_w'][0]).reshape(4, 8, 128).transpose(2, 0, 1)),
        'rg_cb': colT(I['rg_conv_b'][0], 8), 'rg_w_gate': f(I['rg_w_gate'][0]),
        'rg_bg': np.ascontiguousarray(f(I['rg_b_gate'][0]).reshape(2, 4, 4, 128).transpose(3, 0, 1, 2)),
        'rg_lam': np.ascontiguousarray(f(I['rg_lambda'][0]).reshape(2, 8, 128).transpose(2, 0, 1)),
        'rg_w_out': f(I['rg_w_out'][0]),
        'hg_w_in': f(I['hg_w_in'][0]),
        'hg_lbl': np.ascontiguousarray(f(I['hg_lb_logits']).reshape(2, 5, 8, 128).transpose(3, 0, 2, 1).reshape(128, 16, 5)),
        'hg_ng': colT(I['hg_norm_g'][0], 8), 'hg_w_out': f(I['hg_w_out'][0]), 'hg_masks': masks,
        'mla_w_in': f(I['mla_w_in'][0]), 'mla_qn': colT(I['mla_q_norm'][0], 3), 'mla_kvn': colT(I['mla_kv_norm'][0], 2),
        'mla_w_qb': wqb, 'mla_w_qbsw': np.ascontiguousarray(qb_r[:, :, swap].reshape(384, 512)),
        'mla_w_kpesw': np.ascontiguousarray(f(I['mla_w_in'][0])[:, 640:672][:, swap]),
        'mla_w_kvb': f(I['mla_w_kvb'][0]), 'mla_w_out': f(I['mla_w_out'][0]), 'rope_cs': cs,
    }
    maps = []
    xp = f(I['x_prompt']); xs = f(I['x_sample'])
    for cid in range(8):
        k = cid // 2
        m = dict(shared)
        m['xin'] = np.ascontiguousarray(np.concatenate([xp[4 * cid:4 * cid + 4].reshape(1024, D), xs[k]], 0))
        cond = np.stack([f(I['c_ctx']), f(I['c'])[k]], 1)
        m['condT'] = np.ascontiguousarray(cond.reshape(8, 128, 2).transpose(1, 0, 2))
        m['rg_h0'] = np.ascontiguousarray(f(I['state_rglru'])[k, 0].reshape(2, 8, 128).transpose(2, 0, 1))
        m['hg_s0'] = f(I['state_hgrn'])[k, 0]
        m['mla_ckv_ctx'] = f(I['cache_mla_ckv'])[k, 0]; m['mla_kpe_ctx'] = f(I['cache_mla_kpe'])[k, 0]
        maps.append(m)
    return maps

def run_layers(I, layers, trace=False):
    key = tuple(layers)
    if key not in _CACHE:
        _CACHE[key] = build_nc(layers)
    nc, C = _CACHE[key]
    maps = make_in_maps(I)
    res = run_bass_kernel_spmd(nc, maps, core_ids=list(range(8)), trace=trace)
    R = res.results
    y_prompt = np.concatenate([R[c]['y'][:1024].reshape(4, 256, D) for c in range(8)], 0)
    y_sample = np.stack([R[2 * k]['y'][1024:] for k in range(4)], 0)
    o_hg = np.concatenate([R[c]['o_hg'] for c in range(8)], 0)[:, None]
    o_rg = np.concatenate([R[c]['o_rg'].reshape(4, 2, D) for c in range(8)], 0)[:, None]
    o_ckv = np.concatenate([R[c]['o_ckv'].reshape(4, 256, 256) for c in range(8)], 0)[:, None]
    o_kpe = np.concatenate([R[c]['o_kpe'].reshape(4, 256, 32) for c in range(8)], 0)[:, None]
    outs = tuple(np.ascontiguousarray(a, dtype=np.float32) for a in (y_prompt, y_sample, o_hg, o_rg, o_ckv, o_kpe))
    return outs, res

def kernel(**inputs):
    outs, _ = run_layers(inputs, LAYERS)
    return outs
```
